# Optimizing a Trainium2 kernel written in Bass

```python
import math
import jax
import jax.numpy as jnp
from jax import lax
import numpy as np


D_MODEL = 1024
BATCH = 4
SEQ = 8192
DEPTH = 2

HEAD_DIM = 64
EPS = 1e-6
WIN_Q_HEADS = 6
WIN_KV_HEADS = 2
WIN_GROUP = WIN_Q_HEADS // WIN_KV_HEADS
WINDOW = 128
WIN_BLOCK = WINDOW
DIFF_HEADS = 4
DIFF_QK_DIM = HEAD_DIM // 2
DIFF_V_DIM = HEAD_DIM
Q_BLOCK = 128
DN_HEADS = 6
DN_DK = HEAD_DIM
DN_DV = HEAD_DIM
DN_CHUNK = 64
CONV_W = 5
WIN_WIDTH = WIN_Q_HEADS * HEAD_DIM
DIFF_WIDTH = DIFF_HEADS * DIFF_V_DIM
DN_WIDTH = DN_HEADS * DN_DV
MIX_WIDTH = WIN_WIDTH + DIFF_WIDTH + DN_WIDTH
IN_SPLITS = (
    WIN_Q_HEADS * HEAD_DIM,
    WIN_KV_HEADS * HEAD_DIM,
    WIN_KV_HEADS * HEAD_DIM,
    DIFF_HEADS * 2 * DIFF_QK_DIM,
    DIFF_HEADS * 2 * DIFF_QK_DIM,
    DIFF_HEADS * DIFF_V_DIM,
    DN_HEADS * (2 * DN_DK + DN_DV),
    DN_HEADS * DN_DV,
    2 * DN_HEADS,
    2 * DN_HEADS,
)
MIX_IN = sum(IN_SPLITS)
D_FF = 2752

kernel_name = 'hybrid_parallel_head_encoder'


def rms_norm(x, g):
    xf = x.astype(jnp.float32)
    y = xf * lax.rsqrt(jnp.mean(xf * xf, axis=-1, keepdims=True) + EPS)
    return (y * g.astype(jnp.float32)).astype(x.dtype)


def l2_norm(x):
    xf = x.astype(jnp.float32)
    return xf * lax.rsqrt(jnp.sum(xf * xf, axis=-1, keepdims=True) + EPS)


def swiglu(h, w_in, w_out):
    gate, up = jnp.split(h @ w_in, 2, axis=-1)
    return (jax.nn.silu(gate) * up) @ w_out


def alibi_slopes(n):
    return 2.0 ** (-8.0 * jnp.arange(1, n + 1, dtype=jnp.float32) / n)


def window_attention(q, k, v, sink, slopes):
    B, S, KH, G, d = q.shape
    nb = S // WIN_BLOCK
    kw_len = 3 * WIN_BLOCK

    def band(t):
        tp = jnp.pad(t, ((0, 0), (WIN_BLOCK, WIN_BLOCK), (0, 0), (0, 0)))
        tb = tp.reshape(B, nb + 2, WIN_BLOCK, KH, d)
        return jnp.concatenate([tb[:, :-2], tb[:, 1:-1], tb[:, 2:]], axis=2)

    kw, vw = band(k), band(v)
    qb = q.reshape(B, nb, WIN_BLOCK, KH, G, d)
    rel = jnp.arange(kw_len)[None, :] - WIN_BLOCK - jnp.arange(WIN_BLOCK)[:, None]
    kpos = jnp.arange(nb)[:, None] * WIN_BLOCK - WIN_BLOCK + jnp.arange(kw_len)[None, :]
    valid = (jnp.abs(rel) <= WINDOW)[None] & ((kpos >= 0) & (kpos < S))[:, None, :]
    dist = jnp.abs(rel).astype(jnp.float32)
    s = jnp.einsum('bnqhgd,bnkhd->bnhgqk', qb, kw).astype(jnp.float32) * (d ** -0.5)
    s = s - slopes[:, :, None, None] * dist
    s = jnp.where(valid[None, :, None, None], s, -1e30)
    sink_b = sink.astype(jnp.float32)[:, :, None, None]
    m = jnp.maximum(jnp.max(s, axis=-1, keepdims=True), sink_b)
    e = jnp.exp(s - m)
    p = e / (jnp.sum(e, axis=-1, keepdims=True) + jnp.exp(sink_b - m))
    o = jnp.einsum('bnhgqk,bnkhd->bnqhgd', p.astype(v.dtype), vw)
    return o.reshape(B, S, KH * G * d)


def diff_attention(q, k, v, lam, slopes):
    B, S, H, _, dq = q.shape
    dv = v.shape[-1]
    nb = S // Q_BLOCK
    qb = jnp.moveaxis(q.reshape(B, nb, Q_BLOCK, H, 2, dq), 1, 0)
    kpos = jnp.arange(S)

    def one_block(args):
        q_blk, i = args
        qpos = i * Q_BLOCK + jnp.arange(Q_BLOCK)
        dist = jnp.abs(qpos[:, None] - kpos[None, :]).astype(jnp.float32)
        s = jnp.einsum('bqhmd,bkhmd->bhmqk', q_blk, k).astype(jnp.float32) * (dq ** -0.5)
        s = s - slopes[:, None, None, None] * dist
        p = jax.nn.softmax(s, axis=-1)
        pd = p[:, :, 0] - lam * p[:, :, 1]
        return jnp.einsum('bhqk,bkhd->bqhd', pd.astype(v.dtype), v)

    o = lax.map(one_block, (qb, jnp.arange(nb)))
    return jnp.moveaxis(o, 0, 1).reshape(B, S, H, dv)


def short_conv(x, w):
    return lax.conv_general_dilated(
        x, w[:, None, :].astype(x.dtype), window_strides=(1,),
        padding=[(CONV_W // 2, CONV_W // 2)],
        dimension_numbers=('NWC', 'WIO', 'NWC'),
        feature_group_count=x.shape[-1])


def gated_delta_chunked(q, k, v, beta, g):
    B, S, H, dk = q.shape
    dv = v.shape[-1]
    C = DN_CHUNK
    n = S // C
    f32 = jnp.float32
    ch = lambda t: t.astype(f32).reshape(B, n, C, H, t.shape[-1]).transpose(0, 3, 1, 2, 4)
    chs = lambda t: t.astype(f32).reshape(B, n, C, H).transpose(0, 3, 1, 2)
    qc, kc, vc = ch(q), ch(k), ch(v)
    bc, gc = chs(beta), jnp.cumsum(chs(g), axis=-1)
    tril = jnp.tri(C, dtype=bool)
    tril_strict = jnp.tri(C, k=-1, dtype=bool)
    gdiff = gc[..., :, None] - gc[..., None, :]
    decay = jnp.where(tril, jnp.exp(jnp.where(tril, gdiff, 0.0)), 0.0)
    kb = kc * bc[..., None]
    vb = vc * bc[..., None]
    m_low = jnp.where(tril_strict, jnp.einsum('bhncd,bhnsd->bhncs', kb, kc) * decay, 0.0)
    a_mat = jnp.eye(C, dtype=f32) + m_low
    u = lax.linalg.triangular_solve(a_mat, vb, left_side=True, lower=True, unit_diagonal=True)
    w = lax.linalg.triangular_solve(a_mat, kb * jnp.exp(gc)[..., None], left_side=True,
                                    lower=True, unit_diagonal=True)
    qk = jnp.where(tril, jnp.einsum('bhncd,bhnsd->bhncs', qc, kc) * decay, 0.0)
    q_g = qc * jnp.exp(gc)[..., None]
    k_g = kc * jnp.exp(gc[..., -1:] - gc)[..., None]
    g_last = jnp.exp(gc[..., -1])

    def step(state, inp):
        u_i, w_i, qk_i, qg_i, kg_i, gl_i = inp
        v_new = u_i - jnp.einsum('bhck,bhkv->bhcv', w_i, state)
        o_i = jnp.einsum('bhck,bhkv->bhcv', qg_i, state) + jnp.einsum('bhcs,bhsv->bhcv', qk_i, v_new)
        state = state * gl_i[..., None, None] + jnp.einsum('bhck,bhcv->bhkv', kg_i, v_new)
        return state, o_i

    mv = lambda t: jnp.moveaxis(t, 2, 0)
    state0 = jnp.zeros((B, H, dk, dv), f32)
    _, o = lax.scan(step, state0, (mv(u), mv(w), mv(qk), mv(q_g), mv(k_g), mv(g_last)))
    o = o.transpose(1, 0, 3, 2, 4).reshape(B, S, H, dv)
    return o.astype(v.dtype)


def hybrid_mixer(h, w_in, conv_w, sink, diff_lam, diff_g, a_log, dt_bias, dn_g, w_out, lam_init):
    B, S, _ = h.shape
    f32 = jnp.float32
    cuts = [int(c) for c in np.cumsum(IN_SPLITS)[:-1]]
    (wq, wk, wv, dq, dk, dv, dn_qkv, dn_z, dn_beta, dn_a) = jnp.split(h @ w_in, cuts, axis=-1)

    o_win = window_attention(
        wq.reshape(B, S, WIN_KV_HEADS, WIN_GROUP, HEAD_DIM),
        wk.reshape(B, S, WIN_KV_HEADS, HEAD_DIM),
        wv.reshape(B, S, WIN_KV_HEADS, HEAD_DIM),
        sink.reshape(WIN_KV_HEADS, WIN_GROUP),
        alibi_slopes(WIN_Q_HEADS).reshape(WIN_KV_HEADS, WIN_GROUP))

    lq1, lk1, lq2, lk2 = diff_lam.astype(f32)
    lam = jnp.exp(jnp.sum(lq1 * lk1)) - jnp.exp(jnp.sum(lq2 * lk2)) + lam_init
    o_d = diff_attention(
        dq.reshape(B, S, DIFF_HEADS, 2, DIFF_QK_DIM),
        dk.reshape(B, S, DIFF_HEADS, 2, DIFF_QK_DIM),
        dv.reshape(B, S, DIFF_HEADS, DIFF_V_DIM),
        lam, alibi_slopes(DIFF_HEADS))
    o_diff = (rms_norm(o_d, diff_g) * (1.0 - lam_init)).reshape(B, S, DIFF_WIDTH)

    qkv = jax.nn.silu(short_conv(dn_qkv, conv_w))
    q, k, v = jnp.split(qkv, [DN_HEADS * DN_DK, 2 * DN_HEADS * DN_DK], axis=-1)
    q = (l2_norm(q.reshape(B, S, DN_HEADS, DN_DK)) * (DN_DK ** -0.5)).astype(h.dtype)
    k = l2_norm(k.reshape(B, S, DN_HEADS, DN_DK)).astype(h.dtype)
    v = v.reshape(B, S, DN_HEADS, DN_DV)
    beta = jax.nn.sigmoid(dn_beta.astype(f32)).reshape(B, S, 2, DN_HEADS)
    g = -jnp.exp(a_log.astype(f32)) * jax.nn.softplus(
        dn_a.astype(f32).reshape(B, S, 2, DN_HEADS) + dt_bias.astype(f32))
    o_fwd = gated_delta_chunked(q, k, v, beta[:, :, 0], g[:, :, 0])
    flip = lambda t: jnp.flip(t, axis=1)
    o_bwd = flip(gated_delta_chunked(flip(q), flip(k), flip(v), flip(beta[:, :, 1]), flip(g[:, :, 1])))
    o_c = o_fwd + o_bwd
    o_dn = (rms_norm(o_c, dn_g) * jax.nn.silu(dn_z.reshape(B, S, DN_HEADS, DN_DV))).reshape(B, S, DN_WIDTH)

    o = jnp.concatenate([o_win, o_diff.astype(h.dtype), o_dn.astype(h.dtype)], axis=-1)
    return o @ w_out


def setup_inputs(seed: int = 0) -> dict:
    key = jax.random.key(seed)
    ks = jax.random.split(key, 20)
    f32 = jnp.float32

    def dense(k, shape, fan_in):
        return jax.random.normal(k, shape, f32) * fan_in ** -0.5

    def gain(k, shape):
        return 1.0 + 0.02 * jax.random.normal(k, shape, f32)

    x = jax.random.normal(ks[0], (BATCH, SEQ, D_MODEL), f32)
    ln_ffn1 = gain(ks[1], (DEPTH, D_MODEL))
    ffn1_w_in = dense(ks[2], (DEPTH, D_MODEL, 2 * D_FF), D_MODEL)
    ffn1_w_out = dense(ks[3], (DEPTH, D_FF, D_MODEL), D_FF)
    ln_mix = gain(ks[4], (DEPTH, D_MODEL))
    w_mix_in = dense(ks[5], (DEPTH, D_MODEL, MIX_IN), D_MODEL)
    conv_w = dense(ks[6], (DEPTH, CONV_W, DN_HEADS * (2 * DN_DK + DN_DV)), CONV_W)
    sink_logits = 0.5 * jax.random.normal(ks[7], (DEPTH, WIN_Q_HEADS), f32)
    diff_lambda = 0.1 * jax.random.normal(ks[8], (DEPTH, 4, DIFF_QK_DIM), f32)
    diff_norm_g = gain(ks[9], (DEPTH, DIFF_V_DIM))
    dn_A_log = jnp.log(jax.random.uniform(ks[10], (DEPTH, 2, DN_HEADS), f32, 1.0, 16.0))
    dt = jnp.exp(jax.random.uniform(ks[11], (DEPTH, 2, DN_HEADS), f32, math.log(1e-3), math.log(1e-1)))
    dn_dt_bias = dt + jnp.log(-jnp.expm1(-dt))
    dn_norm_g = gain(ks[12], (DEPTH, DN_DV))
    w_mix_out = dense(ks[13], (DEPTH, MIX_WIDTH, D_MODEL), MIX_WIDTH)
    ln_ffn2 = gain(ks[14], (DEPTH, D_MODEL))
    ffn2_w_in = dense(ks[15], (DEPTH, D_MODEL, 2 * D_FF), D_MODEL)
    ffn2_w_out = dense(ks[16], (DEPTH, D_FF, D_MODEL), D_FF)
    ln_final = gain(ks[17], (D_MODEL,))
    return {'x': x, 'ln_ffn1': ln_ffn1, 'ffn1_w_in': ffn1_w_in, 'ffn1_w_out': ffn1_w_out,
            'ln_mix': ln_mix, 'w_mix_in': w_mix_in, 'conv_w': conv_w, 'sink_logits': sink_logits,
            'diff_lambda': diff_lambda, 'diff_norm_g': diff_norm_g, 'dn_A_log': dn_A_log,
            'dn_dt_bias': dn_dt_bias, 'dn_norm_g': dn_norm_g, 'w_mix_out': w_mix_out,
            'ln_ffn2': ln_ffn2, 'ffn2_w_in': ffn2_w_in, 'ffn2_w_out': ffn2_w_out, 'ln_final': ln_final}


def reference(x, ln_ffn1, ffn1_w_in, ffn1_w_out, ln_mix, w_mix_in, conv_w, sink_logits,
              diff_lambda, diff_norm_g, dn_A_log, dn_dt_bias, dn_norm_g, w_mix_out,
              ln_ffn2, ffn2_w_in, ffn2_w_out, ln_final):
    for l in range(DEPTH):
        lam_init = 0.8 - 0.6 * math.exp(-0.3 * l)
        x = x + 0.5 * swiglu(rms_norm(x, ln_ffn1[l]), ffn1_w_in[l], ffn1_w_out[l])
        h = rms_norm(x, ln_mix[l])
        x = x + hybrid_mixer(h, w_mix_in[l], conv_w[l], sink_logits[l], diff_lambda[l],
                             diff_norm_g[l], dn_A_log[l], dn_dt_bias[l], dn_norm_g[l],
                             w_mix_out[l], lam_init)
        x = x + 0.5 * swiglu(rms_norm(x, ln_ffn2[l]), ffn2_w_in[l], ffn2_w_out[l])
    return rms_norm(x, ln_final)
```

```python
import numpy as np
import concourse.bass as bass
import concourse.mybir as mybir

F32 = mybir.dt.float32
BF16 = mybir.dt.bfloat16
AF = mybir.ActivationFunctionType
ALU = mybir.AluOpType
AX = mybir.AxisListType


class Buf:
    __slots__ = ("name", "w", "rs", "excl")

    def __init__(self, name, excl=False):
        self.name = name
        self.excl = excl
        self.w = None
        self.rs = []


class Sched:
    NDMA = 24

    def __init__(self, nc, stack):
        self.nc = nc
        self.eng = {"pe": nc.tensor, "act": nc.scalar, "dve": nc.vector, "pool": nc.gpsimd, "sp": nc.sync}
        self.sem = {}
        self.cnt = {}
        for k in self.eng:
            self.sem[k] = stack.enter_context(nc.semaphore("sem_" + k))
            self.cnt[k] = 0
        self.dsem = [stack.enter_context(nc.semaphore("dsem%d" % i)) for i in range(self.NDMA)]
        self.dgen = [0] * self.NDMA
        self.qslots = {"sp": list(range(0, 16)), "pool": list(range(16, 20)), "act": list(range(20, 24))}
        self.qnext = {"sp": 0, "pool": 0, "act": 0}
        self.waited = {k: {} for k in self.eng}
        self.pending_pe = False
        self.n_ins = 0
        self.n_wait = 0

    def _semobj(self, key):
        if isinstance(key, int):
            return self.dsem[key]
        return self.sem[key]

    def _need(self, e, evs):
        best = {}
        for ev in evs:
            if ev is None:
                continue
            k, v = ev
            if k == e and e in ("pe", "sp"):
                continue
            if best.get(k, 0) < v:
                best[k] = v
        w = self.waited[e]
        for k, v in best.items():
            if w.get(k, 0) >= v:
                continue
            self.eng[e].wait_ge(self._semobj(k), v)
            self.n_wait += 1
            w[k] = v

    def _deps(self, reads, writes, e=None):
        evs = []
        for b in reads:
            evs.append(b.w)
            if b.excl:
                evs.extend(r for r in b.rs if r[0] != e)
        for b in writes:
            evs.append(b.w)
            evs.extend(b.rs)
        return evs

    def _commit(self, ev, reads, writes):
        for b in reads:
            b.rs.append(ev)
            if len(b.rs) > 64:
                mx = {}
                for k, v in b.rs:
                    if mx.get(k, 0) < v:
                        mx[k] = v
                b.rs = list(mx.items())
        for b in writes:
            b.w = ev
            b.rs = []

    def op(self, e, fn, reads=(), writes=(), signal=True):
        if e != "pe":
            assert not self.pending_pe, "non-signaling PE op must be followed by signaling PE op"
        self._need(e, self._deps(reads, writes, e))
        ins = fn(self.eng[e])
        self.n_ins += 1
        if e == "pe":
            self.pending_pe = not signal
        if signal:
            self.cnt[e] += 1
            ins.then_inc(self.sem[e], 1)
            ev = (e, self.cnt[e])
        else:
            ev = (e, self.cnt[e] + 1)
        self._commit(ev, reads, writes)
        return ins

    def dma(self, out, in_, reads=(), writes=(), q="sp", **kw):
        assert not self.pending_pe
        sl = self.qslots[q]
        slot = sl[self.qnext[q] % len(sl)]
        self.qnext[q] += 1
        evs = self._deps(reads, writes)
        if self.dgen[slot] > 0:
            evs.append((slot, 16 * self.dgen[slot]))
        self._need(q, evs)
        self.dgen[slot] += 1
        ins = self.eng[q].dma_start(out=out, in_=in_, **kw)
        ins.then_inc(self.dsem[slot], 16)
        self.n_ins += 1
        ev = (slot, 16 * self.dgen[slot])
        self._commit(ev, reads, writes)
        return ins

    def barrier(self):
        evs = [(k, self.cnt[k]) for k in self.eng if self.cnt[k] > 0]
        evs += [(i, 16 * self.dgen[i]) for i in range(self.NDMA) if self.dgen[i] > 0]
        for e in self.eng:
            w = self.waited[e]
            for k, v in evs:
                if k == e and e in ("pe", "sp"):
                    continue
                if w.get(k, 0) >= v:
                    continue
                self.eng[e].wait_ge(self._semobj(k), v)
                w[k] = v

    def collective(self, stack, kind, in_ap, out_ap, groups):
        import concourse.mybir as mybir
        self.barrier()
        sem = stack.enter_context(self.nc.semaphore("ccsem%d" % self.n_ins))
        g = self.eng["pool"]
        g.collective_compute(kind, mybir.AluOpType.bypass, replica_groups=groups,
                             ins=[in_ap], outs=[out_ap]).then_inc(sem)
        g.wait_ge(sem, 1)
        self.n_ins += 1
        self.cnt["pool"] += 1
        g.engine_nop().then_inc(self.sem["pool"], 1)
        self.barrier()

    def finish(self, out_bufs):
        self.barrier()

import numpy as np
from contextlib import ExitStack
import concourse.bass as bass
import concourse.mybir as mybir

D = 1024
DFF = 2752
NFC = 22
EPS = 1e-6


class K:
    def __init__(self, S, depth=2, paired=False):
        self.S = S
        self.paired = paired
        self.SK = 2 * S if paired else S
        self.NTK = self.SK // 128
        self.NT = S // 128
        self.depth = depth
        self.nc = bass.Bass("TRN2", target_bir_lowering=False)
        self.stack = ExitStack()
        self.sc = Sched(self.nc, self.stack)
        self.ins = {}
        nc = self.nc
        self.ps = []
        self.psb = []
        for i in range(8):
            t = self.stack.enter_context(nc.psum_tensor("ps%d" % i, [128, 512], F32))
            self.ps.append(t)
            self.psb.append(Buf("ps%d" % i, excl=True))

    def inp(self, name, shape, dt=F32):
        t = self.nc.dram_tensor(name, list(shape), dt, kind="ExternalInput").ap()
        self.ins[name] = t
        return t

    def outp(self, name, shape, dt=F32):
        return self.nc.dram_tensor(name, list(shape), dt, kind="ExternalOutput").ap()

    def scratch(self, name, shape, dt=F32):
        return self.nc.dram_tensor(name, list(shape), dt, kind="Internal").ap()

    def sb(self, st, name, shape, dt=F32):
        return st.enter_context(self.nc.sbuf_tensor(name, list(shape), dt))

    def ffn_phase(self, tag, w_in, w_out, g, src, src_bufs, dst, dst_bufs, ident, ident_b):
        sc = self.sc
        S, NT = self.S, self.NT
        GT = 4 if NT % 4 == 0 else 1
        GW = GT * 128
        NG = NT // GT
        with ExitStack() as st:
            w1 = self.sb(st, tag + "w1", [128, 8, 2 * DFF], BF16)
            w2 = self.sb(st, tag + "w2", [128, NFC, D], BF16)
            gB = self.sb(st, tag + "gB", [128, D], F32)
            xt = [self.sb(st, tag + "xt%d" % i, [128, D], F32) for i in range(2)]
            xr = [self.sb(st, tag + "xr%d" % i, [128, D], F32) for i in range(2)]
            xn = self.sb(st, tag + "xn", [128, D], F32)
            xnT = [self.sb(st, tag + "xnT%d" % i, [128, 8, GW], BF16) for i in range(2)]
            hT = self.sb(st, tag + "hT", [128, NFC, GW], BF16)
            sg = [self.sb(st, tag + "sg%d" % i, [128, GW], F32) for i in range(2)]
            stat = self.sb(st, tag + "stat", [128, 8], F32)
            b_w1 = [Buf("w1_%d" % k) for k in range(8)]
            b_w2 = [Buf("w2_%d" % c) for c in range(NFC)]
            b_gB = Buf("gB")
            b_xt = [Buf("xt0"), Buf("xt1")]
            b_xr = [Buf("xr0"), Buf("xr1")]
            b_xn, b_stat = Buf("xn"), Buf("stat")
            b_xnT = [Buf("xnT0"), Buf("xnT1")]
            b_hT = [Buf("hT%d" % c) for c in range(NFC)]
            b_sg = [Buf("sg0"), Buf("sg1")]
            ps, psb = self.ps, self.psb
            for kc in range(8):
                for hf in range(2):
                    sc.dma(w1[:, kc, hf * DFF:(hf + 1) * DFF], w_in[kc * 128:(kc + 1) * 128, hf * DFF:(hf + 1) * DFF],
                           writes=[b_w1[kc]], q="pool")
            for c in range(NFC):
                cw = min(128, DFF - c * 128)
                sc.dma(w2[:cw, c, :], w_out[c * 128:c * 128 + cw, :], writes=[b_w2[c]], q="pool")
            sc.dma(gB[:], g.partition_broadcast(128), writes=[b_gB])
            tcount = [0]

            def ln_tile(gi, tt):
                t = gi * GT + tt
                xb = tcount[0] % 2
                tcount[0] += 1
                xq = xnT[gi % 2]
                bq = b_xnT[gi % 2]
                sc.dma(xt[xb][:], src[t * 128:(t + 1) * 128, :], reads=[src_bufs[t]], writes=[b_xt[xb]])
                sc.op("act", lambda e: e.activation(out=xn[:], in_=xt[xb][:], func=AF.Square, accum_out=stat[:, 0:1]),
                      reads=[b_xt[xb]], writes=[b_xn, b_stat])
                sc.op("dve", lambda e: e.tensor_scalar(out=stat[:, 1:2], in0=stat[:, 0:1], scalar1=1.0 / D,
                                                       scalar2=EPS, op0=ALU.mult, op1=ALU.add),
                      reads=[b_stat], writes=[b_stat])
                sc.op("act", lambda e: e.sqrt(out=stat[:, 3:4], in_=stat[:, 1:2]), reads=[b_stat], writes=[b_stat])
                sc.op("dve", lambda e: e.reciprocal(out=stat[:, 2:3], in_=stat[:, 3:4]), reads=[b_stat], writes=[b_stat])
                sc.op("dve", lambda e: e.scalar_tensor_tensor(out=xn[:], in0=xt[xb][:], scalar=stat[:, 2:3],
                                                              in1=gB[:], op0=ALU.mult, op1=ALU.mult),
                      reads=[b_xt[xb], b_stat, b_gB], writes=[b_xn])
                for kc in range(8):
                    bank = 6 + kc // 4
                    sc.op("pe", lambda e: e.transpose(ps[bank][:, (kc % 4) * 128:(kc % 4 + 1) * 128],
                                                      xn[:, kc * 128:(kc + 1) * 128], ident[:]),
                          reads=[b_xn, ident_b], writes=[psb[bank]], signal=(kc % 4 == 3))
                sc.op("act", lambda e: e.copy(out=xq[:, 0:4, tt * 128:(tt + 1) * 128],
                                              in_=ps[6][:, :].rearrange("p (k t) -> p k t", k=4)),
                      reads=[psb[6]], writes=[bq])
                sc.op("dve", lambda e: e.tensor_copy(out=xq[:, 4:8, tt * 128:(tt + 1) * 128],
                                                     in_=ps[7][:, :].rearrange("p (k t) -> p k t", k=4)),
                      reads=[psb[7]], writes=[bq])

            for tt in range(GT):
                ln_tile(0, tt)
            for gi in range(NG):
                g0 = gi * GT
                xq = xnT[gi % 2]
                bq = b_xnT[gi % 2]
                for c in range(NFC):
                    cw = min(128, DFF - c * 128)
                    pg, pu = 2 + 2 * (c % 2), 3 + 2 * (c % 2)
                    for kc in range(8):
                        sc.op("pe", lambda e: e.matmul(ps[pg][:cw, :GW], lhsT=w1[:, kc, c * 128:c * 128 + cw],
                                                       rhs=xq[:, kc, :], start=(kc == 0), stop=(kc == 7)),
                              reads=[b_w1[kc], bq], writes=[psb[pg]], signal=(kc == 7))
                    for kc in range(8):
                        sc.op("pe", lambda e: e.matmul(ps[pu][:cw, :GW],
                                                       lhsT=w1[:, kc, DFF + c * 128:DFF + c * 128 + cw],
                                                       rhs=xq[:, kc, :], start=(kc == 0), stop=(kc == 7)),
                              reads=[b_w1[kc], bq], writes=[psb[pu]], signal=(kc == 7))
                    sc.op("act", lambda e: e.activation(out=sg[c % 2][:cw, :], in_=ps[pg][:cw, :GW], func=AF.Silu),
                          reads=[psb[pg]], writes=[b_sg[c % 2]])
                    sc.op("dve", lambda e: e.tensor_tensor(out=hT[:cw, c, :], in0=ps[pu][:cw, :GW],
                                                           in1=sg[c % 2][:cw, :], op=ALU.mult),
                          reads=[psb[pu], b_sg[c % 2]], writes=[b_hT[c]])
                for tt in range(GT):
                    t = g0 + tt
                    rb = t % 2
                    sc.dma(xr[rb][:], src[t * 128:(t + 1) * 128, :], reads=[src_bufs[t]], writes=[b_xr[rb]])
                    for hf in range(2):
                        for c in range(NFC):
                            cw = min(128, DFF - c * 128)
                            sc.op("pe", lambda e: e.matmul(ps[hf][:, :], lhsT=hT[:cw, c, tt * 128:(tt + 1) * 128],
                                                           rhs=w2[:cw, c, hf * 512:(hf + 1) * 512],
                                                           start=(c == 0), stop=(c == NFC - 1)),
                                  reads=[b_hT[c], b_w2[c]], writes=[psb[hf]], signal=(c == NFC - 1))
                    if gi + 1 < NG:
                        ln_tile(gi + 1, tt)
                    for hf in range(2):
                        sc.op("dve", lambda e: e.scalar_tensor_tensor(
                            out=xr[rb][:, hf * 512:(hf + 1) * 512], in0=ps[hf][:, :], scalar=0.5,
                            in1=xr[rb][:, hf * 512:(hf + 1) * 512], op0=ALU.mult, op1=ALU.add),
                              reads=[psb[hf], b_xr[rb]], writes=[b_xr[rb]])
                    sc.dma(dst[t * 128:(t + 1) * 128, :], xr[rb][:], reads=[b_xr[rb]], writes=[dst_bufs[t]])
            sc.barrier()


HD = 64
MIX_IN = 2968
C_AQ, C_AK, C_AV = 0, 384, 512
C_BQ, C_BK, C_BV = 640, 896, 1152
C_CQKV, C_CZ, C_CB, C_CA = 1408, 2560, 2944, 2956
FM_CHUNKS = ([(C_AQ + 128 * i, "aq", i) for i in range(3)] + [(C_AK, "ak", 0)] +
             [(C_BQ + 128 * i, "bq", i) for i in range(2)] + [(C_BK + 128 * i, "bk", i) for i in range(2)] +
             [(C_CQKV + 128 * i, "c", i) for i in range(9)])


def _mix_scratch(self):
    S = self.S
    SK = self.SK
    SA = S + 128 if self.paired else S
    d = {}
    d["aqT"] = self.scratch("aqT", [384, S], BF16)
    d["akT"] = self.scratch("akT", [128, SA], BF16)
    d["av"] = self.scratch("av", [SA, 2, 65], BF16)
    d["bqT"] = self.scratch("bqT", [256, S], BF16)
    d["bkT"] = self.scratch("bkT", [256, SK], BF16)
    d["bv"] = self.scratch("bv", [SK, 4, 65], BF16)
    d["cpre"] = self.scratch("cpre", [1152, S + 4], F32)
    d["cz"] = self.scratch("cz", [S, 408], F32)
    d["omix"] = self.scratch("omix", [S, D], F32)
    return d


K.mix_scratch = _mix_scratch


def _inproj_phase(self, tag, w_mi, g, src, src_bufs, ms, ident, ident_b):
    sc = self.sc
    S, NT = self.S, self.NT
    GT = 4 if NT % 4 == 0 else 1
    GW = GT * 128
    ps, psb = self.ps, self.psb
    with ExitStack() as st:
        wm = self.sb(st, tag + "wm", [128, 8, MIX_IN], BF16)
        gB = self.sb(st, tag + "gB", [128, D], F32)
        xt = [self.sb(st, tag + "xt%d" % i, [128, D], F32) for i in range(2)]
        xn = self.sb(st, tag + "xn", [128, D], F32)
        junk = self.sb(st, tag + "junk", [128, D], F32)
        xnT = self.sb(st, tag + "xnT", [128, 8, GW], BF16)
        stat = self.sb(st, tag + "stat", [128, 8], F32)
        ob16 = [self.sb(st, tag + "ob16_%d" % i, [128, GW], BF16) for i in range(3)]
        of32 = [self.sb(st, tag + "of32_%d" % i, [128, GW], F32) for i in range(3)]
        tv = [self.sb(st, tag + "tv%d" % i, [128, 6, 65], BF16) for i in range(2)]
        zt = self.sb(st, tag + "zt", [128, 2], F32)
        tz = [self.sb(st, tag + "tz%d" % i, [128, 408], F32) for i in range(2)]
        b_wm = [Buf("wm%d" % k) for k in range(8)]
        b_gB = Buf("gB")
        b_xt = [Buf("xt0"), Buf("xt1")]
        b_xn, b_junk, b_xnT, b_stat = Buf("xn"), Buf("junk"), Buf("xnT"), Buf("stat")
        b_ob16 = [Buf("ob16_%d" % i) for i in range(3)]
        b_of32 = [Buf("of32_%d" % i) for i in range(3)]
        b_tv = [Buf("tv0"), Buf("tv1")]
        b_tz = [Buf("tz0"), Buf("tz1")]
        for kc in range(8):
            sc.dma(wm[:, kc, :], w_mi[kc * 128:(kc + 1) * 128, :], writes=[b_wm[kc]], q="pool")
        sc.dma(gB[:], g.partition_broadcast(128), writes=[b_gB])
        b_zt = Buf("zt")
        sc.op("pool", lambda e: e.memset(zt[:], 0.0), writes=[b_zt])
        for i in range(9):
            sc.dma(ms["cpre"][i * 128:(i + 1) * 128, 0:2], zt[:, :], reads=[b_zt])
            if not self.paired:
                sc.dma(ms["cpre"][i * 128:(i + 1) * 128, S + 2:S + 4], zt[:, :], reads=[b_zt])
        for i in range(2):
            sc.op("pool", lambda e: e.memset(tv[i][:], 1.0), writes=[b_tv[i]])
        ti = 0
        n16 = n32 = 0
        for g0 in range(0, NT, GT):
            for tt in range(GT):
                t = g0 + tt
                xb = ti % 2
                ti += 1
                sc.dma(xt[xb][:], src[t * 128:(t + 1) * 128, :], reads=[src_bufs[t]], writes=[b_xt[xb]])
                sc.op("act", lambda e: e.activation(out=junk[:], in_=xt[xb][:], func=AF.Square,
                                                    accum_out=stat[:, 0:1]),
                      reads=[b_xt[xb]], writes=[b_junk, b_stat])
                sc.op("dve", lambda e: e.tensor_scalar(out=stat[:, 1:2], in0=stat[:, 0:1], scalar1=1.0 / D,
                                                       scalar2=EPS, op0=ALU.mult, op1=ALU.add),
                      reads=[b_stat], writes=[b_stat])
                sc.op("act", lambda e: e.sqrt(out=stat[:, 3:4], in_=stat[:, 1:2]), reads=[b_stat], writes=[b_stat])
                sc.op("dve", lambda e: e.reciprocal(out=stat[:, 2:3], in_=stat[:, 3:4]),
                      reads=[b_stat], writes=[b_stat])
                sc.op("dve", lambda e: e.scalar_tensor_tensor(out=xn[:], in0=xt[xb][:], scalar=stat[:, 2:3],
                                                              in1=gB[:], op0=ALU.mult, op1=ALU.mult),
                      reads=[b_xt[xb], b_stat, b_gB], writes=[b_xn])
                for kc in range(8):
                    bank = kc // 4
                    sc.op("pe", lambda e: e.transpose(ps[bank][:, (kc % 4) * 128:(kc % 4 + 1) * 128],
                                                      xn[:, kc * 128:(kc + 1) * 128], ident[:]),
                          reads=[b_xn, ident_b], writes=[psb[bank]], signal=(kc % 4 == 3))
                sc.op("act", lambda e: e.copy(out=xnT[:, 0:4, tt * 128:(tt + 1) * 128],
                                              in_=ps[0][:, :].rearrange("p (k t) -> p k t", k=4)),
                      reads=[psb[0]], writes=[b_xnT])
                sc.op("dve", lambda e: e.tensor_copy(out=xnT[:, 4:8, tt * 128:(tt + 1) * 128],
                                                     in_=ps[1][:, :].rearrange("p (k t) -> p k t", k=4)),
                      reads=[psb[1]], writes=[b_xnT])
            tok0 = g0 * 128
            for ci, (c0, kind, idx) in enumerate(FM_CHUNKS):
                pb = 2 + ci % 3
                for kc in range(8):
                    sc.op("pe", lambda e: e.matmul(ps[pb][:, :GW], lhsT=wm[:, kc, c0:c0 + 128], rhs=xnT[:, kc, :],
                                                   start=(kc == 0), stop=(kc == 7)),
                          reads=[b_wm[kc], b_xnT], writes=[psb[pb]], signal=(kc == 7))
                if kind == "c":
                    o = n32 % 3
                    n32 += 1
                    sc.op("dve" if ci % 2 else "act",
                          (lambda e: e.tensor_copy(out=of32[o][:, :], in_=ps[pb][:, :GW])) if ci % 2 else
                          (lambda e: e.copy(out=of32[o][:, :], in_=ps[pb][:, :GW])),
                          reads=[psb[pb]], writes=[b_of32[o]])
                    sc.dma(ms["cpre"][idx * 128:(idx + 1) * 128, 2 + tok0:2 + tok0 + GW], of32[o][:, :],
                           reads=[b_of32[o]])
                else:
                    o = n16 % 3
                    n16 += 1
                    scale = {"aq": HD ** -0.5, "bq": 32 ** -0.5, "ak": 1.0, "bk": 1.0}[kind]
                    sc.op("act", lambda e: e.mul(out=ob16[o][:, :], in_=ps[pb][:, :GW], mul=scale),
                          reads=[psb[pb]], writes=[b_ob16[o]])
                    dst = {"aq": ms["aqT"], "ak": ms["akT"], "bq": ms["bqT"], "bk": ms["bkT"]}[kind]
                    sc.dma(dst[idx * 128:(idx + 1) * 128, tok0:tok0 + GW], ob16[o][:, :], reads=[b_ob16[o]])
            for tt in range(GT):
                t = g0 + tt
                r0 = t * 128
                o = t % 2
                for (pb, c0, cw) in ((5, C_AV, 128), (6, C_BV, 256), (7, C_CZ, 408)):
                    for kc in range(8):
                        sc.op("pe", lambda e: e.matmul(ps[pb][:, :cw], lhsT=xnT[:, kc, tt * 128:(tt + 1) * 128],
                                                       rhs=wm[:, kc, c0:c0 + cw], start=(kc == 0), stop=(kc == 7)),
                              reads=[b_wm[kc], b_xnT], writes=[psb[pb]], signal=(kc == 7))
                sc.op("act", lambda e: e.copy(out=tv[o][:, 0:2, 0:64],
                                              in_=ps[5][:, 0:128].rearrange("p (h d) -> p h d", h=2)),
                      reads=[psb[5]], writes=[b_tv[o]])
                sc.op("dve", lambda e: e.tensor_copy(out=tv[o][:, 2:6, 0:64],
                                                     in_=ps[6][:, 0:256].rearrange("p (h d) -> p h d", h=4)),
                      reads=[psb[6]], writes=[b_tv[o]])
                sc.op("act", lambda e: e.copy(out=tz[o][:, :], in_=ps[7][:, 0:408]),
                      reads=[psb[7]], writes=[b_tz[o]])
                sc.dma(ms["av"][r0:r0 + 128, :, :], tv[o][:, 0:2, :], reads=[b_tv[o]])
                sc.dma(ms["bv"][r0:r0 + 128, :, :], tv[o][:, 2:6, :], reads=[b_tv[o]])
                sc.dma(ms["cz"][r0:r0 + 128, :], tz[o][:, :], reads=[b_tz[o]])
        sc.barrier()


K.inproj_phase = _inproj_phase


def _diffattn_phase(self, tag, ms, cst, dlam, dg, lam_init, identf, b_identf, hook=None):
    sc = self.sc
    S, NT = self.S, self.NT
    SK, NTK = self.SK, self.NTK
    GT = 4 if NT % 4 == 0 else 1
    GW = GT * 128
    NG = NT // GT
    ps, psb = self.ps, self.psb
    with ExitStack() as st:
        kTa = self.sb(st, tag + "kTa", [128, SK], BF16)
        kTb = self.sb(st, tag + "kTb", [128, SK], BF16)
        qTa = self.sb(st, tag + "qTa", [128, S], BF16)
        vA = self.sb(st, tag + "vA", [128, NTK, 65], BF16)
        bd = self.sb(st, tag + "bd", [128, 128], BF16)
        idb = self.sb(st, tag + "idb", [128, 128], BF16)
        oT = self.sb(st, tag + "oT", [65, GW], F32)
        b_oT = Buf("oT")
        pT = [self.sb(st, tag + "pT%d" % i, [128, GW], BF16) for i in range(3)]
        om = [self.sb(st, tag + "om%d" % i, [128, NT, 64], F32) for i in range(2)]
        dd = self.sb(st, tag + "dd", [128, NT, 64], F32)
        sq = self.sb(st, tag + "sq", [128, NT, 64], F32)
        ssq = self.sb(st, tag + "ssq", [128, NT], F32)
        rec = self.sb(st, tag + "rec", [128, 8], F32)
        lmb = self.sb(st, tag + "lmb", [128, 128], F32)
        lw = self.sb(st, tag + "lw", [128, 64], F32)
        ls = self.sb(st, tag + "ls", [128, 8], F32)
        gd = self.sb(st, tag + "gd", [128, 64], F32)
        b_kTa, b_kTb, b_qTa, b_vA, b_bd, b_idb = (Buf(n) for n in ("kTa", "kTb", "qTa", "vA", "bd", "idb"))
        b_pT = [Buf("pT%d" % i) for i in range(3)]
        b_om = [Buf("om0"), Buf("om1")]
        b_dd, b_sq, b_ssq, b_rec, b_lmb, b_lw, b_ls, b_gd = (Buf(n) for n in
                                                             ("dd", "sq", "ssq", "rec", "lmb", "lw", "ls", "gd"))
        sc.dma(idb[:], cst["identb"][:, :], writes=[b_idb])
        sc.op("dve", lambda e: e.memset(kTa[:], 0.0), writes=[b_kTa])
        sc.op("dve", lambda e: e.memset(kTb[:], 0.0), writes=[b_kTb])
        sc.op("dve", lambda e: e.memset(qTa[:], 0.0), writes=[b_qTa])
        sc.dma(lmb[:], dlam.rearrange("a b -> (a b)").partition_broadcast(128), writes=[b_lmb])
        sc.dma(gd[:], dg.partition_broadcast(128), writes=[b_gd])
        sc.op("dve", lambda e: e.tensor_tensor(out=lw[:, 0:32], in0=lmb[:, 0:32], in1=lmb[:, 32:64], op=ALU.mult),
              reads=[b_lmb], writes=[b_lw])
        sc.op("dve", lambda e: e.tensor_tensor(out=lw[:, 32:64], in0=lmb[:, 64:96], in1=lmb[:, 96:128], op=ALU.mult),
              reads=[b_lmb], writes=[b_lw])
        sc.op("dve", lambda e: e.reduce_sum(out=ls[:, 0:2], in_=lw[:, :].rearrange("p (a b) -> p a b", a=2),
                                            axis=AX.X), reads=[b_lw], writes=[b_ls])
        sc.op("act", lambda e: e.activation(out=ls[:, 2:4], in_=ls[:, 0:2], func=AF.Exp), reads=[b_ls], writes=[b_ls])
        sc.op("dve", lambda e: e.tensor_tensor(out=ls[:, 4:5], in0=ls[:, 3:4], in1=ls[:, 2:3], op=ALU.subtract),
              reads=[b_ls], writes=[b_ls])
        sc.op("dve", lambda e: e.tensor_scalar_add(out=ls[:, 5:6], in0=ls[:, 4:5], scalar1=-lam_init),
              reads=[b_ls], writes=[b_ls])
        sc.op("dve", lambda e: e.tensor_scalar_mul(out=gd[:], in0=gd[:], scalar1=1.0 - lam_init),
              reads=[b_gd], writes=[b_gd])
        blk = 0
        accn = 0
        pending = []
        for h in range(4):
            sc.dma(vA[:], ms["bv"][:, h, :].rearrange("(n p) d -> p n d", p=128), reads=[], writes=[b_vA])
            sc.dma(bd[:], cst["dbd"][h], writes=[b_bd])
            for m in range(2):
                r0 = (h * 2 + m) * 32
                sc.dma(kTa[0:32, :], ms["bkT"][r0:r0 + 32, :], writes=[b_kTa])
                sc.dma(kTa[32:36, :], cst["dakp"][h], writes=[b_kTa])
                sc.dma(kTb[0:32, :], ms["bkT"][r0:r0 + 32, :], writes=[b_kTb])
                sc.dma(kTb[32:36, :], cst["dakm"][h], writes=[b_kTb])
                sc.dma(qTa[0:32, :], ms["bqT"][r0:r0 + 32, :], writes=[b_qTa])
                sc.dma(qTa[32:36, :], cst["daq"][h], writes=[b_qTa])
                for qg in range(NG):
                    q0 = qg * GW
                    accb = 2 + accn % 2
                    accn += 1

                    def qk(kt, bank):
                        k0 = kt * 128
                        if kt < qg * GT:
                            sc.op("pe", lambda e: e.matmul(ps[bank][:, :GW], lhsT=kTa[:, k0:k0 + 128],
                                                           rhs=qTa[:, q0:q0 + GW], start=True, stop=True),
                                  reads=[b_kTa, b_qTa], writes=[psb[bank]])
                        elif kt >= (qg + 1) * GT:
                            sc.op("pe", lambda e: e.matmul(ps[bank][:, :GW], lhsT=kTb[:, k0:k0 + 128],
                                                           rhs=qTa[:, q0:q0 + GW], start=True, stop=True),
                                  reads=[b_kTb, b_qTa], writes=[psb[bank]])
                        else:
                            for i in range(GT):
                                qt = qg * GT + i
                                cs = slice(i * 128, (i + 1) * 128)
                                qs = slice(q0 + i * 128, q0 + (i + 1) * 128)
                                last = (i == GT - 1)
                                if kt < qt:
                                    sc.op("pe", lambda e: e.matmul(ps[bank][:, cs], lhsT=kTa[:, k0:k0 + 128],
                                                                   rhs=qTa[:, qs], start=True, stop=True),
                                          reads=[b_kTa, b_qTa], writes=[psb[bank]], signal=last)
                                elif kt > qt:
                                    sc.op("pe", lambda e: e.matmul(ps[bank][:, cs], lhsT=kTb[:, k0:k0 + 128],
                                                                   rhs=qTa[:, qs], start=True, stop=True),
                                          reads=[b_kTb, b_qTa], writes=[psb[bank]], signal=last)
                                else:
                                    sc.op("pe", lambda e: e.matmul(ps[bank][:, cs], lhsT=kTa[0:32, k0:k0 + 128],
                                                                   rhs=qTa[0:32, qs], start=True, stop=False),
                                          reads=[b_kTa, b_qTa], writes=[psb[bank]], signal=False)
                                    sc.op("pe", lambda e: e.matmul(ps[bank][:, cs], lhsT=idb[:, :], rhs=bd[:, :],
                                                                   start=False, stop=True),
                                          reads=[b_idb, b_bd], writes=[psb[bank]], signal=last)

                    def expv(kt, bank, pi):
                        sc.op("act", lambda e: e.activation(out=pT[pi][:, :], in_=ps[bank][:, :GW], func=AF.Exp),
                              reads=[psb[bank]], writes=[b_pT[pi]])
                        sc.op("pe", lambda e: e.matmul(ps[accb][0:65, :GW], lhsT=vA[:, kt, :], rhs=pT[pi][:, :],
                                                       start=(kt == 0), stop=(kt == NTK - 1)),
                              reads=[b_pT[pi], b_vA], writes=[psb[accb]])

                    for step in range(NTK + 1):
                        if step == 3 and pending:
                            pending.pop(0)()
                        if hook is not None and step in (12, 40):
                            hook()
                        if step < NTK:
                            qk(step, (blk + step) % 2)
                        if step >= 1:
                            expv(step - 1, (blk + step - 1) % 2, (blk + step - 1) % 3)
                    blk += NTK
                    def make_fin(accb=accb, qg=qg, m=m, accn=accn):
                        def fin():
                            sc.op("act", lambda e: e.copy(out=oT[:, :GW], in_=ps[accb][0:65, :GW]), reads=[psb[accb]], writes=[b_oT])
                            tb = 4 + (accn % 2)
                            for i in range(GT):
                                sc.op("pe", lambda e: e.transpose(ps[tb][:, i * 65:(i + 1) * 65], oT[:, i * 128:(i + 1) * 128],
                                                                  identf[0:65, 0:65]),
                                      reads=[b_oT, b_identf], writes=[psb[tb]], signal=(i == GT - 1))
                            tv = ps[tb][:, 0:GT * 65].rearrange("p (i d) -> p i d", d=65)
                            sc.op("dve", lambda e: e.reciprocal(out=rec[:, 0:GT], in_=tv[:, :, 64]),
                                  reads=[psb[tb]], writes=[b_rec])
                            sc.op("dve", lambda e: e.tensor_tensor(out=om[m][:, qg * GT:(qg + 1) * GT, :], in0=tv[:, :, 0:64],
                                                                   in1=rec[:, 0:GT].unsqueeze(2).to_broadcast([128, GT, 64]),
                                                                   op=ALU.mult),
                                  reads=[psb[tb], b_rec], writes=[b_om[m]])

                        return fin
                    pending.append(make_fin())
            while pending:
                pending.pop(0)()
            sc.op("dve", lambda e: e.scalar_tensor_tensor(out=dd[:], in0=om[1][:], scalar=ls[:, 5:6], in1=om[0][:],
                                                          op0=ALU.mult, op1=ALU.add),
                  reads=[b_om[0], b_om[1], b_ls], writes=[b_dd])
            sc.op("pool", lambda e: e.tensor_tensor(out=sq[:], in0=dd[:], in1=dd[:], op=ALU.mult),
                  reads=[b_dd], writes=[b_sq])
            sc.op("dve", lambda e: e.reduce_sum(out=ssq[:], in_=sq[:], axis=AX.X), reads=[b_sq], writes=[b_ssq])
            sc.op("dve", lambda e: e.tensor_scalar(out=ssq[:], in0=ssq[:], scalar1=1.0 / 64, scalar2=EPS,
                                                   op0=ALU.mult, op1=ALU.add), reads=[b_ssq], writes=[b_ssq])
            sc.op("act", lambda e: e.sqrt(out=ssq[:], in_=ssq[:]), reads=[b_ssq], writes=[b_ssq])
            sc.op("dve", lambda e: e.reciprocal(out=ssq[:], in_=ssq[:]), reads=[b_ssq], writes=[b_ssq])
            sc.op("dve", lambda e: e.tensor_tensor(out=dd[:], in0=dd[:],
                                                   in1=ssq[:].unsqueeze(2).to_broadcast([128, NT, 64]), op=ALU.mult),
                  reads=[b_dd, b_ssq], writes=[b_dd])
            sc.op("dve", lambda e: e.tensor_tensor(out=dd[:], in0=dd[:],
                                                   in1=gd[:].unsqueeze(1).to_broadcast([128, NT, 64]), op=ALU.mult),
                  reads=[b_dd, b_gd], writes=[b_dd])
            sc.dma(ms["omix"][:, 384 + h * 64:384 + (h + 1) * 64].rearrange("(n p) d -> p n d", p=128), dd[:],
                   reads=[b_dd])
        if hook is not None:
            while hook():
                pass
        sc.barrier()


K.diffattn_phase = _diffattn_phase


def _winattn_phase(self, tag, ms, wbias, sink):
    sc = self.sc
    S, NT = self.S, self.NT
    NKA = NT + 1 if self.paired else NT
    ps, psb = self.ps, self.psb
    with ExitStack() as st:
        qT = self.sb(st, tag + "qT", [128, S], BF16)
        kT = self.sb(st, tag + "kT", [128, NKA * 128], BF16)
        vA = self.sb(st, tag + "vA", [128, NKA, 65], BF16)
        wb = self.sb(st, tag + "wb", [128, 384], F32)
        sT = [self.sb(st, tag + "sT%d" % i, [128, 384], F32) for i in range(2)]
        pT = [self.sb(st, tag + "pT%d" % i, [128, 384], BF16) for i in range(2)]
        ow = self.sb(st, tag + "ow", [128, NT, 64], F32)
        ou = self.sb(st, tag + "ou", [128, NT, 65], F32)
        dn_ = self.sb(st, tag + "dn", [128, NT], F32)
        b_ou, b_dn = Buf("ou"), Buf("dn")
        es = self.sb(st, tag + "es", [128, 6], F32)
        rec = self.sb(st, tag + "rec", [128, 4], F32)
        b_qT, b_kT, b_vA, b_wb, b_ow, b_es, b_rec = (Buf(n) for n in ("qT", "kT", "vA", "wb", "ow", "es", "rec"))
        b_sT = [Buf("sT0"), Buf("sT1")]
        b_pT = [Buf("pT0"), Buf("pT1")]
        sc.op("dve", lambda e: e.memset(qT[:], 0.0), writes=[b_qT])
        sc.op("dve", lambda e: e.memset(kT[:], 0.0), writes=[b_kT])
        sc.dma(es[:], sink.partition_broadcast(128), writes=[b_es])
        sc.op("act", lambda e: e.activation(out=es[:], in_=es[:], func=AF.Exp), reads=[b_es], writes=[b_es])
        cnt = 0
        for h in range(6):
            kh = h // 3
            if h % 3 == 0:
                sc.dma(kT[0:64, :], ms["akT"][kh * 64:(kh + 1) * 64, :], writes=[b_kT])
                sc.dma(vA[:], ms["av"][:, kh, :].rearrange("(n p) d -> p n d", p=128), writes=[b_vA])
            sc.dma(qT[0:64, :], ms["aqT"][h * 64:(h + 1) * 64, :], writes=[b_qT])
            sc.dma(wb[:], wbias[h], writes=[b_wb])
            for n in range(NT):
                js = [j for j in (0, 1, 2) if 0 <= n - 1 + j < NKA]
                c0 = js[0] * 128
                w = len(js) * 128
                sbk = cnt % 2
                acc = 2 + cnt % 6
                cnt += 1
                for jj, j in enumerate(js):
                    kt = n - 1 + j
                    sc.op("pe", lambda e: e.matmul(ps[sbk][:, jj * 128:(jj + 1) * 128], lhsT=kT[:, kt * 128:(kt + 1) * 128],
                                                   rhs=qT[:, n * 128:(n + 1) * 128], start=True, stop=True),
                          reads=[b_kT, b_qT], writes=[psb[sbk]], signal=(jj == len(js) - 1))
                sc.op("dve", lambda e: e.tensor_tensor(out=sT[sbk][:, 0:w], in0=ps[sbk][:, 0:w], in1=wb[:, c0:c0 + w],
                                                       op=ALU.add),
                      reads=[psb[sbk], b_wb], writes=[b_sT[sbk]])
                sc.op("act", lambda e: e.activation(out=pT[sbk][:, 0:w], in_=sT[sbk][:, 0:w], func=AF.Exp),
                      reads=[b_sT[sbk]], writes=[b_pT[sbk]])
                for jj, j in enumerate(js):
                    kt = n - 1 + j
                    sc.op("pe", lambda e: e.matmul(ps[acc][:, 0:65], lhsT=pT[sbk][:, jj * 128:(jj + 1) * 128],
                                                   rhs=vA[:, kt, :], start=(jj == 0), stop=(jj == len(js) - 1)),
                          reads=[b_pT[sbk], b_vA], writes=[psb[acc]], signal=(jj == len(js) - 1))
                if n % 2 == 0:
                    sc.op("dve", lambda e: e.tensor_copy(out=ou[:, n, :], in_=ps[acc][:, 0:65]), reads=[psb[acc]], writes=[b_ou])
                else:
                    sc.op("act", lambda e: e.copy(out=ou[:, n, :], in_=ps[acc][:, 0:65]), reads=[psb[acc]], writes=[b_ou])
            sc.op("dve", lambda e: e.tensor_scalar(out=dn_[:], in0=ou[:, :, 64], scalar1=es[:, h:h + 1], scalar2=None,
                                                   op0=ALU.add), reads=[b_ou, b_es], writes=[b_dn])
            sc.op("dve", lambda e: e.reciprocal(out=dn_[:], in_=dn_[:]), reads=[b_dn], writes=[b_dn])
            sc.op("dve", lambda e: e.tensor_tensor(out=ow[:], in0=ou[:, :, 0:64],
                                                   in1=dn_[:].unsqueeze(2).to_broadcast([128, NT, 64]), op=ALU.mult),
                  reads=[b_ou, b_dn], writes=[b_ow])
            sc.dma(ms["omix"][:, h * 64:(h + 1) * 64].rearrange("(n p) d -> p n d", p=128), ow[:], reads=[b_ow])
        sc.barrier()


K.winattn_phase = _winattn_phase


def _dn_scratch(self):
    S = self.S
    d = {}
    d["cqT"] = self.scratch("cqT", [384, S], F32)
    d["ckT"] = self.scratch("ckT", [384, S], F32)
    d["ck"] = self.scratch("ck", [S, 384], F32)
    d["cv"] = self.scratch("cv", [S, 384], F32)
    d["of"] = self.scratch("of", [S, 384], F32)
    return d


K.dn_scratch = _dn_scratch


def _conv_phase(self, tag, ms, ds, conv_w, dnc, ident, ident_b):
    sc = self.sc
    S, NT = self.S, self.NT
    GT = 4 if NT % 4 == 0 else 1
    GW = GT * 128
    ps, psb = self.ps, self.psb
    with ExitStack() as st:
        blk1 = self.sb(st, tag + "blk1", [128, 128], F32)
        cw = self.sb(st, tag + "cw", [128, 9, 5], F32)
        xin = [self.sb(st, tag + "xin%d" % i, [128, GW + 4], F32) for i in range(2)]
        y = [self.sb(st, tag + "y%d" % i, [128, GW], F32) for i in range(2)]
        sq2 = [self.sb(st, tag + "sq%d" % i, [128, GW], F32) for i in range(2)]
        rs2 = [self.sb(st, tag + "rs%d" % i, [128, GW], F32) for i in range(2)]
        yn = [self.sb(st, tag + "yn%d" % i, [128, GW], F32) for i in range(2)]
        tk = [self.sb(st, tag + "tk%d" % i, [128, GW], F32) for i in range(2)]
        b_blk1, b_cw = Buf("blk1"), Buf("cw")
        b_sq2 = [Buf("sq0"), Buf("sq1")]
        b_rs2 = [Buf("rs0"), Buf("rs1")]
        b_xin = [Buf("xin0"), Buf("xin1")]
        b_y = [Buf("y0"), Buf("y1")]
        b_yn = [Buf("yn0"), Buf("yn1")]
        b_tk = [Buf("tk0"), Buf("tk1")]
        sc.dma(blk1[:], dnc[3], writes=[b_blk1])
        for ci in range(9):
            sc.dma(cw[:, ci, :], conv_w[:, ci * 128:(ci + 1) * 128].rearrange("j c -> c j"), writes=[b_cw],
                   allow_slow_non_contiguous=True)
        it = 0
        for g0 in range(0, NT, GT):
            tok0 = g0 * 128
            for ci in range(9):
                b = it % 2
                it += 1
                sq, rs, b_sq, b_rs = sq2[b], rs2[b], b_sq2[b], b_rs2[b]
                pA, pB = 2 * b, 2 * b + 1
                sc.dma(xin[b][:], ms["cpre"][ci * 128:(ci + 1) * 128, tok0:tok0 + GW + 4], writes=[b_xin[b]])
                eng = "dve"
                sc.op(eng, lambda e: e.tensor_scalar_mul(out=y[b][:], in0=xin[b][:, 0:GW], scalar1=cw[:, ci, 0:1]),
                      reads=[b_xin[b], b_cw], writes=[b_y[b]])
                for j in range(1, 5):
                    sc.op(eng, lambda e: e.scalar_tensor_tensor(out=y[b][:], in0=xin[b][:, j:j + GW],
                                                                scalar=cw[:, ci, j:j + 1], in1=y[b][:],
                                                                op0=ALU.mult, op1=ALU.add),
                          reads=[b_xin[b], b_cw, b_y[b]], writes=[b_y[b]])
                sc.op("act", lambda e: e.activation(out=y[b][:], in_=y[b][:], func=AF.Silu),
                      reads=[b_y[b]], writes=[b_y[b]])
                if ci < 6:
                    sc.op("act", lambda e: e.activation(out=sq[:], in_=y[b][:], func=AF.Square),
                          reads=[b_y[b]], writes=[b_sq])
                    sc.op("pe", lambda e: e.matmul(ps[pA][:, :GW], lhsT=blk1[:], rhs=sq[:], start=True, stop=True),
                          reads=[b_blk1, b_sq], writes=[psb[pA]])
                    mul = 64.0 if ci < 3 else 1.0
                    sc.op("dve", lambda e: e.tensor_scalar(out=rs[:], in0=ps[pA][:, :GW], scalar1=EPS, scalar2=mul,
                                                           op0=ALU.add, op1=ALU.mult),
                          reads=[psb[pA]], writes=[b_rs])
                    sc.op("act", lambda e: e.sqrt(out=rs[:], in_=rs[:]), reads=[b_rs], writes=[b_rs])
                    sc.op("dve", lambda e: e.reciprocal(out=rs[:], in_=rs[:]), reads=[b_rs], writes=[b_rs])
                    sc.op("dve", lambda e: e.tensor_tensor(out=yn[b][:], in0=y[b][:], in1=rs[:], op=ALU.mult),
                          reads=[b_y[b], b_rs], writes=[b_yn[b]])
                    src_t, src_b = yn[b], b_yn[b]
                    if ci < 3:
                        sc.dma(ds["cqT"][ci * 128:(ci + 1) * 128, tok0:tok0 + GW], yn[b][:], reads=[b_yn[b]])
                    else:
                        sc.dma(ds["ckT"][(ci - 3) * 128:(ci - 2) * 128, tok0:tok0 + GW], yn[b][:], reads=[b_yn[b]])
                else:
                    src_t, src_b = y[b], b_y[b]
                if ci >= 3:
                    for tt in range(GT):
                        sc.op("pe", lambda e: e.transpose(ps[pB][:, tt * 128:(tt + 1) * 128],
                                                          src_t[:, tt * 128:(tt + 1) * 128], ident[:]),
                              reads=[src_b, ident_b], writes=[psb[pB]], signal=(tt == GT - 1))
                    sc.op("act", lambda e: e.copy(out=tk[b][:], in_=ps[pB][:, :GW]), reads=[psb[pB]], writes=[b_tk[b]])
                    dst = ds["ck"] if ci < 6 else ds["cv"]
                    cc = (ci - 3) % 3
                    sc.dma(dst[tok0:tok0 + GW, cc * 128:(cc + 1) * 128].rearrange("(t p) c -> p t c", p=128),
                           tk[b][:].rearrange("p (t c) -> p t c", c=128), reads=[b_tk[b]])
        sc.barrier()


K.conv_phase = _conv_phase


def _dn_pass(self, tag, dirn, ms, ds, dnc, a_log, dt_bias, dn_g, ident, ident_b):
    sc = self.sc
    S, NT = self.S, self.NT
    ps, psb = self.ps, self.psb
    import os
    NIT = int(os.environ.get('DN_NIT', '7'))
    with ExitStack() as st:
        def T(name, shape, n=1, dt=F32):
            ts = [self.sb(st, "%s%s%d" % (tag, name, i), shape, dt) for i in range(n)]
            bs = [Buf("%s%d" % (name, i)) for i in range(n)]
            return (ts, bs) if n > 1 else (ts[0], bs[0])
        ones, b_ones = T("ones", [128, 128])
        tri, b_tri = T("tri", [128, 128])
        m1, b_m1 = T("m1", [128, 128])
        m2, b_m2 = T("m2", [128, 128])
        m3, b_m3 = T("m3", [128, 128])
        dtb, b_dtb = T("dtb", [128, 6])
        nega, b_nega = T("nega", [128, 6])
        gdn, b_gdn = T("gdn", [128, 64])
        kTt, b_kTt = T("kTt", [64, 6, 128], 2)
        qTt, b_qTt = T("qTt", [64, 6, 128], 2)
        kt, b_kt = T("kt", [128, 384], 2)
        vt, b_vt = T("vt", [128, 384], 2)
        gz, b_gz = T("gz", [128, 408], 2)
        gs, b_gs = T("gs", [128, 16, 6])
        D1, b_D1 = T("D1", [128, 6, 128])
        D2, b_D2 = T("D2", [128, 6, 128])
        Rs, b_Rs = T("Rs", [128, 6, 128])
        R2s, b_R2s = T("R2s", [128, 6, 128])
        tmp, b_tmp = T("tmp", [128, 3, 128], 2)
        E, b_E = T("E", [128, 3, 128], 2)
        X, b_X = T("X", [128, 128], 4)
        Y, b_Y = T("Y", [128, 128], 4)
        Rr, b_Rr = T("Rr", [128, 128], 4)
        qkT, b_qkT = T("qkT", [128, 128], 2)
        kg, b_kg = T("kg", [128, 64], 2)
        wT, b_wT = T("wT", [64, 128], 2)
        Vn, b_Vn = T("Vn", [128, 64], 2)
        t2, b_t2 = T("t2", [128, 64], 2)
        St, b_St = T("St", [64, 6, 64])
        osb, b_osb = T("osb", [128, 6, 64], 2)
        if dirn == 1:
            oft, b_oft = T("oft", [128, 6, 64], 2)
            sqt, b_sqt = T("sqt", [128, 6, 64])
            sz, b_sz = T("sz", [128, 6, 64])
            rr, b_rr = T("rr", [128, 8])
        sc.dma(ones[:], dnc[0], writes=[b_ones])
        sc.dma(tri[:], dnc[1 + dirn], writes=[b_tri])
        sc.dma(m1[:], dnc[4 + 3 * dirn], writes=[b_m1])
        sc.dma(m2[:], dnc[5 + 3 * dirn], writes=[b_m2])
        sc.dma(m3[:], dnc[6 + 3 * dirn], writes=[b_m3])
        sc.dma(dtb[:], dt_bias[dirn].partition_broadcast(128), writes=[b_dtb])
        sc.dma(nega[:], a_log[dirn].partition_broadcast(128), writes=[b_nega])
        sc.dma(gdn[:], dn_g.partition_broadcast(128), writes=[b_gdn])
        sc.op("act", lambda e: e.activation(out=nega[:], in_=nega[:], func=AF.Exp), reads=[b_nega], writes=[b_nega])
        sc.op("dve", lambda e: e.tensor_scalar_mul(out=nega[:], in0=nega[:], scalar1=-1.0),
              reads=[b_nega], writes=[b_nega])
        sc.op("pool", lambda e: e.memset(St[:], 0.0), writes=[b_St])
        order = list(range(NT)) if dirn == 0 else list(range(NT - 1, -1, -1))
        G_ = lambda i: gs[:, i, :]
        hcnt = 0
        for it, n in enumerate(order):
            tb = it % 2
            c0 = n * 128
            sc.dma(kTt[tb][:], ds["ckT"][:, c0:c0 + 128].rearrange("(h d) t -> d h t", h=6), writes=[b_kTt[tb]])
            sc.dma(qTt[tb][:], ds["cqT"][:, c0:c0 + 128].rearrange("(h d) t -> d h t", h=6), writes=[b_qTt[tb]])
            sc.dma(kt[tb][:], ds["ck"][c0:c0 + 128, :], writes=[b_kt[tb]])
            sc.dma(vt[tb][:], ds["cv"][c0:c0 + 128, :], writes=[b_vt[tb]])
            sc.dma(gz[tb][:], ms["cz"][c0:c0 + 128, :], writes=[b_gz[tb]])
            bcol = gz[tb][:, 384 + dirn * 6:384 + dirn * 6 + 6]
            acol = gz[tb][:, 396 + dirn * 6:396 + dirn * 6 + 6]
            RG = [b_gs]
            sc.op("act", lambda e: e.activation(out=G_(0), in_=bcol, func=AF.Exp, scale=-1.0),
                  reads=[b_gz[tb]], writes=RG)
            sc.op("dve", lambda e: e.tensor_scalar_add(out=G_(0), in0=G_(0), scalar1=1.0), reads=RG, writes=RG)
            sc.op("act", lambda e: e.activation(out=G_(1), in_=G_(0), func=AF.Ln), reads=RG, writes=RG)
            sc.op("dve", lambda e: e.tensor_tensor(out=G_(2), in0=acol, in1=dtb[:], op=ALU.add),
                  reads=[b_gz[tb], b_dtb], writes=RG)
            sc.op("act", lambda e: e.activation(out=G_(3), in_=G_(2), func=AF.Exp), reads=RG, writes=RG)
            sc.op("dve", lambda e: e.tensor_scalar_add(out=G_(3), in0=G_(3), scalar1=1.0), reads=RG, writes=RG)
            sc.op("act", lambda e: e.activation(out=G_(4), in_=G_(3), func=AF.Ln), reads=RG, writes=RG)
            sc.op("dve", lambda e: e.tensor_tensor(out=G_(5), in0=G_(4), in1=nega[:], op=ALU.mult),
                  reads=RG + [b_nega], writes=RG)
            sc.op("pe", lambda e: e.matmul(ps[0][:, 0:6], lhsT=tri[:], rhs=G_(5), start=True, stop=True),
                  reads=[b_tri] + RG, writes=[psb[0]], signal=False)
            sc.op("pe", lambda e: e.matmul(ps[0][:, 8:14], lhsT=ones[:], rhs=G_(5), start=True, stop=True),
                  reads=[b_ones] + RG, writes=[psb[0]])
            sc.op("dve", lambda e: e.tensor_copy(out=G_(6), in_=ps[0][:, 0:6]), reads=[psb[0]], writes=RG)
            sc.op("dve", lambda e: e.tensor_copy(out=G_(14), in_=ps[0][:, 8:14]), reads=[psb[0]], writes=RG)
            sc.op("dve", lambda e: e.tensor_tensor(out=G_(7), in0=G_(6), in1=G_(1), op=ALU.subtract),
                  reads=RG, writes=RG)
            sc.op("act", lambda e: e.activation(out=G_(8), in_=G_(7), func=AF.Exp), reads=RG, writes=RG)
            sc.op("act", lambda e: e.activation(out=G_(9), in_=G_(1), func=AF.Exp, scale=-1.0),
                  reads=RG, writes=RG)
            sc.op("act", lambda e: e.activation(out=G_(10), in_=G_(6), func=AF.Exp), reads=RG, writes=RG)
            sc.op("dve", lambda e: e.tensor_tensor(out=G_(11), in0=G_(14), in1=G_(6), op=ALU.subtract),
                  reads=RG, writes=RG)
            sc.op("act", lambda e: e.activation(out=G_(12), in_=G_(11), func=AF.Exp), reads=RG, writes=RG)
            sc.op("act", lambda e: e.activation(out=G_(13), in_=G_(14), func=AF.Exp), reads=RG, writes=RG)
            idb3 = ident[:].unsqueeze(1).to_broadcast([128, 6, 128])
            sc.op("dve", lambda e: e.tensor_tensor(out=D1[:], in0=idb3,
                                                   in1=G_(6).unsqueeze(2).to_broadcast([128, 6, 128]), op=ALU.mult),
                  reads=RG + [ident_b], writes=[b_D1])
            sc.op("pool", lambda e: e.tensor_tensor(out=D2[:], in0=idb3,
                                                    in1=G_(7).unsqueeze(2).to_broadcast([128, 6, 128]), op=ALU.mult),
                  reads=RG + [ident_b], writes=[b_D2])
            for (Dm, b_Dm, Rm, b_Rm) in ((D1, b_D1, Rs, b_Rs), (D2, b_D2, R2s, b_R2s)):
                Dm2 = Dm[:].rearrange("p j s -> p (j s)")
                Rm2 = Rm[:].rearrange("p j s -> p (j s)")
                sc.op("pe", lambda e: e.matmul(ps[1][:, 0:512], lhsT=ones[:], rhs=Dm2[:, 0:512], start=True, stop=True),
                      reads=[b_ones, b_Dm], writes=[psb[1]])
                sc.op("pe", lambda e: e.matmul(ps[2][:, 0:256], lhsT=ones[:], rhs=Dm2[:, 512:768], start=True,
                                               stop=True),
                      reads=[b_ones, b_Dm], writes=[psb[2]])
                sc.op("act", lambda e: e.copy(out=Rm2[:, 0:512], in_=ps[1][:, 0:512]), reads=[psb[1]], writes=[b_Rm])
                sc.op("act", lambda e: e.copy(out=Rm2[:, 512:768], in_=ps[2][:, 0:256]), reads=[psb[2]],
                      writes=[b_Rm])
            ob = it % 2
            import os
            STOP = os.environ.get("DN_STOP", "")
            for hh in range(6):
                if STOP == "gates":
                    break
                hb = hcnt % 2
                hcnt += 1
                sc.op("pe", lambda e: e.matmul(ps[3][:, 0:128], lhsT=kTt[tb][:, hh, :], rhs=kTt[tb][:, hh, :],
                                               start=True, stop=True),
                      reads=[b_kTt[tb]], writes=[psb[3]], signal=False)
                sc.op("pe", lambda e: e.matmul(ps[3][:, 128:256], lhsT=kTt[tb][:, hh, :], rhs=qTt[tb][:, hh, :],
                                               start=True, stop=True),
                      reads=[b_kTt[tb], b_qTt[tb]], writes=[psb[3]])
                sc.op("dve", lambda e: e.scalar_tensor_tensor(out=tmp[hb][:, 0, :], in0=Rs[:, hh, :],
                                                              scalar=gs[:, 7, hh:hh + 1], in1=m1[:],
                                                              op0=ALU.subtract, op1=ALU.max),
                      reads=[b_Rs, b_gs, b_m1], writes=[b_tmp[hb]])
                sc.op("dve", lambda e: e.scalar_tensor_tensor(out=tmp[hb][:, 1, :], in0=R2s[:, hh, :],
                                                              scalar=gs[:, 6, hh:hh + 1], in1=m2[:],
                                                              op0=ALU.subtract, op1=ALU.min),
                      reads=[b_R2s, b_gs, b_m2], writes=[b_tmp[hb]])
                sc.op("dve", lambda e: e.scalar_tensor_tensor(out=tmp[hb][:, 2, :], in0=Rs[:, hh, :],
                                                              scalar=gs[:, 6, hh:hh + 1], in1=m3[:],
                                                              op0=ALU.subtract, op1=ALU.min),
                      reads=[b_Rs, b_gs, b_m3], writes=[b_tmp[hb]])
                sc.op("act", lambda e: e.activation(out=E[hb][:, 0, :], in_=tmp[hb][:, 0, :], func=AF.Exp, scale=-1.0),
                      reads=[b_tmp[hb]], writes=[b_E[hb]])
                sc.op("act", lambda e: e.activation(out=E[hb][:, 1:3, :], in_=tmp[hb][:, 1:3, :], func=AF.Exp),
                      reads=[b_tmp[hb]], writes=[b_E[hb]])
                xi = 2 * hb
                sc.op("dve", lambda e: e.tensor_tensor(out=X[xi][:], in0=ps[3][:, 0:128], in1=E[hb][:, 0, :],
                                                       op=ALU.mult), reads=[psb[3], b_E[hb]], writes=[b_X[xi]])
                sc.op("dve", lambda e: e.tensor_tensor(out=Y[xi][:], in0=ps[3][:, 0:128], in1=E[hb][:, 1, :],
                                                       op=ALU.mult), reads=[psb[3], b_E[hb]], writes=[b_Y[xi]])
                sc.op("dve", lambda e: e.tensor_tensor(out=qkT[hb][:], in0=ps[3][:, 128:256], in1=E[hb][:, 2, :],
                                                       op=ALU.mult), reads=[psb[3], b_E[hb]], writes=[b_qkT[hb]])
                hs = slice(hh * 64, (hh + 1) * 64)
                sc.op("pool", lambda e: e.tensor_scalar_mul(out=Rr[xi][:, 0:64], in0=vt[tb][:, hs],
                                                            scalar1=gs[:, 9, hh:hh + 1]),
                      reads=[b_vt[tb], b_gs], writes=[b_Rr[xi]])
                sc.op("pool", lambda e: e.tensor_scalar_mul(out=Rr[xi][:, 64:128], in0=kt[tb][:, hs],
                                                            scalar1=gs[:, 8, hh:hh + 1]),
                      reads=[b_kt[tb], b_gs], writes=[b_Rr[xi]])
                sc.op("pool", lambda e: e.tensor_scalar_mul(out=kg[hb][:], in0=kt[tb][:, hs],
                                                            scalar1=gs[:, 12, hh:hh + 1]),
                      reads=[b_kt[tb], b_gs], writes=[b_kg[hb]])
                if STOP == "prep":
                    continue
                pn = 4 + hb
                pq = 6 + hb
                cur = xi
                for i in range(NIT):
                    nxt = 2 * hb + (1 - (cur - 2 * hb))
                    sc.op("pe", lambda e: e.matmul(ps[pn][:, 0:128], lhsT=Y[cur][:], rhs=Rr[cur][:], start=True,
                                                   stop=True),
                          reads=[b_Y[cur], b_Rr[cur]], writes=[psb[pn]], signal=(i == NIT - 1))
                    if i < NIT - 1:
                        sc.op("pe", lambda e: e.matmul(ps[pq][:, 128:256], lhsT=X[cur][:], rhs=Y[cur][:], start=True,
                                                       stop=True),
                              reads=[b_X[cur], b_Y[cur]], writes=[psb[pq]], signal=(i == NIT - 2))
                    if i < NIT - 2:
                        sc.op("pe", lambda e: e.matmul(ps[pq][:, 256:384], lhsT=Y[cur][:], rhs=X[cur][:], start=True,
                                                       stop=True),
                              reads=[b_X[cur], b_Y[cur]], writes=[psb[pq]], signal=True)
                    if i == 0:
                        sc.op("dve", lambda e: e.scalar_tensor_tensor(out=Rr[nxt][:], in0=ps[pn][:, 0:128], scalar=-1.0,
                                                                      in1=Rr[cur][:], op0=ALU.mult, op1=ALU.add),
                              reads=[b_Rr[cur], psb[pn]], writes=[b_Rr[nxt]])
                    else:
                        sc.op("dve", lambda e: e.tensor_tensor(out=Rr[nxt][:], in0=ps[pn][:, 0:128], in1=Rr[cur][:],
                                                               op=ALU.add),
                              reads=[b_Rr[cur], psb[pn]], writes=[b_Rr[nxt]])
                    if i < NIT - 1:
                        sc.op("act", lambda e: e.copy(out=Y[nxt][:], in_=ps[pq][:, 128:256]),
                              reads=[psb[pq]], writes=[b_Y[nxt]])
                    if i < NIT - 2:
                        sc.op("act", lambda e: e.copy(out=X[nxt][:], in_=ps[pq][:, 256:384]),
                              reads=[psb[pq]], writes=[b_X[nxt]])
                    cur = nxt
                Rf, b_Rf = Rr[cur], b_Rr[cur]
                if STOP == "neumann":
                    continue
                sc.op("pe", lambda e: e.transpose(ps[1][0:64, 0:128], Rf[:, 64:128], ident[:]),
                      reads=[b_Rf, ident_b], writes=[psb[1]])
                sc.op("act", lambda e: e.copy(out=wT[hb][:], in_=ps[1][0:64, 0:128]), reads=[psb[1]], writes=[b_wT[hb]])
                if STOP == "transp":
                    continue
                sc.op("pe", lambda e: e.matmul(ps[0][:, 0:64], lhsT=wT[hb][:], rhs=St[:, hh, :], start=True, stop=True),
                      reads=[b_wT[hb], b_St], writes=[psb[0]])
                sc.op("dve", lambda e: e.scalar_tensor_tensor(out=Vn[hb][:], in0=ps[0][:, 0:64], scalar=-1.0,
                                                              in1=Rf[:, 0:64], op0=ALU.mult, op1=ALU.add),
                      reads=[b_Rf, psb[0]], writes=[b_Vn[hb]])
                sc.op("pe", lambda e: e.matmul(ps[1][:, 128:192], lhsT=qTt[tb][:, hh, :], rhs=St[:, hh, :], start=True,
                                               stop=True),
                      reads=[b_qTt[tb], b_St], writes=[psb[1]], signal=False)
                sc.op("pe", lambda e: e.matmul(ps[0][:, 64:128], lhsT=qkT[hb][:], rhs=Vn[hb][:], start=True, stop=True),
                      reads=[b_qkT[hb], b_Vn[hb]], writes=[psb[0]], signal=False)
                sc.op("pe", lambda e: e.matmul(ps[2][0:64, 0:64], lhsT=kg[hb][:], rhs=Vn[hb][:], start=True, stop=True),
                      reads=[b_kg[hb], b_Vn[hb]], writes=[psb[2]])
                sc.op("act", lambda e: e.activation(out=t2[hb][:], in_=ps[1][:, 128:192], func=AF.Copy,
                                                    scale=gs[:, 10, hh:hh + 1]),
                      reads=[psb[1], b_gs], writes=[b_t2[hb]])
                sc.op("dve", lambda e: e.tensor_tensor(out=osb[ob][:, hh, :], in0=ps[0][:, 64:128], in1=t2[hb][:],
                                                       op=ALU.add),
                      reads=[b_t2[hb], psb[0]], writes=[b_osb[ob]])
                sc.op("pool", lambda e: e.tensor_scalar_mul(out=St[:, hh, :], in0=St[:, hh, :],
                                                            scalar1=gs[0:64, 13, hh:hh + 1]),
                      reads=[b_St, b_gs], writes=[b_St])
                sc.op("dve", lambda e: e.tensor_tensor(out=St[:, hh, :], in0=ps[2][0:64, 0:64], in1=St[:, hh, :],
                                                       op=ALU.add),
                      reads=[b_St, psb[2]], writes=[b_St])
            if dirn == 0:
                sc.dma(ds["of"][c0:c0 + 128, :], osb[ob][:].rearrange("p h d -> p (h d)"), reads=[b_osb[ob]])
            else:
                sc.dma(oft[ob][:].rearrange("p h d -> p (h d)"), ds["of"][c0:c0 + 128, :], writes=[b_oft[ob]])
                sc.op("dve", lambda e: e.tensor_tensor(out=oft[ob][:], in0=oft[ob][:], in1=osb[ob][:], op=ALU.add),
                      reads=[b_oft[ob], b_osb[ob]], writes=[b_oft[ob]])
                sc.op("pool", lambda e: e.tensor_tensor(out=sqt[:], in0=oft[ob][:], in1=oft[ob][:], op=ALU.mult),
                      reads=[b_oft[ob]], writes=[b_sqt])
                sc.op("dve", lambda e: e.reduce_sum(out=rr[:, 0:6], in_=sqt[:], axis=AX.X), reads=[b_sqt], writes=[b_rr])
                sc.op("dve", lambda e: e.tensor_scalar(out=rr[:, 0:6], in0=rr[:, 0:6], scalar1=1.0 / 64, scalar2=EPS,
                                                       op0=ALU.mult, op1=ALU.add), reads=[b_rr], writes=[b_rr])
                sc.op("act", lambda e: e.sqrt(out=rr[:, 0:6], in_=rr[:, 0:6]), reads=[b_rr], writes=[b_rr])
                sc.op("dve", lambda e: e.reciprocal(out=rr[:, 0:6], in_=rr[:, 0:6]), reads=[b_rr], writes=[b_rr])
                sc.op("act", lambda e: e.activation(out=sz[:].rearrange("p h d -> p (h d)"), in_=gz[tb][:, 0:384],
                                                    func=AF.Silu), reads=[b_gz[tb]], writes=[b_sz])
                sc.op("dve", lambda e: e.tensor_tensor(out=oft[ob][:], in0=oft[ob][:],
                                                       in1=rr[:, 0:6].unsqueeze(2).to_broadcast([128, 6, 64]),
                                                       op=ALU.mult), reads=[b_oft[ob], b_rr], writes=[b_oft[ob]])
                sc.op("pool", lambda e: e.tensor_tensor(out=sz[:], in0=sz[:],
                                                        in1=gdn[:].unsqueeze(1).to_broadcast([128, 6, 64]),
                                                        op=ALU.mult), reads=[b_sz, b_gdn], writes=[b_sz])
                sc.op("dve", lambda e: e.tensor_tensor(out=oft[ob][:], in0=oft[ob][:], in1=sz[:], op=ALU.mult),
                      reads=[b_oft[ob], b_sz], writes=[b_oft[ob]])
                sc.dma(ms["omix"][c0:c0 + 128, 640:1024], oft[ob][:].rearrange("p h d -> p (h d)"),
                       reads=[b_oft[ob]])
        sc.barrier()


K.dn_pass = _dn_pass


def _outproj_phase(self, tag, w_o, ms, xres, x_bufs, ident, ident_b):
    sc = self.sc
    S, NT = self.S, self.NT
    ps, psb = self.ps, self.psb
    with ExitStack() as st:
        wo = self.sb(st, tag + "wo", [128, 8, D], BF16)
        ot = [self.sb(st, tag + "ot%d" % i, [128, D], F32) for i in range(2)]
        xr = [self.sb(st, tag + "xr%d" % i, [128, D], F32) for i in range(2)]
        oT = [self.sb(st, tag + "oT%d" % i, [128, 8, 128], BF16) for i in range(2)]
        b_wo = [Buf("wo%d" % k) for k in range(8)]
        b_ot = [Buf("ot0"), Buf("ot1")]
        b_xr = [Buf("xr0"), Buf("xr1")]
        b_oT = [Buf("oT0"), Buf("oT1")]
        for kc in range(8):
            sc.dma(wo[:, kc, :], w_o[kc * 128:(kc + 1) * 128, :], writes=[b_wo[kc]], q="pool")
        for t in range(NT):
            b = t % 2
            r0 = t * 128
            sc.dma(ot[b][:], ms["omix"][r0:r0 + 128, :], writes=[b_ot[b]])
            sc.dma(xr[b][:], xres[r0:r0 + 128, :], reads=[x_bufs[t]], writes=[b_xr[b]])
            for kc in range(8):
                bank = kc // 4
                sc.op("pe", lambda e: e.transpose(ps[bank][:, (kc % 4) * 128:(kc % 4 + 1) * 128],
                                                  ot[b][:, kc * 128:(kc + 1) * 128], ident[:]),
                      reads=[b_ot[b], ident_b], writes=[psb[bank]], signal=(kc % 4 == 3))
            sc.op("act", lambda e: e.copy(out=oT[b][:, 0:4, :], in_=ps[0][:, :].rearrange("p (k t) -> p k t", k=4)),
                  reads=[psb[0]], writes=[b_oT[b]])
            sc.op("dve", lambda e: e.tensor_copy(out=oT[b][:, 4:8, :],
                                                 in_=ps[1][:, :].rearrange("p (k t) -> p k t", k=4)),
                  reads=[psb[1]], writes=[b_oT[b]])
            for hf in range(2):
                pb = 2 + 2 * b + hf
                for kc in range(8):
                    sc.op("pe", lambda e: e.matmul(ps[pb][:, :], lhsT=oT[b][:, kc, :], rhs=wo[:, kc, hf * 512:(hf + 1) * 512],
                                                   start=(kc == 0), stop=(kc == 7)),
                          reads=[b_oT[b], b_wo[kc]], writes=[psb[pb]], signal=(kc == 7))
            for hf in range(2):
                pb = 2 + 2 * b + hf
                sc.op("dve", lambda e: e.tensor_tensor(out=xr[b][:, hf * 512:(hf + 1) * 512], in0=ps[pb][:, :],
                                                       in1=xr[b][:, hf * 512:(hf + 1) * 512], op=ALU.add),
                      reads=[psb[pb], b_xr[b]], writes=[b_xr[b]])
            sc.dma(xres[r0:r0 + 128, :], xr[b][:], reads=[b_xr[b]], writes=[x_bufs[t]])
        sc.barrier()


K.outproj_phase = _outproj_phase


def _final_phase(self, tag, g, xres, x_bufs, out):
    sc = self.sc
    S, NT = self.S, self.NT
    with ExitStack() as st:
        gB = self.sb(st, tag + "gB", [128, D], F32)
        xt = [self.sb(st, tag + "xt%d" % i, [128, D], F32) for i in range(2)]
        junk = self.sb(st, tag + "junk", [128, D], F32)
        stat = [self.sb(st, tag + "stat%d" % i, [128, 4], F32) for i in range(2)]
        b_gB, b_junk = Buf("gB"), Buf("junk")
        b_xt = [Buf("xt0"), Buf("xt1")]
        b_stat = [Buf("stat0"), Buf("stat1")]
        sc.dma(gB[:], g.partition_broadcast(128), writes=[b_gB])
        for t in range(NT):
            b = t % 2
            r0 = t * 128
            sc.dma(xt[b][:], xres[r0:r0 + 128, :], reads=[x_bufs[t]], writes=[b_xt[b]])
            sc.op("act", lambda e: e.activation(out=junk[:], in_=xt[b][:], func=AF.Square, accum_out=stat[b][:, 0:1]),
                  reads=[b_xt[b]], writes=[b_junk, b_stat[b]])
            sc.op("dve", lambda e: e.tensor_scalar(out=stat[b][:, 1:2], in0=stat[b][:, 0:1], scalar1=1.0 / D,
                                                   scalar2=EPS, op0=ALU.mult, op1=ALU.add),
                  reads=[b_stat[b]], writes=[b_stat[b]])
            sc.op("act", lambda e: e.sqrt(out=stat[b][:, 3:4], in_=stat[b][:, 1:2]), reads=[b_stat[b]],
                  writes=[b_stat[b]])
            sc.op("dve", lambda e: e.reciprocal(out=stat[b][:, 2:3], in_=stat[b][:, 3:4]), reads=[b_stat[b]],
                  writes=[b_stat[b]])
            sc.op("dve", lambda e: e.scalar_tensor_tensor(out=xt[b][:], in0=xt[b][:], scalar=stat[b][:, 2:3],
                                                          in1=gB[:], op0=ALU.mult, op1=ALU.mult),
                  reads=[b_xt[b], b_stat[b], b_gB], writes=[b_xt[b]])
            sc.dma(out[r0:r0 + 128, :], xt[b][:], reads=[b_xt[b]])
        sc.barrier()


K.final_phase = _final_phase

import math
WNAMES = [("ln_ffn1", [2, D]), ("ffn1_w_in", [2, D, 2 * DFF]), ("ffn1_w_out", [2, DFF, D]), ("ln_mix", [2, D]),
          ("w_mix_in", [2, D, MIX_IN]), ("conv_w", [2, 5, 1152]), ("sink_logits", [2, 6]),
          ("diff_lambda", [2, 4, 32]), ("diff_norm_g", [2, 64]), ("dn_A_log", [2, 2, 6]), ("dn_dt_bias", [2, 2, 6]),
          ("dn_norm_g", [2, 64]), ("w_mix_out", [2, D, D]), ("ln_ffn2", [2, D]), ("ffn2_w_in", [2, D, 2 * DFF]),
          ("ffn2_w_out", [2, DFF, D]), ("ln_final", [D])]


def build_full(S, depth=2, paired=False):
    k = K(S, depth, paired)
    SK = k.SK
    x = k.inp("x", [S, D])
    W = {n: k.inp(n, shp) for n, shp in WNAMES}
    cst = {n: k.inp(n, shp, BF16) for n, shp in (("daq", [4, 4, S]), ("dakp", [4, 4, SK]), ("dakm", [4, 4, SK]),
                                                  ("dbd", [4, 128, 128]), ("identb", [128, 128]))}
    wbias = k.inp("wbias", [6, 128, 384])
    dnc = k.inp("dnc", [10, 128, 128])
    idn = k.inp("ident", [128, 128])
    if paired:
        antiid = k.inp("antiid", [128, 128])
        sel_d = k.inp("sel", [128, 2])
        pr = k.pair_scratch()
    else:
        pr = sel_d = None
    out = k.outp("out", [S, D])
    xres = k.scratch("xres", [S, D])
    ms = k.mix_scratch()
    ds = k.dn_scratch()
    sc = k.sc
    with ExitStack() as st:
        ident = k.sb(st, "ident_sb", [128, 128], F32)
        ib = Buf("ident")
        sc.dma(ident[:], idn[:, :], writes=[ib])
        xb = [Buf("x%d" % i) for i in range(k.NT)]
        xin = [Buf("xin%d" % i) for i in range(k.NT)]
        for l in range(depth):
            lam_init = 0.8 - 0.6 * math.exp(-0.3 * l)
            k.ffn_phase("f1_%d" % l, W["ffn1_w_in"][l], W["ffn1_w_out"][l], W["ln_ffn1"][l],
                        x if l == 0 else xres, xin if l == 0 else xb, xres, xb, ident, ib)
            k.inproj_phase("ip%d" % l, W["w_mix_in"][l], W["ln_mix"][l], xres, xb, ms, ident, ib)
            if paired:
                k.export_phase("ex%d" % l, W["w_mix_in"][l], W["ln_mix"][l], xres, xb, pr, ident, ib, antiid)
                k.exchange_phase("xc%d" % l, pr, ms, sel_d, st)
            with ExitStack() as cst_:
                units = k.conv_units(cst_, "cu%d" % l, ms, ds, W["conv_w"][l], dnc, ident, ib)

                def hook(units=units):
                    if units:
                        units.pop(0)()
                    return len(units) > 0
                k.diffattn_phase("da%d" % l, ms, cst, W["diff_lambda"][l], W["diff_norm_g"][l], lam_init, ident, ib, hook)
            with ExitStack() as wst_:
                wunits = k.win_units(wst_, "wu%d" % l, ms, wbias, W["sink_logits"][l])

                def whook(units=wunits):
                    if units:
                        units.pop(0)()
                    return len(units) > 0
                k.dn2("d2_%d" % l, ms, ds, dnc, W["dn_A_log"][l], W["dn_dt_bias"][l], W["dn_norm_g"][l], ident, ib,
                      pr, sel_d, st, whook)
            k.outproj_phase("op%d" % l, W["w_mix_out"][l], ms, xres, xb, ident, ib)
            k.ffn_phase("f2_%d" % l, W["ffn2_w_in"][l], W["ffn2_w_out"][l], W["ln_ffn2"][l],
                        xres, xb, xres, xb, ident, ib)
        k.final_phase("fin", W["ln_final"], xres, xb, out)
        sc.finish([])
    k.stack.close()
    return k


def pair_feeds(inputs, S_full):
    x = np.ascontiguousarray(np.asarray(inputs["x"], dtype=np.float32))
    B = x.shape[0]
    S = S_full // 2
    base = {n: np.ascontiguousarray(np.asarray(inputs[n], dtype=np.float32)) for n, _ in WNAMES}
    odd = dict(base)
    odd["conv_w"] = np.ascontiguousarray(base["conv_w"][:, ::-1, :])
    wmi = base["w_mix_in"].copy()
    wmi[:, :, C_CB:C_CB + 6] = base["w_mix_in"][:, :, C_CB + 6:C_CB + 12]
    wmi[:, :, C_CB + 6:C_CB + 12] = base["w_mix_in"][:, :, C_CB:C_CB + 6]
    wmi[:, :, C_CA:C_CA + 6] = base["w_mix_in"][:, :, C_CA + 6:C_CA + 12]
    wmi[:, :, C_CA + 6:C_CA + 12] = base["w_mix_in"][:, :, C_CA:C_CA + 6]
    odd["w_mix_in"] = wmi
    odd["dn_A_log"] = np.ascontiguousarray(base["dn_A_log"][:, ::-1, :])
    odd["dn_dt_bias"] = np.ascontiguousarray(base["dn_dt_bias"][:, ::-1, :])
    common = {}
    common.update(diff_consts(S, 2 * S))
    common.update(win_consts())
    common.update(dn_consts())
    common["ident"] = np.eye(128, dtype=np.float32)
    common["antiid"] = np.ascontiguousarray(np.eye(128, dtype=np.float32)[::-1])
    maps = []
    for c in range(2 * B):
        b, r = c // 2, c % 2
        m = dict(base if r == 0 else odd)
        m.update(common)
        if r == 0:
            m["x"] = np.ascontiguousarray(x[b, 0:S])
        else:
            m["x"] = np.ascontiguousarray(x[b, S:2 * S][::-1])
        sel = np.zeros((128, 2), np.float32)
        sel[:, 1 - r] = 1.0
        m["sel"] = sel
        maps.append(m)
    return maps


def pair_gather(results, B, S_full):
    S = S_full // 2
    out = np.empty((B, S_full, D), np.float32)
    for b in range(B):
        out[b, 0:S] = results[2 * b]["out"]
        out[b, S:] = results[2 * b + 1]["out"][::-1]
    return out


GROUPS = [[0, 1], [2, 3], [4, 5], [6, 7]]


def _cdiv(a, b):
    return (a + b - 1) // b


def _pair_chunks(S):
    misc = _cdiv(128 * 128, S) + _cdiv(128 * 130, S)
    return [("bk0", 128), ("bk1", 128), ("bv0", 65), ("bv1", 65), ("bv2", 65), ("bv3", 65), ("misc", misc)]


def _pair_scratch(self):
    S = self.S
    d = {"exp": {}, "gat": {}, "rows": {}}
    for name, rows in _pair_chunks(S):
        d["exp"][name] = self.scratch("exp_" + name, [rows, S], BF16)
        d["gat"][name] = self.scratch("gat_" + name, [2 * rows, S], BF16)
        d["rows"][name] = rows
    d["expf"] = self.scratch("expf", [36, 64], F32)
    d["gatf"] = self.scratch("gatf", [72, 64], F32)
    d["exps"] = self.scratch("exps", [384, 64], F32)
    d["gats"] = self.scratch("gats", [768, 64], F32)
    return d


K.pair_scratch = _pair_scratch


def _pviews(pr, S, slot=None):
    def buf(name):
        if slot is None:
            return pr["exp"][name]
        r = pr["rows"][name]
        return pr["gat"][name][slot * r:(slot + 1) * r, :]
    v = {}
    v["bkT"] = [buf("bk0"), buf("bk1")]
    v["bv"] = [buf("bv%d" % q).rearrange("r c -> (r c)").rearrange("(t d) -> t d", d=260) for q in range(4)]
    mflat = buf("misc").rearrange("r c -> (r c)")
    o2 = _cdiv(128 * 128, S) * S
    v["akT"] = mflat[0:128 * 128].rearrange("(r c) -> r c", c=128)
    v["av"] = mflat[o2:o2 + 128 * 130].rearrange("(t d) -> t d", d=130)
    return v


def _export_phase(self, tag, w_mi, g, src, src_bufs, pr, ident, ident_b, antiid):
    sc = self.sc
    S, NT = self.S, self.NT
    ps, psb = self.ps, self.psb
    ev_ = _pviews(pr, S)
    Q4 = S // 4
    e_akT, e_av = ev_["akT"], ev_["av"]
    e_halo = pr["expf"].rearrange("r c -> (r c)").rearrange("(a b) -> a b", b=2)
    with ExitStack() as st:
        wm = self.sb(st, tag + "wm", [128, 8, MIX_IN], BF16)
        gB = self.sb(st, tag + "gB", [128, D], F32)
        J = self.sb(st, tag + "J", [128, 128], F32)
        xt = [self.sb(st, tag + "xt%d" % i, [128, D], F32) for i in range(2)]
        xn = self.sb(st, tag + "xn", [128, D], F32)
        junk = self.sb(st, tag + "junk", [128, D], F32)
        xr = [self.sb(st, tag + "xr%d" % i, [128, 8, 128], BF16) for i in range(2)]
        stat = self.sb(st, tag + "stat", [128, 8], F32)
        ok_ = [self.sb(st, tag + "ok%d" % i, [128, 2, 128], BF16) for i in range(2)]
        ov = [self.sb(st, tag + "ov%d" % i, [128, 4, 65], BF16) for i in range(2)]
        oak = self.sb(st, tag + "oak", [128, 128], BF16)
        oav = self.sb(st, tag + "oav", [128, 2, 65], BF16)
        oh = self.sb(st, tag + "oh", [128, 9, 2], F32)
        b_wm = [Buf("wm%d" % k) for k in range(8)]
        b_gB, b_J, b_xn, b_junk, b_stat, b_oak, b_oav, b_oh = (Buf(n) for n in
                                                               ("gB", "J", "xn", "junk", "stat", "oak", "oav", "oh"))
        b_xt = [Buf("xt0"), Buf("xt1")]
        b_xr = [Buf("xr0"), Buf("xr1")]
        b_ok = [Buf("ok0"), Buf("ok1")]
        b_ov = [Buf("ov0"), Buf("ov1")]
        for kc in range(8):
            sc.dma(wm[:, kc, :], w_mi[kc * 128:(kc + 1) * 128, :], writes=[b_wm[kc]], q="pool")
        sc.dma(gB[:], g.partition_broadcast(128), writes=[b_gB])
        sc.dma(J[:], antiid[:, :], writes=[b_J])
        for i in range(2):
            sc.op("pool", lambda e: e.memset(ov[i][:], 1.0), writes=[b_ov[i]])
        sc.op("pool", lambda e: e.memset(oav[:], 1.0), writes=[b_oav])
        for it, t in enumerate(range(NT - 1, -1, -1)):
            e = NT - 1 - t
            b = it % 2
            sc.dma(xt[b][:], src[t * 128:(t + 1) * 128, :], reads=[src_bufs[t]], writes=[b_xt[b]])
            sc.op("act", lambda e_: e_.activation(out=junk[:], in_=xt[b][:], func=AF.Square, accum_out=stat[:, 0:1]),
                  reads=[b_xt[b]], writes=[b_junk, b_stat])
            sc.op("dve", lambda e_: e_.tensor_scalar(out=stat[:, 1:2], in0=stat[:, 0:1], scalar1=1.0 / D,
                                                     scalar2=EPS, op0=ALU.mult, op1=ALU.add),
                  reads=[b_stat], writes=[b_stat])
            sc.op("act", lambda e_: e_.sqrt(out=stat[:, 3:4], in_=stat[:, 1:2]), reads=[b_stat], writes=[b_stat])
            sc.op("dve", lambda e_: e_.reciprocal(out=stat[:, 2:3], in_=stat[:, 3:4]), reads=[b_stat], writes=[b_stat])
            sc.op("dve", lambda e_: e_.scalar_tensor_tensor(out=xn[:], in0=xt[b][:], scalar=stat[:, 2:3],
                                                            in1=gB[:], op0=ALU.mult, op1=ALU.mult),
                  reads=[b_xt[b], b_stat, b_gB], writes=[b_xn])
            for kc in range(8):
                bank = kc // 4
                sc.op("pe", lambda e_: e_.matmul(ps[bank][:, (kc % 4) * 128:(kc % 4 + 1) * 128],
                                                 lhsT=xn[:, kc * 128:(kc + 1) * 128], rhs=J[:], start=True, stop=True),
                      reads=[b_xn, b_J], writes=[psb[bank]], signal=(kc % 4 == 3))
            sc.op("act", lambda e_: e_.copy(out=xr[b][:, 0:4, :], in_=ps[0][:, :].rearrange("p (k t) -> p k t", k=4)),
                  reads=[psb[0]], writes=[b_xr[b]])
            sc.op("dve", lambda e_: e_.tensor_copy(out=xr[b][:, 4:8, :],
                                                   in_=ps[1][:, :].rearrange("p (k t) -> p k t", k=4)),
                  reads=[psb[1]], writes=[b_xr[b]])
            for ci in range(2):
                for kc in range(8):
                    sc.op("pe", lambda e_: e_.matmul(ps[2][:, ci * 128:(ci + 1) * 128],
                                                     lhsT=wm[:, kc, C_BK + ci * 128:C_BK + (ci + 1) * 128],
                                                     rhs=xr[b][:, kc, :], start=(kc == 0), stop=(kc == 7)),
                          reads=[b_wm[kc], b_xr[b]], writes=[psb[2]], signal=(kc == 7 and ci == 1))
            for kc in range(8):
                sc.op("pe", lambda e_: e_.matmul(ps[3][:, 0:256], lhsT=xr[b][:, kc, :], rhs=wm[:, kc, C_BV:C_BV + 256],
                                                 start=(kc == 0), stop=(kc == 7)),
                      reads=[b_wm[kc], b_xr[b]], writes=[psb[3]], signal=(kc == 7))
            sc.op("act", lambda e_: e_.copy(out=ok_[b][:].rearrange("p c t -> p (c t)"), in_=ps[2][:, 0:256]),
                  reads=[psb[2]], writes=[b_ok[b]])
            sc.op("dve", lambda e_: e_.tensor_copy(out=ov[b][:, :, 0:64],
                                                   in_=ps[3][:, 0:256].rearrange("p (h d) -> p h d", h=4)),
                  reads=[psb[3]], writes=[b_ov[b]])
            for ci in range(2):
                sc.dma(ev_["bkT"][ci][:, e * 128:(e + 1) * 128], ok_[b][:, ci, :], reads=[b_ok[b]])
            q4 = (e * 128) // Q4
            r4 = e * 128 - q4 * Q4
            sc.dma(ev_["bv"][q4][r4:r4 + 128, :], ov[b][:].rearrange("p h d -> p (h d)"), reads=[b_ov[b]])
            if e == 0:
                for kc in range(8):
                    sc.op("pe", lambda e_: e_.matmul(ps[4][:, 0:128], lhsT=wm[:, kc, C_AK:C_AK + 128], rhs=xr[b][:, kc, :],
                                                     start=(kc == 0), stop=(kc == 7)),
                          reads=[b_wm[kc], b_xr[b]], writes=[psb[4]], signal=(kc == 7))
                for kc in range(8):
                    sc.op("pe", lambda e_: e_.matmul(ps[5][:, 0:128], lhsT=xr[b][:, kc, :], rhs=wm[:, kc, C_AV:C_AV + 128],
                                                     start=(kc == 0), stop=(kc == 7)),
                          reads=[b_wm[kc], b_xr[b]], writes=[psb[5]], signal=(kc == 7))
                for ci in range(9):
                    for kc in range(8):
                        sc.op("pe", lambda e_: e_.matmul(ps[6][:, ci * 2:ci * 2 + 2],
                                                         lhsT=wm[:, kc, C_CQKV + ci * 128:C_CQKV + (ci + 1) * 128],
                                                         rhs=xr[b][:, kc, 0:2], start=(kc == 0), stop=(kc == 7)),
                              reads=[b_wm[kc], b_xr[b]], writes=[psb[6]], signal=(kc == 7 and ci == 8))
                sc.op("act", lambda e_: e_.copy(out=oak[:], in_=ps[4][:, 0:128]), reads=[psb[4]], writes=[b_oak])
                sc.op("dve", lambda e_: e_.tensor_copy(out=oav[:, :, 0:64],
                                                       in_=ps[5][:, 0:128].rearrange("p (h d) -> p h d", h=2)),
                      reads=[psb[5]], writes=[b_oav])
                sc.op("act", lambda e_: e_.copy(out=oh[:].rearrange("p c t -> p (c t)"), in_=ps[6][:, 0:18]),
                      reads=[psb[6]], writes=[b_oh])
                sc.dma(e_akT[:, :], oak[:], reads=[b_oak])
                sc.dma(e_av[:, :], oav[:].rearrange("p h d -> p (h d)"), reads=[b_oav])
                sc.dma(e_halo.rearrange("(c p) t -> p c t", p=128), oh[:], reads=[b_oh])
        sc.barrier()


K.export_phase = _export_phase


def _exchange_phase(self, tag, pr, ms, sel_d, cstack):
    sc = self.sc
    S, NT = self.S, self.NT
    for name, _r in _pair_chunks(S):
        sc.collective(cstack, "AllGather", pr["exp"][name].opt(), pr["gat"][name].opt(), GROUPS)
    Q4 = S // 4
    sc.collective(cstack, "AllGather", pr["expf"].opt(), pr["gatf"].opt(), GROUPS)
    with ExitStack() as st:
        sel = self.sb(st, tag + "sel", [128, 2], F32)
        b_sel = Buf("sel")
        sc.dma(sel[:], sel_d[:, :], writes=[b_sel])
        CW = min(S, 2048)
        a0 = [self.sb(st, tag + "a0_%d" % i, [128, CW], BF16) for i in range(2)]
        a1 = [self.sb(st, tag + "a1_%d" % i, [128, CW], BF16) for i in range(2)]
        b_a0 = [Buf("a0_0"), Buf("a0_1")]
        b_a1 = [Buf("a1_0"), Buf("a1_1")]
        f0 = self.sb(st, tag + "f0", [128, 9, 2], F32)
        f1 = self.sb(st, tag + "f1", [128, 9, 2], F32)
        b_f0, b_f1 = Buf("f0"), Buf("f1")
        cnt = [0]

        def select(dst_ap, src0, src1, np_, w, view=None):
            i = cnt[0] % 2
            cnt[0] += 1
            t0 = a0[i][0:np_, 0:w]
            t1 = a1[i][0:np_, 0:w]
            if view is not None:
                t0v, t1v = view(t0), view(t1)
            else:
                t0v, t1v = t0, t1
            sc.dma(t0v, src0, writes=[b_a0[i]])
            sc.dma(t1v, src1, writes=[b_a1[i]])
            sc.op("dve", lambda e: e.tensor_scalar_mul(out=t0, in0=t0, scalar1=sel[0:np_, 0:1]),
                  reads=[b_a0[i], b_sel], writes=[b_a0[i]])
            sc.op("dve", lambda e: e.scalar_tensor_tensor(out=t1, in0=t1, scalar=sel[0:np_, 1:2], in1=t0,
                                                          op0=ALU.mult, op1=ALU.add),
                  reads=[b_a0[i], b_a1[i], b_sel], writes=[b_a1[i]])
            sc.dma(dst_ap, t1v, reads=[b_a1[i]])

        gv = [_pviews(pr, S, 0), _pviews(pr, S, 1)]
        for ci in range(2):
            for c0 in range(0, S, CW):
                v = lambda slot: gv[slot]["bkT"][ci][:, c0:c0 + CW]
                select(ms["bkT"][ci * 128:(ci + 1) * 128, S + c0:S + c0 + CW], v(0), v(1), 128, CW)
        TPB = max(1, min(CW // 260, Q4 // 128))
        for q4 in range(4):
            for t0_ in range(0, Q4 // 128, TPB):
                tn = min(TPB, Q4 // 128 - t0_)
                v = lambda slot: gv[slot]["bv"][q4][t0_ * 128:(t0_ + tn) * 128, :].rearrange("(n p) d -> p n d", p=128)
                r0 = S + q4 * Q4 + t0_ * 128
                dst = ms["bv"][r0:r0 + tn * 128, :, :].rearrange("(n p) h d -> p n (h d)", p=128)
                select(dst, v(0), v(1), 128, tn * 260, view=lambda t: t.rearrange("p (n d) -> p n d", d=260))
        v = lambda slot: gv[slot]["akT"]
        select(ms["akT"][:, S:S + 128], v(0), v(1), 128, 128)
        v = lambda slot: gv[slot]["av"]
        select(ms["av"][S:S + 128, :, :].rearrange("p h d -> p (h d)"), v(0), v(1), 128, 130)
        hv = lambda slot: pr["gatf"][slot * 36:(slot + 1) * 36, :].rearrange("r c -> (r c)") \
            .rearrange("(c p t) -> p c t", p=128, t=2)
        sc.dma(f0[:], hv(0), writes=[b_f0])
        sc.dma(f1[:], hv(1), writes=[b_f1])
        sc.op("dve", lambda e: e.tensor_scalar_mul(out=f0[:], in0=f0[:], scalar1=sel[:, 0:1]),
              reads=[b_f0, b_sel], writes=[b_f0])
        sc.op("dve", lambda e: e.scalar_tensor_tensor(out=f1[:], in0=f1[:], scalar=sel[:, 1:2], in1=f0[:],
                                                      op0=ALU.mult, op1=ALU.add),
              reads=[b_f0, b_f1, b_sel], writes=[b_f1])
        sc.dma(ms["cpre"][:, S + 2:S + 4].rearrange("(c p) t -> p c t", p=128), f1[:], reads=[b_f1])
        sc.barrier()


K.exchange_phase = _exchange_phase


def _dn2(self, tag, ms, ds, dnc, a_log, dt_bias, dn_g, ident, ident_b, pr=None, sel_d=None, cstack=None, hook=None):
    sc = self.sc
    S, NT = self.S, self.NT
    ps, psb = self.ps, self.psb
    NIT = 7
    gcT = self.scratch(tag + "gcT", [12, S], F32)
    ngcT = self.scratch(tag + "ngcT", [12, S], F32)
    acT = self.scratch(tag + "acT", [12, S], F32)
    with ExitStack() as st0:
        def T0(name, shape, dt=F32):
            return self.sb(st0, tag + name, shape, dt), Buf(name)
        gc, b_gc = T0("gc", [128, NT, 12])
        ac, b_ac = T0("ac", [128, NT, 12])
        beta, b_beta = T0("beta", [128, NT, 12])
        ea, b_ea = T0("ea", [128, NT, 12])
        ekg, b_ekg = T0("ekg", [128, NT, 12])
        glv, b_glv = T0("glv", [128, NT, 12])
        ones, b_ones = T0("ones", [128, 128])
        sc.dma(ones[:], dnc[0], writes=[b_ones])
        with ExitStack() as st:
            def T1(name, shape, dt=F32):
                return self.sb(st, tag + "g_" + name, shape, dt), Buf(name)
            ba, b_ba = T1("ba", [128, NT, 24])
            w1, b_w1 = T1("w1", [128, NT, 12])
            w2, b_w2 = T1("w2", [128, NT, 12])
            sp, b_sp = T1("sp", [128, NT, 12])
            gg, b_gg = T1("gg", [128, NT, 12])
            tt, b_tt = T1("tt", [128, NT, 12])
            triF, b_triF = T1("triF", [128, 128])
            triB, b_triB = T1("triB", [128, 128])
            dtb, b_dtb = T1("dtb", [128, 12])
            nega, b_nega = T1("nega", [128, 12])
            ev = [T1("ev%d" % i, [12, 3, 512]) for i in range(2)]
            sc.dma(ba[:], ms["cz"][:, 384:408].rearrange("(n p) c -> p n c", p=128), writes=[b_ba])
            sc.dma(triF[:], dnc[1], writes=[b_triF])
            sc.dma(triB[:], dnc[2], writes=[b_triB])
            sc.dma(dtb[:], dt_bias.rearrange("a b -> (a b)").partition_broadcast(128), writes=[b_dtb])
            sc.dma(nega[:], a_log.rearrange("a b -> (a b)").partition_broadcast(128), writes=[b_nega])
            sc.op("act", lambda e: e.activation(out=nega[:], in_=nega[:], func=AF.Exp), reads=[b_nega], writes=[b_nega])
            sc.op("dve", lambda e: e.tensor_scalar_mul(out=nega[:], in0=nega[:], scalar1=-1.0),
                  reads=[b_nega], writes=[b_nega])
            bc = lambda t: t[:].unsqueeze(1).to_broadcast([128, NT, 12])
            sc.op("act", lambda e: e.activation(out=w1[:], in_=ba[:, :, 0:12], func=AF.Exp, scale=-1.0),
                  reads=[b_ba], writes=[b_w1])
            sc.op("dve", lambda e: e.tensor_scalar_add(out=w1[:], in0=w1[:], scalar1=1.0), reads=[b_w1], writes=[b_w1])
            sc.op("act", lambda e: e.activation(out=sp[:], in_=w1[:], func=AF.Ln), reads=[b_w1], writes=[b_sp])
            sc.op("dve", lambda e: e.tensor_tensor(out=w2[:], in0=ba[:, :, 12:24], in1=bc(dtb), op=ALU.add),
                  reads=[b_ba, b_dtb], writes=[b_w2])
            sc.op("act", lambda e: e.activation(out=w2[:], in_=w2[:], func=AF.Exp), reads=[b_w2], writes=[b_w2])
            sc.op("dve", lambda e: e.tensor_scalar_add(out=w2[:], in0=w2[:], scalar1=1.0), reads=[b_w2], writes=[b_w2])
            sc.op("act", lambda e: e.activation(out=w2[:], in_=w2[:], func=AF.Ln), reads=[b_w2], writes=[b_w2])
            sc.op("dve", lambda e: e.tensor_tensor(out=gg[:], in0=w2[:], in1=bc(nega), op=ALU.mult),
                  reads=[b_w2, b_nega], writes=[b_gg])
            NC6 = NT * 6
            for c0 in range(0, NT, 64):
                c1 = min(NT, c0 + 64)
                w = (c1 - c0) * 6
                sc.op("pe", lambda e: e.matmul(ps[0][:, 0:w], lhsT=triF[:], rhs=gg[:, c0:c1, 0:6], start=True, stop=True),
                      reads=[b_triF, b_gg], writes=[psb[0]], signal=False)
                sc.op("pe", lambda e: e.matmul(ps[1][:, 0:w], lhsT=triB[:], rhs=gg[:, c0:c1, 6:12], start=True, stop=True),
                      reads=[b_triB, b_gg], writes=[psb[1]], signal=False)
                sc.op("pe", lambda e: e.matmul(ps[2][:, 0:w], lhsT=ones[:], rhs=gg[:, c0:c1, 0:6], start=True, stop=True),
                      reads=[b_ones, b_gg], writes=[psb[2]], signal=False)
                sc.op("pe", lambda e: e.matmul(ps[3][:, 0:w], lhsT=ones[:], rhs=gg[:, c0:c1, 6:12], start=True, stop=True),
                      reads=[b_ones, b_gg], writes=[psb[3]])
                v6 = lambda b: ps[b][:, 0:w].rearrange("p (n j) -> p n j", j=6)
                sc.op("dve", lambda e: e.tensor_copy(out=gc[:, c0:c1, 0:6], in_=v6(0)), reads=[psb[0]], writes=[b_gc])
                sc.op("dve", lambda e: e.tensor_copy(out=gc[:, c0:c1, 6:12], in_=v6(1)), reads=[psb[1]], writes=[b_gc])
                sc.op("dve", lambda e: e.tensor_copy(out=tt[:, c0:c1, 0:6], in_=v6(2)), reads=[psb[2]], writes=[b_tt])
                sc.op("dve", lambda e: e.tensor_copy(out=tt[:, c0:c1, 6:12], in_=v6(3)), reads=[psb[3]], writes=[b_tt])
            sc.op("dve", lambda e: e.tensor_tensor(out=ac[:], in0=gc[:], in1=sp[:], op=ALU.subtract),
                  reads=[b_gc, b_sp], writes=[b_ac])
            sc.op("act", lambda e: e.activation(out=ea[:], in_=ac[:], func=AF.Exp), reads=[b_ac], writes=[b_ea])
            sc.op("act", lambda e: e.activation(out=beta[:], in_=sp[:], func=AF.Exp, scale=-1.0),
                  reads=[b_sp], writes=[b_beta])
            sc.op("dve", lambda e: e.tensor_tensor(out=w1[:], in0=tt[:], in1=gc[:], op=ALU.subtract),
                  reads=[b_tt, b_gc, b_w1], writes=[b_w1])
            sc.op("act", lambda e: e.activation(out=ekg[:], in_=w1[:], func=AF.Exp), reads=[b_w1], writes=[b_ekg])
            sc.op("act", lambda e: e.activation(out=glv[:], in_=tt[:], func=AF.Exp), reads=[b_tt], writes=[b_glv])
            for q0 in range(0, NT, 4):
                qn = min(4, NT - q0)
                (evt, b_evt) = ev[(q0 // 4) % 2]
                for i in range(qn):
                    n = q0 + i
                    sc.op("pe", lambda e: e.transpose(ps[4][0:12, i * 128:(i + 1) * 128], gc[:, n, :], ident[:]),
                          reads=[b_gc, ident_b], writes=[psb[4]], signal=False)
                    sc.op("pe", lambda e: e.transpose(ps[5][0:12, i * 128:(i + 1) * 128], ac[:, n, :], ident[:]),
                          reads=[b_ac, ident_b], writes=[psb[5]], signal=(i == qn - 1))
                w = qn * 128
                sc.op("dve", lambda e: e.tensor_copy(out=evt[:, 0, 0:w], in_=ps[4][0:12, 0:w]), reads=[psb[4]], writes=[b_evt])
                sc.op("dve", lambda e: e.tensor_scalar_mul(out=evt[:, 1, 0:w], in0=ps[4][0:12, 0:w], scalar1=-1.0),
                      reads=[psb[4]], writes=[b_evt])
                sc.op("act", lambda e: e.copy(out=evt[:, 2, 0:w], in_=ps[5][0:12, 0:w]), reads=[psb[5]], writes=[b_evt])
                sc.dma(gcT[:, q0 * 128:q0 * 128 + w], evt[:, 0, 0:w], reads=[b_evt])
                sc.dma(ngcT[:, q0 * 128:q0 * 128 + w], evt[:, 1, 0:w], reads=[b_evt])
                sc.dma(acT[:, q0 * 128:q0 * 128 + w], evt[:, 2, 0:w], reads=[b_evt])
            sc.barrier()
        for dirn in (0, 1):
            with ExitStack() as st:
                def T(name, shape, n=1, dt=F32):
                    ts = [self.sb(st, "%s%d%s%d" % (tag, dirn, name, i), shape, dt) for i in range(n)]
                    bs = [Buf("%s%d" % (name, i)) for i in range(n)]
                    return (ts, bs) if n > 1 else (ts[0], bs[0])
                m1, b_m1 = T("m1", [128, 128])
                m2, b_m2 = T("m2", [128, 128])
                m3, b_m3 = T("m3", [128, 128])
                gdn, b_gdn = T("gdn", [128, 64])
                kTt, b_kTt = T("kTt", [64, 6, 128], 2)
                qTt, b_qTt = T("qTt", [64, 6, 128], 2)
                kt, b_kt = T("kt", [128, 384], 2)
                vt, b_vt = T("vt", [128, 384], 2)
                Rg, b_Rg = T("Rg", [128, 6, 128], 2)
                Rn, b_Rn = T("Rn", [128, 6, 128], 2)
                Ra, b_Ra = T("Ra", [128, 6, 128], 2)
                eR, b_eR = T("eR", [64, 6, 128])
                qg, b_qg = T("qg", [64, 6, 128])
                tmp, b_tmp = T("tmp", [128, 3, 128], 6)
                E, b_E = T("E", [128, 3, 128], 6)
                W0, b_W0 = T("W0", [128, 3, 128], 6)
                W1, b_W1 = T("W1", [128, 3, 128], 6)
                qkT, b_qkT = T("qkT", [128, 128], 6)
                kg, b_kg = T("kg", [128, 64], 6)
                glI, b_glI = T("glI", [64, 64], 6)
                wTn, b_wTn = T("wTn", [64, 128], 6)
                Vn, b_Vn = T("Vn", [128, 64], 6)
                St, b_St = T("St", [64, 64], 6)
                osb, b_osb = T("osb", [128, 6, 64], 2)
                if dirn == 1:
                    gz, b_gz = T("gz", [128, 384], 2)
                    oft, b_oft = T("oft", [128, 6, 64], 2)
                    sqt, b_sqt = T("sqt", [128, 6, 64])
                    sz, b_sz = T("sz", [128, 6, 64])
                    rr, b_rr = T("rr", [128, 8])
                sc.dma(m1[:], dnc[4 + 3 * dirn], writes=[b_m1])
                sc.dma(m2[:], dnc[5 + 3 * dirn], writes=[b_m2])
                sc.dma(m3[:], dnc[6 + 3 * dirn], writes=[b_m3])
                sc.dma(gdn[:], dn_g.partition_broadcast(128), writes=[b_gdn])
                if dirn == 0 or not self.paired:
                    for h in range(6):
                        sc.op("pool", lambda e: e.memset(St[h][:], 0.0), writes=[b_St[h]])
                else:
                    sl, b_sl = T("sl", [128, 2])
                    s0, b_s0 = T("s0", [64, 6, 64])
                    s1, b_s1 = T("s1", [64, 6, 64])
                    sc.dma(sl[:], sel_d[:, :], writes=[b_sl])
                    sc.dma(s0[:], pr["gats"][0:384, :].rearrange("(h k) v -> k h v", h=6), writes=[b_s0])
                    sc.dma(s1[:], pr["gats"][384:768, :].rearrange("(h k) v -> k h v", h=6), writes=[b_s1])
                    sc.op("dve", lambda e: e.tensor_scalar_mul(out=s0[:], in0=s0[:], scalar1=sl[0:64, 0:1]),
                          reads=[b_s0, b_sl], writes=[b_s0])
                    for h in range(6):
                        sc.op("dve", lambda e: e.scalar_tensor_tensor(out=St[h][:], in0=s1[:, h, :], scalar=sl[0:64, 1:2],
                                                                      in1=s0[:, h, :], op0=ALU.mult, op1=ALU.add),
                              reads=[b_s0, b_s1, b_sl], writes=[b_St[h]])
                order = list(range(NT)) if dirn == 0 else list(range(NT - 1, -1, -1))
                j0 = dirn * 6

                def load(it):
                    n = order[it]
                    tb = it % 2
                    c0 = n * 128
                    sc.dma(kTt[tb][:], ds["ckT"][:, c0:c0 + 128].rearrange("(h d) t -> d h t", h=6), writes=[b_kTt[tb]])
                    sc.dma(qTt[tb][:], ds["cqT"][:, c0:c0 + 128].rearrange("(h d) t -> d h t", h=6), writes=[b_qTt[tb]])
                    sc.dma(kt[tb][:], ds["ck"][c0:c0 + 128, :], writes=[b_kt[tb]])
                    sc.dma(vt[tb][:], ds["cv"][c0:c0 + 128, :], writes=[b_vt[tb]])
                    sc.dma(Rg[tb][:], gcT[j0:j0 + 6, c0:c0 + 128].partition_broadcast(128), writes=[b_Rg[tb]])
                    sc.dma(Rn[tb][:], ngcT[j0:j0 + 6, c0:c0 + 128].partition_broadcast(128), writes=[b_Rn[tb]])
                    sc.dma(Ra[tb][:], acT[j0:j0 + 6, c0:c0 + 128].partition_broadcast(128), writes=[b_Ra[tb]])
                    if dirn == 1:
                        sc.dma(gz[tb][:], ms["cz"][c0:c0 + 128, 0:384], writes=[b_gz[tb]])
                        sc.dma(oft[tb][:].rearrange("p h d -> p (h d)"), ds["of"][c0:c0 + 128, :], writes=[b_oft[tb]])

                load(0)
                for it, n in enumerate(order):
                    tb = it % 2
                    c0 = n * 128
                    if it + 1 < NT:
                        load(it + 1)
                    sc.op("act", lambda e: e.activation(out=eR[:], in_=Rg[tb][0:64, :, :], func=AF.Exp),
                          reads=[b_Rg[tb]], writes=[b_eR])
                    sc.op("dve", lambda e: e.tensor_tensor(out=qg[:], in0=qTt[tb][:], in1=eR[:], op=ALU.mult),
                          reads=[b_qTt[tb], b_eR], writes=[b_qg])
                    for h in range(6):
                        bk = 2 + h
                        j = j0 + h
                        hs = slice(h * 64, (h + 1) * 64)
                        sc.op("pe", lambda e: e.matmul(ps[bk][:, 0:128], lhsT=kTt[tb][:, h, :], rhs=kTt[tb][:, h, :],
                                                       start=True, stop=True),
                              reads=[b_kTt[tb]], writes=[psb[bk]], signal=False)
                        sc.op("pe", lambda e: e.matmul(ps[bk][:, 128:256], lhsT=kTt[tb][:, h, :], rhs=qTt[tb][:, h, :],
                                                       start=True, stop=True),
                              reads=[b_kTt[tb], b_qTt[tb]], writes=[psb[bk]])
                        sc.op("dve", lambda e: e.scalar_tensor_tensor(out=tmp[h][:, 0, :], in0=Rn[tb][:, h, :],
                                                                      scalar=ac[:, n, j:j + 1], in1=m1[:],
                                                                      op0=ALU.add, op1=ALU.min),
                              reads=[b_Rn[tb], b_ac, b_m1], writes=[b_tmp[h]])
                        sc.op("dve", lambda e: e.scalar_tensor_tensor(out=tmp[h][:, 1, :], in0=Ra[tb][:, h, :],
                                                                      scalar=gc[:, n, j:j + 1], in1=m2[:],
                                                                      op0=ALU.subtract, op1=ALU.min),
                              reads=[b_Ra[tb], b_gc, b_m2], writes=[b_tmp[h]])
                        sc.op("dve", lambda e: e.scalar_tensor_tensor(out=tmp[h][:, 2, :], in0=Rg[tb][:, h, :],
                                                                      scalar=gc[:, n, j:j + 1], in1=m3[:],
                                                                      op0=ALU.subtract, op1=ALU.min),
                              reads=[b_Rg[tb], b_gc, b_m3], writes=[b_tmp[h]])
                        sc.op("act", lambda e: e.activation(out=E[h][:], in_=tmp[h][:], func=AF.Exp),
                              reads=[b_tmp[h]], writes=[b_E[h]])
                        sc.op("act", lambda e: e.activation(out=W0[h][:, 0, 0:64], in_=vt[tb][:, hs], func=AF.Copy,
                                                            scale=beta[:, n, j:j + 1]),
                              reads=[b_vt[tb], b_beta], writes=[b_W0[h]])
                        sc.op("act", lambda e: e.activation(out=W0[h][:, 0, 64:128], in_=kt[tb][:, hs], func=AF.Copy,
                                                            scale=ea[:, n, j:j + 1]),
                              reads=[b_kt[tb], b_ea], writes=[b_W0[h]])
                        sc.op("act", lambda e: e.activation(out=kg[h][:], in_=kt[tb][:, hs], func=AF.Copy,
                                                            scale=ekg[:, n, j:j + 1]),
                              reads=[b_kt[tb], b_ekg], writes=[b_kg[h]])
                        sc.op("act", lambda e: e.activation(out=glI[h][:], in_=ident[0:64, 0:64], func=AF.Copy,
                                                            scale=glv[0:64, n, j:j + 1]),
                              reads=[ident_b, b_glv], writes=[b_glI[h]])
                    for h in range(6):
                        bk = 2 + h
                        sc.op("dve", lambda e: e.scalar_tensor_tensor(out=W0[h][:, 1, :], in0=ps[bk][:, 0:128], scalar=-1.0,
                                                                      in1=E[h][:, 0, :], op0=ALU.mult, op1=ALU.mult),
                              reads=[psb[bk], b_E[h]], writes=[b_W0[h]])
                        sc.op("dve", lambda e: e.scalar_tensor_tensor(out=W0[h][:, 2, :], in0=ps[bk][:, 0:128], scalar=-1.0,
                                                                      in1=E[h][:, 1, :], op0=ALU.mult, op1=ALU.mult),
                              reads=[psb[bk], b_E[h]], writes=[b_W0[h]])
                        sc.op("dve", lambda e: e.tensor_tensor(out=qkT[h][:], in0=ps[bk][:, 128:256], in1=E[h][:, 2, :],
                                                               op=ALU.mult),
                              reads=[psb[bk], b_E[h]], writes=[b_qkT[h]])
                    WW = [(W0, b_W0), (W1, b_W1)]
                    DVE_H = (1, 3, 4, 5)
                    for i in range(NIT):
                        (Wc, b_Wc), (Wn, b_Wn) = WW[i % 2], WW[(i + 1) % 2]
                        last = (i == NIT - 1)
                        for h in range(6):
                            bk = 2 + h
                            cur = Wc[h]
                            flat = cur[:].rearrange("p a s -> p (a s)")
                            na = 128 if i >= NIT - 2 else 256
                            on_dve = h in DVE_H
                            sc.op("pe", lambda e: e.matmul(ps[bk][:, 0:na], lhsT=cur[:, 2, :], rhs=flat[:, 0:na],
                                                           start=True, stop=on_dve, skip_group_check=True),
                                  reads=[b_Wc[h]], writes=[psb[bk]], signal=(on_dve and last))
                            if not on_dve:
                                sc.op("pe", lambda e: e.matmul(ps[bk][:, 0:128], lhsT=ident[:], rhs=cur[:, 0, :],
                                                               start=False, stop=True, skip_group_check=True),
                                      reads=[b_Wc[h], ident_b], writes=[psb[bk]], signal=last)
                            if not last:
                                sc.op("pe", lambda e: e.matmul(ps[bk][:, 256:384], lhsT=cur[:, 1, :], rhs=cur[:, 2, :],
                                                               start=True, stop=True, skip_group_check=True),
                                      reads=[b_Wc[h]], writes=[psb[bk]])
                        for h in range(6):
                            bk = 2 + h
                            cur = Wc[h]
                            nflat = Wn[h][:].rearrange("p a s -> p (a s)")
                            if h in DVE_H:
                                sc.op("dve", lambda e: e.tensor_tensor(out=nflat[:, 0:128], in0=ps[bk][:, 0:128],
                                                                       in1=cur[:, 0, :], op=ALU.add),
                                      reads=[psb[bk], b_Wc[h]], writes=[b_Wn[h]])
                                if not last:
                                    lo = 128 if i < NIT - 2 else 256
                                    sc.op("dve", lambda e: e.tensor_copy(out=nflat[:, lo:384], in_=ps[bk][:, lo:384]),
                                          reads=[psb[bk]], writes=[b_Wn[h]])
                            else:
                                if last:
                                    sc.op("act", lambda e: e.copy(out=nflat[:, 0:128], in_=ps[bk][:, 0:128]),
                                          reads=[psb[bk]], writes=[b_Wn[h]])
                                elif i < NIT - 2:
                                    sc.op("act", lambda e: e.copy(out=nflat[:, 0:384], in_=ps[bk][:, 0:384]),
                                          reads=[psb[bk]], writes=[b_Wn[h]])
                                else:
                                    sc.op("act", lambda e: e.copy(out=nflat[:, 0:128], in_=ps[bk][:, 0:128]),
                                          reads=[psb[bk]], writes=[b_Wn[h]], )
                                    sc.op("act", lambda e: e.copy(out=nflat[:, 256:384], in_=ps[bk][:, 256:384]),
                                          reads=[psb[bk]], writes=[b_Wn[h]])
                        if hook is not None:
                            hook()
                    (Wf, b_Wf) = WW[NIT % 2]
                    ob = it % 2
                    for h in range(6):
                        bk = 2 + h
                        sc.op("pe", lambda e: e.transpose(ps[bk][0:64, 0:128], Wf[h][:, 0, 64:128], ident[:]),
                              reads=[b_Wf[h], ident_b], writes=[psb[bk]])
                    for h in range(6):
                        bk = 2 + h
                        if h % 2 == 0:
                            sc.op("act", lambda e: e.mul(out=wTn[h][:], in_=ps[bk][0:64, 0:128], mul=-1.0),
                                  reads=[psb[bk]], writes=[b_wTn[h]])
                        else:
                            sc.op("dve", lambda e: e.tensor_scalar_mul(out=wTn[h][:], in0=ps[bk][0:64, 0:128], scalar1=-1.0),
                                  reads=[psb[bk]], writes=[b_wTn[h]])
                    for h in range(6):
                        bk = 2 + h
                        sc.op("pe", lambda e: e.matmul(ps[bk][:, 128:192], lhsT=ident[:], rhs=Wf[h][:, 0, 0:64],
                                                       start=True, stop=False, skip_group_check=True),
                              reads=[b_Wf[h], ident_b], writes=[psb[bk]], signal=False)
                        sc.op("pe", lambda e: e.matmul(ps[bk][:, 128:192], lhsT=wTn[h][:], rhs=St[h][:],
                                                       start=False, stop=True, skip_group_check=True),
                              reads=[b_wTn[h], b_St[h]], writes=[psb[bk]])
                    for h in range(6):
                        bk = 2 + h
                        if h % 2 == 0:
                            sc.op("act", lambda e: e.copy(out=Vn[h][:], in_=ps[bk][:, 128:192]), reads=[psb[bk]], writes=[b_Vn[h]])
                        else:
                            sc.op("dve", lambda e: e.tensor_copy(out=Vn[h][:], in_=ps[bk][:, 128:192]), reads=[psb[bk]],
                                  writes=[b_Vn[h]])
                    for h in range(6):
                        bk = 2 + h
                        sc.op("pe", lambda e: e.matmul(ps[bk][:, 192:256], lhsT=qg[:, h, :], rhs=St[h][:],
                                                       start=True, stop=False, skip_group_check=True),
                              reads=[b_qg, b_St[h]], writes=[psb[bk]], signal=False)
                        sc.op("pe", lambda e: e.matmul(ps[bk][:, 192:256], lhsT=qkT[h][:], rhs=Vn[h][:],
                                                       start=False, stop=True, skip_group_check=True),
                              reads=[b_qkT[h], b_Vn[h]], writes=[psb[bk]], signal=False)
                        sc.op("pe", lambda e: e.matmul(ps[bk][0:64, 256:320], lhsT=kg[h][:], rhs=Vn[h][:],
                                                       start=True, stop=False, skip_group_check=True),
                              reads=[b_kg[h], b_Vn[h]], writes=[psb[bk]], signal=False)
                        sc.op("pe", lambda e: e.matmul(ps[bk][0:64, 256:320], lhsT=glI[h][:], rhs=St[h][:],
                                                       start=False, stop=True, skip_group_check=True),
                              reads=[b_glI[h], b_St[h]], writes=[psb[bk]])
                    for h in range(6):
                        bk = 2 + h
                        if h % 2 == 0:
                            sc.op("act", lambda e: e.copy(out=osb[ob][:, h, :], in_=ps[bk][:, 192:256]), reads=[psb[bk]],
                                  writes=[b_osb[ob]])
                            sc.op("act", lambda e: e.copy(out=St[h][:], in_=ps[bk][0:64, 256:320]), reads=[psb[bk]],
                                  writes=[b_St[h]])
                        else:
                            sc.op("dve", lambda e: e.tensor_copy(out=osb[ob][:, h, :], in_=ps[bk][:, 192:256]),
                                  reads=[psb[bk]], writes=[b_osb[ob]])
                            sc.op("dve", lambda e: e.tensor_copy(out=St[h][:], in_=ps[bk][0:64, 256:320]), reads=[psb[bk]],
                                  writes=[b_St[h]])
                    if dirn == 0:
                        sc.dma(ds["of"][c0:c0 + 128, :], osb[ob][:].rearrange("p h d -> p (h d)"), reads=[b_osb[ob]])
                    else:
                        sc.op("dve", lambda e: e.tensor_tensor(out=oft[tb][:], in0=oft[tb][:], in1=osb[ob][:], op=ALU.add),
                              reads=[b_oft[tb], b_osb[ob]], writes=[b_oft[tb]])
                        sc.op("act", lambda e: e.activation(out=sqt[:], in_=oft[tb][:], func=AF.Square),
                              reads=[b_oft[tb]], writes=[b_sqt])
                        sc.op("dve", lambda e: e.reduce_sum(out=rr[:, 0:6], in_=sqt[:], axis=AX.X), reads=[b_sqt],
                              writes=[b_rr])
                        sc.op("dve", lambda e: e.tensor_scalar(out=rr[:, 0:6], in0=rr[:, 0:6], scalar1=1.0 / 64,
                                                               scalar2=EPS, op0=ALU.mult, op1=ALU.add),
                              reads=[b_rr], writes=[b_rr])
                        sc.op("act", lambda e: e.sqrt(out=rr[:, 0:6], in_=rr[:, 0:6]), reads=[b_rr], writes=[b_rr])
                        sc.op("dve", lambda e: e.reciprocal(out=rr[:, 0:6], in_=rr[:, 0:6]), reads=[b_rr], writes=[b_rr])
                        sc.op("act", lambda e: e.activation(out=sz[:].rearrange("p h d -> p (h d)"), in_=gz[tb][:, 0:384],
                                                            func=AF.Silu), reads=[b_gz[tb]], writes=[b_sz])
                        sc.op("dve", lambda e: e.tensor_tensor(out=sz[:], in0=sz[:],
                                                                in1=gdn[:].unsqueeze(1).to_broadcast([128, 6, 64]),
                                                                op=ALU.mult), reads=[b_sz, b_gdn], writes=[b_sz])
                        sc.op("dve", lambda e: e.tensor_tensor(out=oft[tb][:], in0=oft[tb][:],
                                                                in1=rr[:, 0:6].unsqueeze(2).to_broadcast([128, 6, 64]),
                                                                op=ALU.mult), reads=[b_oft[tb], b_rr], writes=[b_oft[tb]])
                        sc.op("dve", lambda e: e.tensor_tensor(out=oft[tb][:], in0=oft[tb][:], in1=sz[:], op=ALU.mult),
                              reads=[b_oft[tb], b_sz], writes=[b_oft[tb]])
                        sc.dma(ms["omix"][c0:c0 + 128, 640:1024], oft[tb][:].rearrange("p h d -> p (h d)"),
                               reads=[b_oft[tb]])
                if dirn == 0 and self.paired:
                    for h in range(6):
                        sc.dma(pr["exps"][h * 64:(h + 1) * 64, :], St[h][:], reads=[b_St[h]])
                if dirn == 1 and hook is not None:
                    while hook():
                        pass
                sc.barrier()
            if dirn == 0 and self.paired:
                sc.collective(cstack, "AllGather", pr["exps"].opt(), pr["gats"].opt(), GROUPS)


K.dn2 = _dn2


def _conv_units(self, st, tag, ms, ds, conv_w, dnc, ident, ident_b):
    sc = self.sc
    S, NT = self.S, self.NT
    GT = 4 if NT % 4 == 0 else 1
    GW = GT * 128
    ps, psb = self.ps, self.psb
    blk1 = self.sb(st, tag + "blk1", [128, 128], F32)
    cw = self.sb(st, tag + "cw", [128, 9, 5], F32)
    xin = [self.sb(st, tag + "xin%d" % i, [128, GW + 4], F32) for i in range(2)]
    y = [self.sb(st, tag + "y%d" % i, [128, GW], F32) for i in range(2)]
    ee = [self.sb(st, tag + "ee%d" % i, [128, GW], F32) for i in range(2)]
    sq2 = [self.sb(st, tag + "sq%d" % i, [128, GW], F32) for i in range(2)]
    rs2 = [self.sb(st, tag + "rs%d" % i, [128, GW], F32) for i in range(2)]
    yn = [self.sb(st, tag + "yn%d" % i, [128, GW], F32) for i in range(2)]
    tk = [self.sb(st, tag + "tk%d" % i, [128, GW], F32) for i in range(2)]
    b_blk1, b_cw = Buf("blk1"), Buf("cw")
    b_xin = [Buf("xin0"), Buf("xin1")]
    b_y = [Buf("y0"), Buf("y1")]
    b_ee = [Buf("ee0"), Buf("ee1")]
    b_sq2 = [Buf("sq0"), Buf("sq1")]
    b_rs2 = [Buf("rs0"), Buf("rs1")]
    b_yn = [Buf("yn0"), Buf("yn1")]
    b_tk = [Buf("tk0"), Buf("tk1")]
    sc.dma(blk1[:], dnc[3], writes=[b_blk1])
    for ci in range(9):
        sc.dma(cw[:, ci, :], conv_w[:, ci * 128:(ci + 1) * 128].rearrange("j c -> c j"), writes=[b_cw],
               allow_slow_non_contiguous=True)
    units = []
    cnt = [0]

    def make(g0, ci):
        def unit():
            tok0 = g0 * 128
            b = cnt[0] % 2
            cnt[0] += 1
            sq, rs, b_sq, b_rs = sq2[b], rs2[b], b_sq2[b], b_rs2[b]
            sc.dma(xin[b][:], ms["cpre"][ci * 128:(ci + 1) * 128, tok0:tok0 + GW + 4], writes=[b_xin[b]])
            sc.op("dve", lambda e: e.tensor_scalar_mul(out=y[b][:], in0=xin[b][:, 0:GW], scalar1=cw[:, ci, 0:1]),
                  reads=[b_xin[b], b_cw], writes=[b_y[b]])
            for j in range(1, 5):
                sc.op("dve", lambda e: e.scalar_tensor_tensor(out=y[b][:], in0=xin[b][:, j:j + GW],
                                                              scalar=cw[:, ci, j:j + 1], in1=y[b][:],
                                                              op0=ALU.mult, op1=ALU.add),
                      reads=[b_xin[b], b_cw, b_y[b]], writes=[b_y[b]])
            sc.op("act", lambda e: e.activation(out=ee[b][:], in_=y[b][:], func=AF.Exp, scale=-1.0),
                  reads=[b_y[b]], writes=[b_ee[b]])
            sc.op("dve", lambda e: e.tensor_scalar_add(out=ee[b][:], in0=ee[b][:], scalar1=1.0),
                  reads=[b_ee[b]], writes=[b_ee[b]])
            sc.op("dve", lambda e: e.reciprocal(out=ee[b][:], in_=ee[b][:]), reads=[b_ee[b]], writes=[b_ee[b]])
            sc.op("dve", lambda e: e.tensor_tensor(out=y[b][:], in0=y[b][:], in1=ee[b][:], op=ALU.mult),
                  reads=[b_y[b], b_ee[b]], writes=[b_y[b]])
            if ci < 6:
                sc.op("dve", lambda e: e.tensor_tensor(out=sq[:], in0=y[b][:], in1=y[b][:], op=ALU.mult),
                      reads=[b_y[b]], writes=[b_sq])
                sc.op("pe", lambda e: e.matmul(ps[6][:, :GW], lhsT=blk1[:], rhs=sq[:], start=True, stop=True),
                      reads=[b_blk1, b_sq], writes=[psb[6]])
                mul = 64.0 if ci < 3 else 1.0
                sc.op("dve", lambda e: e.tensor_scalar(out=rs[:], in0=ps[6][:, :GW], scalar1=EPS, scalar2=mul,
                                                       op0=ALU.add, op1=ALU.mult),
                      reads=[psb[6]], writes=[b_rs])
                sc.op("act", lambda e: e.activation(out=rs[:], in_=rs[:], func=AF.Ln), reads=[b_rs], writes=[b_rs])
                sc.op("act", lambda e: e.activation(out=rs[:], in_=rs[:], func=AF.Exp, scale=-0.5),
                      reads=[b_rs], writes=[b_rs])
                sc.op("dve", lambda e: e.tensor_tensor(out=yn[b][:], in0=y[b][:], in1=rs[:], op=ALU.mult),
                      reads=[b_y[b], b_rs], writes=[b_yn[b]])
                src_t, src_b = yn[b], b_yn[b]
                if ci < 3:
                    sc.dma(ds["cqT"][ci * 128:(ci + 1) * 128, tok0:tok0 + GW], yn[b][:], reads=[b_yn[b]])
                else:
                    sc.dma(ds["ckT"][(ci - 3) * 128:(ci - 2) * 128, tok0:tok0 + GW], yn[b][:], reads=[b_yn[b]])
            else:
                src_t, src_b = y[b], b_y[b]
            if ci >= 3:
                for tt in range(GT):
                    sc.op("pe", lambda e: e.transpose(ps[7][:, tt * 128:(tt + 1) * 128],
                                                      src_t[:, tt * 128:(tt + 1) * 128], ident[:]),
                          reads=[src_b, ident_b], writes=[psb[7]], signal=(tt == GT - 1))
                sc.op("dve", lambda e: e.tensor_copy(out=tk[b][:], in_=ps[7][:, :GW]), reads=[psb[7]], writes=[b_tk[b]])
                dst = ds["ck"] if ci < 6 else ds["cv"]
                cc = (ci - 3) % 3
                sc.dma(dst[tok0:tok0 + GW, cc * 128:(cc + 1) * 128].rearrange("(t p) c -> p t c", p=128),
                       tk[b][:].rearrange("p (t c) -> p t c", c=128), reads=[b_tk[b]])
        return unit

    for g0 in range(0, NT, GT):
        for ci in range(9):
            units.append(make(g0, ci))
    return units


K.conv_units = _conv_units


def _win_units(self, st, tag, ms, wbias, sink):
    sc = self.sc
    S, NT = self.S, self.NT
    NKA = NT + 1 if self.paired else NT
    ps, psb = self.ps, self.psb
    qT = self.sb(st, tag + "qT", [128, S], BF16)
    kT = self.sb(st, tag + "kT", [128, NKA * 128], BF16)
    vA = self.sb(st, tag + "vA", [128, NKA, 65], BF16)
    wb = self.sb(st, tag + "wb", [128, 384], F32)
    sT = [self.sb(st, tag + "sT%d" % i, [128, 384], F32) for i in range(2)]
    pT = [self.sb(st, tag + "pT%d" % i, [128, 384], BF16) for i in range(2)]
    ow = self.sb(st, tag + "ow", [128, NT, 64], F32)
    ou = self.sb(st, tag + "ou", [128, NT, 65], F32)
    dn_ = self.sb(st, tag + "dn", [128, NT], F32)
    es = self.sb(st, tag + "es", [128, 6], F32)
    b_qT, b_kT, b_vA, b_wb, b_ow, b_es, b_ou, b_dn = (Buf(n) for n in ("qT", "kT", "vA", "wb", "ow", "es", "ou", "dn"))
    b_sT = [Buf("sT0"), Buf("sT1")]
    b_pT = [Buf("pT0"), Buf("pT1")]
    units = []
    cnt = [0]

    def setup0():
        sc.op("dve", lambda e: e.memset(qT[:], 0.0), writes=[b_qT])
        sc.op("dve", lambda e: e.memset(kT[:], 0.0), writes=[b_kT])
        sc.dma(es[:], sink.partition_broadcast(128), writes=[b_es])
        sc.op("act", lambda e: e.activation(out=es[:], in_=es[:], func=AF.Exp), reads=[b_es], writes=[b_es])
    units.append(setup0)

    def mk_head(h):
        def f():
            kh = h // 3
            if h % 3 == 0:
                sc.dma(kT[0:64, :], ms["akT"][kh * 64:(kh + 1) * 64, :], writes=[b_kT])
                sc.dma(vA[:], ms["av"][:, kh, :].rearrange("(n p) d -> p n d", p=128), writes=[b_vA])
            sc.dma(qT[0:64, :], ms["aqT"][h * 64:(h + 1) * 64, :], writes=[b_qT])
            sc.dma(wb[:], wbias[h], writes=[b_wb])
        return f

    def mk_tile(h, n):
        def f():
            js = [j for j in (0, 1, 2) if 0 <= n - 1 + j < NKA]
            c0 = js[0] * 128
            w = len(js) * 128
            sb_ = cnt[0] % 2
            cnt[0] += 1
            for jj, j in enumerate(js):
                kt = n - 1 + j
                sc.op("pe", lambda e: e.matmul(ps[0][:, jj * 128:(jj + 1) * 128], lhsT=kT[:, kt * 128:(kt + 1) * 128],
                                               rhs=qT[:, n * 128:(n + 1) * 128], start=True, stop=True),
                      reads=[b_kT, b_qT], writes=[psb[0]], signal=(jj == len(js) - 1))
            sc.op("dve", lambda e: e.tensor_tensor(out=sT[sb_][:, 0:w], in0=ps[0][:, 0:w], in1=wb[:, c0:c0 + w],
                                                   op=ALU.add),
                  reads=[psb[0], b_wb], writes=[b_sT[sb_]])
            sc.op("act", lambda e: e.activation(out=pT[sb_][:, 0:w], in_=sT[sb_][:, 0:w], func=AF.Exp),
                  reads=[b_sT[sb_]], writes=[b_pT[sb_]])
            for jj, j in enumerate(js):
                kt = n - 1 + j
                sc.op("pe", lambda e: e.matmul(ps[1][:, 0:65], lhsT=pT[sb_][:, jj * 128:(jj + 1) * 128],
                                               rhs=vA[:, kt, :], start=(jj == 0), stop=(jj == len(js) - 1)),
                      reads=[b_pT[sb_], b_vA], writes=[psb[1]], signal=(jj == len(js) - 1))
            sc.op("dve", lambda e: e.tensor_copy(out=ou[:, n, :], in_=ps[1][:, 0:65]), reads=[psb[1]], writes=[b_ou])
        return f

    def mk_fin(h):
        def f():
            sc.op("dve", lambda e: e.tensor_scalar(out=dn_[:], in0=ou[:, :, 64], scalar1=es[:, h:h + 1], scalar2=None,
                                                   op0=ALU.add), reads=[b_ou, b_es], writes=[b_dn])
            sc.op("dve", lambda e: e.reciprocal(out=dn_[:], in_=dn_[:]), reads=[b_dn], writes=[b_dn])
            sc.op("dve", lambda e: e.tensor_tensor(out=ow[:], in0=ou[:, :, 0:64],
                                                   in1=dn_[:].unsqueeze(2).to_broadcast([128, NT, 64]), op=ALU.mult),
                  reads=[b_ou, b_dn], writes=[b_ow])
            sc.dma(ms["omix"][:, h * 64:(h + 1) * 64].rearrange("(n p) d -> p n d", p=128), ow[:], reads=[b_ow])
        return f

    for h in range(6):
        units.append(mk_head(h))
        for n in range(NT):
            units.append(mk_tile(h, n))
        units.append(mk_fin(h))
    return units


K.win_units = _win_units

import numpy as np, ml_dtypes
BF = ml_dtypes.bfloat16
def diff_consts(S, SK=None):
    SK = SK or S
    pos = np.arange(SK)
    H = (pos // 128) * 128.0
    L = (pos % 128) * 1.0
    daq = np.zeros((4, 4, SK), np.float32); dakp = np.zeros((4, 4, SK), np.float32)
    dbd = np.zeros((4, 128, 128), np.float32)
    for h in range(4):
        s = 2.0 ** (-8.0 * (h + 1) / 4)
        daq[h, 0] = -s * H; daq[h, 1] = -s * L; daq[h, 2] = 1; daq[h, 3] = 1
        dakp[h, 0] = 1; dakp[h, 1] = 1; dakp[h, 2] = s * H; dakp[h, 3] = s * L
        kk = np.arange(128)[:, None]; qq = np.arange(128)[None, :]
        dbd[h] = -s * np.abs(qq - kk)
    daq = daq[:, :, :S]
    c = dict(daq=np.ascontiguousarray(daq).astype(BF), dakp=dakp.astype(BF), dakm=(-dakp).astype(BF), dbd=dbd.astype(BF),
             identb=np.eye(128, dtype=np.float32).astype(BF))
    assert np.array_equal(c["daq"].astype(np.float32), daq) and np.array_equal(c["dakp"].astype(np.float32), dakp)
    assert np.array_equal(c["dbd"].astype(np.float32), dbd)
    return c

def win_consts():
    wb = np.zeros((6, 128, 384), np.float32)
    k = np.arange(128)[:, None]; q = np.arange(128)[None, :]
    for h in range(6):
        s = np.float32(2.0) ** np.float32(-8.0 * (h + 1) / 6)
        for j in range(3):
            rel = (j - 1) * 128 + k - q
            b = np.where(np.abs(rel) <= 128, -np.float32(s) * np.abs(rel).astype(np.float32), np.float32(-30000.0))
            wb[h, :, j * 128:(j + 1) * 128] = b
    return dict(wbias=wb)

def dn_consts():
    c = np.zeros((10, 128, 128), np.float32)
    p = np.arange(128)[:, None]; f = np.arange(128)[None, :]
    c[0] = 1.0
    c[1] = (p <= f)
    c[2] = (p >= f)
    c[3] = ((p // 64) == (f // 64))
    BIG = 30000.0
    c[4] = np.where(p > f, 0.0, -BIG)
    c[5] = np.where(f > p, 0.0, -BIG)
    c[6] = np.where(f >= p, 0.0, -BIG)
    c[7] = np.where(p < f, 0.0, -BIG)
    c[8] = np.where(f < p, 0.0, -BIG)
    c[9] = np.where(f <= p, 0.0, -BIG)
    return dict(dnc=c)


_CACHE = {}
N_CORES = 8


def _get_built(S_loc):
    if S_loc not in _CACHE:
        _CACHE[S_loc] = build_full(S_loc, 2, paired=True)
    return _CACHE[S_loc]


def kernel(**inputs):
    from concourse.bass_utils import run_bass_kernel_spmd
    x = np.asarray(inputs["x"])
    B, S, _ = x.shape
    assert 2 * B == N_CORES
    k = _get_built(S // 2)
    in_maps = pair_feeds(inputs, S)
    res = run_bass_kernel_spmd(k.nc, in_maps, core_ids=list(range(N_CORES)))
    return pair_gather(res.results, B, S)
```

```python
import numpy as np
import concourse.bass as bass
import concourse.mybir as mybir

F32 = mybir.dt.float32
BF16 = mybir.dt.bfloat16
AF = mybir.ActivationFunctionType
ALU = mybir.AluOpType
AX = mybir.AxisListType


class Buf:
    __slots__ = ("name", "w", "rs", "excl")

    def __init__(self, name, excl=False):
        self.name = name
        self.excl = excl
        self.w = None
        self.rs = []


class Sched:
    NDMA = 24

    def __init__(self, nc, stack):
        self.nc = nc
        self.eng = {"pe": nc.tensor, "act": nc.scalar, "dve": nc.vector, "pool": nc.gpsimd, "sp": nc.sync}
        self.sem = {}
        self.cnt = {}
        for k in self.eng:
            self.sem[k] = stack.enter_context(nc.semaphore("sem_" + k))
            self.cnt[k] = 0
        self.dsem = [stack.enter_context(nc.semaphore("dsem%d" % i)) for i in range(self.NDMA)]
        self.dgen = [0] * self.NDMA
        self.qslots = {"sp": list(range(0, 16)), "pool": list(range(16, 20)), "act": list(range(20, 24))}
        self.qnext = {"sp": 0, "pool": 0, "act": 0}
        self.waited = {k: {} for k in self.eng}
        self.pending_pe = False
        self.n_ins = 0
        self.n_wait = 0

    def _semobj(self, key):
        if isinstance(key, int):
            return self.dsem[key]
        return self.sem[key]

    def _need(self, e, evs):
        best = {}
        for ev in evs:
            if ev is None:
                continue
            k, v = ev
            if k == e and e in ("pe", "sp"):
                continue
            if best.get(k, 0) < v:
                best[k] = v
        w = self.waited[e]
        for k, v in best.items():
            if w.get(k, 0) >= v:
                continue
            self.eng[e].wait_ge(self._semobj(k), v)
            self.n_wait += 1
            w[k] = v

    def _deps(self, reads, writes, e=None):
        evs = []
        for b in reads:
            evs.append(b.w)
            if b.excl:
                evs.extend(r for r in b.rs if r[0] != e)
        for b in writes:
            evs.append(b.w)
            evs.extend(b.rs)
        return evs

    def _commit(self, ev, reads, writes):
        for b in reads:
            b.rs.append(ev)
            if len(b.rs) > 64:
                mx = {}
                for k, v in b.rs:
                    if mx.get(k, 0) < v:
                        mx[k] = v
                b.rs = list(mx.items())
        for b in writes:
            b.w = ev
            b.rs = []

    def op(self, e, fn, reads=(), writes=(), signal=True):
        if e != "pe":
            assert not self.pending_pe, "non-signaling PE op must be followed by signaling PE op"
        self._need(e, self._deps(reads, writes, e))
        ins = fn(self.eng[e])
        self.n_ins += 1
        if e == "pe":
            self.pending_pe = not signal
        if signal:
            self.cnt[e] += 1
            ins.then_inc(self.sem[e], 1)
            ev = (e, self.cnt[e])
        else:
            ev = (e, self.cnt[e] + 1)
        self._commit(ev, reads, writes)
        return ins

    def dma(self, out, in_, reads=(), writes=(), q="sp", **kw):
        assert not self.pending_pe
        sl = self.qslots[q]
        slot = sl[self.qnext[q] % len(sl)]
        self.qnext[q] += 1
        evs = self._deps(reads, writes)
        if self.dgen[slot] > 0:
            evs.append((slot, 16 * self.dgen[slot]))
        self._need(q, evs)
        self.dgen[slot] += 1
        ins = self.eng[q].dma_start(out=out, in_=in_, **kw)
        ins.then_inc(self.dsem[slot], 16)
        self.n_ins += 1
        ev = (slot, 16 * self.dgen[slot])
        self._commit(ev, reads, writes)
        return ins

    def barrier(self):
        evs = [(k, self.cnt[k]) for k in self.eng if self.cnt[k] > 0]
        evs += [(i, 16 * self.dgen[i]) for i in range(self.NDMA) if self.dgen[i] > 0]
        for e in self.eng:
            w = self.waited[e]
            for k, v in evs:
                if k == e and e in ("pe", "sp"):
                    continue
                if w.get(k, 0) >= v:
                    continue
                self.eng[e].wait_ge(self._semobj(k), v)
                w[k] = v

    def collective(self, stack, kind, in_ap, out_ap, groups):
        import concourse.mybir as mybir
        self.barrier()
        sem = stack.enter_context(self.nc.semaphore("ccsem%d" % self.n_ins))
        g = self.eng["pool"]
        g.collective_compute(kind, mybir.AluOpType.bypass, replica_groups=groups,
                             ins=[in_ap], outs=[out_ap]).then_inc(sem)
        g.wait_ge(sem, 1)
        self.n_ins += 1
        self.cnt["pool"] += 1
        g.engine_nop().then_inc(self.sem["pool"], 1)
        self.barrier()

    def finish(self, out_bufs):
        self.barrier()

import numpy as np
from contextlib import ExitStack
import concourse.bass as bass
import concourse.mybir as mybir

D = 1024
DFF = 2752
NFC = 22
EPS = 1e-6


class K:
    def __init__(self, S, depth=2, paired=False):
        self.S = S
        self.paired = paired
        self.SK = 2 * S if paired else S
        self.NTK = self.SK // 128
        self.NT = S // 128
        self.depth = depth
        self.nc = bass.Bass("TRN2", target_bir_lowering=False)
        self.stack = ExitStack()
        self.sc = Sched(self.nc, self.stack)
        self.ins = {}
        nc = self.nc
        self.ps = []
        self.psb = []
        for i in range(8):
            t = self.stack.enter_context(nc.psum_tensor("ps%d" % i, [128, 512], F32))
            self.ps.append(t)
            self.psb.append(Buf("ps%d" % i, excl=True))

    def inp(self, name, shape, dt=F32):
        t = self.nc.dram_tensor(name, list(shape), dt, kind="ExternalInput").ap()
        self.ins[name] = t
        return t

    def outp(self, name, shape, dt=F32):
        return self.nc.dram_tensor(name, list(shape), dt, kind="ExternalOutput").ap()

    def scratch(self, name, shape, dt=F32):
        return self.nc.dram_tensor(name, list(shape), dt, kind="Internal").ap()

    def sb(self, st, name, shape, dt=F32):
        return st.enter_context(self.nc.sbuf_tensor(name, list(shape), dt))

    def ffn_phase(self, tag, w_in, w_out, g, src, src_bufs, dst, dst_bufs, ident, ident_b):
        sc = self.sc
        S, NT = self.S, self.NT
        GT = 4 if NT % 4 == 0 else 1
        GW = GT * 128
        NG = NT // GT
        with ExitStack() as st:
            w1 = self.sb(st, tag + "w1", [128, 8, 2 * DFF], BF16)
            w2 = self.sb(st, tag + "w2", [128, NFC, D], BF16)
            gB = self.sb(st, tag + "gB", [128, D], F32)
            xt = [self.sb(st, tag + "xt%d" % i, [128, D], F32) for i in range(2)]
            xr = [self.sb(st, tag + "xr%d" % i, [128, D], F32) for i in range(2)]
            xn = self.sb(st, tag + "xn", [128, D], F32)
            xnT = [self.sb(st, tag + "xnT%d" % i, [128, 8, GW], BF16) for i in range(2)]
            hT = self.sb(st, tag + "hT", [128, NFC, GW], BF16)
            sg = [self.sb(st, tag + "sg%d" % i, [128, GW], F32) for i in range(2)]
            stat = self.sb(st, tag + "stat", [128, 8], F32)
            b_w1 = [Buf("w1_%d" % k) for k in range(8)]
            b_w2 = [Buf("w2_%d" % c) for c in range(NFC)]
            b_gB = Buf("gB")
            b_xt = [Buf("xt0"), Buf("xt1")]
            b_xr = [Buf("xr0"), Buf("xr1")]
            b_xn, b_stat = Buf("xn"), Buf("stat")
            b_xnT = [Buf("xnT0"), Buf("xnT1")]
            b_hT = [Buf("hT%d" % c) for c in range(NFC)]
            b_sg = [Buf("sg0"), Buf("sg1")]
            ps, psb = self.ps, self.psb
            for kc in range(8):
                for hf in range(2):
                    sc.dma(w1[:, kc, hf * DFF:(hf + 1) * DFF], w_in[kc * 128:(kc + 1) * 128, hf * DFF:(hf + 1) * DFF],
                           writes=[b_w1[kc]], q="pool")
            for c in range(NFC):
                cw = min(128, DFF - c * 128)
                sc.dma(w2[:cw, c, :], w_out[c * 128:c * 128 + cw, :], writes=[b_w2[c]], q="pool")
            sc.dma(gB[:], g.partition_broadcast(128), writes=[b_gB])
            tcount = [0]

            def ln_tile(gi, tt):
                t = gi * GT + tt
                xb = tcount[0] % 2
                tcount[0] += 1
                xq = xnT[gi % 2]
                bq = b_xnT[gi % 2]
                sc.dma(xt[xb][:], src[t * 128:(t + 1) * 128, :], reads=[src_bufs[t]], writes=[b_xt[xb]])
                sc.op("act", lambda e: e.activation(out=xn[:], in_=xt[xb][:], func=AF.Square, accum_out=stat[:, 0:1]),
                      reads=[b_xt[xb]], writes=[b_xn, b_stat])
                sc.op("dve", lambda e: e.tensor_scalar(out=stat[:, 1:2], in0=stat[:, 0:1], scalar1=1.0 / D,
                                                       scalar2=EPS, op0=ALU.mult, op1=ALU.add),
                      reads=[b_stat], writes=[b_stat])
                sc.op("act", lambda e: e.sqrt(out=stat[:, 3:4], in_=stat[:, 1:2]), reads=[b_stat], writes=[b_stat])
                sc.op("dve", lambda e: e.reciprocal(out=stat[:, 2:3], in_=stat[:, 3:4]), reads=[b_stat], writes=[b_stat])
                sc.op("dve", lambda e: e.scalar_tensor_tensor(out=xn[:], in0=xt[xb][:], scalar=stat[:, 2:3],
                                                              in1=gB[:], op0=ALU.mult, op1=ALU.mult),
                      reads=[b_xt[xb], b_stat, b_gB], writes=[b_xn])
                for kc in range(8):
                    bank = 6 + kc // 4
                    sc.op("pe", lambda e: e.transpose(ps[bank][:, (kc % 4) * 128:(kc % 4 + 1) * 128],
                                                      xn[:, kc * 128:(kc + 1) * 128], ident[:]),
                          reads=[b_xn, ident_b], writes=[psb[bank]], signal=(kc % 4 == 3))
                sc.op("act", lambda e: e.copy(out=xq[:, 0:4, tt * 128:(tt + 1) * 128],
                                              in_=ps[6][:, :].rearrange("p (k t) -> p k t", k=4)),
                      reads=[psb[6]], writes=[bq])
                sc.op("dve", lambda e: e.tensor_copy(out=xq[:, 4:8, tt * 128:(tt + 1) * 128],
                                                     in_=ps[7][:, :].rearrange("p (k t) -> p k t", k=4)),
                      reads=[psb[7]], writes=[bq])

            for tt in range(GT):
                ln_tile(0, tt)
            for gi in range(NG):
                g0 = gi * GT
                xq = xnT[gi % 2]
                bq = b_xnT[gi % 2]
                for c in range(NFC):
                    cw = min(128, DFF - c * 128)
                    pg, pu = 2 + 2 * (c % 2), 3 + 2 * (c % 2)
                    for kc in range(8):
                        sc.op("pe", lambda e: e.matmul(ps[pg][:cw, :GW], lhsT=w1[:, kc, c * 128:c * 128 + cw],
                                                       rhs=xq[:, kc, :], start=(kc == 0), stop=(kc == 7)),
                              reads=[b_w1[kc], bq], writes=[psb[pg]], signal=(kc == 7))
                    for kc in range(8):
                        sc.op("pe", lambda e: e.matmul(ps[pu][:cw, :GW],
                                                       lhsT=w1[:, kc, DFF + c * 128:DFF + c * 128 + cw],
                                                       rhs=xq[:, kc, :], start=(kc == 0), stop=(kc == 7)),
                              reads=[b_w1[kc], bq], writes=[psb[pu]], signal=(kc == 7))
                    sc.op("act", lambda e: e.activation(out=sg[c % 2][:cw, :], in_=ps[pg][:cw, :GW], func=AF.Silu),
                          reads=[psb[pg]], writes=[b_sg[c % 2]])
                    sc.op("dve", lambda e: e.tensor_tensor(out=hT[:cw, c, :], in0=ps[pu][:cw, :GW],
                                                           in1=sg[c % 2][:cw, :], op=ALU.mult),
                          reads=[psb[pu], b_sg[c % 2]], writes=[b_hT[c]])
                for tt in range(GT):
                    t = g0 + tt
                    rb = t % 2
                    sc.dma(xr[rb][:], src[t * 128:(t + 1) * 128, :], reads=[src_bufs[t]], writes=[b_xr[rb]])
                    for hf in range(2):
                        for c in range(NFC):
                            cw = min(128, DFF - c * 128)
                            sc.op("pe", lambda e: e.matmul(ps[hf][:, :], lhsT=hT[:cw, c, tt * 128:(tt + 1) * 128],
                                                           rhs=w2[:cw, c, hf * 512:(hf + 1) * 512],
                                                           start=(c == 0), stop=(c == NFC - 1)),
                                  reads=[b_hT[c], b_w2[c]], writes=[psb[hf]], signal=(c == NFC - 1))
                    if gi + 1 < NG:
                        ln_tile(gi + 1, tt)
                    for hf in range(2):
                        sc.op("dve", lambda e: e.scalar_tensor_tensor(
                            out=xr[rb][:, hf * 512:(hf + 1) * 512], in0=ps[hf][:, :], scalar=0.5,
                            in1=xr[rb][:, hf * 512:(hf + 1) * 512], op0=ALU.mult, op1=ALU.add),
                              reads=[psb[hf], b_xr[rb]], writes=[b_xr[rb]])
                    sc.dma(dst[t * 128:(t + 1) * 128, :], xr[rb][:], reads=[b_xr[rb]], writes=[dst_bufs[t]])
            sc.barrier()


HD = 64
MIX_IN = 2968
C_AQ, C_AK, C_AV = 0, 384, 512
C_BQ, C_BK, C_BV = 640, 896, 1152
C_CQKV, C_CZ, C_CB, C_CA = 1408, 2560, 2944, 2956
FM_CHUNKS = ([(C_AQ + 128 * i, "aq", i) for i in range(3)] + [(C_AK, "ak", 0)] +
             [(C_BQ + 128 * i, "bq", i) for i in range(2)] + [(C_BK + 128 * i, "bk", i) for i in range(2)] +
             [(C_CQKV + 128 * i, "c", i) for i in range(9)])


def _mix_scratch(self):
    S = self.S
    SK = self.SK
    SA = S + 128 if self.paired else S
    d = {}
    d["aqT"] = self.scratch("aqT", [384, S], BF16)
    d["akT"] = self.scratch("akT", [128, SA], BF16)
    d["av"] = self.scratch("av", [SA, 2, 65], BF16)
    d["bqT"] = self.scratch("bqT", [256, S], BF16)
    d["bkT"] = self.scratch("bkT", [256, SK], BF16)
    d["bv"] = self.scratch("bv", [SK, 4, 65], BF16)
    d["cpre"] = self.scratch("cpre", [1152, S + 4], F32)
    d["cz"] = self.scratch("cz", [S, 408], F32)
    d["omix"] = self.scratch("omix", [S, D], F32)
    return d


K.mix_scratch = _mix_scratch


def _inproj_phase(self, tag, w_mi, g, src, src_bufs, ms, ident, ident_b):
    sc = self.sc
    S, NT = self.S, self.NT
    GT = 4 if NT % 4 == 0 else 1
    GW = GT * 128
    ps, psb = self.ps, self.psb
    with ExitStack() as st:
        wm = self.sb(st, tag + "wm", [128, 8, MIX_IN], BF16)
        gB = self.sb(st, tag + "gB", [128, D], F32)
        xt = [self.sb(st, tag + "xt%d" % i, [128, D], F32) for i in range(2)]
        xn = self.sb(st, tag + "xn", [128, D], F32)
        junk = self.sb(st, tag + "junk", [128, D], F32)
        xnT = self.sb(st, tag + "xnT", [128, 8, GW], BF16)
        stat = self.sb(st, tag + "stat", [128, 8], F32)
        ob16 = [self.sb(st, tag + "ob16_%d" % i, [128, GW], BF16) for i in range(3)]
        of32 = [self.sb(st, tag + "of32_%d" % i, [128, GW], F32) for i in range(3)]
        tv = [self.sb(st, tag + "tv%d" % i, [128, 6, 65], BF16) for i in range(2)]
        zt = self.sb(st, tag + "zt", [128, 2], F32)
        tz = [self.sb(st, tag + "tz%d" % i, [128, 408], F32) for i in range(2)]
        b_wm = [Buf("wm%d" % k) for k in range(8)]
        b_gB = Buf("gB")
        b_xt = [Buf("xt0"), Buf("xt1")]
        b_xn, b_junk, b_xnT, b_stat = Buf("xn"), Buf("junk"), Buf("xnT"), Buf("stat")
        b_ob16 = [Buf("ob16_%d" % i) for i in range(3)]
        b_of32 = [Buf("of32_%d" % i) for i in range(3)]
        b_tv = [Buf("tv0"), Buf("tv1")]
        b_tz = [Buf("tz0"), Buf("tz1")]
        for kc in range(8):
            sc.dma(wm[:, kc, :], w_mi[kc * 128:(kc + 1) * 128, :], writes=[b_wm[kc]], q="pool")
        sc.dma(gB[:], g.partition_broadcast(128), writes=[b_gB])
        b_zt = Buf("zt")
        sc.op("pool", lambda e: e.memset(zt[:], 0.0), writes=[b_zt])
        for i in range(9):
            sc.dma(ms["cpre"][i * 128:(i + 1) * 128, 0:2], zt[:, :], reads=[b_zt])
            if not self.paired:
                sc.dma(ms["cpre"][i * 128:(i + 1) * 128, S + 2:S + 4], zt[:, :], reads=[b_zt])
        for i in range(2):
            sc.op("pool", lambda e: e.memset(tv[i][:], 1.0), writes=[b_tv[i]])
        ti = 0
        n16 = n32 = 0
        for g0 in range(0, NT, GT):
            for tt in range(GT):
                t = g0 + tt
                xb = ti % 2
                ti += 1
                sc.dma(xt[xb][:], src[t * 128:(t + 1) * 128, :], reads=[src_bufs[t]], writes=[b_xt[xb]])
                sc.op("act", lambda e: e.activation(out=junk[:], in_=xt[xb][:], func=AF.Square,
                                                    accum_out=stat[:, 0:1]),
                      reads=[b_xt[xb]], writes=[b_junk, b_stat])
                sc.op("dve", lambda e: e.tensor_scalar(out=stat[:, 1:2], in0=stat[:, 0:1], scalar1=1.0 / D,
                                                       scalar2=EPS, op0=ALU.mult, op1=ALU.add),
                      reads=[b_stat], writes=[b_stat])
                sc.op("act", lambda e: e.sqrt(out=stat[:, 3:4], in_=stat[:, 1:2]), reads=[b_stat], writes=[b_stat])
                sc.op("dve", lambda e: e.reciprocal(out=stat[:, 2:3], in_=stat[:, 3:4]),
                      reads=[b_stat], writes=[b_stat])
                sc.op("dve", lambda e: e.scalar_tensor_tensor(out=xn[:], in0=xt[xb][:], scalar=stat[:, 2:3],
                                                              in1=gB[:], op0=ALU.mult, op1=ALU.mult),
                      reads=[b_xt[xb], b_stat, b_gB], writes=[b_xn])
                for kc in range(8):
                    bank = kc // 4
                    sc.op("pe", lambda e: e.transpose(ps[bank][:, (kc % 4) * 128:(kc % 4 + 1) * 128],
                                                      xn[:, kc * 128:(kc + 1) * 128], ident[:]),
                          reads=[b_xn, ident_b], writes=[psb[bank]], signal=(kc % 4 == 3))
                sc.op("act", lambda e: e.copy(out=xnT[:, 0:4, tt * 128:(tt + 1) * 128],
                                              in_=ps[0][:, :].rearrange("p (k t) -> p k t", k=4)),
                      reads=[psb[0]], writes=[b_xnT])
                sc.op("dve", lambda e: e.tensor_copy(out=xnT[:, 4:8, tt * 128:(tt + 1) * 128],
                                                     in_=ps[1][:, :].rearrange("p (k t) -> p k t", k=4)),
                      reads=[psb[1]], writes=[b_xnT])
            tok0 = g0 * 128
            for ci, (c0, kind, idx) in enumerate(FM_CHUNKS):
                pb = 2 + ci % 3
                for kc in range(8):
                    sc.op("pe", lambda e: e.matmul(ps[pb][:, :GW], lhsT=wm[:, kc, c0:c0 + 128], rhs=xnT[:, kc, :],
                                                   start=(kc == 0), stop=(kc == 7)),
                          reads=[b_wm[kc], b_xnT], writes=[psb[pb]], signal=(kc == 7))
                if kind == "c":
                    o = n32 % 3
                    n32 += 1
                    sc.op("dve" if ci % 2 else "act",
                          (lambda e: e.tensor_copy(out=of32[o][:, :], in_=ps[pb][:, :GW])) if ci % 2 else
                          (lambda e: e.copy(out=of32[o][:, :], in_=ps[pb][:, :GW])),
                          reads=[psb[pb]], writes=[b_of32[o]])
                    sc.dma(ms["cpre"][idx * 128:(idx + 1) * 128, 2 + tok0:2 + tok0 + GW], of32[o][:, :],
                           reads=[b_of32[o]])
                else:
                    o = n16 % 3
                    n16 += 1
                    scale = {"aq": HD ** -0.5, "bq": 32 ** -0.5, "ak": 1.0, "bk": 1.0}[kind]
                    sc.op("act", lambda e: e.mul(out=ob16[o][:, :], in_=ps[pb][:, :GW], mul=scale),
                          reads=[psb[pb]], writes=[b_ob16[o]])
                    dst = {"aq": ms["aqT"], "ak": ms["akT"], "bq": ms["bqT"], "bk": ms["bkT"]}[kind]
                    sc.dma(dst[idx * 128:(idx + 1) * 128, tok0:tok0 + GW], ob16[o][:, :], reads=[b_ob16[o]])
            for tt in range(GT):
                t = g0 + tt
                r0 = t * 128
                o = t % 2
                for (pb, c0, cw) in ((5, C_AV, 128), (6, C_BV, 256), (7, C_CZ, 408)):
                    for kc in range(8):
                        sc.op("pe", lambda e: e.matmul(ps[pb][:, :cw], lhsT=xnT[:, kc, tt * 128:(tt + 1) * 128],
                                                       rhs=wm[:, kc, c0:c0 + cw], start=(kc == 0), stop=(kc == 7)),
                              reads=[b_wm[kc], b_xnT], writes=[psb[pb]], signal=(kc == 7))
                sc.op("act", lambda e: e.copy(out=tv[o][:, 0:2, 0:64],
                                              in_=ps[5][:, 0:128].rearrange("p (h d) -> p h d", h=2)),
                      reads=[psb[5]], writes=[b_tv[o]])
                sc.op("dve", lambda e: e.tensor_copy(out=tv[o][:, 2:6, 0:64],
                                                     in_=ps[6][:, 0:256].rearrange("p (h d) -> p h d", h=4)),
                      reads=[psb[6]], writes=[b_tv[o]])
                sc.op("act", lambda e: e.copy(out=tz[o][:, :], in_=ps[7][:, 0:408]),
                      reads=[psb[7]], writes=[b_tz[o]])
                sc.dma(ms["av"][r0:r0 + 128, :, :], tv[o][:, 0:2, :], reads=[b_tv[o]])
                sc.dma(ms["bv"][r0:r0 + 128, :, :], tv[o][:, 2:6, :], reads=[b_tv[o]])
                sc.dma(ms["cz"][r0:r0 + 128, :], tz[o][:, :], reads=[b_tz[o]])
        sc.barrier()


K.inproj_phase = _inproj_phase


def _diffattn_phase(self, tag, ms, cst, dlam, dg, lam_init, identf, b_identf, hook=None):
    sc = self.sc
    S, NT = self.S, self.NT
    SK, NTK = self.SK, self.NTK
    GT = 4 if NT % 4 == 0 else 1
    GW = GT * 128
    NG = NT // GT
    ps, psb = self.ps, self.psb
    with ExitStack() as st:
        kTa = self.sb(st, tag + "kTa", [128, SK], BF16)
        kTb = self.sb(st, tag + "kTb", [128, SK], BF16)
        qTa = self.sb(st, tag + "qTa", [128, S], BF16)
        vA = self.sb(st, tag + "vA", [128, NTK, 65], BF16)
        bd = self.sb(st, tag + "bd", [128, 128], BF16)
        idb = self.sb(st, tag + "idb", [128, 128], BF16)
        oT = self.sb(st, tag + "oT", [65, GW], F32)
        b_oT = Buf("oT")
        pT = [self.sb(st, tag + "pT%d" % i, [128, GW], BF16) for i in range(3)]
        om = [self.sb(st, tag + "om%d" % i, [128, NT, 64], F32) for i in range(2)]
        dd = self.sb(st, tag + "dd", [128, NT, 64], F32)
        sq = self.sb(st, tag + "sq", [128, NT, 64], F32)
        ssq = self.sb(st, tag + "ssq", [128, NT], F32)
        rec = self.sb(st, tag + "rec", [128, 8], F32)
        lmb = self.sb(st, tag + "lmb", [128, 128], F32)
        lw = self.sb(st, tag + "lw", [128, 64], F32)
        ls = self.sb(st, tag + "ls", [128, 8], F32)
        gd = self.sb(st, tag + "gd", [128, 64], F32)
        b_kTa, b_kTb, b_qTa, b_vA, b_bd, b_idb = (Buf(n) for n in ("kTa", "kTb", "qTa", "vA", "bd", "idb"))
        b_pT = [Buf("pT%d" % i) for i in range(3)]
        b_om = [Buf("om0"), Buf("om1")]
        b_dd, b_sq, b_ssq, b_rec, b_lmb, b_lw, b_ls, b_gd = (Buf(n) for n in
                                                             ("dd", "sq", "ssq", "rec", "lmb", "lw", "ls", "gd"))
        sc.dma(idb[:], cst["identb"][:, :], writes=[b_idb])
        sc.op("dve", lambda e: e.memset(kTa[:], 0.0), writes=[b_kTa])
        sc.op("dve", lambda e: e.memset(kTb[:], 0.0), writes=[b_kTb])
        sc.op("dve", lambda e: e.memset(qTa[:], 0.0), writes=[b_qTa])
        sc.dma(lmb[:], dlam.rearrange("a b -> (a b)").partition_broadcast(128), writes=[b_lmb])
        sc.dma(gd[:], dg.partition_broadcast(128), writes=[b_gd])
        sc.op("dve", lambda e: e.tensor_tensor(out=lw[:, 0:32], in0=lmb[:, 0:32], in1=lmb[:, 32:64], op=ALU.mult),
              reads=[b_lmb], writes=[b_lw])
        sc.op("dve", lambda e: e.tensor_tensor(out=lw[:, 32:64], in0=lmb[:, 64:96], in1=lmb[:, 96:128], op=ALU.mult),
              reads=[b_lmb], writes=[b_lw])
        sc.op("dve", lambda e: e.reduce_sum(out=ls[:, 0:2], in_=lw[:, :].rearrange("p (a b) -> p a b", a=2),
                                            axis=AX.X), reads=[b_lw], writes=[b_ls])
        sc.op("act", lambda e: e.activation(out=ls[:, 2:4], in_=ls[:, 0:2], func=AF.Exp), reads=[b_ls], writes=[b_ls])
        sc.op("dve", lambda e: e.tensor_tensor(out=ls[:, 4:5], in0=ls[:, 3:4], in1=ls[:, 2:3], op=ALU.subtract),
              reads=[b_ls], writes=[b_ls])
        sc.op("dve", lambda e: e.tensor_scalar_add(out=ls[:, 5:6], in0=ls[:, 4:5], scalar1=-lam_init),
              reads=[b_ls], writes=[b_ls])
        sc.op("dve", lambda e: e.tensor_scalar_mul(out=gd[:], in0=gd[:], scalar1=1.0 - lam_init),
              reads=[b_gd], writes=[b_gd])
        blk = 0
        accn = 0
        pending = []
        for h in range(4):
            sc.dma(vA[:], ms["bv"][:, h, :].rearrange("(n p) d -> p n d", p=128), reads=[], writes=[b_vA])
            sc.dma(bd[:], cst["dbd"][h], writes=[b_bd])
            for m in range(2):
                r0 = (h * 2 + m) * 32
                sc.dma(kTa[0:32, :], ms["bkT"][r0:r0 + 32, :], writes=[b_kTa])
                sc.dma(kTa[32:36, :], cst["dakp"][h], writes=[b_kTa])
                sc.dma(kTb[0:32, :], ms["bkT"][r0:r0 + 32, :], writes=[b_kTb])
                sc.dma(kTb[32:36, :], cst["dakm"][h], writes=[b_kTb])
                sc.dma(qTa[0:32, :], ms["bqT"][r0:r0 + 32, :], writes=[b_qTa])
                sc.dma(qTa[32:36, :], cst["daq"][h], writes=[b_qTa])
                for qg in range(NG):
                    q0 = qg * GW
                    accb = 2 + accn % 2
                    accn += 1

                    def qk(kt, bank):
                        k0 = kt * 128
                        if kt < qg * GT:
                            sc.op("pe", lambda e: e.matmul(ps[bank][:, :GW], lhsT=kTa[:, k0:k0 + 128],
                                                           rhs=qTa[:, q0:q0 + GW], start=True, stop=True),
                                  reads=[b_kTa, b_qTa], writes=[psb[bank]])
                        elif kt >= (qg + 1) * GT:
                            sc.op("pe", lambda e: e.matmul(ps[bank][:, :GW], lhsT=kTb[:, k0:k0 + 128],
                                                           rhs=qTa[:, q0:q0 + GW], start=True, stop=True),
                                  reads=[b_kTb, b_qTa], writes=[psb[bank]])
                        else:
                            for i in range(GT):
                                qt = qg * GT + i
                                cs = slice(i * 128, (i + 1) * 128)
                                qs = slice(q0 + i * 128, q0 + (i + 1) * 128)
                                last = (i == GT - 1)
                                if kt < qt:
                                    sc.op("pe", lambda e: e.matmul(ps[bank][:, cs], lhsT=kTa[:, k0:k0 + 128],
                                                                   rhs=qTa[:, qs], start=True, stop=True),
                                          reads=[b_kTa, b_qTa], writes=[psb[bank]], signal=last)
                                elif kt > qt:
                                    sc.op("pe", lambda e: e.matmul(ps[bank][:, cs], lhsT=kTb[:, k0:k0 + 128],
                                                                   rhs=qTa[:, qs], start=True, stop=True),
                                          reads=[b_kTb, b_qTa], writes=[psb[bank]], signal=last)
                                else:
                                    sc.op("pe", lambda e: e.matmul(ps[bank][:, cs], lhsT=kTa[0:32, k0:k0 + 128],
                                                                   rhs=qTa[0:32, qs], start=True, stop=False),
                                          reads=[b_kTa, b_qTa], writes=[psb[bank]], signal=False)
                                    sc.op("pe", lambda e: e.matmul(ps[bank][:, cs], lhsT=idb[:, :], rhs=bd[:, :],
                                                                   start=False, stop=True),
                                          reads=[b_idb, b_bd], writes=[psb[bank]], signal=last)

                    def expv(kt, bank, pi):
                        sc.op("act", lambda e: e.activation(out=pT[pi][:, :], in_=ps[bank][:, :GW], func=AF.Exp),
                              reads=[psb[bank]], writes=[b_pT[pi]])
                        sc.op("pe", lambda e: e.matmul(ps[accb][0:65, :GW], lhsT=vA[:, kt, :], rhs=pT[pi][:, :],
                                                       start=(kt == 0), stop=(kt == NTK - 1)),
                              reads=[b_pT[pi], b_vA], writes=[psb[accb]])

                    for step in range(NTK + 1):
                        if step == 3 and pending:
                            pending.pop(0)()
                        if hook is not None and step in (12, 40):
                            hook()
                        if step < NTK:
                            qk(step, (blk + step) % 2)
                        if step >= 1:
                            expv(step - 1, (blk + step - 1) % 2, (blk + step - 1) % 3)
                    blk += NTK
                    def make_fin(accb=accb, qg=qg, m=m, accn=accn):
                        def fin():
                            sc.op("act", lambda e: e.copy(out=oT[:, :GW], in_=ps[accb][0:65, :GW]), reads=[psb[accb]], writes=[b_oT])
                            tb = 4 + (accn % 2)
                            for i in range(GT):
                                sc.op("pe", lambda e: e.transpose(ps[tb][:, i * 65:(i + 1) * 65], oT[:, i * 128:(i + 1) * 128],
                                                                  identf[0:65, 0:65]),
                                      reads=[b_oT, b_identf], writes=[psb[tb]], signal=(i == GT - 1))
                            tv = ps[tb][:, 0:GT * 65].rearrange("p (i d) -> p i d", d=65)
                            sc.op("dve", lambda e: e.reciprocal(out=rec[:, 0:GT], in_=tv[:, :, 64]),
                                  reads=[psb[tb]], writes=[b_rec])
                            sc.op("dve", lambda e: e.tensor_tensor(out=om[m][:, qg * GT:(qg + 1) * GT, :], in0=tv[:, :, 0:64],
                                                                   in1=rec[:, 0:GT].unsqueeze(2).to_broadcast([128, GT, 64]),
                                                                   op=ALU.mult),
                                  reads=[psb[tb], b_rec], writes=[b_om[m]])

                        return fin
                    pending.append(make_fin())
            while pending:
                pending.pop(0)()
            sc.op("dve", lambda e: e.scalar_tensor_tensor(out=dd[:], in0=om[1][:], scalar=ls[:, 5:6], in1=om[0][:],
                                                          op0=ALU.mult, op1=ALU.add),
                  reads=[b_om[0], b_om[1], b_ls], writes=[b_dd])
            sc.op("pool", lambda e: e.tensor_tensor(out=sq[:], in0=dd[:], in1=dd[:], op=ALU.mult),
                  reads=[b_dd], writes=[b_sq])
            sc.op("dve", lambda e: e.reduce_sum(out=ssq[:], in_=sq[:], axis=AX.X), reads=[b_sq], writes=[b_ssq])
            sc.op("dve", lambda e: e.tensor_scalar(out=ssq[:], in0=ssq[:], scalar1=1.0 / 64, scalar2=EPS,
                                                   op0=ALU.mult, op1=ALU.add), reads=[b_ssq], writes=[b_ssq])
            sc.op("act", lambda e: e.sqrt(out=ssq[:], in_=ssq[:]), reads=[b_ssq], writes=[b_ssq])
            sc.op("dve", lambda e: e.reciprocal(out=ssq[:], in_=ssq[:]), reads=[b_ssq], writes=[b_ssq])
            sc.op("dve", lambda e: e.tensor_tensor(out=dd[:], in0=dd[:],
                                                   in1=ssq[:].unsqueeze(2).to_broadcast([128, NT, 64]), op=ALU.mult),
                  reads=[b_dd, b_ssq], writes=[b_dd])
            sc.op("dve", lambda e: e.tensor_tensor(out=dd[:], in0=dd[:],
                                                   in1=gd[:].unsqueeze(1).to_broadcast([128, NT, 64]), op=ALU.mult),
                  reads=[b_dd, b_gd], writes=[b_dd])
            sc.dma(ms["omix"][:, 384 + h * 64:384 + (h + 1) * 64].rearrange("(n p) d -> p n d", p=128), dd[:],
                   reads=[b_dd])
        if hook is not None:
            while hook():
                pass
        sc.barrier()


K.diffattn_phase = _diffattn_phase


def _winattn_phase(self, tag, ms, wbias, sink):
    sc = self.sc
    S, NT = self.S, self.NT
    NKA = NT + 1 if self.paired else NT
    ps, psb = self.ps, self.psb
    with ExitStack() as st:
        qT = self.sb(st, tag + "qT", [128, S], BF16)
        kT = self.sb(st, tag + "kT", [128, NKA * 128], BF16)
        vA = self.sb(st, tag + "vA", [128, NKA, 65], BF16)
        wb = self.sb(st, tag + "wb", [128, 384], F32)
        sT = [self.sb(st, tag + "sT%d" % i, [128, 384], F32) for i in range(2)]
        pT = [self.sb(st, tag + "pT%d" % i, [128, 384], BF16) for i in range(2)]
        ow = self.sb(st, tag + "ow", [128, NT, 64], F32)
        ou = self.sb(st, tag + "ou", [128, NT, 65], F32)
        dn_ = self.sb(st, tag + "dn", [128, NT], F32)
        b_ou, b_dn = Buf("ou"), Buf("dn")
        es = self.sb(st, tag + "es", [128, 6], F32)
        rec = self.sb(st, tag + "rec", [128, 4], F32)
        b_qT, b_kT, b_vA, b_wb, b_ow, b_es, b_rec = (Buf(n) for n in ("qT", "kT", "vA", "wb", "ow", "es", "rec"))
        b_sT = [Buf("sT0"), Buf("sT1")]
        b_pT = [Buf("pT0"), Buf("pT1")]
        sc.op("dve", lambda e: e.memset(qT[:], 0.0), writes=[b_qT])
        sc.op("dve", lambda e: e.memset(kT[:], 0.0), writes=[b_kT])
        sc.dma(es[:], sink.partition_broadcast(128), writes=[b_es])
        sc.op("act", lambda e: e.activation(out=es[:], in_=es[:], func=AF.Exp), reads=[b_es], writes=[b_es])
        cnt = 0
        for h in range(6):
            kh = h // 3
            if h % 3 == 0:
                sc.dma(kT[0:64, :], ms["akT"][kh * 64:(kh + 1) * 64, :], writes=[b_kT])
                sc.dma(vA[:], ms["av"][:, kh, :].rearrange("(n p) d -> p n d", p=128), writes=[b_vA])
            sc.dma(qT[0:64, :], ms["aqT"][h * 64:(h + 1) * 64, :], writes=[b_qT])
            sc.dma(wb[:], wbias[h], writes=[b_wb])
            for n in range(NT):
                js = [j for j in (0, 1, 2) if 0 <= n - 1 + j < NKA]
                c0 = js[0] * 128
                w = len(js) * 128
                sbk = cnt % 2
                acc = 2 + cnt % 6
                cnt += 1
                for jj, j in enumerate(js):
                    kt = n - 1 + j
                    sc.op("pe", lambda e: e.matmul(ps[sbk][:, jj * 128:(jj + 1) * 128], lhsT=kT[:, kt * 128:(kt + 1) * 128],
                                                   rhs=qT[:, n * 128:(n + 1) * 128], start=True, stop=True),
                          reads=[b_kT, b_qT], writes=[psb[sbk]], signal=(jj == len(js) - 1))
                sc.op("dve", lambda e: e.tensor_tensor(out=sT[sbk][:, 0:w], in0=ps[sbk][:, 0:w], in1=wb[:, c0:c0 + w],
                                                       op=ALU.add),
                      reads=[psb[sbk], b_wb], writes=[b_sT[sbk]])
                sc.op("act", lambda e: e.activation(out=pT[sbk][:, 0:w], in_=sT[sbk][:, 0:w], func=AF.Exp),
                      reads=[b_sT[sbk]], writes=[b_pT[sbk]])
                for jj, j in enumerate(js):
                    kt = n - 1 + j
                    sc.op("pe", lambda e: e.matmul(ps[acc][:, 0:65], lhsT=pT[sbk][:, jj * 128:(jj + 1) * 128],
                                                   rhs=vA[:, kt, :], start=(jj == 0), stop=(jj == len(js) - 1)),
                          reads=[b_pT[sbk], b_vA], writes=[psb[acc]], signal=(jj == len(js) - 1))
                if n % 2 == 0:
                    sc.op("dve", lambda e: e.tensor_copy(out=ou[:, n, :], in_=ps[acc][:, 0:65]), reads=[psb[acc]], writes=[b_ou])
                else:
                    sc.op("act", lambda e: e.copy(out=ou[:, n, :], in_=ps[acc][:, 0:65]), reads=[psb[acc]], writes=[b_ou])
            sc.op("dve", lambda e: e.tensor_scalar(out=dn_[:], in0=ou[:, :, 64], scalar1=es[:, h:h + 1], scalar2=None,
                                                   op0=ALU.add), reads=[b_ou, b_es], writes=[b_dn])
            sc.op("dve", lambda e: e.reciprocal(out=dn_[:], in_=dn_[:]), reads=[b_dn], writes=[b_dn])
            sc.op("dve", lambda e: e.tensor_tensor(out=ow[:], in0=ou[:, :, 0:64],
                                                   in1=dn_[:].unsqueeze(2).to_broadcast([128, NT, 64]), op=ALU.mult),
                  reads=[b_ou, b_dn], writes=[b_ow])
            sc.dma(ms["omix"][:, h * 64:(h + 1) * 64].rearrange("(n p) d -> p n d", p=128), ow[:], reads=[b_ow])
        sc.barrier()


K.winattn_phase = _winattn_phase


def _dn_scratch(self):
    S = self.S
    d = {}
    d["cqT"] = self.scratch("cqT", [384, S], F32)
    d["ckT"] = self.scratch("ckT", [384, S], F32)
    d["ck"] = self.scratch("ck", [S, 384], F32)
    d["cv"] = self.scratch("cv", [S, 384], F32)
    d["of"] = self.scratch("of", [S, 384], F32)
    return d


K.dn_scratch = _dn_scratch


def _conv_phase(self, tag, ms, ds, conv_w, dnc, ident, ident_b):
    sc = self.sc
    S, NT = self.S, self.NT
    GT = 4 if NT % 4 == 0 else 1
    GW = GT * 128
    ps, psb = self.ps, self.psb
    with ExitStack() as st:
        blk1 = self.sb(st, tag + "blk1", [128, 128], F32)
        cw = self.sb(st, tag + "cw", [128, 9, 5], F32)
        xin = [self.sb(st, tag + "xin%d" % i, [128, GW + 4], F32) for i in range(2)]
        y = [self.sb(st, tag + "y%d" % i, [128, GW], F32) for i in range(2)]
        sq2 = [self.sb(st, tag + "sq%d" % i, [128, GW], F32) for i in range(2)]
        rs2 = [self.sb(st, tag + "rs%d" % i, [128, GW], F32) for i in range(2)]
        yn = [self.sb(st, tag + "yn%d" % i, [128, GW], F32) for i in range(2)]
        tk = [self.sb(st, tag + "tk%d" % i, [128, GW], F32) for i in range(2)]
        b_blk1, b_cw = Buf("blk1"), Buf("cw")
        b_sq2 = [Buf("sq0"), Buf("sq1")]
        b_rs2 = [Buf("rs0"), Buf("rs1")]
        b_xin = [Buf("xin0"), Buf("xin1")]
        b_y = [Buf("y0"), Buf("y1")]
        b_yn = [Buf("yn0"), Buf("yn1")]
        b_tk = [Buf("tk0"), Buf("tk1")]
        sc.dma(blk1[:], dnc[3], writes=[b_blk1])
        for ci in range(9):
            sc.dma(cw[:, ci, :], conv_w[:, ci * 128:(ci + 1) * 128].rearrange("j c -> c j"), writes=[b_cw],
                   allow_slow_non_contiguous=True)
        it = 0
        for g0 in range(0, NT, GT):
            tok0 = g0 * 128
            for ci in range(9):
                b = it % 2
                it += 1
                sq, rs, b_sq, b_rs = sq2[b], rs2[b], b_sq2[b], b_rs2[b]
                pA, pB = 2 * b, 2 * b + 1
                sc.dma(xin[b][:], ms["cpre"][ci * 128:(ci + 1) * 128, tok0:tok0 + GW + 4], writes=[b_xin[b]])
                eng = "dve"
                sc.op(eng, lambda e: e.tensor_scalar_mul(out=y[b][:], in0=xin[b][:, 0:GW], scalar1=cw[:, ci, 0:1]),
                      reads=[b_xin[b], b_cw], writes=[b_y[b]])
                for j in range(1, 5):
                    sc.op(eng, lambda e: e.scalar_tensor_tensor(out=y[b][:], in0=xin[b][:, j:j + GW],
                                                                scalar=cw[:, ci, j:j + 1], in1=y[b][:],
                                                                op0=ALU.mult, op1=ALU.add),
                          reads=[b_xin[b], b_cw, b_y[b]], writes=[b_y[b]])
                sc.op("act", lambda e: e.activation(out=y[b][:], in_=y[b][:], func=AF.Silu),
                      reads=[b_y[b]], writes=[b_y[b]])
                if ci < 6:
                    sc.op("act", lambda e: e.activation(out=sq[:], in_=y[b][:], func=AF.Square),
                          reads=[b_y[b]], writes=[b_sq])
                    sc.op("pe", lambda e: e.matmul(ps[pA][:, :GW], lhsT=blk1[:], rhs=sq[:], start=True, stop=True),
                          reads=[b_blk1, b_sq], writes=[psb[pA]])
                    mul = 64.0 if ci < 3 else 1.0
                    sc.op("dve", lambda e: e.tensor_scalar(out=rs[:], in0=ps[pA][:, :GW], scalar1=EPS, scalar2=mul,
                                                           op0=ALU.add, op1=ALU.mult),
                          reads=[psb[pA]], writes=[b_rs])
                    sc.op("act", lambda e: e.sqrt(out=rs[:], in_=rs[:]), reads=[b_rs], writes=[b_rs])
                    sc.op("dve", lambda e: e.reciprocal(out=rs[:], in_=rs[:]), reads=[b_rs], writes=[b_rs])
                    sc.op("dve", lambda e: e.tensor_tensor(out=yn[b][:], in0=y[b][:], in1=rs[:], op=ALU.mult),
                          reads=[b_y[b], b_rs], writes=[b_yn[b]])
                    src_t, src_b = yn[b], b_yn[b]
                    if ci < 3:
                        sc.dma(ds["cqT"][ci * 128:(ci + 1) * 128, tok0:tok0 + GW], yn[b][:], reads=[b_yn[b]])
                    else:
                        sc.dma(ds["ckT"][(ci - 3) * 128:(ci - 2) * 128, tok0:tok0 + GW], yn[b][:], reads=[b_yn[b]])
                else:
                    src_t, src_b = y[b], b_y[b]
                if ci >= 3:
                    for tt in range(GT):
                        sc.op("pe", lambda e: e.transpose(ps[pB][:, tt * 128:(tt + 1) * 128],
                                                          src_t[:, tt * 128:(tt + 1) * 128], ident[:]),
                              reads=[src_b, ident_b], writes=[psb[pB]], signal=(tt == GT - 1))
                    sc.op("act", lambda e: e.copy(out=tk[b][:], in_=ps[pB][:, :GW]), reads=[psb[pB]], writes=[b_tk[b]])
                    dst = ds["ck"] if ci < 6 else ds["cv"]
                    cc = (ci - 3) % 3
                    sc.dma(dst[tok0:tok0 + GW, cc * 128:(cc + 1) * 128].rearrange("(t p) c -> p t c", p=128),
                           tk[b][:].rearrange("p (t c) -> p t c", c=128), reads=[b_tk[b]])
        sc.barrier()


K.conv_phase = _conv_phase


def _dn_pass(self, tag, dirn, ms, ds, dnc, a_log, dt_bias, dn_g, ident, ident_b):
    sc = self.sc
    S, NT = self.S, self.NT
    ps, psb = self.ps, self.psb
    import os
    NIT = int(os.environ.get('DN_NIT', '7'))
    with ExitStack() as st:
        def T(name, shape, n=1, dt=F32):
            ts = [self.sb(st, "%s%s%d" % (tag, name, i), shape, dt) for i in range(n)]
            bs = [Buf("%s%d" % (name, i)) for i in range(n)]
            return (ts, bs) if n > 1 else (ts[0], bs[0])
        ones, b_ones = T("ones", [128, 128])
        tri, b_tri = T("tri", [128, 128])
        m1, b_m1 = T("m1", [128, 128])
        m2, b_m2 = T("m2", [128, 128])
        m3, b_m3 = T("m3", [128, 128])
        dtb, b_dtb = T("dtb", [128, 6])
        nega, b_nega = T("nega", [128, 6])
        gdn, b_gdn = T("gdn", [128, 64])
        kTt, b_kTt = T("kTt", [64, 6, 128], 2)
        qTt, b_qTt = T("qTt", [64, 6, 128], 2)
        kt, b_kt = T("kt", [128, 384], 2)
        vt, b_vt = T("vt", [128, 384], 2)
        gz, b_gz = T("gz", [128, 408], 2)
        gs, b_gs = T("gs", [128, 16, 6])
        D1, b_D1 = T("D1", [128, 6, 128])
        D2, b_D2 = T("D2", [128, 6, 128])
        Rs, b_Rs = T("Rs", [128, 6, 128])
        R2s, b_R2s = T("R2s", [128, 6, 128])
        tmp, b_tmp = T("tmp", [128, 3, 128], 2)
        E, b_E = T("E", [128, 3, 128], 2)
        X, b_X = T("X", [128, 128], 4)
        Y, b_Y = T("Y", [128, 128], 4)
        Rr, b_Rr = T("Rr", [128, 128], 4)
        qkT, b_qkT = T("qkT", [128, 128], 2)
        kg, b_kg = T("kg", [128, 64], 2)
        wT, b_wT = T("wT", [64, 128], 2)
        Vn, b_Vn = T("Vn", [128, 64], 2)
        t2, b_t2 = T("t2", [128, 64], 2)
        St, b_St = T("St", [64, 6, 64])
        osb, b_osb = T("osb", [128, 6, 64], 2)
        if dirn == 1:
            oft, b_oft = T("oft", [128, 6, 64], 2)
            sqt, b_sqt = T("sqt", [128, 6, 64])
            sz, b_sz = T("sz", [128, 6, 64])
            rr, b_rr = T("rr", [128, 8])
        sc.dma(ones[:], dnc[0], writes=[b_ones])
        sc.dma(tri[:], dnc[1 + dirn], writes=[b_tri])
        sc.dma(m1[:], dnc[4 + 3 * dirn], writes=[b_m1])
        sc.dma(m2[:], dnc[5 + 3 * dirn], writes=[b_m2])
        sc.dma(m3[:], dnc[6 + 3 * dirn], writes=[b_m3])
        sc.dma(dtb[:], dt_bias[dirn].partition_broadcast(128), writes=[b_dtb])
        sc.dma(nega[:], a_log[dirn].partition_broadcast(128), writes=[b_nega])
        sc.dma(gdn[:], dn_g.partition_broadcast(128), writes=[b_gdn])
        sc.op("act", lambda e: e.activation(out=nega[:], in_=nega[:], func=AF.Exp), reads=[b_nega], writes=[b_nega])
        sc.op("dve", lambda e: e.tensor_scalar_mul(out=nega[:], in0=nega[:], scalar1=-1.0),
              reads=[b_nega], writes=[b_nega])
        sc.op("pool", lambda e: e.memset(St[:], 0.0), writes=[b_St])
        order = list(range(NT)) if dirn == 0 else list(range(NT - 1, -1, -1))
        G_ = lambda i: gs[:, i, :]
        hcnt = 0
        for it, n in enumerate(order):
            tb = it % 2
            c0 = n * 128
            sc.dma(kTt[tb][:], ds["ckT"][:, c0:c0 + 128].rearrange("(h d) t -> d h t", h=6), writes=[b_kTt[tb]])
            sc.dma(qTt[tb][:], ds["cqT"][:, c0:c0 + 128].rearrange("(h d) t -> d h t", h=6), writes=[b_qTt[tb]])
            sc.dma(kt[tb][:], ds["ck"][c0:c0 + 128, :], writes=[b_kt[tb]])
            sc.dma(vt[tb][:], ds["cv"][c0:c0 + 128, :], writes=[b_vt[tb]])
            sc.dma(gz[tb][:], ms["cz"][c0:c0 + 128, :], writes=[b_gz[tb]])
            bcol = gz[tb][:, 384 + dirn * 6:384 + dirn * 6 + 6]
            acol = gz[tb][:, 396 + dirn * 6:396 + dirn * 6 + 6]
            RG = [b_gs]
            sc.op("act", lambda e: e.activation(out=G_(0), in_=bcol, func=AF.Exp, scale=-1.0),
                  reads=[b_gz[tb]], writes=RG)
            sc.op("dve", lambda e: e.tensor_scalar_add(out=G_(0), in0=G_(0), scalar1=1.0), reads=RG, writes=RG)
            sc.op("act", lambda e: e.activation(out=G_(1), in_=G_(0), func=AF.Ln), reads=RG, writes=RG)
            sc.op("dve", lambda e: e.tensor_tensor(out=G_(2), in0=acol, in1=dtb[:], op=ALU.add),
                  reads=[b_gz[tb], b_dtb], writes=RG)
            sc.op("act", lambda e: e.activation(out=G_(3), in_=G_(2), func=AF.Exp), reads=RG, writes=RG)
            sc.op("dve", lambda e: e.tensor_scalar_add(out=G_(3), in0=G_(3), scalar1=1.0), reads=RG, writes=RG)
            sc.op("act", lambda e: e.activation(out=G_(4), in_=G_(3), func=AF.Ln), reads=RG, writes=RG)
            sc.op("dve", lambda e: e.tensor_tensor(out=G_(5), in0=G_(4), in1=nega[:], op=ALU.mult),
                  reads=RG + [b_nega], writes=RG)
            sc.op("pe", lambda e: e.matmul(ps[0][:, 0:6], lhsT=tri[:], rhs=G_(5), start=True, stop=True),
                  reads=[b_tri] + RG, writes=[psb[0]], signal=False)
            sc.op("pe", lambda e: e.matmul(ps[0][:, 8:14], lhsT=ones[:], rhs=G_(5), start=True, stop=True),
                  reads=[b_ones] + RG, writes=[psb[0]])
            sc.op("dve", lambda e: e.tensor_copy(out=G_(6), in_=ps[0][:, 0:6]), reads=[psb[0]], writes=RG)
            sc.op("dve", lambda e: e.tensor_copy(out=G_(14), in_=ps[0][:, 8:14]), reads=[psb[0]], writes=RG)
            sc.op("dve", lambda e: e.tensor_tensor(out=G_(7), in0=G_(6), in1=G_(1), op=ALU.subtract),
                  reads=RG, writes=RG)
            sc.op("act", lambda e: e.activation(out=G_(8), in_=G_(7), func=AF.Exp), reads=RG, writes=RG)
            sc.op("act", lambda e: e.activation(out=G_(9), in_=G_(1), func=AF.Exp, scale=-1.0),
                  reads=RG, writes=RG)
            sc.op("act", lambda e: e.activation(out=G_(10), in_=G_(6), func=AF.Exp), reads=RG, writes=RG)
            sc.op("dve", lambda e: e.tensor_tensor(out=G_(11), in0=G_(14), in1=G_(6), op=ALU.subtract),
                  reads=RG, writes=RG)
            sc.op("act", lambda e: e.activation(out=G_(12), in_=G_(11), func=AF.Exp), reads=RG, writes=RG)
            sc.op("act", lambda e: e.activation(out=G_(13), in_=G_(14), func=AF.Exp), reads=RG, writes=RG)
            idb3 = ident[:].unsqueeze(1).to_broadcast([128, 6, 128])
            sc.op("dve", lambda e: e.tensor_tensor(out=D1[:], in0=idb3,
                                                   in1=G_(6).unsqueeze(2).to_broadcast([128, 6, 128]), op=ALU.mult),
                  reads=RG + [ident_b], writes=[b_D1])
            sc.op("pool", lambda e: e.tensor_tensor(out=D2[:], in0=idb3,
                                                    in1=G_(7).unsqueeze(2).to_broadcast([128, 6, 128]), op=ALU.mult),
                  reads=RG + [ident_b], writes=[b_D2])
            for (Dm, b_Dm, Rm, b_Rm) in ((D1, b_D1, Rs, b_Rs), (D2, b_D2, R2s, b_R2s)):
                Dm2 = Dm[:].rearrange("p j s -> p (j s)")
                Rm2 = Rm[:].rearrange("p j s -> p (j s)")
                sc.op("pe", lambda e: e.matmul(ps[1][:, 0:512], lhsT=ones[:], rhs=Dm2[:, 0:512], start=True, stop=True),
                      reads=[b_ones, b_Dm], writes=[psb[1]])
                sc.op("pe", lambda e: e.matmul(ps[2][:, 0:256], lhsT=ones[:], rhs=Dm2[:, 512:768], start=True,
                                               stop=True),
                      reads=[b_ones, b_Dm], writes=[psb[2]])
                sc.op("act", lambda e: e.copy(out=Rm2[:, 0:512], in_=ps[1][:, 0:512]), reads=[psb[1]], writes=[b_Rm])
                sc.op("act", lambda e: e.copy(out=Rm2[:, 512:768], in_=ps[2][:, 0:256]), reads=[psb[2]],
                      writes=[b_Rm])
            ob = it % 2
            import os
            STOP = os.environ.get("DN_STOP", "")
            for hh in range(6):
                if STOP == "gates":
                    break
                hb = hcnt % 2
                hcnt += 1
                sc.op("pe", lambda e: e.matmul(ps[3][:, 0:128], lhsT=kTt[tb][:, hh, :], rhs=kTt[tb][:, hh, :],
                                               start=True, stop=True),
                      reads=[b_kTt[tb]], writes=[psb[3]], signal=False)
                sc.op("pe", lambda e: e.matmul(ps[3][:, 128:256], lhsT=kTt[tb][:, hh, :], rhs=qTt[tb][:, hh, :],
                                               start=True, stop=True),
                      reads=[b_kTt[tb], b_qTt[tb]], writes=[psb[3]])
                sc.op("dve", lambda e: e.scalar_tensor_tensor(out=tmp[hb][:, 0, :], in0=Rs[:, hh, :],
                                                              scalar=gs[:, 7, hh:hh + 1], in1=m1[:],
                                                              op0=ALU.subtract, op1=ALU.max),
                      reads=[b_Rs, b_gs, b_m1], writes=[b_tmp[hb]])
                sc.op("dve", lambda e: e.scalar_tensor_tensor(out=tmp[hb][:, 1, :], in0=R2s[:, hh, :],
                                                              scalar=gs[:, 6, hh:hh + 1], in1=m2[:],
                                                              op0=ALU.subtract, op1=ALU.min),
                      reads=[b_R2s, b_gs, b_m2], writes=[b_tmp[hb]])
                sc.op("dve", lambda e: e.scalar_tensor_tensor(out=tmp[hb][:, 2, :], in0=Rs[:, hh, :],
                                                              scalar=gs[:, 6, hh:hh + 1], in1=m3[:],
                                                              op0=ALU.subtract, op1=ALU.min),
                      reads=[b_Rs, b_gs, b_m3], writes=[b_tmp[hb]])
                sc.op("act", lambda e: e.activation(out=E[hb][:, 0, :], in_=tmp[hb][:, 0, :], func=AF.Exp, scale=-1.0),
                      reads=[b_tmp[hb]], writes=[b_E[hb]])
                sc.op("act", lambda e: e.activation(out=E[hb][:, 1:3, :], in_=tmp[hb][:, 1:3, :], func=AF.Exp),
                      reads=[b_tmp[hb]], writes=[b_E[hb]])
                xi = 2 * hb
                sc.op("dve", lambda e: e.tensor_tensor(out=X[xi][:], in0=ps[3][:, 0:128], in1=E[hb][:, 0, :],
                                                       op=ALU.mult), reads=[psb[3], b_E[hb]], writes=[b_X[xi]])
                sc.op("dve", lambda e: e.tensor_tensor(out=Y[xi][:], in0=ps[3][:, 0:128], in1=E[hb][:, 1, :],
                                                       op=ALU.mult), reads=[psb[3], b_E[hb]], writes=[b_Y[xi]])
                sc.op("dve", lambda e: e.tensor_tensor(out=qkT[hb][:], in0=ps[3][:, 128:256], in1=E[hb][:, 2, :],
                                                       op=ALU.mult), reads=[psb[3], b_E[hb]], writes=[b_qkT[hb]])
                hs = slice(hh * 64, (hh + 1) * 64)
                sc.op("pool", lambda e: e.tensor_scalar_mul(out=Rr[xi][:, 0:64], in0=vt[tb][:, hs],
                                                            scalar1=gs[:, 9, hh:hh + 1]),
                      reads=[b_vt[tb], b_gs], writes=[b_Rr[xi]])
                sc.op("pool", lambda e: e.tensor_scalar_mul(out=Rr[xi][:, 64:128], in0=kt[tb][:, hs],
                                                            scalar1=gs[:, 8, hh:hh + 1]),
                      reads=[b_kt[tb], b_gs], writes=[b_Rr[xi]])
                sc.op("pool", lambda e: e.tensor_scalar_mul(out=kg[hb][:], in0=kt[tb][:, hs],
                                                            scalar1=gs[:, 12, hh:hh + 1]),
                      reads=[b_kt[tb], b_gs], writes=[b_kg[hb]])
                if STOP == "prep":
                    continue
                pn = 4 + hb
                pq = 6 + hb
                cur = xi
                for i in range(NIT):
                    nxt = 2 * hb + (1 - (cur - 2 * hb))
                    sc.op("pe", lambda e: e.matmul(ps[pn][:, 0:128], lhsT=Y[cur][:], rhs=Rr[cur][:], start=True,
                                                   stop=True),
                          reads=[b_Y[cur], b_Rr[cur]], writes=[psb[pn]], signal=(i == NIT - 1))
                    if i < NIT - 1:
                        sc.op("pe", lambda e: e.matmul(ps[pq][:, 128:256], lhsT=X[cur][:], rhs=Y[cur][:], start=True,
                                                       stop=True),
                              reads=[b_X[cur], b_Y[cur]], writes=[psb[pq]], signal=(i == NIT - 2))
                    if i < NIT - 2:
                        sc.op("pe", lambda e: e.matmul(ps[pq][:, 256:384], lhsT=Y[cur][:], rhs=X[cur][:], start=True,
                                                       stop=True),
                              reads=[b_X[cur], b_Y[cur]], writes=[psb[pq]], signal=True)
                    if i == 0:
                        sc.op("dve", lambda e: e.scalar_tensor_tensor(out=Rr[nxt][:], in0=ps[pn][:, 0:128], scalar=-1.0,
                                                                      in1=Rr[cur][:], op0=ALU.mult, op1=ALU.add),
                              reads=[b_Rr[cur], psb[pn]], writes=[b_Rr[nxt]])
                    else:
                        sc.op("dve", lambda e: e.tensor_tensor(out=Rr[nxt][:], in0=ps[pn][:, 0:128], in1=Rr[cur][:],
                                                               op=ALU.add),
                              reads=[b_Rr[cur], psb[pn]], writes=[b_Rr[nxt]])
                    if i < NIT - 1:
                        sc.op("act", lambda e: e.copy(out=Y[nxt][:], in_=ps[pq][:, 128:256]),
                              reads=[psb[pq]], writes=[b_Y[nxt]])
                    if i < NIT - 2:
                        sc.op("act", lambda e: e.copy(out=X[nxt][:], in_=ps[pq][:, 256:384]),
                              reads=[psb[pq]], writes=[b_X[nxt]])
                    cur = nxt
                Rf, b_Rf = Rr[cur], b_Rr[cur]
                if STOP == "neumann":
                    continue
                sc.op("pe", lambda e: e.transpose(ps[1][0:64, 0:128], Rf[:, 64:128], ident[:]),
                      reads=[b_Rf, ident_b], writes=[psb[1]])
                sc.op("act", lambda e: e.copy(out=wT[hb][:], in_=ps[1][0:64, 0:128]), reads=[psb[1]], writes=[b_wT[hb]])
                if STOP == "transp":
                    continue
                sc.op("pe", lambda e: e.matmul(ps[0][:, 0:64], lhsT=wT[hb][:], rhs=St[:, hh, :], start=True, stop=True),
                      reads=[b_wT[hb], b_St], writes=[psb[0]])
                sc.op("dve", lambda e: e.scalar_tensor_tensor(out=Vn[hb][:], in0=ps[0][:, 0:64], scalar=-1.0,
                                                              in1=Rf[:, 0:64], op0=ALU.mult, op1=ALU.add),
                      reads=[b_Rf, psb[0]], writes=[b_Vn[hb]])
                sc.op("pe", lambda e: e.matmul(ps[1][:, 128:192], lhsT=qTt[tb][:, hh, :], rhs=St[:, hh, :], start=True,
                                               stop=True),
                      reads=[b_qTt[tb], b_St], writes=[psb[1]], signal=False)
                sc.op("pe", lambda e: e.matmul(ps[0][:, 64:128], lhsT=qkT[hb][:], rhs=Vn[hb][:], start=True, stop=True),
                      reads=[b_qkT[hb], b_Vn[hb]], writes=[psb[0]], signal=False)
                sc.op("pe", lambda e: e.matmul(ps[2][0:64, 0:64], lhsT=kg[hb][:], rhs=Vn[hb][:], start=True, stop=True),
                      reads=[b_kg[hb], b_Vn[hb]], writes=[psb[2]])
                sc.op("act", lambda e: e.activation(out=t2[hb][:], in_=ps[1][:, 128:192], func=AF.Copy,
                                                    scale=gs[:, 10, hh:hh + 1]),
                      reads=[psb[1], b_gs], writes=[b_t2[hb]])
                sc.op("dve", lambda e: e.tensor_tensor(out=osb[ob][:, hh, :], in0=ps[0][:, 64:128], in1=t2[hb][:],
                                                       op=ALU.add),
                      reads=[b_t2[hb], psb[0]], writes=[b_osb[ob]])
                sc.op("pool", lambda e: e.tensor_scalar_mul(out=St[:, hh, :], in0=St[:, hh, :],
                                                            scalar1=gs[0:64, 13, hh:hh + 1]),
                      reads=[b_St, b_gs], writes=[b_St])
                sc.op("dve", lambda e: e.tensor_tensor(out=St[:, hh, :], in0=ps[2][0:64, 0:64], in1=St[:, hh, :],
                                                       op=ALU.add),
                      reads=[b_St, psb[2]], writes=[b_St])
            if dirn == 0:
                sc.dma(ds["of"][c0:c0 + 128, :], osb[ob][:].rearrange("p h d -> p (h d)"), reads=[b_osb[ob]])
            else:
                sc.dma(oft[ob][:].rearrange("p h d -> p (h d)"), ds["of"][c0:c0 + 128, :], writes=[b_oft[ob]])
                sc.op("dve", lambda e: e.tensor_tensor(out=oft[ob][:], in0=oft[ob][:], in1=osb[ob][:], op=ALU.add),
                      reads=[b_oft[ob], b_osb[ob]], writes=[b_oft[ob]])
                sc.op("pool", lambda e: e.tensor_tensor(out=sqt[:], in0=oft[ob][:], in1=oft[ob][:], op=ALU.mult),
                      reads=[b_oft[ob]], writes=[b_sqt])
                sc.op("dve", lambda e: e.reduce_sum(out=rr[:, 0:6], in_=sqt[:], axis=AX.X), reads=[b_sqt], writes=[b_rr])
                sc.op("dve", lambda e: e.tensor_scalar(out=rr[:, 0:6], in0=rr[:, 0:6], scalar1=1.0 / 64, scalar2=EPS,
                                                       op0=ALU.mult, op1=ALU.add), reads=[b_rr], writes=[b_rr])
                sc.op("act", lambda e: e.sqrt(out=rr[:, 0:6], in_=rr[:, 0:6]), reads=[b_rr], writes=[b_rr])
                sc.op("dve", lambda e: e.reciprocal(out=rr[:, 0:6], in_=rr[:, 0:6]), reads=[b_rr], writes=[b_rr])
                sc.op("act", lambda e: e.activation(out=sz[:].rearrange("p h d -> p (h d)"), in_=gz[tb][:, 0:384],
                                                    func=AF.Silu), reads=[b_gz[tb]], writes=[b_sz])
                sc.op("dve", lambda e: e.tensor_tensor(out=oft[ob][:], in0=oft[ob][:],
                                                       in1=rr[:, 0:6].unsqueeze(2).to_broadcast([128, 6, 64]),
                                                       op=ALU.mult), reads=[b_oft[ob], b_rr], writes=[b_oft[ob]])
                sc.op("pool", lambda e: e.tensor_tensor(out=sz[:], in0=sz[:],
                                                        in1=gdn[:].unsqueeze(1).to_broadcast([128, 6, 64]),
                                                        op=ALU.mult), reads=[b_sz, b_gdn], writes=[b_sz])
                sc.op("dve", lambda e: e.tensor_tensor(out=oft[ob][:], in0=oft[ob][:], in1=sz[:], op=ALU.mult),
                      reads=[b_oft[ob], b_sz], writes=[b_oft[ob]])
                sc.dma(ms["omix"][c0:c0 + 128, 640:1024], oft[ob][:].rearrange("p h d -> p (h d)"),
                       reads=[b_oft[ob]])
        sc.barrier()


K.dn_pass = _dn_pass


def _outproj_phase(self, tag, w_o, ms, xres, x_bufs, ident, ident_b):
    sc = self.sc
    S, NT = self.S, self.NT
    ps, psb = self.ps, self.psb
    with ExitStack() as st:
        wo = self.sb(st, tag + "wo", [128, 8, D], BF16)
        ot = [self.sb(st, tag + "ot%d" % i, [128, D], F32) for i in range(2)]
        xr = [self.sb(st, tag + "xr%d" % i, [128, D], F32) for i in range(2)]
        oT = [self.sb(st, tag + "oT%d" % i, [128, 8, 128], BF16) for i in range(2)]
        b_wo = [Buf("wo%d" % k) for k in range(8)]
        b_ot = [Buf("ot0"), Buf("ot1")]
        b_xr = [Buf("xr0"), Buf("xr1")]
        b_oT = [Buf("oT0"), Buf("oT1")]
        for kc in range(8):
            sc.dma(wo[:, kc, :], w_o[kc * 128:(kc + 1) * 128, :], writes=[b_wo[kc]], q="pool")
        for t in range(NT):
            b = t % 2
            r0 = t * 128
            sc.dma(ot[b][:], ms["omix"][r0:r0 + 128, :], writes=[b_ot[b]])
            sc.dma(xr[b][:], xres[r0:r0 + 128, :], reads=[x_bufs[t]], writes=[b_xr[b]])
            for kc in range(8):
                bank = kc // 4
                sc.op("pe", lambda e: e.transpose(ps[bank][:, (kc % 4) * 128:(kc % 4 + 1) * 128],
                                                  ot[b][:, kc * 128:(kc + 1) * 128], ident[:]),
                      reads=[b_ot[b], ident_b], writes=[psb[bank]], signal=(kc % 4 == 3))
            sc.op("act", lambda e: e.copy(out=oT[b][:, 0:4, :], in_=ps[0][:, :].rearrange("p (k t) -> p k t", k=4)),
                  reads=[psb[0]], writes=[b_oT[b]])
            sc.op("dve", lambda e: e.tensor_copy(out=oT[b][:, 4:8, :],
                                                 in_=ps[1][:, :].rearrange("p (k t) -> p k t", k=4)),
                  reads=[psb[1]], writes=[b_oT[b]])
            for hf in range(2):
                pb = 2 + 2 * b + hf
                for kc in range(8):
                    sc.op("pe", lambda e: e.matmul(ps[pb][:, :], lhsT=oT[b][:, kc, :], rhs=wo[:, kc, hf * 512:(hf + 1) * 512],
                                                   start=(kc == 0), stop=(kc == 7)),
                          reads=[b_oT[b], b_wo[kc]], writes=[psb[pb]], signal=(kc == 7))
            for hf in range(2):
                pb = 2 + 2 * b + hf
                sc.op("dve", lambda e: e.tensor_tensor(out=xr[b][:, hf * 512:(hf + 1) * 512], in0=ps[pb][:, :],
                                                       in1=xr[b][:, hf * 512:(hf + 1) * 512], op=ALU.add),
                      reads=[psb[pb], b_xr[b]], writes=[b_xr[b]])
            sc.dma(xres[r0:r0 + 128, :], xr[b][:], reads=[b_xr[b]], writes=[x_bufs[t]])
        sc.barrier()


K.outproj_phase = _outproj_phase


def _final_phase(self, tag, g, xres, x_bufs, out):
    sc = self.sc
    S, NT = self.S, self.NT
    with ExitStack() as st:
        gB = self.sb(st, tag + "gB", [128, D], F32)
        xt = [self.sb(st, tag + "xt%d" % i, [128, D], F32) for i in range(2)]
        junk = self.sb(st, tag + "junk", [128, D], F32)
        stat = [self.sb(st, tag + "stat%d" % i, [128, 4], F32) for i in range(2)]
        b_gB, b_junk = Buf("gB"), Buf("junk")
        b_xt = [Buf("xt0"), Buf("xt1")]
        b_stat = [Buf("stat0"), Buf("stat1")]
        sc.dma(gB[:], g.partition_broadcast(128), writes=[b_gB])
        for t in range(NT):
            b = t % 2
            r0 = t * 128
            sc.dma(xt[b][:], xres[r0:r0 + 128, :], reads=[x_bufs[t]], writes=[b_xt[b]])
            sc.op("act", lambda e: e.activation(out=junk[:], in_=xt[b][:], func=AF.Square, accum_out=stat[b][:, 0:1]),
                  reads=[b_xt[b]], writes=[b_junk, b_stat[b]])
            sc.op("dve", lambda e: e.tensor_scalar(out=stat[b][:, 1:2], in0=stat[b][:, 0:1], scalar1=1.0 / D,
                                                   scalar2=EPS, op0=ALU.mult, op1=ALU.add),
                  reads=[b_stat[b]], writes=[b_stat[b]])
            sc.op("act", lambda e: e.sqrt(out=stat[b][:, 3:4], in_=stat[b][:, 1:2]), reads=[b_stat[b]],
                  writes=[b_stat[b]])
            sc.op("dve", lambda e: e.reciprocal(out=stat[b][:, 2:3], in_=stat[b][:, 3:4]), reads=[b_stat[b]],
                  writes=[b_stat[b]])
            sc.op("dve", lambda e: e.scalar_tensor_tensor(out=xt[b][:], in0=xt[b][:], scalar=stat[b][:, 2:3],
                                                          in1=gB[:], op0=ALU.mult, op1=ALU.mult),
                  reads=[b_xt[b], b_stat[b], b_gB], writes=[b_xt[b]])
            sc.dma(out[r0:r0 + 128, :], xt[b][:], reads=[b_xt[b]])
        sc.barrier()


K.final_phase = _final_phase

import math
WNAMES = [("ln_ffn1", [2, D]), ("ffn1_w_in", [2, D, 2 * DFF]), ("ffn1_w_out", [2, DFF, D]), ("ln_mix", [2, D]),
          ("w_mix_in", [2, D, MIX_IN]), ("conv_w", [2, 5, 1152]), ("sink_logits", [2, 6]),
          ("diff_lambda", [2, 4, 32]), ("diff_norm_g", [2, 64]), ("dn_A_log", [2, 2, 6]), ("dn_dt_bias", [2, 2, 6]),
          ("dn_norm_g", [2, 64]), ("w_mix_out", [2, D, D]), ("ln_ffn2", [2, D]), ("ffn2_w_in", [2, D, 2 * DFF]),
          ("ffn2_w_out", [2, DFF, D]), ("ln_final", [D])]


def build_full(S, depth=2, paired=False):
    k = K(S, depth, paired)
    SK = k.SK
    x = k.inp("x", [S, D])
    W = {n: k.inp(n, shp) for n, shp in WNAMES}
    cst = {n: k.inp(n, shp, BF16) for n, shp in (("daq", [4, 4, S]), ("dakp", [4, 4, SK]), ("dakm", [4, 4, SK]),
                                                  ("dbd", [4, 128, 128]), ("identb", [128, 128]))}
    wbias = k.inp("wbias", [6, 128, 384])
    dnc = k.inp("dnc", [10, 128, 128])
    idn = k.inp("ident", [128, 128])
    if paired:
        antiid = k.inp("antiid", [128, 128])
        sel_d = k.inp("sel", [128, 2])
        pr = k.pair_scratch()
    else:
        pr = sel_d = None
    out = k.outp("out", [S, D])
    xres = k.scratch("xres", [S, D])
    ms = k.mix_scratch()
    ds = k.dn_scratch()
    sc = k.sc
    with ExitStack() as st:
        ident = k.sb(st, "ident_sb", [128, 128], F32)
        ib = Buf("ident")
        sc.dma(ident[:], idn[:, :], writes=[ib])
        xb = [Buf("x%d" % i) for i in range(k.NT)]
        xin = [Buf("xin%d" % i) for i in range(k.NT)]
        for l in range(depth):
            lam_init = 0.8 - 0.6 * math.exp(-0.3 * l)
            k.ffn_phase("f1_%d" % l, W["ffn1_w_in"][l], W["ffn1_w_out"][l], W["ln_ffn1"][l],
                        x if l == 0 else xres, xin if l == 0 else xb, xres, xb, ident, ib)
            k.inproj_phase("ip%d" % l, W["w_mix_in"][l], W["ln_mix"][l], xres, xb, ms, ident, ib)
            if paired:
                k.export_phase("ex%d" % l, W["w_mix_in"][l], W["ln_mix"][l], xres, xb, pr, ident, ib, antiid)
                k.exchange_phase("xc%d" % l, pr, ms, sel_d, st)
            with ExitStack() as cst_:
                units = k.conv_units(cst_, "cu%d" % l, ms, ds, W["conv_w"][l], dnc, ident, ib)

                def hook(units=units):
                    if units:
                        units.pop(0)()
                    return len(units) > 0
                k.diffattn_phase("da%d" % l, ms, cst, W["diff_lambda"][l], W["diff_norm_g"][l], lam_init, ident, ib, hook)
            with ExitStack() as wst_:
                wunits = k.win_units(wst_, "wu%d" % l, ms, wbias, W["sink_logits"][l])

                def whook(units=wunits):
                    if units:
                        units.pop(0)()
                    return len(units) > 0
                k.dn2("d2_%d" % l, ms, ds, dnc, W["dn_A_log"][l], W["dn_dt_bias"][l], W["dn_norm_g"][l], ident, ib,
                      pr, sel_d, st, whook)
            k.outproj_phase("op%d" % l, W["w_mix_out"][l], ms, xres, xb, ident, ib)
            k.ffn_phase("f2_%d" % l, W["ffn2_w_in"][l], W["ffn2_w_out"][l], W["ln_ffn2"][l],
                        xres, xb, xres, xb, ident, ib)
        k.final_phase("fin", W["ln_final"], xres, xb, out)
        sc.finish([])
    k.stack.close()
    return k


def pair_feeds(inputs, S_full):
    x = np.ascontiguousarray(np.asarray(inputs["x"], dtype=np.float32))
    B = x.shape[0]
    S = S_full // 2
    base = {n: np.ascontiguousarray(np.asarray(inputs[n], dtype=np.float32)) for n, _ in WNAMES}
    odd = dict(base)
    odd["conv_w"] = np.ascontiguousarray(base["conv_w"][:, ::-1, :])
    wmi = base["w_mix_in"].copy()
    wmi[:, :, C_CB:C_CB + 6] = base["w_mix_in"][:, :, C_CB + 6:C_CB + 12]
    wmi[:, :, C_CB + 6:C_CB + 12] = base["w_mix_in"][:, :, C_CB:C_CB + 6]
    wmi[:, :, C_CA:C_CA + 6] = base["w_mix_in"][:, :, C_CA + 6:C_CA + 12]
    wmi[:, :, C_CA + 6:C_CA + 12] = base["w_mix_in"][:, :, C_CA:C_CA + 6]
    odd["w_mix_in"] = wmi
    odd["dn_A_log"] = np.ascontiguousarray(base["dn_A_log"][:, ::-1, :])
    odd["dn_dt_bias"] = np.ascontiguousarray(base["dn_dt_bias"][:, ::-1, :])
    common = {}
    common.update(diff_consts(S, 2 * S))
    common.update(win_consts())
    common.update(dn_consts())
    common["ident"] = np.eye(128, dtype=np.float32)
    common["antiid"] = np.ascontiguousarray(np.eye(128, dtype=np.float32)[::-1])
    maps = []
    for c in range(2 * B):
        b, r = c // 2, c % 2
        m = dict(base if r == 0 else odd)
        m.update(common)
        if r == 0:
            m["x"] = np.ascontiguousarray(x[b, 0:S])
        else:
            m["x"] = np.ascontiguousarray(x[b, S:2 * S][::-1])
        sel = np.zeros((128, 2), np.float32)
        sel[:, 1 - r] = 1.0
        m["sel"] = sel
        maps.append(m)
    return maps


def pair_gather(results, B, S_full):
    S = S_full // 2
    out = np.empty((B, S_full, D), np.float32)
    for b in range(B):
        out[b, 0:S] = results[2 * b]["out"]
        out[b, S:] = results[2 * b + 1]["out"][::-1]
    return out


GROUPS = [[0, 1], [2, 3], [4, 5], [6, 7]]


def _cdiv(a, b):
    return (a + b - 1) // b


def _pair_chunks(S):
    misc = _cdiv(128 * 128, S) + _cdiv(128 * 130, S)
    return [("bk0", 128), ("bk1", 128), ("bv0", 65), ("bv1", 65), ("bv2", 65), ("bv3", 65), ("misc", misc)]


def _pair_scratch(self):
    S = self.S
    d = {"exp": {}, "gat": {}, "rows": {}}
    for name, rows in _pair_chunks(S):
        d["exp"][name] = self.scratch("exp_" + name, [rows, S], BF16)
        d["gat"][name] = self.scratch("gat_" + name, [2 * rows, S], BF16)
        d["rows"][name] = rows
    d["expf"] = self.scratch("expf", [36, 64], F32)
    d["gatf"] = self.scratch("gatf", [72, 64], F32)
    d["exps"] = self.scratch("exps", [384, 64], F32)
    d["gats"] = self.scratch("gats", [768, 64], F32)
    return d


K.pair_scratch = _pair_scratch


def _pviews(pr, S, slot=None):
    def buf(name):
        if slot is None:
            return pr["exp"][name]
        r = pr["rows"][name]
        return pr["gat"][name][slot * r:(slot + 1) * r, :]
    v = {}
    v["bkT"] = [buf("bk0"), buf("bk1")]
    v["bv"] = [buf("bv%d" % q).rearrange("r c -> (r c)").rearrange("(t d) -> t d", d=260) for q in range(4)]
    mflat = buf("misc").rearrange("r c -> (r c)")
    o2 = _cdiv(128 * 128, S) * S
    v["akT"] = mflat[0:128 * 128].rearrange("(r c) -> r c", c=128)
    v["av"] = mflat[o2:o2 + 128 * 130].rearrange("(t d) -> t d", d=130)
    return v


def _export_phase(self, tag, w_mi, g, src, src_bufs, pr, ident, ident_b, antiid):
    sc = self.sc
    S, NT = self.S, self.NT
    ps, psb = self.ps, self.psb
    ev_ = _pviews(pr, S)
    Q4 = S // 4
    e_akT, e_av = ev_["akT"], ev_["av"]
    e_halo = pr["expf"].rearrange("r c -> (r c)").rearrange("(a b) -> a b", b=2)
    with ExitStack() as st:
        wm = self.sb(st, tag + "wm", [128, 8, MIX_IN], BF16)
        gB = self.sb(st, tag + "gB", [128, D], F32)
        J = self.sb(st, tag + "J", [128, 128], F32)
        xt = [self.sb(st, tag + "xt%d" % i, [128, D], F32) for i in range(2)]
        xn = self.sb(st, tag + "xn", [128, D], F32)
        junk = self.sb(st, tag + "junk", [128, D], F32)
        xr = [self.sb(st, tag + "xr%d" % i, [128, 8, 128], BF16) for i in range(2)]
        stat = self.sb(st, tag + "stat", [128, 8], F32)
        ok_ = [self.sb(st, tag + "ok%d" % i, [128, 2, 128], BF16) for i in range(2)]
        ov = [self.sb(st, tag + "ov%d" % i, [128, 4, 65], BF16) for i in range(2)]
        oak = self.sb(st, tag + "oak", [128, 128], BF16)
        oav = self.sb(st, tag + "oav", [128, 2, 65], BF16)
        oh = self.sb(st, tag + "oh", [128, 9, 2], F32)
        b_wm = [Buf("wm%d" % k) for k in range(8)]
        b_gB, b_J, b_xn, b_junk, b_stat, b_oak, b_oav, b_oh = (Buf(n) for n in
                                                               ("gB", "J", "xn", "junk", "stat", "oak", "oav", "oh"))
        b_xt = [Buf("xt0"), Buf("xt1")]
        b_xr = [Buf("xr0"), Buf("xr1")]
        b_ok = [Buf("ok0"), Buf("ok1")]
        b_ov = [Buf("ov0"), Buf("ov1")]
        for kc in range(8):
            sc.dma(wm[:, kc, :], w_mi[kc * 128:(kc + 1) * 128, :], writes=[b_wm[kc]], q="pool")
        sc.dma(gB[:], g.partition_broadcast(128), writes=[b_gB])
        sc.dma(J[:], antiid[:, :], writes=[b_J])
        for i in range(2):
            sc.op("pool", lambda e: e.memset(ov[i][:], 1.0), writes=[b_ov[i]])
        sc.op("pool", lambda e: e.memset(oav[:], 1.0), writes=[b_oav])
        for it, t in enumerate(range(NT - 1, -1, -1)):
            e = NT - 1 - t
            b = it % 2
            sc.dma(xt[b][:], src[t * 128:(t + 1) * 128, :], reads=[src_bufs[t]], writes=[b_xt[b]])
            sc.op("act", lambda e_: e_.activation(out=junk[:], in_=xt[b][:], func=AF.Square, accum_out=stat[:, 0:1]),
                  reads=[b_xt[b]], writes=[b_junk, b_stat])
            sc.op("dve", lambda e_: e_.tensor_scalar(out=stat[:, 1:2], in0=stat[:, 0:1], scalar1=1.0 / D,
                                                     scalar2=EPS, op0=ALU.mult, op1=ALU.add),
                  reads=[b_stat], writes=[b_stat])
            sc.op("act", lambda e_: e_.sqrt(out=stat[:, 3:4], in_=stat[:, 1:2]), reads=[b_stat], writes=[b_stat])
            sc.op("dve", lambda e_: e_.reciprocal(out=stat[:, 2:3], in_=stat[:, 3:4]), reads=[b_stat], writes=[b_stat])
            sc.op("dve", lambda e_: e_.scalar_tensor_tensor(out=xn[:], in0=xt[b][:], scalar=stat[:, 2:3],
                                                            in1=gB[:], op0=ALU.mult, op1=ALU.mult),
                  reads=[b_xt[b], b_stat, b_gB], writes=[b_xn])
            for kc in range(8):
                bank = kc // 4
                sc.op("pe", lambda e_: e_.matmul(ps[bank][:, (kc % 4) * 128:(kc % 4 + 1) * 128],
                                                 lhsT=xn[:, kc * 128:(kc + 1) * 128], rhs=J[:], start=True, stop=True),
                      reads=[b_xn, b_J], writes=[psb[bank]], signal=(kc % 4 == 3))
            sc.op("act", lambda e_: e_.copy(out=xr[b][:, 0:4, :], in_=ps[0][:, :].rearrange("p (k t) -> p k t", k=4)),
                  reads=[psb[0]], writes=[b_xr[b]])
            sc.op("dve", lambda e_: e_.tensor_copy(out=xr[b][:, 4:8, :],
                                                   in_=ps[1][:, :].rearrange("p (k t) -> p k t", k=4)),
                  reads=[psb[1]], writes=[b_xr[b]])
            for ci in range(2):
                for kc in range(8):
                    sc.op("pe", lambda e_: e_.matmul(ps[2][:, ci * 128:(ci + 1) * 128],
                                                     lhsT=wm[:, kc, C_BK + ci * 128:C_BK + (ci + 1) * 128],
                                                     rhs=xr[b][:, kc, :], start=(kc == 0), stop=(kc == 7)),
                          reads=[b_wm[kc], b_xr[b]], writes=[psb[2]], signal=(kc == 7 and ci == 1))
            for kc in range(8):
                sc.op("pe", lambda e_: e_.matmul(ps[3][:, 0:256], lhsT=xr[b][:, kc, :], rhs=wm[:, kc, C_BV:C_BV + 256],
                                                 start=(kc == 0), stop=(kc == 7)),
                      reads=[b_wm[kc], b_xr[b]], writes=[psb[3]], signal=(kc == 7))
            sc.op("act", lambda e_: e_.copy(out=ok_[b][:].rearrange("p c t -> p (c t)"), in_=ps[2][:, 0:256]),
                  reads=[psb[2]], writes=[b_ok[b]])
            sc.op("dve", lambda e_: e_.tensor_copy(out=ov[b][:, :, 0:64],
                                                   in_=ps[3][:, 0:256].rearrange("p (h d) -> p h d", h=4)),
                  reads=[psb[3]], writes=[b_ov[b]])
            for ci in range(2):
                sc.dma(ev_["bkT"][ci][:, e * 128:(e + 1) * 128], ok_[b][:, ci, :], reads=[b_ok[b]])
            q4 = (e * 128) // Q4
            r4 = e * 128 - q4 * Q4
            sc.dma(ev_["bv"][q4][r4:r4 + 128, :], ov[b][:].rearrange("p h d -> p (h d)"), reads=[b_ov[b]])
            if e == 0:
                for kc in range(8):
                    sc.op("pe", lambda e_: e_.matmul(ps[4][:, 0:128], lhsT=wm[:, kc, C_AK:C_AK + 128], rhs=xr[b][:, kc, :],
                                                     start=(kc == 0), stop=(kc == 7)),
                          reads=[b_wm[kc], b_xr[b]], writes=[psb[4]], signal=(kc == 7))
                for kc in range(8):
                    sc.op("pe", lambda e_: e_.matmul(ps[5][:, 0:128], lhsT=xr[b][:, kc, :], rhs=wm[:, kc, C_AV:C_AV + 128],
                                                     start=(kc == 0), stop=(kc == 7)),
                          reads=[b_wm[kc], b_xr[b]], writes=[psb[5]], signal=(kc == 7))
                for ci in range(9):
                    for kc in range(8):
                        sc.op("pe", lambda e_: e_.matmul(ps[6][:, ci * 2:ci * 2 + 2],
                                                         lhsT=wm[:, kc, C_CQKV + ci * 128:C_CQKV + (ci + 1) * 128],
                                                         rhs=xr[b][:, kc, 0:2], start=(kc == 0), stop=(kc == 7)),
                              reads=[b_wm[kc], b_xr[b]], writes=[psb[6]], signal=(kc == 7 and ci == 8))
                sc.op("act", lambda e_: e_.copy(out=oak[:], in_=ps[4][:, 0:128]), reads=[psb[4]], writes=[b_oak])
                sc.op("dve", lambda e_: e_.tensor_copy(out=oav[:, :, 0:64],
                                                       in_=ps[5][:, 0:128].rearrange("p (h d) -> p h d", h=2)),
                      reads=[psb[5]], writes=[b_oav])
                sc.op("act", lambda e_: e_.copy(out=oh[:].rearrange("p c t -> p (c t)"), in_=ps[6][:, 0:18]),
                      reads=[psb[6]], writes=[b_oh])
                sc.dma(e_akT[:, :], oak[:], reads=[b_oak])
                sc.dma(e_av[:, :], oav[:].rearrange("p h d -> p (h d)"), reads=[b_oav])
                sc.dma(e_halo.rearrange("(c p) t -> p c t", p=128), oh[:], reads=[b_oh])
        sc.barrier()


K.export_phase = _export_phase


def _exchange_phase(self, tag, pr, ms, sel_d, cstack):
    sc = self.sc
    S, NT = self.S, self.NT
    for name, _r in _pair_chunks(S):
        sc.collective(cstack, "AllGather", pr["exp"][name].opt(), pr["gat"][name].opt(), GROUPS)
    Q4 = S // 4
    sc.collective(cstack, "AllGather", pr["expf"].opt(), pr["gatf"].opt(), GROUPS)
    with ExitStack() as st:
        sel = self.sb(st, tag + "sel", [128, 2], F32)
        b_sel = Buf("sel")
        sc.dma(sel[:], sel_d[:, :], writes=[b_sel])
        CW = min(S, 2048)
        a0 = [self.sb(st, tag + "a0_%d" % i, [128, CW], BF16) for i in range(2)]
        a1 = [self.sb(st, tag + "a1_%d" % i, [128, CW], BF16) for i in range(2)]
        b_a0 = [Buf("a0_0"), Buf("a0_1")]
        b_a1 = [Buf("a1_0"), Buf("a1_1")]
        f0 = self.sb(st, tag + "f0", [128, 9, 2], F32)
        f1 = self.sb(st, tag + "f1", [128, 9, 2], F32)
        b_f0, b_f1 = Buf("f0"), Buf("f1")
        cnt = [0]

        def select(dst_ap, src0, src1, np_, w, view=None):
            i = cnt[0] % 2
            cnt[0] += 1
            t0 = a0[i][0:np_, 0:w]
            t1 = a1[i][0:np_, 0:w]
            if view is not None:
                t0v, t1v = view(t0), view(t1)
            else:
                t0v, t1v = t0, t1
            sc.dma(t0v, src0, writes=[b_a0[i]])
            sc.dma(t1v, src1, writes=[b_a1[i]])
            sc.op("dve", lambda e: e.tensor_scalar_mul(out=t0, in0=t0, scalar1=sel[0:np_, 0:1]),
                  reads=[b_a0[i], b_sel], writes=[b_a0[i]])
            sc.op("dve", lambda e: e.scalar_tensor_tensor(out=t1, in0=t1, scalar=sel[0:np_, 1:2], in1=t0,
                                                          op0=ALU.mult, op1=ALU.add),
                  reads=[b_a0[i], b_a1[i], b_sel], writes=[b_a1[i]])
            sc.dma(dst_ap, t1v, reads=[b_a1[i]])

        gv = [_pviews(pr, S, 0), _pviews(pr, S, 1)]
        for ci in range(2):
            for c0 in range(0, S, CW):
                v = lambda slot: gv[slot]["bkT"][ci][:, c0:c0 + CW]
                select(ms["bkT"][ci * 128:(ci + 1) * 128, S + c0:S + c0 + CW], v(0), v(1), 128, CW)
        TPB = max(1, min(CW // 260, Q4 // 128))
        for q4 in range(4):
            for t0_ in range(0, Q4 // 128, TPB):
                tn = min(TPB, Q4 // 128 - t0_)
                v = lambda slot: gv[slot]["bv"][q4][t0_ * 128:(t0_ + tn) * 128, :].rearrange("(n p) d -> p n d", p=128)
                r0 = S + q4 * Q4 + t0_ * 128
                dst = ms["bv"][r0:r0 + tn * 128, :, :].rearrange("(n p) h d -> p n (h d)", p=128)
                select(dst, v(0), v(1), 128, tn * 260, view=lambda t: t.rearrange("p (n d) -> p n d", d=260))
        v = lambda slot: gv[slot]["akT"]
        select(ms["akT"][:, S:S + 128], v(0), v(1), 128, 128)
        v = lambda slot: gv[slot]["av"]
        select(ms["av"][S:S + 128, :, :].rearrange("p h d -> p (h d)"), v(0), v(1), 128, 130)
        hv = lambda slot: pr["gatf"][slot * 36:(slot + 1) * 36, :].rearrange("r c -> (r c)") \
            .rearrange("(c p t) -> p c t", p=128, t=2)
        sc.dma(f0[:], hv(0), writes=[b_f0])
        sc.dma(f1[:], hv(1), writes=[b_f1])
        sc.op("dve", lambda e: e.tensor_scalar_mul(out=f0[:], in0=f0[:], scalar1=sel[:, 0:1]),
              reads=[b_f0, b_sel], writes=[b_f0])
        sc.op("dve", lambda e: e.scalar_tensor_tensor(out=f1[:], in0=f1[:], scalar=sel[:, 1:2], in1=f0[:],
                                                      op0=ALU.mult, op1=ALU.add),
              reads=[b_f0, b_f1, b_sel], writes=[b_f1])
        sc.dma(ms["cpre"][:, S + 2:S + 4].rearrange("(c p) t -> p c t", p=128), f1[:], reads=[b_f1])
        sc.barrier()


K.exchange_phase = _exchange_phase


def _dn2(self, tag, ms, ds, dnc, a_log, dt_bias, dn_g, ident, ident_b, pr=None, sel_d=None, cstack=None, hook=None):
    sc = self.sc
    S, NT = self.S, self.NT
    ps, psb = self.ps, self.psb
    NIT = 7
    gcT = self.scratch(tag + "gcT", [12, S], F32)
    ngcT = self.scratch(tag + "ngcT", [12, S], F32)
    acT = self.scratch(tag + "acT", [12, S], F32)
    with ExitStack() as st0:
        def T0(name, shape, dt=F32):
            return self.sb(st0, tag + name, shape, dt), Buf(name)
        gc, b_gc = T0("gc", [128, NT, 12])
        ac, b_ac = T0("ac", [128, NT, 12])
        beta, b_beta = T0("beta", [128, NT, 12])
        ea, b_ea = T0("ea", [128, NT, 12])
        ekg, b_ekg = T0("ekg", [128, NT, 12])
        glv, b_glv = T0("glv", [128, NT, 12])
        ones, b_ones = T0("ones", [128, 128])
        sc.dma(ones[:], dnc[0], writes=[b_ones])
        with ExitStack() as st:
            def T1(name, shape, dt=F32):
                return self.sb(st, tag + "g_" + name, shape, dt), Buf(name)
            ba, b_ba = T1("ba", [128, NT, 24])
            w1, b_w1 = T1("w1", [128, NT, 12])
            w2, b_w2 = T1("w2", [128, NT, 12])
            sp, b_sp = T1("sp", [128, NT, 12])
            gg, b_gg = T1("gg", [128, NT, 12])
            tt, b_tt = T1("tt", [128, NT, 12])
            triF, b_triF = T1("triF", [128, 128])
            triB, b_triB = T1("triB", [128, 128])
            dtb, b_dtb = T1("dtb", [128, 12])
            nega, b_nega = T1("nega", [128, 12])
            ev = [T1("ev%d" % i, [12, 3, 512]) for i in range(2)]
            sc.dma(ba[:], ms["cz"][:, 384:408].rearrange("(n p) c -> p n c", p=128), writes=[b_ba])
            sc.dma(triF[:], dnc[1], writes=[b_triF])
            sc.dma(triB[:], dnc[2], writes=[b_triB])
            sc.dma(dtb[:], dt_bias.rearrange("a b -> (a b)").partition_broadcast(128), writes=[b_dtb])
            sc.dma(nega[:], a_log.rearrange("a b -> (a b)").partition_broadcast(128), writes=[b_nega])
            sc.op("act", lambda e: e.activation(out=nega[:], in_=nega[:], func=AF.Exp), reads=[b_nega], writes=[b_nega])
            sc.op("dve", lambda e: e.tensor_scalar_mul(out=nega[:], in0=nega[:], scalar1=-1.0),
                  reads=[b_nega], writes=[b_nega])
            bc = lambda t: t[:].unsqueeze(1).to_broadcast([128, NT, 12])
            sc.op("act", lambda e: e.activation(out=w1[:], in_=ba[:, :, 0:12], func=AF.Exp, scale=-1.0),
                  reads=[b_ba], writes=[b_w1])
            sc.op("dve", lambda e: e.tensor_scalar_add(out=w1[:], in0=w1[:], scalar1=1.0), reads=[b_w1], writes=[b_w1])
            sc.op("act", lambda e: e.activation(out=sp[:], in_=w1[:], func=AF.Ln), reads=[b_w1], writes=[b_sp])
            sc.op("dve", lambda e: e.tensor_tensor(out=w2[:], in0=ba[:, :, 12:24], in1=bc(dtb), op=ALU.add),
                  reads=[b_ba, b_dtb], writes=[b_w2])
            sc.op("act", lambda e: e.activation(out=w2[:], in_=w2[:], func=AF.Exp), reads=[b_w2], writes=[b_w2])
            sc.op("dve", lambda e: e.tensor_scalar_add(out=w2[:], in0=w2[:], scalar1=1.0), reads=[b_w2], writes=[b_w2])
            sc.op("act", lambda e: e.activation(out=w2[:], in_=w2[:], func=AF.Ln), reads=[b_w2], writes=[b_w2])
            sc.op("dve", lambda e: e.tensor_tensor(out=gg[:], in0=w2[:], in1=bc(nega), op=ALU.mult),
                  reads=[b_w2, b_nega], writes=[b_gg])
            NC6 = NT * 6
            for c0 in range(0, NT, 64):
                c1 = min(NT, c0 + 64)
                w = (c1 - c0) * 6
                sc.op("pe", lambda e: e.matmul(ps[0][:, 0:w], lhsT=triF[:], rhs=gg[:, c0:c1, 0:6], start=True, stop=True),
                      reads=[b_triF, b_gg], writes=[psb[0]], signal=False)
                sc.op("pe", lambda e: e.matmul(ps[1][:, 0:w], lhsT=triB[:], rhs=gg[:, c0:c1, 6:12], start=True, stop=True),
                      reads=[b_triB, b_gg], writes=[psb[1]], signal=False)
                sc.op("pe", lambda e: e.matmul(ps[2][:, 0:w], lhsT=ones[:], rhs=gg[:, c0:c1, 0:6], start=True, stop=True),
                      reads=[b_ones, b_gg], writes=[psb[2]], signal=False)
                sc.op("pe", lambda e: e.matmul(ps[3][:, 0:w], lhsT=ones[:], rhs=gg[:, c0:c1, 6:12], start=True, stop=True),
                      reads=[b_ones, b_gg], writes=[psb[3]])
                v6 = lambda b: ps[b][:, 0:w].rearrange("p (n j) -> p n j", j=6)
                sc.op("dve", lambda e: e.tensor_copy(out=gc[:, c0:c1, 0:6], in_=v6(0)), reads=[psb[0]], writes=[b_gc])
                sc.op("dve", lambda e: e.tensor_copy(out=gc[:, c0:c1, 6:12], in_=v6(1)), reads=[psb[1]], writes=[b_gc])
                sc.op("dve", lambda e: e.tensor_copy(out=tt[:, c0:c1, 0:6], in_=v6(2)), reads=[psb[2]], writes=[b_tt])
                sc.op("dve", lambda e: e.tensor_copy(out=tt[:, c0:c1, 6:12], in_=v6(3)), reads=[psb[3]], writes=[b_tt])
            sc.op("dve", lambda e: e.tensor_tensor(out=ac[:], in0=gc[:], in1=sp[:], op=ALU.subtract),
                  reads=[b_gc, b_sp], writes=[b_ac])
            sc.op("act", lambda e: e.activation(out=ea[:], in_=ac[:], func=AF.Exp), reads=[b_ac], writes=[b_ea])
            sc.op("act", lambda e: e.activation(out=beta[:], in_=sp[:], func=AF.Exp, scale=-1.0),
                  reads=[b_sp], writes=[b_beta])
            sc.op("dve", lambda e: e.tensor_tensor(out=w1[:], in0=tt[:], in1=gc[:], op=ALU.subtract),
                  reads=[b_tt, b_gc, b_w1], writes=[b_w1])
            sc.op("act", lambda e: e.activation(out=ekg[:], in_=w1[:], func=AF.Exp), reads=[b_w1], writes=[b_ekg])
            sc.op("act", lambda e: e.activation(out=glv[:], in_=tt[:], func=AF.Exp), reads=[b_tt], writes=[b_glv])
            for q0 in range(0, NT, 4):
                qn = min(4, NT - q0)
                (evt, b_evt) = ev[(q0 // 4) % 2]
                for i in range(qn):
                    n = q0 + i
                    sc.op("pe", lambda e: e.transpose(ps[4][0:12, i * 128:(i + 1) * 128], gc[:, n, :], ident[:]),
                          reads=[b_gc, ident_b], writes=[psb[4]], signal=False)
                    sc.op("pe", lambda e: e.transpose(ps[5][0:12, i * 128:(i + 1) * 128], ac[:, n, :], ident[:]),
                          reads=[b_ac, ident_b], writes=[psb[5]], signal=(i == qn - 1))
                w = qn * 128
                sc.op("dve", lambda e: e.tensor_copy(out=evt[:, 0, 0:w], in_=ps[4][0:12, 0:w]), reads=[psb[4]], writes=[b_evt])
                sc.op("dve", lambda e: e.tensor_scalar_mul(out=evt[:, 1, 0:w], in0=ps[4][0:12, 0:w], scalar1=-1.0),
                      reads=[psb[4]], writes=[b_evt])
                sc.op("act", lambda e: e.copy(out=evt[:, 2, 0:w], in_=ps[5][0:12, 0:w]), reads=[psb[5]], writes=[b_evt])
                sc.dma(gcT[:, q0 * 128:q0 * 128 + w], evt[:, 0, 0:w], reads=[b_evt])
                sc.dma(ngcT[:, q0 * 128:q0 * 128 + w], evt[:, 1, 0:w], reads=[b_evt])
                sc.dma(acT[:, q0 * 128:q0 * 128 + w], evt[:, 2, 0:w], reads=[b_evt])
            sc.barrier()
        for dirn in (0, 1):
            with ExitStack() as st:
                def T(name, shape, n=1, dt=F32):
                    ts = [self.sb(st, "%s%d%s%d" % (tag, dirn, name, i), shape, dt) for i in range(n)]
                    bs = [Buf("%s%d" % (name, i)) for i in range(n)]
                    return (ts, bs) if n > 1 else (ts[0], bs[0])
                m1, b_m1 = T("m1", [128, 128])
                m2, b_m2 = T("m2", [128, 128])
                m3, b_m3 = T("m3", [128, 128])
                gdn, b_gdn = T("gdn", [128, 64])
                kTt, b_kTt = T("kTt", [64, 6, 128], 2)
                qTt, b_qTt = T("qTt", [64, 6, 128], 2)
                kt, b_kt = T("kt", [128, 384], 2)
                vt, b_vt = T("vt", [128, 384], 2)
                Rg, b_Rg = T("Rg", [128, 6, 128], 2)
                Rn, b_Rn = T("Rn", [128, 6, 128], 2)
                Ra, b_Ra = T("Ra", [128, 6, 128], 2)
                eR, b_eR = T("eR", [64, 6, 128])
                qg, b_qg = T("qg", [64, 6, 128])
                tmp, b_tmp = T("tmp", [128, 3, 128], 6)
                E, b_E = T("E", [128, 3, 128], 6)
                W0, b_W0 = T("W0", [128, 3, 128], 6)
                W1, b_W1 = T("W1", [128, 3, 128], 6)
                qkT, b_qkT = T("qkT", [128, 128], 6)
                kg, b_kg = T("kg", [128, 64], 6)
                glI, b_glI = T("glI", [64, 64], 6)
                wTn, b_wTn = T("wTn", [64, 128], 6)
                Vn, b_Vn = T("Vn", [128, 64], 6)
                St, b_St = T("St", [64, 64], 6)
                osb, b_osb = T("osb", [128, 6, 64], 2)
                if dirn == 1:
                    gz, b_gz = T("gz", [128, 384], 2)
                    oft, b_oft = T("oft", [128, 6, 64], 2)
                    sqt, b_sqt = T("sqt", [128, 6, 64])
                    sz, b_sz = T("sz", [128, 6, 64])
                    rr, b_rr = T("rr", [128, 8])
                sc.dma(m1[:], dnc[4 + 3 * dirn], writes=[b_m1])
                sc.dma(m2[:], dnc[5 + 3 * dirn], writes=[b_m2])
                sc.dma(m3[:], dnc[6 + 3 * dirn], writes=[b_m3])
                sc.dma(gdn[:], dn_g.partition_broadcast(128), writes=[b_gdn])
                if dirn == 0 or not self.paired:
                    for h in range(6):
                        sc.op("pool", lambda e: e.memset(St[h][:], 0.0), writes=[b_St[h]])
                else:
                    sl, b_sl = T("sl", [128, 2])
                    s0, b_s0 = T("s0", [64, 6, 64])
                    s1, b_s1 = T("s1", [64, 6, 64])
                    sc.dma(sl[:], sel_d[:, :], writes=[b_sl])
                    sc.dma(s0[:], pr["gats"][0:384, :].rearrange("(h k) v -> k h v", h=6), writes=[b_s0])
                    sc.dma(s1[:], pr["gats"][384:768, :].rearrange("(h k) v -> k h v", h=6), writes=[b_s1])
                    sc.op("dve", lambda e: e.tensor_scalar_mul(out=s0[:], in0=s0[:], scalar1=sl[0:64, 0:1]),
                          reads=[b_s0, b_sl], writes=[b_s0])
                    for h in range(6):
                        sc.op("dve", lambda e: e.scalar_tensor_tensor(out=St[h][:], in0=s1[:, h, :], scalar=sl[0:64, 1:2],
                                                                      in1=s0[:, h, :], op0=ALU.mult, op1=ALU.add),
                              reads=[b_s0, b_s1, b_sl], writes=[b_St[h]])
                order = list(range(NT)) if dirn == 0 else list(range(NT - 1, -1, -1))
                j0 = dirn * 6

                def load(it):
                    n = order[it]
                    tb = it % 2
                    c0 = n * 128
                    sc.dma(kTt[tb][:], ds["ckT"][:, c0:c0 + 128].rearrange("(h d) t -> d h t", h=6), writes=[b_kTt[tb]])
                    sc.dma(qTt[tb][:], ds["cqT"][:, c0:c0 + 128].rearrange("(h d) t -> d h t", h=6), writes=[b_qTt[tb]])
                    sc.dma(kt[tb][:], ds["ck"][c0:c0 + 128, :], writes=[b_kt[tb]])
                    sc.dma(vt[tb][:], ds["cv"][c0:c0 + 128, :], writes=[b_vt[tb]])
                    sc.dma(Rg[tb][:], gcT[j0:j0 + 6, c0:c0 + 128].partition_broadcast(128), writes=[b_Rg[tb]])
                    sc.dma(Rn[tb][:], ngcT[j0:j0 + 6, c0:c0 + 128].partition_broadcast(128), writes=[b_Rn[tb]])
                    sc.dma(Ra[tb][:], acT[j0:j0 + 6, c0:c0 + 128].partition_broadcast(128), writes=[b_Ra[tb]])
                    if dirn == 1:
                        sc.dma(gz[tb][:], ms["cz"][c0:c0 + 128, 0:384], writes=[b_gz[tb]])
                        sc.dma(oft[tb][:].rearrange("p h d -> p (h d)"), ds["of"][c0:c0 + 128, :], writes=[b_oft[tb]])

                load(0)
                for it, n in enumerate(order):
                    tb = it % 2
                    c0 = n * 128
                    if it + 1 < NT:
                        load(it + 1)
                    sc.op("act", lambda e: e.activation(out=eR[:], in_=Rg[tb][0:64, :, :], func=AF.Exp),
                          reads=[b_Rg[tb]], writes=[b_eR])
                    sc.op("dve", lambda e: e.tensor_tensor(out=qg[:], in0=qTt[tb][:], in1=eR[:], op=ALU.mult),
                          reads=[b_qTt[tb], b_eR], writes=[b_qg])
                    for h in range(6):
                        bk = 2 + h
                        j = j0 + h
                        hs = slice(h * 64, (h + 1) * 64)
                        sc.op("pe", lambda e: e.matmul(ps[bk][:, 0:128], lhsT=kTt[tb][:, h, :], rhs=kTt[tb][:, h, :],
                                                       start=True, stop=True),
                              reads=[b_kTt[tb]], writes=[psb[bk]], signal=False)
                        sc.op("pe", lambda e: e.matmul(ps[bk][:, 128:256], lhsT=kTt[tb][:, h, :], rhs=qTt[tb][:, h, :],
                                                       start=True, stop=True),
                              reads=[b_kTt[tb], b_qTt[tb]], writes=[psb[bk]])
                        sc.op("dve", lambda e: e.scalar_tensor_tensor(out=tmp[h][:, 0, :], in0=Rn[tb][:, h, :],
                                                                      scalar=ac[:, n, j:j + 1], in1=m1[:],
                                                                      op0=ALU.add, op1=ALU.min),
                              reads=[b_Rn[tb], b_ac, b_m1], writes=[b_tmp[h]])
                        sc.op("dve", lambda e: e.scalar_tensor_tensor(out=tmp[h][:, 1, :], in0=Ra[tb][:, h, :],
                                                                      scalar=gc[:, n, j:j + 1], in1=m2[:],
                                                                      op0=ALU.subtract, op1=ALU.min),
                              reads=[b_Ra[tb], b_gc, b_m2], writes=[b_tmp[h]])
                        sc.op("dve", lambda e: e.scalar_tensor_tensor(out=tmp[h][:, 2, :], in0=Rg[tb][:, h, :],
                                                                      scalar=gc[:, n, j:j + 1], in1=m3[:],
                                                                      op0=ALU.subtract, op1=ALU.min),
                              reads=[b_Rg[tb], b_gc, b_m3], writes=[b_tmp[h]])
                        sc.op("act", lambda e: e.activation(out=E[h][:], in_=tmp[h][:], func=AF.Exp),
                              reads=[b_tmp[h]], writes=[b_E[h]])
                        sc.op("act", lambda e: e.activation(out=W0[h][:, 0, 0:64], in_=vt[tb][:, hs], func=AF.Copy,
                                                            scale=beta[:, n, j:j + 1]),
                              reads=[b_vt[tb], b_beta], writes=[b_W0[h]])
                        sc.op("act", lambda e: e.activation(out=W0[h][:, 0, 64:128], in_=kt[tb][:, hs], func=AF.Copy,
                                                            scale=ea[:, n, j:j + 1]),
                              reads=[b_kt[tb], b_ea], writes=[b_W0[h]])
                        sc.op("act", lambda e: e.activation(out=kg[h][:], in_=kt[tb][:, hs], func=AF.Copy,
                                                            scale=ekg[:, n, j:j + 1]),
                              reads=[b_kt[tb], b_ekg], writes=[b_kg[h]])
                        sc.op("act", lambda e: e.activation(out=glI[h][:], in_=ident[0:64, 0:64], func=AF.Copy,
                                                            scale=glv[0:64, n, j:j + 1]),
                              reads=[ident_b, b_glv], writes=[b_glI[h]])
                    for h in range(6):
                        bk = 2 + h
                        sc.op("dve", lambda e: e.scalar_tensor_tensor(out=W0[h][:, 1, :], in0=ps[bk][:, 0:128], scalar=-1.0,
                                                                      in1=E[h][:, 0, :], op0=ALU.mult, op1=ALU.mult),
                              reads=[psb[bk], b_E[h]], writes=[b_W0[h]])
                        sc.op("dve", lambda e: e.scalar_tensor_tensor(out=W0[h][:, 2, :], in0=ps[bk][:, 0:128], scalar=-1.0,
                                                                      in1=E[h][:, 1, :], op0=ALU.mult, op1=ALU.mult),
                              reads=[psb[bk], b_E[h]], writes=[b_W0[h]])
                        sc.op("dve", lambda e: e.tensor_tensor(out=qkT[h][:], in0=ps[bk][:, 128:256], in1=E[h][:, 2, :],
                                                               op=ALU.mult),
                              reads=[psb[bk], b_E[h]], writes=[b_qkT[h]])
                    WW = [(W0, b_W0), (W1, b_W1)]
                    DVE_H = (1, 3, 4, 5)
                    for i in range(NIT):
                        (Wc, b_Wc), (Wn, b_Wn) = WW[i % 2], WW[(i + 1) % 2]
                        last = (i == NIT - 1)
                        for h in range(6):
                            bk = 2 + h
                            cur = Wc[h]
                            flat = cur[:].rearrange("p a s -> p (a s)")
                            na = 128 if i >= NIT - 2 else 256
                            on_dve = h in DVE_H
                            sc.op("pe", lambda e: e.matmul(ps[bk][:, 0:na], lhsT=cur[:, 2, :], rhs=flat[:, 0:na],
                                                           start=True, stop=on_dve, skip_group_check=True),
                                  reads=[b_Wc[h]], writes=[psb[bk]], signal=(on_dve and last))
                            if not on_dve:
                                sc.op("pe", lambda e: e.matmul(ps[bk][:, 0:128], lhsT=ident[:], rhs=cur[:, 0, :],
                                                               start=False, stop=True, skip_group_check=True),
                                      reads=[b_Wc[h], ident_b], writes=[psb[bk]], signal=last)
                            if not last:
                                sc.op("pe", lambda e: e.matmul(ps[bk][:, 256:384], lhsT=cur[:, 1, :], rhs=cur[:, 2, :],
                                                               start=True, stop=True, skip_group_check=True),
                                      reads=[b_Wc[h]], writes=[psb[bk]])
                        for h in range(6):
                            bk = 2 + h
                            cur = Wc[h]
                            nflat = Wn[h][:].rearrange("p a s -> p (a s)")
                            if h in DVE_H:
                                sc.op("dve", lambda e: e.tensor_tensor(out=nflat[:, 0:128], in0=ps[bk][:, 0:128],
                                                                       in1=cur[:, 0, :], op=ALU.add),
                                      reads=[psb[bk], b_Wc[h]], writes=[b_Wn[h]])
                                if not last:
                                    lo = 128 if i < NIT - 2 else 256
                                    sc.op("dve", lambda e: e.tensor_copy(out=nflat[:, lo:384], in_=ps[bk][:, lo:384]),
                                          reads=[psb[bk]], writes=[b_Wn[h]])
                            else:
                                if last:
                                    sc.op("act", lambda e: e.copy(out=nflat[:, 0:128], in_=ps[bk][:, 0:128]),
                                          reads=[psb[bk]], writes=[b_Wn[h]])
                                elif i < NIT - 2:
                                    sc.op("act", lambda e: e.copy(out=nflat[:, 0:384], in_=ps[bk][:, 0:384]),
                                          reads=[psb[bk]], writes=[b_Wn[h]])
                                else:
                                    sc.op("act", lambda e: e.copy(out=nflat[:, 0:128], in_=ps[bk][:, 0:128]),
                                          reads=[psb[bk]], writes=[b_Wn[h]], )
                                    sc.op("act", lambda e: e.copy(out=nflat[:, 256:384], in_=ps[bk][:, 256:384]),
                                          reads=[psb[bk]], writes=[b_Wn[h]])
                        if hook is not None:
                            hook()
                    (Wf, b_Wf) = WW[NIT % 2]
                    ob = it % 2
                    for h in range(6):
                        bk = 2 + h
                        sc.op("pe", lambda e: e.transpose(ps[bk][0:64, 0:128], Wf[h][:, 0, 64:128], ident[:]),
                              reads=[b_Wf[h], ident_b], writes=[psb[bk]])
                    for h in range(6):
                        bk = 2 + h
                        if h % 2 == 0:
                            sc.op("act", lambda e: e.mul(out=wTn[h][:], in_=ps[bk][0:64, 0:128], mul=-1.0),
                                  reads=[psb[bk]], writes=[b_wTn[h]])
                        else:
                            sc.op("dve", lambda e: e.tensor_scalar_mul(out=wTn[h][:], in0=ps[bk][0:64, 0:128], scalar1=-1.0),
                                  reads=[psb[bk]], writes=[b_wTn[h]])
                    for h in range(6):
                        bk = 2 + h
                        sc.op("pe", lambda e: e.matmul(ps[bk][:, 128:192], lhsT=ident[:], rhs=Wf[h][:, 0, 0:64],
                                                       start=True, stop=False, skip_group_check=True),
                              reads=[b_Wf[h], ident_b], writes=[psb[bk]], signal=False)
                        sc.op("pe", lambda e: e.matmul(ps[bk][:, 128:192], lhsT=wTn[h][:], rhs=St[h][:],
                                                       start=False, stop=True, skip_group_check=True),
                              reads=[b_wTn[h], b_St[h]], writes=[psb[bk]])
                    for h in range(6):
                        bk = 2 + h
                        if h % 2 == 0:
                            sc.op("act", lambda e: e.copy(out=Vn[h][:], in_=ps[bk][:, 128:192]), reads=[psb[bk]], writes=[b_Vn[h]])
                        else:
                            sc.op("dve", lambda e: e.tensor_copy(out=Vn[h][:], in_=ps[bk][:, 128:192]), reads=[psb[bk]],
                                  writes=[b_Vn[h]])
                    for h in range(6):
                        bk = 2 + h
                        sc.op("pe", lambda e: e.matmul(ps[bk][:, 192:256], lhsT=qg[:, h, :], rhs=St[h][:],
                                                       start=True, stop=False, skip_group_check=True),
                              reads=[b_qg, b_St[h]], writes=[psb[bk]], signal=False)
                        sc.op("pe", lambda e: e.matmul(ps[bk][:, 192:256], lhsT=qkT[h][:], rhs=Vn[h][:],
                                                       start=False, stop=True, skip_group_check=True),
                              reads=[b_qkT[h], b_Vn[h]], writes=[psb[bk]], signal=False)
                        sc.op("pe", lambda e: e.matmul(ps[bk][0:64, 256:320], lhsT=kg[h][:], rhs=Vn[h][:],
                                                       start=True, stop=False, skip_group_check=True),
                              reads=[b_kg[h], b_Vn[h]], writes=[psb[bk]], signal=False)
                        sc.op("pe", lambda e: e.matmul(ps[bk][0:64, 256:320], lhsT=glI[h][:], rhs=St[h][:],
                                                       start=False, stop=True, skip_group_check=True),
                              reads=[b_glI[h], b_St[h]], writes=[psb[bk]])
                    for h in range(6):
                        bk = 2 + h
                        if h % 2 == 0:
                            sc.op("act", lambda e: e.copy(out=osb[ob][:, h, :], in_=ps[bk][:, 192:256]), reads=[psb[bk]],
                                  writes=[b_osb[ob]])
                            sc.op("act", lambda e: e.copy(out=St[h][:], in_=ps[bk][0:64, 256:320]), reads=[psb[bk]],
                                  writes=[b_St[h]])
                        else:
                            sc.op("dve", lambda e: e.tensor_copy(out=osb[ob][:, h, :], in_=ps[bk][:, 192:256]),
                                  reads=[psb[bk]], writes=[b_osb[ob]])
                            sc.op("dve", lambda e: e.tensor_copy(out=St[h][:], in_=ps[bk][0:64, 256:320]), reads=[psb[bk]],
                                  writes=[b_St[h]])
                    if dirn == 0:
                        sc.dma(ds["of"][c0:c0 + 128, :], osb[ob][:].rearrange("p h d -> p (h d)"), reads=[b_osb[ob]])
                    else:
                        sc.op("dve", lambda e: e.tensor_tensor(out=oft[tb][:], in0=oft[tb][:], in1=osb[ob][:], op=ALU.add),
                              reads=[b_oft[tb], b_osb[ob]], writes=[b_oft[tb]])
                        sc.op("act", lambda e: e.activation(out=sqt[:], in_=oft[tb][:], func=AF.Square),
                              reads=[b_oft[tb]], writes=[b_sqt])
                        sc.op("dve", lambda e: e.reduce_sum(out=rr[:, 0:6], in_=sqt[:], axis=AX.X), reads=[b_sqt],
                              writes=[b_rr])
                        sc.op("dve", lambda e: e.tensor_scalar(out=rr[:, 0:6], in0=rr[:, 0:6], scalar1=1.0 / 64,
                                                               scalar2=EPS, op0=ALU.mult, op1=ALU.add),
                              reads=[b_rr], writes=[b_rr])
                        sc.op("act", lambda e: e.sqrt(out=rr[:, 0:6], in_=rr[:, 0:6]), reads=[b_rr], writes=[b_rr])
                        sc.op("dve", lambda e: e.reciprocal(out=rr[:, 0:6], in_=rr[:, 0:6]), reads=[b_rr], writes=[b_rr])
                        sc.op("act", lambda e: e.activation(out=sz[:].rearrange("p h d -> p (h d)"), in_=gz[tb][:, 0:384],
                                                            func=AF.Silu), reads=[b_gz[tb]], writes=[b_sz])
                        sc.op("dve", lambda e: e.tensor_tensor(out=sz[:], in0=sz[:],
                                                                in1=gdn[:].unsqueeze(1).to_broadcast([128, 6, 64]),
                                                                op=ALU.mult), reads=[b_sz, b_gdn], writes=[b_sz])
                        sc.op("dve", lambda e: e.tensor_tensor(out=oft[tb][:], in0=oft[tb][:],
                                                                in1=rr[:, 0:6].unsqueeze(2).to_broadcast([128, 6, 64]),
                                                                op=ALU.mult), reads=[b_oft[tb], b_rr], writes=[b_oft[tb]])
                        sc.op("dve", lambda e: e.tensor_tensor(out=oft[tb][:], in0=oft[tb][:], in1=sz[:], op=ALU.mult),
                              reads=[b_oft[tb], b_sz], writes=[b_oft[tb]])
                        sc.dma(ms["omix"][c0:c0 + 128, 640:1024], oft[tb][:].rearrange("p h d -> p (h d)"),
                               reads=[b_oft[tb]])
                if dirn == 0 and self.paired:
                    for h in range(6):
                        sc.dma(pr["exps"][h * 64:(h + 1) * 64, :], St[h][:], reads=[b_St[h]])
                if dirn == 1 and hook is not None:
                    while hook():
                        pass
                sc.barrier()
            if dirn == 0 and self.paired:
                sc.collective(cstack, "AllGather", pr["exps"].opt(), pr["gats"].opt(), GROUPS)


K.dn2 = _dn2


def _conv_units(self, st, tag, ms, ds, conv_w, dnc, ident, ident_b):
    sc = self.sc
    S, NT = self.S, self.NT
    GT = 4 if NT % 4 == 0 else 1
    GW = GT * 128
    ps, psb = self.ps, self.psb
    blk1 = self.sb(st, tag + "blk1", [128, 128], F32)
    cw = self.sb(st, tag + "cw", [128, 9, 5], F32)
    xin = [self.sb(st, tag + "xin%d" % i, [128, GW + 4], F32) for i in range(2)]
    y = [self.sb(st, tag + "y%d" % i, [128, GW], F32) for i in range(2)]
    ee = [self.sb(st, tag + "ee%d" % i, [128, GW], F32) for i in range(2)]
    sq2 = [self.sb(st, tag + "sq%d" % i, [128, GW], F32) for i in range(2)]
    rs2 = [self.sb(st, tag + "rs%d" % i, [128, GW], F32) for i in range(2)]
    yn = [self.sb(st, tag + "yn%d" % i, [128, GW], F32) for i in range(2)]
    tk = [self.sb(st, tag + "tk%d" % i, [128, GW], F32) for i in range(2)]
    b_blk1, b_cw = Buf("blk1"), Buf("cw")
    b_xin = [Buf("xin0"), Buf("xin1")]
    b_y = [Buf("y0"), Buf("y1")]
    b_ee = [Buf("ee0"), Buf("ee1")]
    b_sq2 = [Buf("sq0"), Buf("sq1")]
    b_rs2 = [Buf("rs0"), Buf("rs1")]
    b_yn = [Buf("yn0"), Buf("yn1")]
    b_tk = [Buf("tk0"), Buf("tk1")]
    sc.dma(blk1[:], dnc[3], writes=[b_blk1])
    for ci in range(9):
        sc.dma(cw[:, ci, :], conv_w[:, ci * 128:(ci + 1) * 128].rearrange("j c -> c j"), writes=[b_cw],
               allow_slow_non_contiguous=True)
    units = []
    cnt = [0]

    def make(g0, ci):
        def unit():
            tok0 = g0 * 128
            b = cnt[0] % 2
            cnt[0] += 1
            sq, rs, b_sq, b_rs = sq2[b], rs2[b], b_sq2[b], b_rs2[b]
            sc.dma(xin[b][:], ms["cpre"][ci * 128:(ci + 1) * 128, tok0:tok0 + GW + 4], writes=[b_xin[b]])
            sc.op("dve", lambda e: e.tensor_scalar_mul(out=y[b][:], in0=xin[b][:, 0:GW], scalar1=cw[:, ci, 0:1]),
                  reads=[b_xin[b], b_cw], writes=[b_y[b]])
            for j in range(1, 5):
                sc.op("dve", lambda e: e.scalar_tensor_tensor(out=y[b][:], in0=xin[b][:, j:j + GW],
                                                              scalar=cw[:, ci, j:j + 1], in1=y[b][:],
                                                              op0=ALU.mult, op1=ALU.add),
                      reads=[b_xin[b], b_cw, b_y[b]], writes=[b_y[b]])
            sc.op("act", lambda e: e.activation(out=ee[b][:], in_=y[b][:], func=AF.Exp, scale=-1.0),
                  reads=[b_y[b]], writes=[b_ee[b]])
            sc.op("dve", lambda e: e.tensor_scalar_add(out=ee[b][:], in0=ee[b][:], scalar1=1.0),
                  reads=[b_ee[b]], writes=[b_ee[b]])
            sc.op("dve", lambda e: e.reciprocal(out=ee[b][:], in_=ee[b][:]), reads=[b_ee[b]], writes=[b_ee[b]])
            sc.op("dve", lambda e: e.tensor_tensor(out=y[b][:], in0=y[b][:], in1=ee[b][:], op=ALU.mult),
                  reads=[b_y[b], b_ee[b]], writes=[b_y[b]])
            if ci < 6:
                sc.op("dve", lambda e: e.tensor_tensor(out=sq[:], in0=y[b][:], in1=y[b][:], op=ALU.mult),
                      reads=[b_y[b]], writes=[b_sq])
                sc.op("pe", lambda e: e.matmul(ps[6][:, :GW], lhsT=blk1[:], rhs=sq[:], start=True, stop=True),
                      reads=[b_blk1, b_sq], writes=[psb[6]])
                mul = 64.0 if ci < 3 else 1.0
                sc.op("dve", lambda e: e.tensor_scalar(out=rs[:], in0=ps[6][:, :GW], scalar1=EPS, scalar2=mul,
                                                       op0=ALU.add, op1=ALU.mult),
                      reads=[psb[6]], writes=[b_rs])
                sc.op("act", lambda e: e.activation(out=rs[:], in_=rs[:], func=AF.Ln), reads=[b_rs], writes=[b_rs])
                sc.op("act", lambda e: e.activation(out=rs[:], in_=rs[:], func=AF.Exp, scale=-0.5),
                      reads=[b_rs], writes=[b_rs])
                sc.op("dve", lambda e: e.tensor_tensor(out=yn[b][:], in0=y[b][:], in1=rs[:], op=ALU.mult),
                      reads=[b_y[b], b_rs], writes=[b_yn[b]])
                src_t, src_b = yn[b], b_yn[b]
                if ci < 3:
                    sc.dma(ds["cqT"][ci * 128:(ci + 1) * 128, tok0:tok0 + GW], yn[b][:], reads=[b_yn[b]])
                else:
                    sc.dma(ds["ckT"][(ci - 3) * 128:(ci - 2) * 128, tok0:tok0 + GW], yn[b][:], reads=[b_yn[b]])
            else:
                src_t, src_b = y[b], b_y[b]
            if ci >= 3:
                for tt in range(GT):
                    sc.op("pe", lambda e: e.transpose(ps[7][:, tt * 128:(tt + 1) * 128],
                                                      src_t[:, tt * 128:(tt + 1) * 128], ident[:]),
                          reads=[src_b, ident_b], writes=[psb[7]], signal=(tt == GT - 1))
                sc.op("dve", lambda e: e.tensor_copy(out=tk[b][:], in_=ps[7][:, :GW]), reads=[psb[7]], writes=[b_tk[b]])
                dst = ds["ck"] if ci < 6 else ds["cv"]
                cc = (ci - 3) % 3
                sc.dma(dst[tok0:tok0 + GW, cc * 128:(cc + 1) * 128].rearrange("(t p) c -> p t c", p=128),
                       tk[b][:].rearrange("p (t c) -> p t c", c=128), reads=[b_tk[b]])
        return unit

    for g0 in range(0, NT, GT):
        for ci in range(9):
            units.append(make(g0, ci))
    return units


K.conv_units = _conv_units


def _win_units(self, st, tag, ms, wbias, sink):
    sc = self.sc
    S, NT = self.S, self.NT
    NKA = NT + 1 if self.paired else NT
    ps, psb = self.ps, self.psb
    qT = self.sb(st, tag + "qT", [128, S], BF16)
    kT = self.sb(st, tag + "kT", [128, NKA * 128], BF16)
    vA = self.sb(st, tag + "vA", [128, NKA, 65], BF16)
    wb = self.sb(st, tag + "wb", [128, 384], F32)
    sT = [self.sb(st, tag + "sT%d" % i, [128, 384], F32) for i in range(2)]
    pT = [self.sb(st, tag + "pT%d" % i, [128, 384], BF16) for i in range(2)]
    ow = self.sb(st, tag + "ow", [128, NT, 64], F32)
    ou = self.sb(st, tag + "ou", [128, NT, 65], F32)
    dn_ = self.sb(st, tag + "dn", [128, NT], F32)
    es = self.sb(st, tag + "es", [128, 6], F32)
    b_qT, b_kT, b_vA, b_wb, b_ow, b_es, b_ou, b_dn = (Buf(n) for n in ("qT", "kT", "vA", "wb", "ow", "es", "ou", "dn"))
    b_sT = [Buf("sT0"), Buf("sT1")]
    b_pT = [Buf("pT0"), Buf("pT1")]
    units = []
    cnt = [0]

    def setup0():
        sc.op("dve", lambda e: e.memset(qT[:], 0.0), writes=[b_qT])
        sc.op("dve", lambda e: e.memset(kT[:], 0.0), writes=[b_kT])
        sc.dma(es[:], sink.partition_broadcast(128), writes=[b_es])
        sc.op("act", lambda e: e.activation(out=es[:], in_=es[:], func=AF.Exp), reads=[b_es], writes=[b_es])
    units.append(setup0)

    def mk_head(h):
        def f():
            kh = h // 3
            if h % 3 == 0:
                sc.dma(kT[0:64, :], ms["akT"][kh * 64:(kh + 1) * 64, :], writes=[b_kT])
                sc.dma(vA[:], ms["av"][:, kh, :].rearrange("(n p) d -> p n d", p=128), writes=[b_vA])
            sc.dma(qT[0:64, :], ms["aqT"][h * 64:(h + 1) * 64, :], writes=[b_qT])
            sc.dma(wb[:], wbias[h], writes=[b_wb])
        return f

    def mk_a(h, n):
        def f():
            js = [j for j in (0, 1, 2) if 0 <= n - 1 + j < NKA]
            c0 = js[0] * 128
            w = len(js) * 128
            sb_ = n % 2
            for jj, j in enumerate(js):
                kt = n - 1 + j
                sc.op("pe", lambda e: e.matmul(ps[0][:, jj * 128:(jj + 1) * 128], lhsT=kT[:, kt * 128:(kt + 1) * 128],
                                               rhs=qT[:, n * 128:(n + 1) * 128], start=True, stop=True),
                      reads=[b_kT, b_qT], writes=[psb[0]], signal=(jj == len(js) - 1))
            sc.op("dve", lambda e: e.tensor_tensor(out=sT[sb_][:, 0:w], in0=ps[0][:, 0:w], in1=wb[:, c0:c0 + w],
                                                   op=ALU.add),
                  reads=[psb[0], b_wb], writes=[b_sT[sb_]])
            sc.op("act", lambda e: e.activation(out=pT[sb_][:, 0:w], in_=sT[sb_][:, 0:w], func=AF.Exp),
                  reads=[b_sT[sb_]], writes=[b_pT[sb_]])
        return f

    def mk_b(h, n):
        def f():
            js = [j for j in (0, 1, 2) if 0 <= n - 1 + j < NKA]
            sb_ = n % 2
            for jj, j in enumerate(js):
                kt = n - 1 + j
                sc.op("pe", lambda e: e.matmul(ps[1][:, 0:65], lhsT=pT[sb_][:, jj * 128:(jj + 1) * 128],
                                               rhs=vA[:, kt, :], start=(jj == 0), stop=(jj == len(js) - 1)),
                      reads=[b_pT[sb_], b_vA], writes=[psb[1]], signal=(jj == len(js) - 1))
            sc.op("dve", lambda e: e.tensor_copy(out=ou[:, n, :], in_=ps[1][:, 0:65]), reads=[psb[1]], writes=[b_ou])
        return f

    def both(fb, fa):
        def f():
            fb()
            fa()
        return f

    def mk_fin(h):
        def f():
            sc.op("dve", lambda e: e.tensor_scalar(out=dn_[:], in0=ou[:, :, 64], scalar1=es[:, h:h + 1], scalar2=None,
                                                   op0=ALU.add), reads=[b_ou, b_es], writes=[b_dn])
            sc.op("dve", lambda e: e.reciprocal(out=dn_[:], in_=dn_[:]), reads=[b_dn], writes=[b_dn])
            sc.op("dve", lambda e: e.tensor_tensor(out=ow[:], in0=ou[:, :, 0:64],
                                                   in1=dn_[:].unsqueeze(2).to_broadcast([128, NT, 64]), op=ALU.mult),
                  reads=[b_ou, b_dn], writes=[b_ow])
            sc.dma(ms["omix"][:, h * 64:(h + 1) * 64].rearrange("(n p) d -> p n d", p=128), ow[:], reads=[b_ow])
        return f

    for h in range(6):
        units.append(mk_head(h))
        units.append(mk_a(h, 0))
        for n in range(1, NT):
            units.append(both(mk_b(h, n - 1), mk_a(h, n)))
        units.append(mk_b(h, NT - 1))
        units.append(mk_fin(h))
    return units


K.win_units = _win_units

import numpy as np, ml_dtypes
BF = ml_dtypes.bfloat16
def diff_consts(S, SK=None):
    SK = SK or S
    pos = np.arange(SK)
    H = (pos // 128) * 128.0
    L = (pos % 128) * 1.0
    daq = np.zeros((4, 4, SK), np.float32); dakp = np.zeros((4, 4, SK), np.float32)
    dbd = np.zeros((4, 128, 128), np.float32)
    for h in range(4):
        s = 2.0 ** (-8.0 * (h + 1) / 4)
        daq[h, 0] = -s * H; daq[h, 1] = -s * L; daq[h, 2] = 1; daq[h, 3] = 1
        dakp[h, 0] = 1; dakp[h, 1] = 1; dakp[h, 2] = s * H; dakp[h, 3] = s * L
        kk = np.arange(128)[:, None]; qq = np.arange(128)[None, :]
        dbd[h] = -s * np.abs(qq - kk)
    daq = daq[:, :, :S]
    c = dict(daq=np.ascontiguousarray(daq).astype(BF), dakp=dakp.astype(BF), dakm=(-dakp).astype(BF), dbd=dbd.astype(BF),
             identb=np.eye(128, dtype=np.float32).astype(BF))
    assert np.array_equal(c["daq"].astype(np.float32), daq) and np.array_equal(c["dakp"].astype(np.float32), dakp)
    assert np.array_equal(c["dbd"].astype(np.float32), dbd)
    return c

def win_consts():
    wb = np.zeros((6, 128, 384), np.float32)
    k = np.arange(128)[:, None]; q = np.arange(128)[None, :]
    for h in range(6):
        s = np.float32(2.0) ** np.float32(-8.0 * (h + 1) / 6)
        for j in range(3):
            rel = (j - 1) * 128 + k - q
            b = np.where(np.abs(rel) <= 128, -np.float32(s) * np.abs(rel).astype(np.float32), np.float32(-30000.0))
            wb[h, :, j * 128:(j + 1) * 128] = b
    return dict(wbias=wb)

def dn_consts():
    c = np.zeros((10, 128, 128), np.float32)
    p = np.arange(128)[:, None]; f = np.arange(128)[None, :]
    c[0] = 1.0
    c[1] = (p <= f)
    c[2] = (p >= f)
    c[3] = ((p // 64) == (f // 64))
    BIG = 30000.0
    c[4] = np.where(p > f, 0.0, -BIG)
    c[5] = np.where(f > p, 0.0, -BIG)
    c[6] = np.where(f >= p, 0.0, -BIG)
    c[7] = np.where(p < f, 0.0, -BIG)
    c[8] = np.where(f < p, 0.0, -BIG)
    c[9] = np.where(f <= p, 0.0, -BIG)
    return dict(dnc=c)


_CACHE = {}
N_CORES = 8


def _get_built(S_loc):
    if S_loc not in _CACHE:
        _CACHE[S_loc] = build_full(S_loc, 2, paired=True)
    return _CACHE[S_loc]


def kernel(**inputs):
    from concourse.bass_utils import run_bass_kernel_spmd
    x = np.asarray(inputs["x"])
    B, S, _ = x.shape
    assert 2 * B == N_CORES
    k = _get_built(S // 2)
    in_maps = pair_feeds(inputs, S)
    res = run_bass_kernel_spmd(k.nc, in_maps, core_ids=list(range(N_CORES)))
    return pair_gather(res.results, B, S)
```

```python
import numpy as np
import concourse.bass as bass
import concourse.mybir as mybir

F32 = mybir.dt.float32
BF16 = mybir.dt.bfloat16
AF = mybir.ActivationFunctionType
ALU = mybir.AluOpType
AX = mybir.AxisListType


class Buf:
    __slots__ = ("name", "w", "rs", "excl")

    def __init__(self, name, excl=False):
        self.name = name
        self.excl = excl
        self.w = None
        self.rs = []


class Sched:
    NDMA = 24

    def __init__(self, nc, stack):
        self.nc = nc
        self.eng = {"pe": nc.tensor, "act": nc.scalar, "dve": nc.vector, "pool": nc.gpsimd, "sp": nc.sync}
        self.sem = {}
        self.cnt = {}
        for k in self.eng:
            self.sem[k] = stack.enter_context(nc.semaphore("sem_" + k))
            self.cnt[k] = 0
        self.dsem = [stack.enter_context(nc.semaphore("dsem%d" % i)) for i in range(self.NDMA)]
        self.dgen = [0] * self.NDMA
        self.qslots = {"sp": list(range(0, 16)), "pool": list(range(16, 20)), "act": list(range(20, 24))}
        self.qnext = {"sp": 0, "pool": 0, "act": 0}
        self.waited = {k: {} for k in self.eng}
        self.pending_pe = False
        self.n_ins = 0
        self.n_wait = 0

    def _semobj(self, key):
        if isinstance(key, int):
            return self.dsem[key]
        return self.sem[key]

    def _need(self, e, evs):
        best = {}
        for ev in evs:
            if ev is None:
                continue
            k, v = ev
            if k == e and e in ("pe", "sp"):
                continue
            if best.get(k, 0) < v:
                best[k] = v
        w = self.waited[e]
        for k, v in best.items():
            if w.get(k, 0) >= v:
                continue
            self.eng[e].wait_ge(self._semobj(k), v)
            self.n_wait += 1
            w[k] = v

    def _deps(self, reads, writes, e=None):
        evs = []
        for b in reads:
            evs.append(b.w)
            if b.excl:
                evs.extend(r for r in b.rs if r[0] != e)
        for b in writes:
            evs.append(b.w)
            evs.extend(b.rs)
        return evs

    def _commit(self, ev, reads, writes):
        for b in reads:
            b.rs.append(ev)
            if len(b.rs) > 64:
                mx = {}
                for k, v in b.rs:
                    if mx.get(k, 0) < v:
                        mx[k] = v
                b.rs = list(mx.items())
        for b in writes:
            b.w = ev
            b.rs = []

    def op(self, e, fn, reads=(), writes=(), signal=True):
        if e != "pe":
            assert not self.pending_pe, "non-signaling PE op must be followed by signaling PE op"
        self._need(e, self._deps(reads, writes, e))
        ins = fn(self.eng[e])
        self.n_ins += 1
        if e == "pe":
            self.pending_pe = not signal
        if signal:
            self.cnt[e] += 1
            ins.then_inc(self.sem[e], 1)
            ev = (e, self.cnt[e])
        else:
            ev = (e, self.cnt[e] + 1)
        self._commit(ev, reads, writes)
        return ins

    def dma(self, out, in_, reads=(), writes=(), q="sp", **kw):
        assert not self.pending_pe
        sl = self.qslots[q]
        slot = sl[self.qnext[q] % len(sl)]
        self.qnext[q] += 1
        evs = self._deps(reads, writes)
        if self.dgen[slot] > 0:
            evs.append((slot, 16 * self.dgen[slot]))
        self._need(q, evs)
        self.dgen[slot] += 1
        ins = self.eng[q].dma_start(out=out, in_=in_, **kw)
        ins.then_inc(self.dsem[slot], 16)
        self.n_ins += 1
        ev = (slot, 16 * self.dgen[slot])
        self._commit(ev, reads, writes)
        return ins

    def barrier(self):
        evs = [(k, self.cnt[k]) for k in self.eng if self.cnt[k] > 0]
        evs += [(i, 16 * self.dgen[i]) for i in range(self.NDMA) if self.dgen[i] > 0]
        for e in self.eng:
            w = self.waited[e]
            for k, v in evs:
                if k == e and e in ("pe", "sp"):
                    continue
                if w.get(k, 0) >= v:
                    continue
                self.eng[e].wait_ge(self._semobj(k), v)
                w[k] = v

    def collective(self, stack, kind, in_ap, out_ap, groups):
        import concourse.mybir as mybir
        self.barrier()
        sem = stack.enter_context(self.nc.semaphore("ccsem%d" % self.n_ins))
        g = self.eng["pool"]
        g.collective_compute(kind, mybir.AluOpType.bypass, replica_groups=groups,
                             ins=[in_ap], outs=[out_ap]).then_inc(sem)
        g.wait_ge(sem, 1)
        self.n_ins += 1
        self.cnt["pool"] += 1
        g.engine_nop().then_inc(self.sem["pool"], 1)
        self.barrier()

    def finish(self, out_bufs):
        self.barrier()

import numpy as np
from contextlib import ExitStack
import concourse.bass as bass
import concourse.mybir as mybir

D = 1024
DFF = 2752
NFC = 22
EPS = 1e-6


class K:
    def __init__(self, S, depth=2, paired=False):
        self.S = S
        self.paired = paired
        self.SK = 2 * S if paired else S
        self.NTK = self.SK // 128
        self.NT = S // 128
        self.depth = depth
        self.nc = bass.Bass("TRN2", target_bir_lowering=False)
        self.stack = ExitStack()
        self.sc = Sched(self.nc, self.stack)
        self.ins = {}
        nc = self.nc
        self.ps = []
        self.psb = []
        for i in range(8):
            t = self.stack.enter_context(nc.psum_tensor("ps%d" % i, [128, 512], F32))
            self.ps.append(t)
            self.psb.append(Buf("ps%d" % i, excl=True))

    def inp(self, name, shape, dt=F32):
        t = self.nc.dram_tensor(name, list(shape), dt, kind="ExternalInput").ap()
        self.ins[name] = t
        return t

    def outp(self, name, shape, dt=F32):
        return self.nc.dram_tensor(name, list(shape), dt, kind="ExternalOutput").ap()

    def scratch(self, name, shape, dt=F32):
        return self.nc.dram_tensor(name, list(shape), dt, kind="Internal").ap()

    def sb(self, st, name, shape, dt=F32):
        return st.enter_context(self.nc.sbuf_tensor(name, list(shape), dt))

    def ffn_phase(self, tag, w_in, w_out, g, src, src_bufs, dst, dst_bufs, ident, ident_b):
        sc = self.sc
        S, NT = self.S, self.NT
        GT = 4 if NT % 4 == 0 else 1
        GW = GT * 128
        NG = NT // GT
        with ExitStack() as st:
            w1 = self.sb(st, tag + "w1", [128, 8, 2 * DFF], BF16)
            w2 = self.sb(st, tag + "w2", [128, NFC, D], BF16)
            gB = self.sb(st, tag + "gB", [128, D], F32)
            xt = [self.sb(st, tag + "xt%d" % i, [128, D], F32) for i in range(2)]
            xr = [self.sb(st, tag + "xr%d" % i, [128, D], F32) for i in range(2)]
            xn = self.sb(st, tag + "xn", [128, D], F32)
            xnT = [self.sb(st, tag + "xnT%d" % i, [128, 8, GW], BF16) for i in range(2)]
            hT = self.sb(st, tag + "hT", [128, NFC, GW], BF16)
            sg = [self.sb(st, tag + "sg%d" % i, [128, GW], F32) for i in range(2)]
            stat = self.sb(st, tag + "stat", [128, 8], F32)
            b_w1 = [Buf("w1_%d" % k) for k in range(8)]
            b_w2 = [Buf("w2_%d" % c) for c in range(NFC)]
            b_gB = Buf("gB")
            b_xt = [Buf("xt0"), Buf("xt1")]
            b_xr = [Buf("xr0"), Buf("xr1")]
            b_xn, b_stat = Buf("xn"), Buf("stat")
            b_xnT = [Buf("xnT0"), Buf("xnT1")]
            b_hT = [Buf("hT%d" % c) for c in range(NFC)]
            b_sg = [Buf("sg0"), Buf("sg1")]
            ps, psb = self.ps, self.psb
            for kc in range(8):
                for hf in range(2):
                    sc.dma(w1[:, kc, hf * DFF:(hf + 1) * DFF], w_in[kc * 128:(kc + 1) * 128, hf * DFF:(hf + 1) * DFF],
                           writes=[b_w1[kc]], q="pool")
            for c in range(NFC):
                cw = min(128, DFF - c * 128)
                sc.dma(w2[:cw, c, :], w_out[c * 128:c * 128 + cw, :], writes=[b_w2[c]], q="pool")
            sc.dma(gB[:], g.partition_broadcast(128), writes=[b_gB])
            tcount = [0]

            def ln_tile(gi, tt):
                t = gi * GT + tt
                xb = tcount[0] % 2
                tcount[0] += 1
                xq = xnT[gi % 2]
                bq = b_xnT[gi % 2]
                sc.dma(xt[xb][:], src[t * 128:(t + 1) * 128, :], reads=[src_bufs[t]], writes=[b_xt[xb]])
                sc.op("act", lambda e: e.activation(out=xn[:], in_=xt[xb][:], func=AF.Square, accum_out=stat[:, 0:1]),
                      reads=[b_xt[xb]], writes=[b_xn, b_stat])
                sc.op("dve", lambda e: e.tensor_scalar(out=stat[:, 1:2], in0=stat[:, 0:1], scalar1=1.0 / D,
                                                       scalar2=EPS, op0=ALU.mult, op1=ALU.add),
                      reads=[b_stat], writes=[b_stat])
                sc.op("act", lambda e: e.sqrt(out=stat[:, 3:4], in_=stat[:, 1:2]), reads=[b_stat], writes=[b_stat])
                sc.op("dve", lambda e: e.reciprocal(out=stat[:, 2:3], in_=stat[:, 3:4]), reads=[b_stat], writes=[b_stat])
                sc.op("dve", lambda e: e.scalar_tensor_tensor(out=xn[:], in0=xt[xb][:], scalar=stat[:, 2:3],
                                                              in1=gB[:], op0=ALU.mult, op1=ALU.mult),
                      reads=[b_xt[xb], b_stat, b_gB], writes=[b_xn])
                for kc in range(8):
                    bank = 6 + kc // 4
                    sc.op("pe", lambda e: e.transpose(ps[bank][:, (kc % 4) * 128:(kc % 4 + 1) * 128],
                                                      xn[:, kc * 128:(kc + 1) * 128], ident[:]),
                          reads=[b_xn, ident_b], writes=[psb[bank]], signal=(kc % 4 == 3))
                sc.op("act", lambda e: e.copy(out=xq[:, 0:4, tt * 128:(tt + 1) * 128],
                                              in_=ps[6][:, :].rearrange("p (k t) -> p k t", k=4)),
                      reads=[psb[6]], writes=[bq])
                sc.op("dve", lambda e: e.tensor_copy(out=xq[:, 4:8, tt * 128:(tt + 1) * 128],
                                                     in_=ps[7][:, :].rearrange("p (k t) -> p k t", k=4)),
                      reads=[psb[7]], writes=[bq])

            for tt in range(GT):
                ln_tile(0, tt)
            for gi in range(NG):
                g0 = gi * GT
                xq = xnT[gi % 2]
                bq = b_xnT[gi % 2]
                for c in range(NFC):
                    cw = min(128, DFF - c * 128)
                    pg, pu = 2 + 2 * (c % 2), 3 + 2 * (c % 2)
                    for kc in range(8):
                        sc.op("pe", lambda e: e.matmul(ps[pg][:cw, :GW], lhsT=w1[:, kc, c * 128:c * 128 + cw],
                                                       rhs=xq[:, kc, :], start=(kc == 0), stop=(kc == 7)),
                              reads=[b_w1[kc], bq], writes=[psb[pg]], signal=(kc == 7))
                    for kc in range(8):
                        sc.op("pe", lambda e: e.matmul(ps[pu][:cw, :GW],
                                                       lhsT=w1[:, kc, DFF + c * 128:DFF + c * 128 + cw],
                                                       rhs=xq[:, kc, :], start=(kc == 0), stop=(kc == 7)),
                              reads=[b_w1[kc], bq], writes=[psb[pu]], signal=(kc == 7))
                    sc.op("act", lambda e: e.activation(out=sg[c % 2][:cw, :], in_=ps[pg][:cw, :GW], func=AF.Silu),
                          reads=[psb[pg]], writes=[b_sg[c % 2]])
                    sc.op("dve", lambda e: e.tensor_tensor(out=hT[:cw, c, :], in0=ps[pu][:cw, :GW],
                                                           in1=sg[c % 2][:cw, :], op=ALU.mult),
                          reads=[psb[pu], b_sg[c % 2]], writes=[b_hT[c]])
                for tt in range(GT):
                    t = g0 + tt
                    rb = t % 2
                    sc.dma(xr[rb][:], src[t * 128:(t + 1) * 128, :], reads=[src_bufs[t]], writes=[b_xr[rb]])
                    for hf in range(2):
                        for c in range(NFC):
                            cw = min(128, DFF - c * 128)
                            sc.op("pe", lambda e: e.matmul(ps[hf][:, :], lhsT=hT[:cw, c, tt * 128:(tt + 1) * 128],
                                                           rhs=w2[:cw, c, hf * 512:(hf + 1) * 512],
                                                           start=(c == 0), stop=(c == NFC - 1)),
                                  reads=[b_hT[c], b_w2[c]], writes=[psb[hf]], signal=(c == NFC - 1))
                    if gi + 1 < NG:
                        ln_tile(gi + 1, tt)
                    for hf in range(2):
                        sc.op("dve", lambda e: e.scalar_tensor_tensor(
                            out=xr[rb][:, hf * 512:(hf + 1) * 512], in0=ps[hf][:, :], scalar=0.5,
                            in1=xr[rb][:, hf * 512:(hf + 1) * 512], op0=ALU.mult, op1=ALU.add),
                              reads=[psb[hf], b_xr[rb]], writes=[b_xr[rb]])
                    sc.dma(dst[t * 128:(t + 1) * 128, :], xr[rb][:], reads=[b_xr[rb]], writes=[dst_bufs[t]])
            sc.barrier()


HD = 64
MIX_IN = 2968
C_AQ, C_AK, C_AV = 0, 384, 512
C_BQ, C_BK, C_BV = 640, 896, 1152
C_CQKV, C_CZ, C_CB, C_CA = 1408, 2560, 2944, 2956
FM_CHUNKS = ([(C_AQ + 128 * i, "aq", i) for i in range(3)] + [(C_AK, "ak", 0)] +
             [(C_BQ + 128 * i, "bq", i) for i in range(2)] + [(C_BK + 128 * i, "bk", i) for i in range(2)] +
             [(C_CQKV + 128 * i, "c", i) for i in range(9)])


def _mix_scratch(self):
    S = self.S
    SK = self.SK
    SA = S + 128 if self.paired else S
    d = {}
    d["aqT"] = self.scratch("aqT", [384, S], BF16)
    d["akT"] = self.scratch("akT", [128, SA], BF16)
    d["av"] = self.scratch("av", [SA, 2, 65], BF16)
    d["bqT"] = self.scratch("bqT", [256, S], BF16)
    d["bkT"] = self.scratch("bkT", [256, SK], BF16)
    d["bv"] = self.scratch("bv", [SK, 4, 65], BF16)
    d["cpre"] = self.scratch("cpre", [1152, S + 4], F32)
    d["cz"] = self.scratch("cz", [S, 408], F32)
    d["omix"] = self.scratch("omix", [S, D], F32)
    return d


K.mix_scratch = _mix_scratch


def _inproj_phase(self, tag, w_mi, g, src, src_bufs, ms, ident, ident_b):
    sc = self.sc
    S, NT = self.S, self.NT
    GT = 4 if NT % 4 == 0 else 1
    GW = GT * 128
    ps, psb = self.ps, self.psb
    with ExitStack() as st:
        wm = self.sb(st, tag + "wm", [128, 8, MIX_IN], BF16)
        gB = self.sb(st, tag + "gB", [128, D], F32)
        xt = [self.sb(st, tag + "xt%d" % i, [128, D], F32) for i in range(2)]
        xn = self.sb(st, tag + "xn", [128, D], F32)
        junk = self.sb(st, tag + "junk", [128, D], F32)
        xnT = self.sb(st, tag + "xnT", [128, 8, GW], BF16)
        stat = self.sb(st, tag + "stat", [128, 8], F32)
        ob16 = [self.sb(st, tag + "ob16_%d" % i, [128, GW], BF16) for i in range(3)]
        of32 = [self.sb(st, tag + "of32_%d" % i, [128, GW], F32) for i in range(3)]
        tv = [self.sb(st, tag + "tv%d" % i, [128, 6, 65], BF16) for i in range(2)]
        zt = self.sb(st, tag + "zt", [128, 2], F32)
        tz = [self.sb(st, tag + "tz%d" % i, [128, 408], F32) for i in range(2)]
        b_wm = [Buf("wm%d" % k) for k in range(8)]
        b_gB = Buf("gB")
        b_xt = [Buf("xt0"), Buf("xt1")]
        b_xn, b_junk, b_xnT, b_stat = Buf("xn"), Buf("junk"), Buf("xnT"), Buf("stat")
        b_ob16 = [Buf("ob16_%d" % i) for i in range(3)]
        b_of32 = [Buf("of32_%d" % i) for i in range(3)]
        b_tv = [Buf("tv0"), Buf("tv1")]
        b_tz = [Buf("tz0"), Buf("tz1")]
        for kc in range(8):
            sc.dma(wm[:, kc, :], w_mi[kc * 128:(kc + 1) * 128, :], writes=[b_wm[kc]], q="pool")
        sc.dma(gB[:], g.partition_broadcast(128), writes=[b_gB])
        b_zt = Buf("zt")
        sc.op("pool", lambda e: e.memset(zt[:], 0.0), writes=[b_zt])
        for i in range(9):
            sc.dma(ms["cpre"][i * 128:(i + 1) * 128, 0:2], zt[:, :], reads=[b_zt])
            if not self.paired:
                sc.dma(ms["cpre"][i * 128:(i + 1) * 128, S + 2:S + 4], zt[:, :], reads=[b_zt])
        for i in range(2):
            sc.op("pool", lambda e: e.memset(tv[i][:], 1.0), writes=[b_tv[i]])
        ti = 0
        n16 = n32 = 0
        for g0 in range(0, NT, GT):
            for tt in range(GT):
                t = g0 + tt
                xb = ti % 2
                ti += 1
                sc.dma(xt[xb][:], src[t * 128:(t + 1) * 128, :], reads=[src_bufs[t]], writes=[b_xt[xb]])
                sc.op("act", lambda e: e.activation(out=junk[:], in_=xt[xb][:], func=AF.Square,
                                                    accum_out=stat[:, 0:1]),
                      reads=[b_xt[xb]], writes=[b_junk, b_stat])
                sc.op("dve", lambda e: e.tensor_scalar(out=stat[:, 1:2], in0=stat[:, 0:1], scalar1=1.0 / D,
                                                       scalar2=EPS, op0=ALU.mult, op1=ALU.add),
                      reads=[b_stat], writes=[b_stat])
                sc.op("act", lambda e: e.sqrt(out=stat[:, 3:4], in_=stat[:, 1:2]), reads=[b_stat], writes=[b_stat])
                sc.op("dve", lambda e: e.reciprocal(out=stat[:, 2:3], in_=stat[:, 3:4]),
                      reads=[b_stat], writes=[b_stat])
                sc.op("dve", lambda e: e.scalar_tensor_tensor(out=xn[:], in0=xt[xb][:], scalar=stat[:, 2:3],
                                                              in1=gB[:], op0=ALU.mult, op1=ALU.mult),
                      reads=[b_xt[xb], b_stat, b_gB], writes=[b_xn])
                for kc in range(8):
                    bank = kc // 4
                    sc.op("pe", lambda e: e.transpose(ps[bank][:, (kc % 4) * 128:(kc % 4 + 1) * 128],
                                                      xn[:, kc * 128:(kc + 1) * 128], ident[:]),
                          reads=[b_xn, ident_b], writes=[psb[bank]], signal=(kc % 4 == 3))
                sc.op("act", lambda e: e.copy(out=xnT[:, 0:4, tt * 128:(tt + 1) * 128],
                                              in_=ps[0][:, :].rearrange("p (k t) -> p k t", k=4)),
                      reads=[psb[0]], writes=[b_xnT])
                sc.op("dve", lambda e: e.tensor_copy(out=xnT[:, 4:8, tt * 128:(tt + 1) * 128],
                                                     in_=ps[1][:, :].rearrange("p (k t) -> p k t", k=4)),
                      reads=[psb[1]], writes=[b_xnT])
            tok0 = g0 * 128
            for ci, (c0, kind, idx) in enumerate(FM_CHUNKS):
                pb = 2 + ci % 3
                for kc in range(8):
                    sc.op("pe", lambda e: e.matmul(ps[pb][:, :GW], lhsT=wm[:, kc, c0:c0 + 128], rhs=xnT[:, kc, :],
                                                   start=(kc == 0), stop=(kc == 7)),
                          reads=[b_wm[kc], b_xnT], writes=[psb[pb]], signal=(kc == 7))
                if kind == "c":
                    o = n32 % 3
                    n32 += 1
                    sc.op("dve" if ci % 2 else "act",
                          (lambda e: e.tensor_copy(out=of32[o][:, :], in_=ps[pb][:, :GW])) if ci % 2 else
                          (lambda e: e.copy(out=of32[o][:, :], in_=ps[pb][:, :GW])),
                          reads=[psb[pb]], writes=[b_of32[o]])
                    sc.dma(ms["cpre"][idx * 128:(idx + 1) * 128, 2 + tok0:2 + tok0 + GW], of32[o][:, :],
                           reads=[b_of32[o]])
                else:
                    o = n16 % 3
                    n16 += 1
                    scale = {"aq": HD ** -0.5, "bq": 32 ** -0.5, "ak": 1.0, "bk": 1.0}[kind]
                    sc.op("act", lambda e: e.mul(out=ob16[o][:, :], in_=ps[pb][:, :GW], mul=scale),
                          reads=[psb[pb]], writes=[b_ob16[o]])
                    dst = {"aq": ms["aqT"], "ak": ms["akT"], "bq": ms["bqT"], "bk": ms["bkT"]}[kind]
                    sc.dma(dst[idx * 128:(idx + 1) * 128, tok0:tok0 + GW], ob16[o][:, :], reads=[b_ob16[o]])
            for tt in range(GT):
                t = g0 + tt
                r0 = t * 128
                o = t % 2
                for (pb, c0, cw) in ((5, C_AV, 128), (6, C_BV, 256), (7, C_CZ, 408)):
                    for kc in range(8):
                        sc.op("pe", lambda e: e.matmul(ps[pb][:, :cw], lhsT=xnT[:, kc, tt * 128:(tt + 1) * 128],
                                                       rhs=wm[:, kc, c0:c0 + cw], start=(kc == 0), stop=(kc == 7)),
                              reads=[b_wm[kc], b_xnT], writes=[psb[pb]], signal=(kc == 7))
                sc.op("act", lambda e: e.copy(out=tv[o][:, 0:2, 0:64],
                                              in_=ps[5][:, 0:128].rearrange("p (h d) -> p h d", h=2)),
                      reads=[psb[5]], writes=[b_tv[o]])
                sc.op("dve", lambda e: e.tensor_copy(out=tv[o][:, 2:6, 0:64],
                                                     in_=ps[6][:, 0:256].rearrange("p (h d) -> p h d", h=4)),
                      reads=[psb[6]], writes=[b_tv[o]])
                sc.op("act", lambda e: e.copy(out=tz[o][:, :], in_=ps[7][:, 0:408]),
                      reads=[psb[7]], writes=[b_tz[o]])
                sc.dma(ms["av"][r0:r0 + 128, :, :], tv[o][:, 0:2, :], reads=[b_tv[o]])
                sc.dma(ms["bv"][r0:r0 + 128, :, :], tv[o][:, 2:6, :], reads=[b_tv[o]])
                sc.dma(ms["cz"][r0:r0 + 128, :], tz[o][:, :], reads=[b_tz[o]])
        sc.barrier()


K.inproj_phase = _inproj_phase


def _diffattn_phase(self, tag, ms, cst, dlam, dg, lam_init, identf, b_identf, hook=None):
    sc = self.sc
    S, NT = self.S, self.NT
    SK, NTK = self.SK, self.NTK
    GT = 4 if NT % 4 == 0 else 1
    GW = GT * 128
    NG = NT // GT
    ps, psb = self.ps, self.psb
    with ExitStack() as st:
        kTa = self.sb(st, tag + "kTa", [128, SK], BF16)
        kTb = self.sb(st, tag + "kTb", [128, SK], BF16)
        qTa = self.sb(st, tag + "qTa", [128, S], BF16)
        vA = self.sb(st, tag + "vA", [128, NTK, 65], BF16)
        bd = self.sb(st, tag + "bd", [128, 128], BF16)
        idb = self.sb(st, tag + "idb", [128, 128], BF16)
        oT = self.sb(st, tag + "oT", [65, GW], F32)
        b_oT = Buf("oT")
        pT = [self.sb(st, tag + "pT%d" % i, [128, GW], BF16) for i in range(3)]
        om = [self.sb(st, tag + "om%d" % i, [128, NT, 64], F32) for i in range(2)]
        dd = self.sb(st, tag + "dd", [128, NT, 64], F32)
        sq = self.sb(st, tag + "sq", [128, NT, 64], F32)
        ssq = self.sb(st, tag + "ssq", [128, NT], F32)
        rec = self.sb(st, tag + "rec", [128, 8], F32)
        lmb = self.sb(st, tag + "lmb", [128, 128], F32)
        lw = self.sb(st, tag + "lw", [128, 64], F32)
        ls = self.sb(st, tag + "ls", [128, 8], F32)
        gd = self.sb(st, tag + "gd", [128, 64], F32)
        b_kTa, b_kTb, b_qTa, b_vA, b_bd, b_idb = (Buf(n) for n in ("kTa", "kTb", "qTa", "vA", "bd", "idb"))
        b_pT = [Buf("pT%d" % i) for i in range(3)]
        b_om = [Buf("om0"), Buf("om1")]
        b_dd, b_sq, b_ssq, b_rec, b_lmb, b_lw, b_ls, b_gd = (Buf(n) for n in
                                                             ("dd", "sq", "ssq", "rec", "lmb", "lw", "ls", "gd"))
        sc.dma(idb[:], cst["identb"][:, :], writes=[b_idb])
        sc.op("dve", lambda e: e.memset(kTa[:], 0.0), writes=[b_kTa])
        sc.op("dve", lambda e: e.memset(kTb[:], 0.0), writes=[b_kTb])
        sc.op("dve", lambda e: e.memset(qTa[:], 0.0), writes=[b_qTa])
        sc.dma(lmb[:], dlam.rearrange("a b -> (a b)").partition_broadcast(128), writes=[b_lmb])
        sc.dma(gd[:], dg.partition_broadcast(128), writes=[b_gd])
        sc.op("dve", lambda e: e.tensor_tensor(out=lw[:, 0:32], in0=lmb[:, 0:32], in1=lmb[:, 32:64], op=ALU.mult),
              reads=[b_lmb], writes=[b_lw])
        sc.op("dve", lambda e: e.tensor_tensor(out=lw[:, 32:64], in0=lmb[:, 64:96], in1=lmb[:, 96:128], op=ALU.mult),
              reads=[b_lmb], writes=[b_lw])
        sc.op("dve", lambda e: e.reduce_sum(out=ls[:, 0:2], in_=lw[:, :].rearrange("p (a b) -> p a b", a=2),
                                            axis=AX.X), reads=[b_lw], writes=[b_ls])
        sc.op("act", lambda e: e.activation(out=ls[:, 2:4], in_=ls[:, 0:2], func=AF.Exp), reads=[b_ls], writes=[b_ls])
        sc.op("dve", lambda e: e.tensor_tensor(out=ls[:, 4:5], in0=ls[:, 3:4], in1=ls[:, 2:3], op=ALU.subtract),
              reads=[b_ls], writes=[b_ls])
        sc.op("dve", lambda e: e.tensor_scalar_add(out=ls[:, 5:6], in0=ls[:, 4:5], scalar1=-lam_init),
              reads=[b_ls], writes=[b_ls])
        sc.op("dve", lambda e: e.tensor_scalar_mul(out=gd[:], in0=gd[:], scalar1=1.0 - lam_init),
              reads=[b_gd], writes=[b_gd])
        blk = 0
        accn = 0
        pending = []
        for h in range(4):
            sc.dma(vA[:], ms["bv"][:, h, :].rearrange("(n p) d -> p n d", p=128), reads=[], writes=[b_vA])
            sc.dma(bd[:], cst["dbd"][h], writes=[b_bd])
            for m in range(2):
                r0 = (h * 2 + m) * 32
                sc.dma(kTa[0:32, :], ms["bkT"][r0:r0 + 32, :], writes=[b_kTa])
                sc.dma(kTa[32:36, :], cst["dakp"][h], writes=[b_kTa])
                sc.dma(kTb[0:32, :], ms["bkT"][r0:r0 + 32, :], writes=[b_kTb])
                sc.dma(kTb[32:36, :], cst["dakm"][h], writes=[b_kTb])
                sc.dma(qTa[0:32, :], ms["bqT"][r0:r0 + 32, :], writes=[b_qTa])
                sc.dma(qTa[32:36, :], cst["daq"][h], writes=[b_qTa])
                for qg in range(NG):
                    q0 = qg * GW
                    accb = 2 + accn % 2
                    accn += 1

                    def qk(kt, bank):
                        k0 = kt * 128
                        if kt < qg * GT:
                            sc.op("pe", lambda e: e.matmul(ps[bank][:, :GW], lhsT=kTa[:, k0:k0 + 128],
                                                           rhs=qTa[:, q0:q0 + GW], start=True, stop=True),
                                  reads=[b_kTa, b_qTa], writes=[psb[bank]])
                        elif kt >= (qg + 1) * GT:
                            sc.op("pe", lambda e: e.matmul(ps[bank][:, :GW], lhsT=kTb[:, k0:k0 + 128],
                                                           rhs=qTa[:, q0:q0 + GW], start=True, stop=True),
                                  reads=[b_kTb, b_qTa], writes=[psb[bank]])
                        else:
                            for i in range(GT):
                                qt = qg * GT + i
                                cs = slice(i * 128, (i + 1) * 128)
                                qs = slice(q0 + i * 128, q0 + (i + 1) * 128)
                                last = (i == GT - 1)
                                if kt < qt:
                                    sc.op("pe", lambda e: e.matmul(ps[bank][:, cs], lhsT=kTa[:, k0:k0 + 128],
                                                                   rhs=qTa[:, qs], start=True, stop=True),
                                          reads=[b_kTa, b_qTa], writes=[psb[bank]], signal=last)
                                elif kt > qt:
                                    sc.op("pe", lambda e: e.matmul(ps[bank][:, cs], lhsT=kTb[:, k0:k0 + 128],
                                                                   rhs=qTa[:, qs], start=True, stop=True),
                                          reads=[b_kTb, b_qTa], writes=[psb[bank]], signal=last)
                                else:
                                    sc.op("pe", lambda e: e.matmul(ps[bank][:, cs], lhsT=kTa[0:32, k0:k0 + 128],
                                                                   rhs=qTa[0:32, qs], start=True, stop=False),
                                          reads=[b_kTa, b_qTa], writes=[psb[bank]], signal=False)
                                    sc.op("pe", lambda e: e.matmul(ps[bank][:, cs], lhsT=idb[:, :], rhs=bd[:, :],
                                                                   start=False, stop=True),
                                          reads=[b_idb, b_bd], writes=[psb[bank]], signal=last)

                    def expv(kt, bank, pi):
                        sc.op("act", lambda e: e.activation(out=pT[pi][:, :], in_=ps[bank][:, :GW], func=AF.Exp),
                              reads=[psb[bank]], writes=[b_pT[pi]])
                        sc.op("pe", lambda e: e.matmul(ps[accb][0:65, :GW], lhsT=vA[:, kt, :], rhs=pT[pi][:, :],
                                                       start=(kt == 0), stop=(kt == NTK - 1)),
                              reads=[b_pT[pi], b_vA], writes=[psb[accb]])

                    for step in range(NTK + 1):
                        if step == 3 and pending:
                            pending.pop(0)()
                        if hook is not None and step in (8, 24, 40, 56):
                            hook()
                        if step < NTK:
                            qk(step, (blk + step) % 2)
                        if step >= 1:
                            expv(step - 1, (blk + step - 1) % 2, (blk + step - 1) % 3)
                    blk += NTK
                    def make_fin(accb=accb, qg=qg, m=m, accn=accn):
                        def fin():
                            sc.op("act", lambda e: e.copy(out=oT[:, :GW], in_=ps[accb][0:65, :GW]), reads=[psb[accb]], writes=[b_oT])
                            tb = 4 + (accn % 2)
                            for i in range(GT):
                                sc.op("pe", lambda e: e.transpose(ps[tb][:, i * 65:(i + 1) * 65], oT[:, i * 128:(i + 1) * 128],
                                                                  identf[0:65, 0:65]),
                                      reads=[b_oT, b_identf], writes=[psb[tb]], signal=(i == GT - 1))
                            tv = ps[tb][:, 0:GT * 65].rearrange("p (i d) -> p i d", d=65)
                            sc.op("dve", lambda e: e.reciprocal(out=rec[:, 0:GT], in_=tv[:, :, 64]),
                                  reads=[psb[tb]], writes=[b_rec])
                            sc.op("dve", lambda e: e.tensor_tensor(out=om[m][:, qg * GT:(qg + 1) * GT, :], in0=tv[:, :, 0:64],
                                                                   in1=rec[:, 0:GT].unsqueeze(2).to_broadcast([128, GT, 64]),
                                                                   op=ALU.mult),
                                  reads=[psb[tb], b_rec], writes=[b_om[m]])

                        return fin
                    pending.append(make_fin())
            while pending:
                pending.pop(0)()
            sc.op("dve", lambda e: e.scalar_tensor_tensor(out=dd[:], in0=om[1][:], scalar=ls[:, 5:6], in1=om[0][:],
                                                          op0=ALU.mult, op1=ALU.add),
                  reads=[b_om[0], b_om[1], b_ls], writes=[b_dd])
            sc.op("pool", lambda e: e.tensor_tensor(out=sq[:], in0=dd[:], in1=dd[:], op=ALU.mult),
                  reads=[b_dd], writes=[b_sq])
            sc.op("dve", lambda e: e.reduce_sum(out=ssq[:], in_=sq[:], axis=AX.X), reads=[b_sq], writes=[b_ssq])
            sc.op("dve", lambda e: e.tensor_scalar(out=ssq[:], in0=ssq[:], scalar1=1.0 / 64, scalar2=EPS,
                                                   op0=ALU.mult, op1=ALU.add), reads=[b_ssq], writes=[b_ssq])
            sc.op("act", lambda e: e.sqrt(out=ssq[:], in_=ssq[:]), reads=[b_ssq], writes=[b_ssq])
            sc.op("dve", lambda e: e.reciprocal(out=ssq[:], in_=ssq[:]), reads=[b_ssq], writes=[b_ssq])
            sc.op("dve", lambda e: e.tensor_tensor(out=dd[:], in0=dd[:],
                                                   in1=ssq[:].unsqueeze(2).to_broadcast([128, NT, 64]), op=ALU.mult),
                  reads=[b_dd, b_ssq], writes=[b_dd])
            sc.op("dve", lambda e: e.tensor_tensor(out=dd[:], in0=dd[:],
                                                   in1=gd[:].unsqueeze(1).to_broadcast([128, NT, 64]), op=ALU.mult),
                  reads=[b_dd, b_gd], writes=[b_dd])
            sc.dma(ms["omix"][:, 384 + h * 64:384 + (h + 1) * 64].rearrange("(n p) d -> p n d", p=128), dd[:],
                   reads=[b_dd])
        if hook is not None:
            while hook():
                pass
        sc.barrier()


K.diffattn_phase = _diffattn_phase


def _winattn_phase(self, tag, ms, wbias, sink):
    sc = self.sc
    S, NT = self.S, self.NT
    NKA = NT + 1 if self.paired else NT
    ps, psb = self.ps, self.psb
    with ExitStack() as st:
        qT = self.sb(st, tag + "qT", [128, S], BF16)
        kT = self.sb(st, tag + "kT", [128, NKA * 128], BF16)
        vA = self.sb(st, tag + "vA", [128, NKA, 65], BF16)
        wb = self.sb(st, tag + "wb", [128, 384], F32)
        sT = [self.sb(st, tag + "sT%d" % i, [128, 384], F32) for i in range(2)]
        pT = [self.sb(st, tag + "pT%d" % i, [128, 384], BF16) for i in range(2)]
        ow = self.sb(st, tag + "ow", [128, NT, 64], F32)
        ou = self.sb(st, tag + "ou", [128, NT, 65], F32)
        dn_ = self.sb(st, tag + "dn", [128, NT], F32)
        b_ou, b_dn = Buf("ou"), Buf("dn")
        es = self.sb(st, tag + "es", [128, 6], F32)
        rec = self.sb(st, tag + "rec", [128, 4], F32)
        b_qT, b_kT, b_vA, b_wb, b_ow, b_es, b_rec = (Buf(n) for n in ("qT", "kT", "vA", "wb", "ow", "es", "rec"))
        b_sT = [Buf("sT0"), Buf("sT1")]
        b_pT = [Buf("pT0"), Buf("pT1")]
        sc.op("dve", lambda e: e.memset(qT[:], 0.0), writes=[b_qT])
        sc.op("dve", lambda e: e.memset(kT[:], 0.0), writes=[b_kT])
        sc.dma(es[:], sink.partition_broadcast(128), writes=[b_es])
        sc.op("act", lambda e: e.activation(out=es[:], in_=es[:], func=AF.Exp), reads=[b_es], writes=[b_es])
        cnt = 0
        for h in range(6):
            kh = h // 3
            if h % 3 == 0:
                sc.dma(kT[0:64, :], ms["akT"][kh * 64:(kh + 1) * 64, :], writes=[b_kT])
                sc.dma(vA[:], ms["av"][:, kh, :].rearrange("(n p) d -> p n d", p=128), writes=[b_vA])
            sc.dma(qT[0:64, :], ms["aqT"][h * 64:(h + 1) * 64, :], writes=[b_qT])
            sc.dma(wb[:], wbias[h], writes=[b_wb])
            for n in range(NT):
                js = [j for j in (0, 1, 2) if 0 <= n - 1 + j < NKA]
                c0 = js[0] * 128
                w = len(js) * 128
                sbk = cnt % 2
                acc = 2 + cnt % 6
                cnt += 1
                for jj, j in enumerate(js):
                    kt = n - 1 + j
                    sc.op("pe", lambda e: e.matmul(ps[sbk][:, jj * 128:(jj + 1) * 128], lhsT=kT[:, kt * 128:(kt + 1) * 128],
                                                   rhs=qT[:, n * 128:(n + 1) * 128], start=True, stop=True),
                          reads=[b_kT, b_qT], writes=[psb[sbk]], signal=(jj == len(js) - 1))
                sc.op("dve", lambda e: e.tensor_tensor(out=sT[sbk][:, 0:w], in0=ps[sbk][:, 0:w], in1=wb[:, c0:c0 + w],
                                                       op=ALU.add),
                      reads=[psb[sbk], b_wb], writes=[b_sT[sbk]])
                sc.op("act", lambda e: e.activation(out=pT[sbk][:, 0:w], in_=sT[sbk][:, 0:w], func=AF.Exp),
                      reads=[b_sT[sbk]], writes=[b_pT[sbk]])
                for jj, j in enumerate(js):
                    kt = n - 1 + j
                    sc.op("pe", lambda e: e.matmul(ps[acc][:, 0:65], lhsT=pT[sbk][:, jj * 128:(jj + 1) * 128],
                                                   rhs=vA[:, kt, :], start=(jj == 0), stop=(jj == len(js) - 1)),
                          reads=[b_pT[sbk], b_vA], writes=[psb[acc]], signal=(jj == len(js) - 1))
                if n % 2 == 0:
                    sc.op("dve", lambda e: e.tensor_copy(out=ou[:, n, :], in_=ps[acc][:, 0:65]), reads=[psb[acc]], writes=[b_ou])
                else:
                    sc.op("act", lambda e: e.copy(out=ou[:, n, :], in_=ps[acc][:, 0:65]), reads=[psb[acc]], writes=[b_ou])
            sc.op("dve", lambda e: e.tensor_scalar(out=dn_[:], in0=ou[:, :, 64], scalar1=es[:, h:h + 1], scalar2=None,
                                                   op0=ALU.add), reads=[b_ou, b_es], writes=[b_dn])
            sc.op("dve", lambda e: e.reciprocal(out=dn_[:], in_=dn_[:]), reads=[b_dn], writes=[b_dn])
            sc.op("dve", lambda e: e.tensor_tensor(out=ow[:], in0=ou[:, :, 0:64],
                                                   in1=dn_[:].unsqueeze(2).to_broadcast([128, NT, 64]), op=ALU.mult),
                  reads=[b_ou, b_dn], writes=[b_ow])
            sc.dma(ms["omix"][:, h * 64:(h + 1) * 64].rearrange("(n p) d -> p n d", p=128), ow[:], reads=[b_ow])
        sc.barrier()


K.winattn_phase = _winattn_phase


def _dn_scratch(self):
    S = self.S
    d = {}
    d["cqT"] = self.scratch("cqT", [384, S], F32)
    d["ckT"] = self.scratch("ckT", [384, S], F32)
    d["ck"] = self.scratch("ck", [S, 384], F32)
    d["cv"] = self.scratch("cv", [S, 384], F32)
    d["of"] = self.scratch("of", [S, 384], F32)
    return d


K.dn_scratch = _dn_scratch


def _conv_phase(self, tag, ms, ds, conv_w, dnc, ident, ident_b):
    sc = self.sc
    S, NT = self.S, self.NT
    GT = 4 if NT % 4 == 0 else 1
    GW = GT * 128
    ps, psb = self.ps, self.psb
    with ExitStack() as st:
        blk1 = self.sb(st, tag + "blk1", [128, 128], F32)
        cw = self.sb(st, tag + "cw", [128, 9, 5], F32)
        xin = [self.sb(st, tag + "xin%d" % i, [128, GW + 4], F32) for i in range(2)]
        y = [self.sb(st, tag + "y%d" % i, [128, GW], F32) for i in range(2)]
        sq2 = [self.sb(st, tag + "sq%d" % i, [128, GW], F32) for i in range(2)]
        rs2 = [self.sb(st, tag + "rs%d" % i, [128, GW], F32) for i in range(2)]
        yn = [self.sb(st, tag + "yn%d" % i, [128, GW], F32) for i in range(2)]
        tk = [self.sb(st, tag + "tk%d" % i, [128, GW], F32) for i in range(2)]
        b_blk1, b_cw = Buf("blk1"), Buf("cw")
        b_sq2 = [Buf("sq0"), Buf("sq1")]
        b_rs2 = [Buf("rs0"), Buf("rs1")]
        b_xin = [Buf("xin0"), Buf("xin1")]
        b_y = [Buf("y0"), Buf("y1")]
        b_yn = [Buf("yn0"), Buf("yn1")]
        b_tk = [Buf("tk0"), Buf("tk1")]
        sc.dma(blk1[:], dnc[3], writes=[b_blk1])
        for ci in range(9):
            sc.dma(cw[:, ci, :], conv_w[:, ci * 128:(ci + 1) * 128].rearrange("j c -> c j"), writes=[b_cw],
                   allow_slow_non_contiguous=True)
        it = 0
        for g0 in range(0, NT, GT):
            tok0 = g0 * 128
            for ci in range(9):
                b = it % 2
                it += 1
                sq, rs, b_sq, b_rs = sq2[b], rs2[b], b_sq2[b], b_rs2[b]
                pA, pB = 2 * b, 2 * b + 1
                sc.dma(xin[b][:], ms["cpre"][ci * 128:(ci + 1) * 128, tok0:tok0 + GW + 4], writes=[b_xin[b]])
                eng = "dve"
                sc.op(eng, lambda e: e.tensor_scalar_mul(out=y[b][:], in0=xin[b][:, 0:GW], scalar1=cw[:, ci, 0:1]),
                      reads=[b_xin[b], b_cw], writes=[b_y[b]])
                for j in range(1, 5):
                    sc.op(eng, lambda e: e.scalar_tensor_tensor(out=y[b][:], in0=xin[b][:, j:j + GW],
                                                                scalar=cw[:, ci, j:j + 1], in1=y[b][:],
                                                                op0=ALU.mult, op1=ALU.add),
                          reads=[b_xin[b], b_cw, b_y[b]], writes=[b_y[b]])
                sc.op("act", lambda e: e.activation(out=y[b][:], in_=y[b][:], func=AF.Silu),
                      reads=[b_y[b]], writes=[b_y[b]])
                if ci < 6:
                    sc.op("act", lambda e: e.activation(out=sq[:], in_=y[b][:], func=AF.Square),
                          reads=[b_y[b]], writes=[b_sq])
                    sc.op("pe", lambda e: e.matmul(ps[pA][:, :GW], lhsT=blk1[:], rhs=sq[:], start=True, stop=True),
                          reads=[b_blk1, b_sq], writes=[psb[pA]])
                    mul = 64.0 if ci < 3 else 1.0
                    sc.op("dve", lambda e: e.tensor_scalar(out=rs[:], in0=ps[pA][:, :GW], scalar1=EPS, scalar2=mul,
                                                           op0=ALU.add, op1=ALU.mult),
                          reads=[psb[pA]], writes=[b_rs])
                    sc.op("act", lambda e: e.sqrt(out=rs[:], in_=rs[:]), reads=[b_rs], writes=[b_rs])
                    sc.op("dve", lambda e: e.reciprocal(out=rs[:], in_=rs[:]), reads=[b_rs], writes=[b_rs])
                    sc.op("dve", lambda e: e.tensor_tensor(out=yn[b][:], in0=y[b][:], in1=rs[:], op=ALU.mult),
                          reads=[b_y[b], b_rs], writes=[b_yn[b]])
                    src_t, src_b = yn[b], b_yn[b]
                    if ci < 3:
                        sc.dma(ds["cqT"][ci * 128:(ci + 1) * 128, tok0:tok0 + GW], yn[b][:], reads=[b_yn[b]])
                    else:
                        sc.dma(ds["ckT"][(ci - 3) * 128:(ci - 2) * 128, tok0:tok0 + GW], yn[b][:], reads=[b_yn[b]])
                else:
                    src_t, src_b = y[b], b_y[b]
                if ci >= 3:
                    for tt in range(GT):
                        sc.op("pe", lambda e: e.transpose(ps[pB][:, tt * 128:(tt + 1) * 128],
                                                          src_t[:, tt * 128:(tt + 1) * 128], ident[:]),
                              reads=[src_b, ident_b], writes=[psb[pB]], signal=(tt == GT - 1))
                    sc.op("act", lambda e: e.copy(out=tk[b][:], in_=ps[pB][:, :GW]), reads=[psb[pB]], writes=[b_tk[b]])
                    dst = ds["ck"] if ci < 6 else ds["cv"]
                    cc = (ci - 3) % 3
                    sc.dma(dst[tok0:tok0 + GW, cc * 128:(cc + 1) * 128].rearrange("(t p) c -> p t c", p=128),
                           tk[b][:].rearrange("p (t c) -> p t c", c=128), reads=[b_tk[b]])
        sc.barrier()


K.conv_phase = _conv_phase


def _dn_pass(self, tag, dirn, ms, ds, dnc, a_log, dt_bias, dn_g, ident, ident_b):
    sc = self.sc
    S, NT = self.S, self.NT
    ps, psb = self.ps, self.psb
    import os
    NIT = int(os.environ.get('DN_NIT', '7'))
    with ExitStack() as st:
        def T(name, shape, n=1, dt=F32):
            ts = [self.sb(st, "%s%s%d" % (tag, name, i), shape, dt) for i in range(n)]
            bs = [Buf("%s%d" % (name, i)) for i in range(n)]
            return (ts, bs) if n > 1 else (ts[0], bs[0])
        ones, b_ones = T("ones", [128, 128])
        tri, b_tri = T("tri", [128, 128])
        m1, b_m1 = T("m1", [128, 128])
        m2, b_m2 = T("m2", [128, 128])
        m3, b_m3 = T("m3", [128, 128])
        dtb, b_dtb = T("dtb", [128, 6])
        nega, b_nega = T("nega", [128, 6])
        gdn, b_gdn = T("gdn", [128, 64])
        kTt, b_kTt = T("kTt", [64, 6, 128], 2)
        qTt, b_qTt = T("qTt", [64, 6, 128], 2)
        kt, b_kt = T("kt", [128, 384], 2)
        vt, b_vt = T("vt", [128, 384], 2)
        gz, b_gz = T("gz", [128, 408], 2)
        gs, b_gs = T("gs", [128, 16, 6])
        D1, b_D1 = T("D1", [128, 6, 128])
        D2, b_D2 = T("D2", [128, 6, 128])
        Rs, b_Rs = T("Rs", [128, 6, 128])
        R2s, b_R2s = T("R2s", [128, 6, 128])
        tmp, b_tmp = T("tmp", [128, 3, 128], 2)
        E, b_E = T("E", [128, 3, 128], 2)
        X, b_X = T("X", [128, 128], 4)
        Y, b_Y = T("Y", [128, 128], 4)
        Rr, b_Rr = T("Rr", [128, 128], 4)
        qkT, b_qkT = T("qkT", [128, 128], 2)
        kg, b_kg = T("kg", [128, 64], 2)
        wT, b_wT = T("wT", [64, 128], 2)
        Vn, b_Vn = T("Vn", [128, 64], 2)
        t2, b_t2 = T("t2", [128, 64], 2)
        St, b_St = T("St", [64, 6, 64])
        osb, b_osb = T("osb", [128, 6, 64], 2)
        if dirn == 1:
            oft, b_oft = T("oft", [128, 6, 64], 2)
            sqt, b_sqt = T("sqt", [128, 6, 64])
            sz, b_sz = T("sz", [128, 6, 64])
            rr, b_rr = T("rr", [128, 8])
        sc.dma(ones[:], dnc[0], writes=[b_ones])
        sc.dma(tri[:], dnc[1 + dirn], writes=[b_tri])
        sc.dma(m1[:], dnc[4 + 3 * dirn], writes=[b_m1])
        sc.dma(m2[:], dnc[5 + 3 * dirn], writes=[b_m2])
        sc.dma(m3[:], dnc[6 + 3 * dirn], writes=[b_m3])
        sc.dma(dtb[:], dt_bias[dirn].partition_broadcast(128), writes=[b_dtb])
        sc.dma(nega[:], a_log[dirn].partition_broadcast(128), writes=[b_nega])
        sc.dma(gdn[:], dn_g.partition_broadcast(128), writes=[b_gdn])
        sc.op("act", lambda e: e.activation(out=nega[:], in_=nega[:], func=AF.Exp), reads=[b_nega], writes=[b_nega])
        sc.op("dve", lambda e: e.tensor_scalar_mul(out=nega[:], in0=nega[:], scalar1=-1.0),
              reads=[b_nega], writes=[b_nega])
        sc.op("pool", lambda e: e.memset(St[:], 0.0), writes=[b_St])
        order = list(range(NT)) if dirn == 0 else list(range(NT - 1, -1, -1))
        G_ = lambda i: gs[:, i, :]
        hcnt = 0
        for it, n in enumerate(order):
            tb = it % 2
            c0 = n * 128
            sc.dma(kTt[tb][:], ds["ckT"][:, c0:c0 + 128].rearrange("(h d) t -> d h t", h=6), writes=[b_kTt[tb]])
            sc.dma(qTt[tb][:], ds["cqT"][:, c0:c0 + 128].rearrange("(h d) t -> d h t", h=6), writes=[b_qTt[tb]])
            sc.dma(kt[tb][:], ds["ck"][c0:c0 + 128, :], writes=[b_kt[tb]])
            sc.dma(vt[tb][:], ds["cv"][c0:c0 + 128, :], writes=[b_vt[tb]])
            sc.dma(gz[tb][:], ms["cz"][c0:c0 + 128, :], writes=[b_gz[tb]])
            bcol = gz[tb][:, 384 + dirn * 6:384 + dirn * 6 + 6]
            acol = gz[tb][:, 396 + dirn * 6:396 + dirn * 6 + 6]
            RG = [b_gs]
            sc.op("act", lambda e: e.activation(out=G_(0), in_=bcol, func=AF.Exp, scale=-1.0),
                  reads=[b_gz[tb]], writes=RG)
            sc.op("dve", lambda e: e.tensor_scalar_add(out=G_(0), in0=G_(0), scalar1=1.0), reads=RG, writes=RG)
            sc.op("act", lambda e: e.activation(out=G_(1), in_=G_(0), func=AF.Ln), reads=RG, writes=RG)
            sc.op("dve", lambda e: e.tensor_tensor(out=G_(2), in0=acol, in1=dtb[:], op=ALU.add),
                  reads=[b_gz[tb], b_dtb], writes=RG)
            sc.op("act", lambda e: e.activation(out=G_(3), in_=G_(2), func=AF.Exp), reads=RG, writes=RG)
            sc.op("dve", lambda e: e.tensor_scalar_add(out=G_(3), in0=G_(3), scalar1=1.0), reads=RG, writes=RG)
            sc.op("act", lambda e: e.activation(out=G_(4), in_=G_(3), func=AF.Ln), reads=RG, writes=RG)
            sc.op("dve", lambda e: e.tensor_tensor(out=G_(5), in0=G_(4), in1=nega[:], op=ALU.mult),
                  reads=RG + [b_nega], writes=RG)
            sc.op("pe", lambda e: e.matmul(ps[0][:, 0:6], lhsT=tri[:], rhs=G_(5), start=True, stop=True),
                  reads=[b_tri] + RG, writes=[psb[0]], signal=False)
            sc.op("pe", lambda e: e.matmul(ps[0][:, 8:14], lhsT=ones[:], rhs=G_(5), start=True, stop=True),
                  reads=[b_ones] + RG, writes=[psb[0]])
            sc.op("dve", lambda e: e.tensor_copy(out=G_(6), in_=ps[0][:, 0:6]), reads=[psb[0]], writes=RG)
            sc.op("dve", lambda e: e.tensor_copy(out=G_(14), in_=ps[0][:, 8:14]), reads=[psb[0]], writes=RG)
            sc.op("dve", lambda e: e.tensor_tensor(out=G_(7), in0=G_(6), in1=G_(1), op=ALU.subtract),
                  reads=RG, writes=RG)
            sc.op("act", lambda e: e.activation(out=G_(8), in_=G_(7), func=AF.Exp), reads=RG, writes=RG)
            sc.op("act", lambda e: e.activation(out=G_(9), in_=G_(1), func=AF.Exp, scale=-1.0),
                  reads=RG, writes=RG)
            sc.op("act", lambda e: e.activation(out=G_(10), in_=G_(6), func=AF.Exp), reads=RG, writes=RG)
            sc.op("dve", lambda e: e.tensor_tensor(out=G_(11), in0=G_(14), in1=G_(6), op=ALU.subtract),
                  reads=RG, writes=RG)
            sc.op("act", lambda e: e.activation(out=G_(12), in_=G_(11), func=AF.Exp), reads=RG, writes=RG)
            sc.op("act", lambda e: e.activation(out=G_(13), in_=G_(14), func=AF.Exp), reads=RG, writes=RG)
            idb3 = ident[:].unsqueeze(1).to_broadcast([128, 6, 128])
            sc.op("dve", lambda e: e.tensor_tensor(out=D1[:], in0=idb3,
                                                   in1=G_(6).unsqueeze(2).to_broadcast([128, 6, 128]), op=ALU.mult),
                  reads=RG + [ident_b], writes=[b_D1])
            sc.op("pool", lambda e: e.tensor_tensor(out=D2[:], in0=idb3,
                                                    in1=G_(7).unsqueeze(2).to_broadcast([128, 6, 128]), op=ALU.mult),
                  reads=RG + [ident_b], writes=[b_D2])
            for (Dm, b_Dm, Rm, b_Rm) in ((D1, b_D1, Rs, b_Rs), (D2, b_D2, R2s, b_R2s)):
                Dm2 = Dm[:].rearrange("p j s -> p (j s)")
                Rm2 = Rm[:].rearrange("p j s -> p (j s)")
                sc.op("pe", lambda e: e.matmul(ps[1][:, 0:512], lhsT=ones[:], rhs=Dm2[:, 0:512], start=True, stop=True),
                      reads=[b_ones, b_Dm], writes=[psb[1]])
                sc.op("pe", lambda e: e.matmul(ps[2][:, 0:256], lhsT=ones[:], rhs=Dm2[:, 512:768], start=True,
                                               stop=True),
                      reads=[b_ones, b_Dm], writes=[psb[2]])
                sc.op("act", lambda e: e.copy(out=Rm2[:, 0:512], in_=ps[1][:, 0:512]), reads=[psb[1]], writes=[b_Rm])
                sc.op("act", lambda e: e.copy(out=Rm2[:, 512:768], in_=ps[2][:, 0:256]), reads=[psb[2]],
                      writes=[b_Rm])
            ob = it % 2
            import os
            STOP = os.environ.get("DN_STOP", "")
            for hh in range(6):
                if STOP == "gates":
                    break
                hb = hcnt % 2
                hcnt += 1
                sc.op("pe", lambda e: e.matmul(ps[3][:, 0:128], lhsT=kTt[tb][:, hh, :], rhs=kTt[tb][:, hh, :],
                                               start=True, stop=True),
                      reads=[b_kTt[tb]], writes=[psb[3]], signal=False)
                sc.op("pe", lambda e: e.matmul(ps[3][:, 128:256], lhsT=kTt[tb][:, hh, :], rhs=qTt[tb][:, hh, :],
                                               start=True, stop=True),
                      reads=[b_kTt[tb], b_qTt[tb]], writes=[psb[3]])
                sc.op("dve", lambda e: e.scalar_tensor_tensor(out=tmp[hb][:, 0, :], in0=Rs[:, hh, :],
                                                              scalar=gs[:, 7, hh:hh + 1], in1=m1[:],
                                                              op0=ALU.subtract, op1=ALU.max),
                      reads=[b_Rs, b_gs, b_m1], writes=[b_tmp[hb]])
                sc.op("dve", lambda e: e.scalar_tensor_tensor(out=tmp[hb][:, 1, :], in0=R2s[:, hh, :],
                                                              scalar=gs[:, 6, hh:hh + 1], in1=m2[:],
                                                              op0=ALU.subtract, op1=ALU.min),
                      reads=[b_R2s, b_gs, b_m2], writes=[b_tmp[hb]])
                sc.op("dve", lambda e: e.scalar_tensor_tensor(out=tmp[hb][:, 2, :], in0=Rs[:, hh, :],
                                                              scalar=gs[:, 6, hh:hh + 1], in1=m3[:],
                                                              op0=ALU.subtract, op1=ALU.min),
                      reads=[b_Rs, b_gs, b_m3], writes=[b_tmp[hb]])
                sc.op("act", lambda e: e.activation(out=E[hb][:, 0, :], in_=tmp[hb][:, 0, :], func=AF.Exp, scale=-1.0),
                      reads=[b_tmp[hb]], writes=[b_E[hb]])
                sc.op("act", lambda e: e.activation(out=E[hb][:, 1:3, :], in_=tmp[hb][:, 1:3, :], func=AF.Exp),
                      reads=[b_tmp[hb]], writes=[b_E[hb]])
                xi = 2 * hb
                sc.op("dve", lambda e: e.tensor_tensor(out=X[xi][:], in0=ps[3][:, 0:128], in1=E[hb][:, 0, :],
                                                       op=ALU.mult), reads=[psb[3], b_E[hb]], writes=[b_X[xi]])
                sc.op("dve", lambda e: e.tensor_tensor(out=Y[xi][:], in0=ps[3][:, 0:128], in1=E[hb][:, 1, :],
                                                       op=ALU.mult), reads=[psb[3], b_E[hb]], writes=[b_Y[xi]])
                sc.op("dve", lambda e: e.tensor_tensor(out=qkT[hb][:], in0=ps[3][:, 128:256], in1=E[hb][:, 2, :],
                                                       op=ALU.mult), reads=[psb[3], b_E[hb]], writes=[b_qkT[hb]])
                hs = slice(hh * 64, (hh + 1) * 64)
                sc.op("pool", lambda e: e.tensor_scalar_mul(out=Rr[xi][:, 0:64], in0=vt[tb][:, hs],
                                                            scalar1=gs[:, 9, hh:hh + 1]),
                      reads=[b_vt[tb], b_gs], writes=[b_Rr[xi]])
                sc.op("pool", lambda e: e.tensor_scalar_mul(out=Rr[xi][:, 64:128], in0=kt[tb][:, hs],
                                                            scalar1=gs[:, 8, hh:hh + 1]),
                      reads=[b_kt[tb], b_gs], writes=[b_Rr[xi]])
                sc.op("pool", lambda e: e.tensor_scalar_mul(out=kg[hb][:], in0=kt[tb][:, hs],
                                                            scalar1=gs[:, 12, hh:hh + 1]),
                      reads=[b_kt[tb], b_gs], writes=[b_kg[hb]])
                if STOP == "prep":
                    continue
                pn = 4 + hb
                pq = 6 + hb
                cur = xi
                for i in range(NIT):
                    nxt = 2 * hb + (1 - (cur - 2 * hb))
                    sc.op("pe", lambda e: e.matmul(ps[pn][:, 0:128], lhsT=Y[cur][:], rhs=Rr[cur][:], start=True,
                                                   stop=True),
                          reads=[b_Y[cur], b_Rr[cur]], writes=[psb[pn]], signal=(i == NIT - 1))
                    if i < NIT - 1:
                        sc.op("pe", lambda e: e.matmul(ps[pq][:, 128:256], lhsT=X[cur][:], rhs=Y[cur][:], start=True,
                                                       stop=True),
                              reads=[b_X[cur], b_Y[cur]], writes=[psb[pq]], signal=(i == NIT - 2))
                    if i < NIT - 2:
                        sc.op("pe", lambda e: e.matmul(ps[pq][:, 256:384], lhsT=Y[cur][:], rhs=X[cur][:], start=True,
                                                       stop=True),
                              reads=[b_X[cur], b_Y[cur]], writes=[psb[pq]], signal=True)
                    if i == 0:
                        sc.op("dve", lambda e: e.scalar_tensor_tensor(out=Rr[nxt][:], in0=ps[pn][:, 0:128], scalar=-1.0,
                                                                      in1=Rr[cur][:], op0=ALU.mult, op1=ALU.add),
                              reads=[b_Rr[cur], psb[pn]], writes=[b_Rr[nxt]])
                    else:
                        sc.op("dve", lambda e: e.tensor_tensor(out=Rr[nxt][:], in0=ps[pn][:, 0:128], in1=Rr[cur][:],
                                                               op=ALU.add),
                              reads=[b_Rr[cur], psb[pn]], writes=[b_Rr[nxt]])
                    if i < NIT - 1:
                        sc.op("act", lambda e: e.copy(out=Y[nxt][:], in_=ps[pq][:, 128:256]),
                              reads=[psb[pq]], writes=[b_Y[nxt]])
                    if i < NIT - 2:
                        sc.op("act", lambda e: e.copy(out=X[nxt][:], in_=ps[pq][:, 256:384]),
                              reads=[psb[pq]], writes=[b_X[nxt]])
                    cur = nxt
                Rf, b_Rf = Rr[cur], b_Rr[cur]
                if STOP == "neumann":
                    continue
                sc.op("pe", lambda e: e.transpose(ps[1][0:64, 0:128], Rf[:, 64:128], ident[:]),
                      reads=[b_Rf, ident_b], writes=[psb[1]])
                sc.op("act", lambda e: e.copy(out=wT[hb][:], in_=ps[1][0:64, 0:128]), reads=[psb[1]], writes=[b_wT[hb]])
                if STOP == "transp":
                    continue
                sc.op("pe", lambda e: e.matmul(ps[0][:, 0:64], lhsT=wT[hb][:], rhs=St[:, hh, :], start=True, stop=True),
                      reads=[b_wT[hb], b_St], writes=[psb[0]])
                sc.op("dve", lambda e: e.scalar_tensor_tensor(out=Vn[hb][:], in0=ps[0][:, 0:64], scalar=-1.0,
                                                              in1=Rf[:, 0:64], op0=ALU.mult, op1=ALU.add),
                      reads=[b_Rf, psb[0]], writes=[b_Vn[hb]])
                sc.op("pe", lambda e: e.matmul(ps[1][:, 128:192], lhsT=qTt[tb][:, hh, :], rhs=St[:, hh, :], start=True,
                                               stop=True),
                      reads=[b_qTt[tb], b_St], writes=[psb[1]], signal=False)
                sc.op("pe", lambda e: e.matmul(ps[0][:, 64:128], lhsT=qkT[hb][:], rhs=Vn[hb][:], start=True, stop=True),
                      reads=[b_qkT[hb], b_Vn[hb]], writes=[psb[0]], signal=False)
                sc.op("pe", lambda e: e.matmul(ps[2][0:64, 0:64], lhsT=kg[hb][:], rhs=Vn[hb][:], start=True, stop=True),
                      reads=[b_kg[hb], b_Vn[hb]], writes=[psb[2]])
                sc.op("act", lambda e: e.activation(out=t2[hb][:], in_=ps[1][:, 128:192], func=AF.Copy,
                                                    scale=gs[:, 10, hh:hh + 1]),
                      reads=[psb[1], b_gs], writes=[b_t2[hb]])
                sc.op("dve", lambda e: e.tensor_tensor(out=osb[ob][:, hh, :], in0=ps[0][:, 64:128], in1=t2[hb][:],
                                                       op=ALU.add),
                      reads=[b_t2[hb], psb[0]], writes=[b_osb[ob]])
                sc.op("pool", lambda e: e.tensor_scalar_mul(out=St[:, hh, :], in0=St[:, hh, :],
                                                            scalar1=gs[0:64, 13, hh:hh + 1]),
                      reads=[b_St, b_gs], writes=[b_St])
                sc.op("dve", lambda e: e.tensor_tensor(out=St[:, hh, :], in0=ps[2][0:64, 0:64], in1=St[:, hh, :],
                                                       op=ALU.add),
                      reads=[b_St, psb[2]], writes=[b_St])
            if dirn == 0:
                sc.dma(ds["of"][c0:c0 + 128, :], osb[ob][:].rearrange("p h d -> p (h d)"), reads=[b_osb[ob]])
            else:
                sc.dma(oft[ob][:].rearrange("p h d -> p (h d)"), ds["of"][c0:c0 + 128, :], writes=[b_oft[ob]])
                sc.op("dve", lambda e: e.tensor_tensor(out=oft[ob][:], in0=oft[ob][:], in1=osb[ob][:], op=ALU.add),
                      reads=[b_oft[ob], b_osb[ob]], writes=[b_oft[ob]])
                sc.op("pool", lambda e: e.tensor_tensor(out=sqt[:], in0=oft[ob][:], in1=oft[ob][:], op=ALU.mult),
                      reads=[b_oft[ob]], writes=[b_sqt])
                sc.op("dve", lambda e: e.reduce_sum(out=rr[:, 0:6], in_=sqt[:], axis=AX.X), reads=[b_sqt], writes=[b_rr])
                sc.op("dve", lambda e: e.tensor_scalar(out=rr[:, 0:6], in0=rr[:, 0:6], scalar1=1.0 / 64, scalar2=EPS,
                                                       op0=ALU.mult, op1=ALU.add), reads=[b_rr], writes=[b_rr])
                sc.op("act", lambda e: e.sqrt(out=rr[:, 0:6], in_=rr[:, 0:6]), reads=[b_rr], writes=[b_rr])
                sc.op("dve", lambda e: e.reciprocal(out=rr[:, 0:6], in_=rr[:, 0:6]), reads=[b_rr], writes=[b_rr])
                sc.op("act", lambda e: e.activation(out=sz[:].rearrange("p h d -> p (h d)"), in_=gz[tb][:, 0:384],
                                                    func=AF.Silu), reads=[b_gz[tb]], writes=[b_sz])
                sc.op("dve", lambda e: e.tensor_tensor(out=oft[ob][:], in0=oft[ob][:],
                                                       in1=rr[:, 0:6].unsqueeze(2).to_broadcast([128, 6, 64]),
                                                       op=ALU.mult), reads=[b_oft[ob], b_rr], writes=[b_oft[ob]])
                sc.op("pool", lambda e: e.tensor_tensor(out=sz[:], in0=sz[:],
                                                        in1=gdn[:].unsqueeze(1).to_broadcast([128, 6, 64]),
                                                        op=ALU.mult), reads=[b_sz, b_gdn], writes=[b_sz])
                sc.op("dve", lambda e: e.tensor_tensor(out=oft[ob][:], in0=oft[ob][:], in1=sz[:], op=ALU.mult),
                      reads=[b_oft[ob], b_sz], writes=[b_oft[ob]])
                sc.dma(ms["omix"][c0:c0 + 128, 640:1024], oft[ob][:].rearrange("p h d -> p (h d)"),
                       reads=[b_oft[ob]])
        sc.barrier()


K.dn_pass = _dn_pass


def _outproj_phase(self, tag, w_o, ms, xres, x_bufs, ident, ident_b):
    sc = self.sc
    S, NT = self.S, self.NT
    ps, psb = self.ps, self.psb
    with ExitStack() as st:
        wo = self.sb(st, tag + "wo", [128, 8, D], BF16)
        ot = [self.sb(st, tag + "ot%d" % i, [128, D], F32) for i in range(2)]
        xr = [self.sb(st, tag + "xr%d" % i, [128, D], F32) for i in range(2)]
        oT = [self.sb(st, tag + "oT%d" % i, [128, 8, 128], BF16) for i in range(2)]
        b_wo = [Buf("wo%d" % k) for k in range(8)]
        b_ot = [Buf("ot0"), Buf("ot1")]
        b_xr = [Buf("xr0"), Buf("xr1")]
        b_oT = [Buf("oT0"), Buf("oT1")]
        for kc in range(8):
            sc.dma(wo[:, kc, :], w_o[kc * 128:(kc + 1) * 128, :], writes=[b_wo[kc]], q="pool")
        for t in range(NT):
            b = t % 2
            r0 = t * 128
            sc.dma(ot[b][:], ms["omix"][r0:r0 + 128, :], writes=[b_ot[b]])
            sc.dma(xr[b][:], xres[r0:r0 + 128, :], reads=[x_bufs[t]], writes=[b_xr[b]])
            for kc in range(8):
                bank = kc // 4
                sc.op("pe", lambda e: e.transpose(ps[bank][:, (kc % 4) * 128:(kc % 4 + 1) * 128],
                                                  ot[b][:, kc * 128:(kc + 1) * 128], ident[:]),
                      reads=[b_ot[b], ident_b], writes=[psb[bank]], signal=(kc % 4 == 3))
            sc.op("act", lambda e: e.copy(out=oT[b][:, 0:4, :], in_=ps[0][:, :].rearrange("p (k t) -> p k t", k=4)),
                  reads=[psb[0]], writes=[b_oT[b]])
            sc.op("dve", lambda e: e.tensor_copy(out=oT[b][:, 4:8, :],
                                                 in_=ps[1][:, :].rearrange("p (k t) -> p k t", k=4)),
                  reads=[psb[1]], writes=[b_oT[b]])
            for hf in range(2):
                pb = 2 + 2 * b + hf
                for kc in range(8):
                    sc.op("pe", lambda e: e.matmul(ps[pb][:, :], lhsT=oT[b][:, kc, :], rhs=wo[:, kc, hf * 512:(hf + 1) * 512],
                                                   start=(kc == 0), stop=(kc == 7)),
                          reads=[b_oT[b], b_wo[kc]], writes=[psb[pb]], signal=(kc == 7))
            for hf in range(2):
                pb = 2 + 2 * b + hf
                sc.op("dve", lambda e: e.tensor_tensor(out=xr[b][:, hf * 512:(hf + 1) * 512], in0=ps[pb][:, :],
                                                       in1=xr[b][:, hf * 512:(hf + 1) * 512], op=ALU.add),
                      reads=[psb[pb], b_xr[b]], writes=[b_xr[b]])
            sc.dma(xres[r0:r0 + 128, :], xr[b][:], reads=[b_xr[b]], writes=[x_bufs[t]])
        sc.barrier()


K.outproj_phase = _outproj_phase


def _final_phase(self, tag, g, xres, x_bufs, out):
    sc = self.sc
    S, NT = self.S, self.NT
    with ExitStack() as st:
        gB = self.sb(st, tag + "gB", [128, D], F32)
        xt = [self.sb(st, tag + "xt%d" % i, [128, D], F32) for i in range(2)]
        junk = self.sb(st, tag + "junk", [128, D], F32)
        stat = [self.sb(st, tag + "stat%d" % i, [128, 4], F32) for i in range(2)]
        b_gB, b_junk = Buf("gB"), Buf("junk")
        b_xt = [Buf("xt0"), Buf("xt1")]
        b_stat = [Buf("stat0"), Buf("stat1")]
        sc.dma(gB[:], g.partition_broadcast(128), writes=[b_gB])
        for t in range(NT):
            b = t % 2
            r0 = t * 128
            sc.dma(xt[b][:], xres[r0:r0 + 128, :], reads=[x_bufs[t]], writes=[b_xt[b]])
            sc.op("act", lambda e: e.activation(out=junk[:], in_=xt[b][:], func=AF.Square, accum_out=stat[b][:, 0:1]),
                  reads=[b_xt[b]], writes=[b_junk, b_stat[b]])
            sc.op("dve", lambda e: e.tensor_scalar(out=stat[b][:, 1:2], in0=stat[b][:, 0:1], scalar1=1.0 / D,
                                                   scalar2=EPS, op0=ALU.mult, op1=ALU.add),
                  reads=[b_stat[b]], writes=[b_stat[b]])
            sc.op("act", lambda e: e.sqrt(out=stat[b][:, 3:4], in_=stat[b][:, 1:2]), reads=[b_stat[b]],
                  writes=[b_stat[b]])
            sc.op("dve", lambda e: e.reciprocal(out=stat[b][:, 2:3], in_=stat[b][:, 3:4]), reads=[b_stat[b]],
                  writes=[b_stat[b]])
            sc.op("dve", lambda e: e.scalar_tensor_tensor(out=xt[b][:], in0=xt[b][:], scalar=stat[b][:, 2:3],
                                                          in1=gB[:], op0=ALU.mult, op1=ALU.mult),
                  reads=[b_xt[b], b_stat[b], b_gB], writes=[b_xt[b]])
            sc.dma(out[r0:r0 + 128, :], xt[b][:], reads=[b_xt[b]])
        sc.barrier()


K.final_phase = _final_phase

import math
WNAMES = [("ln_ffn1", [2, D]), ("ffn1_w_in", [2, D, 2 * DFF]), ("ffn1_w_out", [2, DFF, D]), ("ln_mix", [2, D]),
          ("w_mix_in", [2, D, MIX_IN]), ("conv_w", [2, 5, 1152]), ("sink_logits", [2, 6]),
          ("diff_lambda", [2, 4, 32]), ("diff_norm_g", [2, 64]), ("dn_A_log", [2, 2, 6]), ("dn_dt_bias", [2, 2, 6]),
          ("dn_norm_g", [2, 64]), ("w_mix_out", [2, D, D]), ("ln_ffn2", [2, D]), ("ffn2_w_in", [2, D, 2 * DFF]),
          ("ffn2_w_out", [2, DFF, D]), ("ln_final", [D])]


def build_full(S, depth=2, paired=False):
    k = K(S, depth, paired)
    SK = k.SK
    x = k.inp("x", [S, D])
    W = {n: k.inp(n, shp) for n, shp in WNAMES}
    cst = {n: k.inp(n, shp, BF16) for n, shp in (("daq", [4, 4, S]), ("dakp", [4, 4, SK]), ("dakm", [4, 4, SK]),
                                                  ("dbd", [4, 128, 128]), ("identb", [128, 128]))}
    wbias = k.inp("wbias", [6, 128, 384])
    dnc = k.inp("dnc", [10, 128, 128])
    idn = k.inp("ident", [128, 128])
    if paired:
        antiid = k.inp("antiid", [128, 128])
        sel_d = k.inp("sel", [128, 2])
        pr = k.pair_scratch()
    else:
        pr = sel_d = None
    out = k.outp("out", [S, D])
    xres = k.scratch("xres", [S, D])
    ms = k.mix_scratch()
    ds = k.dn_scratch()
    sc = k.sc
    with ExitStack() as st:
        ident = k.sb(st, "ident_sb", [128, 128], F32)
        ib = Buf("ident")
        sc.dma(ident[:], idn[:, :], writes=[ib])
        xb = [Buf("x%d" % i) for i in range(k.NT)]
        xin = [Buf("xin%d" % i) for i in range(k.NT)]
        for l in range(depth):
            lam_init = 0.8 - 0.6 * math.exp(-0.3 * l)
            k.ffn_phase("f1_%d" % l, W["ffn1_w_in"][l], W["ffn1_w_out"][l], W["ln_ffn1"][l],
                        x if l == 0 else xres, xin if l == 0 else xb, xres, xb, ident, ib)
            k.inproj_phase("ip%d" % l, W["w_mix_in"][l], W["ln_mix"][l], xres, xb, ms, ident, ib)
            if paired:
                k.export_phase("ex%d" % l, W["w_mix_in"][l], W["ln_mix"][l], xres, xb, pr, ident, ib, antiid)
                k.exchange_phase("xc%d" % l, pr, ms, sel_d, st)
            with ExitStack() as cst_:
                units = k.conv_units(cst_, "cu%d" % l, ms, ds, W["conv_w"][l], dnc, ident, ib)

                def hook(units=units):
                    if units:
                        units.pop(0)()
                    return len(units) > 0
                k.diffattn_phase("da%d" % l, ms, cst, W["diff_lambda"][l], W["diff_norm_g"][l], lam_init, ident, ib, hook)
            with ExitStack() as wst_:
                wunits = k.win_units(wst_, "wu%d" % l, ms, wbias, W["sink_logits"][l])

                def whook(units=wunits):
                    if units:
                        units.pop(0)()
                    return len(units) > 0
                k.dn2("d2_%d" % l, ms, ds, dnc, W["dn_A_log"][l], W["dn_dt_bias"][l], W["dn_norm_g"][l], ident, ib,
                      pr, sel_d, st, whook)
            k.outproj_phase("op%d" % l, W["w_mix_out"][l], ms, xres, xb, ident, ib)
            k.ffn_phase("f2_%d" % l, W["ffn2_w_in"][l], W["ffn2_w_out"][l], W["ln_ffn2"][l],
                        xres, xb, xres, xb, ident, ib)
        k.final_phase("fin", W["ln_final"], xres, xb, out)
        sc.finish([])
    k.stack.close()
    return k


def pair_feeds(inputs, S_full):
    x = np.ascontiguousarray(np.asarray(inputs["x"], dtype=np.float32))
    B = x.shape[0]
    S = S_full // 2
    base = {n: np.ascontiguousarray(np.asarray(inputs[n], dtype=np.float32)) for n, _ in WNAMES}
    odd = dict(base)
    odd["conv_w"] = np.ascontiguousarray(base["conv_w"][:, ::-1, :])
    wmi = base["w_mix_in"].copy()
    wmi[:, :, C_CB:C_CB + 6] = base["w_mix_in"][:, :, C_CB + 6:C_CB + 12]
    wmi[:, :, C_CB + 6:C_CB + 12] = base["w_mix_in"][:, :, C_CB:C_CB + 6]
    wmi[:, :, C_CA:C_CA + 6] = base["w_mix_in"][:, :, C_CA + 6:C_CA + 12]
    wmi[:, :, C_CA + 6:C_CA + 12] = base["w_mix_in"][:, :, C_CA:C_CA + 6]
    odd["w_mix_in"] = wmi
    odd["dn_A_log"] = np.ascontiguousarray(base["dn_A_log"][:, ::-1, :])
    odd["dn_dt_bias"] = np.ascontiguousarray(base["dn_dt_bias"][:, ::-1, :])
    common = {}
    common.update(diff_consts(S, 2 * S))
    common.update(win_consts())
    common.update(dn_consts())
    common["ident"] = np.eye(128, dtype=np.float32)
    common["antiid"] = np.ascontiguousarray(np.eye(128, dtype=np.float32)[::-1])
    maps = []
    for c in range(2 * B):
        b, r = c // 2, c % 2
        m = dict(base if r == 0 else odd)
        m.update(common)
        if r == 0:
            m["x"] = np.ascontiguousarray(x[b, 0:S])
        else:
            m["x"] = np.ascontiguousarray(x[b, S:2 * S][::-1])
        sel = np.zeros((128, 2), np.float32)
        sel[:, 1 - r] = 1.0
        m["sel"] = sel
        maps.append(m)
    return maps


def pair_gather(results, B, S_full):
    S = S_full // 2
    out = np.empty((B, S_full, D), np.float32)
    for b in range(B):
        out[b, 0:S] = results[2 * b]["out"]
        out[b, S:] = results[2 * b + 1]["out"][::-1]
    return out


GROUPS = [[0, 1], [2, 3], [4, 5], [6, 7]]


def _cdiv(a, b):
    return (a + b - 1) // b


def _pair_chunks(S):
    misc = _cdiv(128 * 128, S) + _cdiv(128 * 130, S)
    return [("bk0", 128), ("bk1", 128), ("bv0", 65), ("bv1", 65), ("bv2", 65), ("bv3", 65), ("misc", misc)]


def _pair_scratch(self):
    S = self.S
    d = {"exp": {}, "gat": {}, "rows": {}}
    for name, rows in _pair_chunks(S):
        d["exp"][name] = self.scratch("exp_" + name, [rows, S], BF16)
        d["gat"][name] = self.scratch("gat_" + name, [2 * rows, S], BF16)
        d["rows"][name] = rows
    d["expf"] = self.scratch("expf", [36, 64], F32)
    d["gatf"] = self.scratch("gatf", [72, 64], F32)
    d["exps"] = self.scratch("exps", [384, 64], F32)
    d["gats"] = self.scratch("gats", [768, 64], F32)
    return d


K.pair_scratch = _pair_scratch


def _pviews(pr, S, slot=None):
    def buf(name):
        if slot is None:
            return pr["exp"][name]
        r = pr["rows"][name]
        return pr["gat"][name][slot * r:(slot + 1) * r, :]
    v = {}
    v["bkT"] = [buf("bk0"), buf("bk1")]
    v["bv"] = [buf("bv%d" % q).rearrange("r c -> (r c)").rearrange("(t d) -> t d", d=260) for q in range(4)]
    mflat = buf("misc").rearrange("r c -> (r c)")
    o2 = _cdiv(128 * 128, S) * S
    v["akT"] = mflat[0:128 * 128].rearrange("(r c) -> r c", c=128)
    v["av"] = mflat[o2:o2 + 128 * 130].rearrange("(t d) -> t d", d=130)
    return v


def _export_phase(self, tag, w_mi, g, src, src_bufs, pr, ident, ident_b, antiid):
    sc = self.sc
    S, NT = self.S, self.NT
    ps, psb = self.ps, self.psb
    ev_ = _pviews(pr, S)
    Q4 = S // 4
    e_akT, e_av = ev_["akT"], ev_["av"]
    e_halo = pr["expf"].rearrange("r c -> (r c)").rearrange("(a b) -> a b", b=2)
    with ExitStack() as st:
        wm = self.sb(st, tag + "wm", [128, 8, MIX_IN], BF16)
        gB = self.sb(st, tag + "gB", [128, D], F32)
        J = self.sb(st, tag + "J", [128, 128], F32)
        xt = [self.sb(st, tag + "xt%d" % i, [128, D], F32) for i in range(2)]
        xn = self.sb(st, tag + "xn", [128, D], F32)
        junk = self.sb(st, tag + "junk", [128, D], F32)
        xr = [self.sb(st, tag + "xr%d" % i, [128, 8, 128], BF16) for i in range(2)]
        stat = self.sb(st, tag + "stat", [128, 8], F32)
        ok_ = [self.sb(st, tag + "ok%d" % i, [128, 2, 128], BF16) for i in range(2)]
        ov = [self.sb(st, tag + "ov%d" % i, [128, 4, 65], BF16) for i in range(2)]
        oak = self.sb(st, tag + "oak", [128, 128], BF16)
        oav = self.sb(st, tag + "oav", [128, 2, 65], BF16)
        oh = self.sb(st, tag + "oh", [128, 9, 2], F32)
        b_wm = [Buf("wm%d" % k) for k in range(8)]
        b_gB, b_J, b_xn, b_junk, b_stat, b_oak, b_oav, b_oh = (Buf(n) for n in
                                                               ("gB", "J", "xn", "junk", "stat", "oak", "oav", "oh"))
        b_xt = [Buf("xt0"), Buf("xt1")]
        b_xr = [Buf("xr0"), Buf("xr1")]
        b_ok = [Buf("ok0"), Buf("ok1")]
        b_ov = [Buf("ov0"), Buf("ov1")]
        for kc in range(8):
            sc.dma(wm[:, kc, :], w_mi[kc * 128:(kc + 1) * 128, :], writes=[b_wm[kc]], q="pool")
        sc.dma(gB[:], g.partition_broadcast(128), writes=[b_gB])
        sc.dma(J[:], antiid[:, :], writes=[b_J])
        for i in range(2):
            sc.op("pool", lambda e: e.memset(ov[i][:], 1.0), writes=[b_ov[i]])
        sc.op("pool", lambda e: e.memset(oav[:], 1.0), writes=[b_oav])
        for it, t in enumerate(range(NT - 1, -1, -1)):
            e = NT - 1 - t
            b = it % 2
            sc.dma(xt[b][:], src[t * 128:(t + 1) * 128, :], reads=[src_bufs[t]], writes=[b_xt[b]])
            sc.op("act", lambda e_: e_.activation(out=junk[:], in_=xt[b][:], func=AF.Square, accum_out=stat[:, 0:1]),
                  reads=[b_xt[b]], writes=[b_junk, b_stat])
            sc.op("dve", lambda e_: e_.tensor_scalar(out=stat[:, 1:2], in0=stat[:, 0:1], scalar1=1.0 / D,
                                                     scalar2=EPS, op0=ALU.mult, op1=ALU.add),
                  reads=[b_stat], writes=[b_stat])
            sc.op("act", lambda e_: e_.sqrt(out=stat[:, 3:4], in_=stat[:, 1:2]), reads=[b_stat], writes=[b_stat])
            sc.op("dve", lambda e_: e_.reciprocal(out=stat[:, 2:3], in_=stat[:, 3:4]), reads=[b_stat], writes=[b_stat])
            sc.op("dve", lambda e_: e_.scalar_tensor_tensor(out=xn[:], in0=xt[b][:], scalar=stat[:, 2:3],
                                                            in1=gB[:], op0=ALU.mult, op1=ALU.mult),
                  reads=[b_xt[b], b_stat, b_gB], writes=[b_xn])
            for kc in range(8):
                bank = kc // 4
                sc.op("pe", lambda e_: e_.matmul(ps[bank][:, (kc % 4) * 128:(kc % 4 + 1) * 128],
                                                 lhsT=xn[:, kc * 128:(kc + 1) * 128], rhs=J[:], start=True, stop=True),
                      reads=[b_xn, b_J], writes=[psb[bank]], signal=(kc % 4 == 3))
            sc.op("act", lambda e_: e_.copy(out=xr[b][:, 0:4, :], in_=ps[0][:, :].rearrange("p (k t) -> p k t", k=4)),
                  reads=[psb[0]], writes=[b_xr[b]])
            sc.op("dve", lambda e_: e_.tensor_copy(out=xr[b][:, 4:8, :],
                                                   in_=ps[1][:, :].rearrange("p (k t) -> p k t", k=4)),
                  reads=[psb[1]], writes=[b_xr[b]])
            for ci in range(2):
                for kc in range(8):
                    sc.op("pe", lambda e_: e_.matmul(ps[2][:, ci * 128:(ci + 1) * 128],
                                                     lhsT=wm[:, kc, C_BK + ci * 128:C_BK + (ci + 1) * 128],
                                                     rhs=xr[b][:, kc, :], start=(kc == 0), stop=(kc == 7)),
                          reads=[b_wm[kc], b_xr[b]], writes=[psb[2]], signal=(kc == 7 and ci == 1))
            for kc in range(8):
                sc.op("pe", lambda e_: e_.matmul(ps[3][:, 0:256], lhsT=xr[b][:, kc, :], rhs=wm[:, kc, C_BV:C_BV + 256],
                                                 start=(kc == 0), stop=(kc == 7)),
                      reads=[b_wm[kc], b_xr[b]], writes=[psb[3]], signal=(kc == 7))
            sc.op("act", lambda e_: e_.copy(out=ok_[b][:].rearrange("p c t -> p (c t)"), in_=ps[2][:, 0:256]),
                  reads=[psb[2]], writes=[b_ok[b]])
            sc.op("dve", lambda e_: e_.tensor_copy(out=ov[b][:, :, 0:64],
                                                   in_=ps[3][:, 0:256].rearrange("p (h d) -> p h d", h=4)),
                  reads=[psb[3]], writes=[b_ov[b]])
            for ci in range(2):
                sc.dma(ev_["bkT"][ci][:, e * 128:(e + 1) * 128], ok_[b][:, ci, :], reads=[b_ok[b]])
            q4 = (e * 128) // Q4
            r4 = e * 128 - q4 * Q4
            sc.dma(ev_["bv"][q4][r4:r4 + 128, :], ov[b][:].rearrange("p h d -> p (h d)"), reads=[b_ov[b]])
            if e == 0:
                for kc in range(8):
                    sc.op("pe", lambda e_: e_.matmul(ps[4][:, 0:128], lhsT=wm[:, kc, C_AK:C_AK + 128], rhs=xr[b][:, kc, :],
                                                     start=(kc == 0), stop=(kc == 7)),
                          reads=[b_wm[kc], b_xr[b]], writes=[psb[4]], signal=(kc == 7))
                for kc in range(8):
                    sc.op("pe", lambda e_: e_.matmul(ps[5][:, 0:128], lhsT=xr[b][:, kc, :], rhs=wm[:, kc, C_AV:C_AV + 128],
                                                     start=(kc == 0), stop=(kc == 7)),
                          reads=[b_wm[kc], b_xr[b]], writes=[psb[5]], signal=(kc == 7))
                for ci in range(9):
                    for kc in range(8):
                        sc.op("pe", lambda e_: e_.matmul(ps[6][:, ci * 2:ci * 2 + 2],
                                                         lhsT=wm[:, kc, C_CQKV + ci * 128:C_CQKV + (ci + 1) * 128],
                                                         rhs=xr[b][:, kc, 0:2], start=(kc == 0), stop=(kc == 7)),
                              reads=[b_wm[kc], b_xr[b]], writes=[psb[6]], signal=(kc == 7 and ci == 8))
                sc.op("act", lambda e_: e_.copy(out=oak[:], in_=ps[4][:, 0:128]), reads=[psb[4]], writes=[b_oak])
                sc.op("dve", lambda e_: e_.tensor_copy(out=oav[:, :, 0:64],
                                                       in_=ps[5][:, 0:128].rearrange("p (h d) -> p h d", h=2)),
                      reads=[psb[5]], writes=[b_oav])
                sc.op("act", lambda e_: e_.copy(out=oh[:].rearrange("p c t -> p (c t)"), in_=ps[6][:, 0:18]),
                      reads=[psb[6]], writes=[b_oh])
                sc.dma(e_akT[:, :], oak[:], reads=[b_oak])
                sc.dma(e_av[:, :], oav[:].rearrange("p h d -> p (h d)"), reads=[b_oav])
                sc.dma(e_halo.rearrange("(c p) t -> p c t", p=128), oh[:], reads=[b_oh])
        sc.barrier()


K.export_phase = _export_phase


def _exchange_phase(self, tag, pr, ms, sel_d, cstack):
    sc = self.sc
    S, NT = self.S, self.NT
    for name, _r in _pair_chunks(S):
        sc.collective(cstack, "AllGather", pr["exp"][name].opt(), pr["gat"][name].opt(), GROUPS)
    Q4 = S // 4
    sc.collective(cstack, "AllGather", pr["expf"].opt(), pr["gatf"].opt(), GROUPS)
    with ExitStack() as st:
        sel = self.sb(st, tag + "sel", [128, 2], F32)
        b_sel = Buf("sel")
        sc.dma(sel[:], sel_d[:, :], writes=[b_sel])
        CW = min(S, 2048)
        a0 = [self.sb(st, tag + "a0_%d" % i, [128, CW], BF16) for i in range(2)]
        a1 = [self.sb(st, tag + "a1_%d" % i, [128, CW], BF16) for i in range(2)]
        b_a0 = [Buf("a0_0"), Buf("a0_1")]
        b_a1 = [Buf("a1_0"), Buf("a1_1")]
        f0 = self.sb(st, tag + "f0", [128, 9, 2], F32)
        f1 = self.sb(st, tag + "f1", [128, 9, 2], F32)
        b_f0, b_f1 = Buf("f0"), Buf("f1")
        cnt = [0]

        def select(dst_ap, src0, src1, np_, w, view=None):
            i = cnt[0] % 2
            cnt[0] += 1
            t0 = a0[i][0:np_, 0:w]
            t1 = a1[i][0:np_, 0:w]
            if view is not None:
                t0v, t1v = view(t0), view(t1)
            else:
                t0v, t1v = t0, t1
            sc.dma(t0v, src0, writes=[b_a0[i]])
            sc.dma(t1v, src1, writes=[b_a1[i]])
            sc.op("dve", lambda e: e.tensor_scalar_mul(out=t0, in0=t0, scalar1=sel[0:np_, 0:1]),
                  reads=[b_a0[i], b_sel], writes=[b_a0[i]])
            sc.op("dve", lambda e: e.scalar_tensor_tensor(out=t1, in0=t1, scalar=sel[0:np_, 1:2], in1=t0,
                                                          op0=ALU.mult, op1=ALU.add),
                  reads=[b_a0[i], b_a1[i], b_sel], writes=[b_a1[i]])
            sc.dma(dst_ap, t1v, reads=[b_a1[i]])

        gv = [_pviews(pr, S, 0), _pviews(pr, S, 1)]
        for ci in range(2):
            for c0 in range(0, S, CW):
                v = lambda slot: gv[slot]["bkT"][ci][:, c0:c0 + CW]
                select(ms["bkT"][ci * 128:(ci + 1) * 128, S + c0:S + c0 + CW], v(0), v(1), 128, CW)
        TPB = max(1, min(CW // 260, Q4 // 128))
        for q4 in range(4):
            for t0_ in range(0, Q4 // 128, TPB):
                tn = min(TPB, Q4 // 128 - t0_)
                v = lambda slot: gv[slot]["bv"][q4][t0_ * 128:(t0_ + tn) * 128, :].rearrange("(n p) d -> p n d", p=128)
                r0 = S + q4 * Q4 + t0_ * 128
                dst = ms["bv"][r0:r0 + tn * 128, :, :].rearrange("(n p) h d -> p n (h d)", p=128)
                select(dst, v(0), v(1), 128, tn * 260, view=lambda t: t.rearrange("p (n d) -> p n d", d=260))
        v = lambda slot: gv[slot]["akT"]
        select(ms["akT"][:, S:S + 128], v(0), v(1), 128, 128)
        v = lambda slot: gv[slot]["av"]
        select(ms["av"][S:S + 128, :, :].rearrange("p h d -> p (h d)"), v(0), v(1), 128, 130)
        hv = lambda slot: pr["gatf"][slot * 36:(slot + 1) * 36, :].rearrange("r c -> (r c)") \
            .rearrange("(c p t) -> p c t", p=128, t=2)
        sc.dma(f0[:], hv(0), writes=[b_f0])
        sc.dma(f1[:], hv(1), writes=[b_f1])
        sc.op("dve", lambda e: e.tensor_scalar_mul(out=f0[:], in0=f0[:], scalar1=sel[:, 0:1]),
              reads=[b_f0, b_sel], writes=[b_f0])
        sc.op("dve", lambda e: e.scalar_tensor_tensor(out=f1[:], in0=f1[:], scalar=sel[:, 1:2], in1=f0[:],
                                                      op0=ALU.mult, op1=ALU.add),
              reads=[b_f0, b_f1, b_sel], writes=[b_f1])
        sc.dma(ms["cpre"][:, S + 2:S + 4].rearrange("(c p) t -> p c t", p=128), f1[:], reads=[b_f1])
        sc.barrier()


K.exchange_phase = _exchange_phase


def _dn2(self, tag, ms, ds, dnc, a_log, dt_bias, dn_g, ident, ident_b, pr=None, sel_d=None, cstack=None, hook=None):
    sc = self.sc
    S, NT = self.S, self.NT
    ps, psb = self.ps, self.psb
    NIT = 7
    gcT = self.scratch(tag + "gcT", [12, S], F32)
    ngcT = self.scratch(tag + "ngcT", [12, S], F32)
    acT = self.scratch(tag + "acT", [12, S], F32)
    with ExitStack() as st0:
        def T0(name, shape, dt=F32):
            return self.sb(st0, tag + name, shape, dt), Buf(name)
        gc, b_gc = T0("gc", [128, NT, 12])
        ac, b_ac = T0("ac", [128, NT, 12])
        beta, b_beta = T0("beta", [128, NT, 12])
        ea, b_ea = T0("ea", [128, NT, 12])
        ekg, b_ekg = T0("ekg", [128, NT, 12])
        glv, b_glv = T0("glv", [128, NT, 12])
        ones, b_ones = T0("ones", [128, 128])
        sc.dma(ones[:], dnc[0], writes=[b_ones])
        with ExitStack() as st:
            def T1(name, shape, dt=F32):
                return self.sb(st, tag + "g_" + name, shape, dt), Buf(name)
            ba, b_ba = T1("ba", [128, NT, 24])
            w1, b_w1 = T1("w1", [128, NT, 12])
            w2, b_w2 = T1("w2", [128, NT, 12])
            sp, b_sp = T1("sp", [128, NT, 12])
            gg, b_gg = T1("gg", [128, NT, 12])
            tt, b_tt = T1("tt", [128, NT, 12])
            triF, b_triF = T1("triF", [128, 128])
            triB, b_triB = T1("triB", [128, 128])
            dtb, b_dtb = T1("dtb", [128, 12])
            nega, b_nega = T1("nega", [128, 12])
            ev = [T1("ev%d" % i, [12, 3, 512]) for i in range(2)]
            sc.dma(ba[:], ms["cz"][:, 384:408].rearrange("(n p) c -> p n c", p=128), writes=[b_ba])
            sc.dma(triF[:], dnc[1], writes=[b_triF])
            sc.dma(triB[:], dnc[2], writes=[b_triB])
            sc.dma(dtb[:], dt_bias.rearrange("a b -> (a b)").partition_broadcast(128), writes=[b_dtb])
            sc.dma(nega[:], a_log.rearrange("a b -> (a b)").partition_broadcast(128), writes=[b_nega])
            sc.op("act", lambda e: e.activation(out=nega[:], in_=nega[:], func=AF.Exp), reads=[b_nega], writes=[b_nega])
            sc.op("dve", lambda e: e.tensor_scalar_mul(out=nega[:], in0=nega[:], scalar1=-1.0),
                  reads=[b_nega], writes=[b_nega])
            bc = lambda t: t[:].unsqueeze(1).to_broadcast([128, NT, 12])
            sc.op("act", lambda e: e.activation(out=w1[:], in_=ba[:, :, 0:12], func=AF.Exp, scale=-1.0),
                  reads=[b_ba], writes=[b_w1])
            sc.op("dve", lambda e: e.tensor_scalar_add(out=w1[:], in0=w1[:], scalar1=1.0), reads=[b_w1], writes=[b_w1])
            sc.op("act", lambda e: e.activation(out=sp[:], in_=w1[:], func=AF.Ln), reads=[b_w1], writes=[b_sp])
            sc.op("dve", lambda e: e.tensor_tensor(out=w2[:], in0=ba[:, :, 12:24], in1=bc(dtb), op=ALU.add),
                  reads=[b_ba, b_dtb], writes=[b_w2])
            sc.op("act", lambda e: e.activation(out=w2[:], in_=w2[:], func=AF.Exp), reads=[b_w2], writes=[b_w2])
            sc.op("dve", lambda e: e.tensor_scalar_add(out=w2[:], in0=w2[:], scalar1=1.0), reads=[b_w2], writes=[b_w2])
            sc.op("act", lambda e: e.activation(out=w2[:], in_=w2[:], func=AF.Ln), reads=[b_w2], writes=[b_w2])
            sc.op("dve", lambda e: e.tensor_tensor(out=gg[:], in0=w2[:], in1=bc(nega), op=ALU.mult),
                  reads=[b_w2, b_nega], writes=[b_gg])
            NC6 = NT * 6
            for c0 in range(0, NT, 64):
                c1 = min(NT, c0 + 64)
                w = (c1 - c0) * 6
                sc.op("pe", lambda e: e.matmul(ps[0][:, 0:w], lhsT=triF[:], rhs=gg[:, c0:c1, 0:6], start=True, stop=True),
                      reads=[b_triF, b_gg], writes=[psb[0]], signal=False)
                sc.op("pe", lambda e: e.matmul(ps[1][:, 0:w], lhsT=triB[:], rhs=gg[:, c0:c1, 6:12], start=True, stop=True),
                      reads=[b_triB, b_gg], writes=[psb[1]], signal=False)
                sc.op("pe", lambda e: e.matmul(ps[2][:, 0:w], lhsT=ones[:], rhs=gg[:, c0:c1, 0:6], start=True, stop=True),
                      reads=[b_ones, b_gg], writes=[psb[2]], signal=False)
                sc.op("pe", lambda e: e.matmul(ps[3][:, 0:w], lhsT=ones[:], rhs=gg[:, c0:c1, 6:12], start=True, stop=True),
                      reads=[b_ones, b_gg], writes=[psb[3]])
                v6 = lambda b: ps[b][:, 0:w].rearrange("p (n j) -> p n j", j=6)
                sc.op("dve", lambda e: e.tensor_copy(out=gc[:, c0:c1, 0:6], in_=v6(0)), reads=[psb[0]], writes=[b_gc])
                sc.op("dve", lambda e: e.tensor_copy(out=gc[:, c0:c1, 6:12], in_=v6(1)), reads=[psb[1]], writes=[b_gc])
                sc.op("dve", lambda e: e.tensor_copy(out=tt[:, c0:c1, 0:6], in_=v6(2)), reads=[psb[2]], writes=[b_tt])
                sc.op("dve", lambda e: e.tensor_copy(out=tt[:, c0:c1, 6:12], in_=v6(3)), reads=[psb[3]], writes=[b_tt])
            sc.op("dve", lambda e: e.tensor_tensor(out=ac[:], in0=gc[:], in1=sp[:], op=ALU.subtract),
                  reads=[b_gc, b_sp], writes=[b_ac])
            sc.op("act", lambda e: e.activation(out=ea[:], in_=ac[:], func=AF.Exp), reads=[b_ac], writes=[b_ea])
            sc.op("act", lambda e: e.activation(out=beta[:], in_=sp[:], func=AF.Exp, scale=-1.0),
                  reads=[b_sp], writes=[b_beta])
            sc.op("dve", lambda e: e.tensor_tensor(out=w1[:], in0=tt[:], in1=gc[:], op=ALU.subtract),
                  reads=[b_tt, b_gc, b_w1], writes=[b_w1])
            sc.op("act", lambda e: e.activation(out=ekg[:], in_=w1[:], func=AF.Exp), reads=[b_w1], writes=[b_ekg])
            sc.op("act", lambda e: e.activation(out=glv[:], in_=tt[:], func=AF.Exp), reads=[b_tt], writes=[b_glv])
            for q0 in range(0, NT, 4):
                qn = min(4, NT - q0)
                (evt, b_evt) = ev[(q0 // 4) % 2]
                for i in range(qn):
                    n = q0 + i
                    sc.op("pe", lambda e: e.transpose(ps[4][0:12, i * 128:(i + 1) * 128], gc[:, n, :], ident[:]),
                          reads=[b_gc, ident_b], writes=[psb[4]], signal=False)
                    sc.op("pe", lambda e: e.transpose(ps[5][0:12, i * 128:(i + 1) * 128], ac[:, n, :], ident[:]),
                          reads=[b_ac, ident_b], writes=[psb[5]], signal=(i == qn - 1))
                w = qn * 128
                sc.op("dve", lambda e: e.tensor_copy(out=evt[:, 0, 0:w], in_=ps[4][0:12, 0:w]), reads=[psb[4]], writes=[b_evt])
                sc.op("dve", lambda e: e.tensor_scalar_mul(out=evt[:, 1, 0:w], in0=ps[4][0:12, 0:w], scalar1=-1.0),
                      reads=[psb[4]], writes=[b_evt])
                sc.op("act", lambda e: e.copy(out=evt[:, 2, 0:w], in_=ps[5][0:12, 0:w]), reads=[psb[5]], writes=[b_evt])
                sc.dma(gcT[:, q0 * 128:q0 * 128 + w], evt[:, 0, 0:w], reads=[b_evt])
                sc.dma(ngcT[:, q0 * 128:q0 * 128 + w], evt[:, 1, 0:w], reads=[b_evt])
                sc.dma(acT[:, q0 * 128:q0 * 128 + w], evt[:, 2, 0:w], reads=[b_evt])
            sc.barrier()
        for dirn in (0, 1):
            with ExitStack() as st:
                def T(name, shape, n=1, dt=F32):
                    ts = [self.sb(st, "%s%d%s%d" % (tag, dirn, name, i), shape, dt) for i in range(n)]
                    bs = [Buf("%s%d" % (name, i)) for i in range(n)]
                    return (ts, bs) if n > 1 else (ts[0], bs[0])
                m1, b_m1 = T("m1", [128, 128])
                m2, b_m2 = T("m2", [128, 128])
                m3, b_m3 = T("m3", [128, 128])
                gdn, b_gdn = T("gdn", [128, 64])
                kTt, b_kTt = T("kTt", [64, 6, 128], 2)
                qTt, b_qTt = T("qTt", [64, 6, 128], 2)
                kt, b_kt = T("kt", [128, 384], 2)
                vt, b_vt = T("vt", [128, 384], 2)
                Rg, b_Rg = T("Rg", [128, 6, 128], 2)
                Rn, b_Rn = T("Rn", [128, 6, 128], 2)
                Ra, b_Ra = T("Ra", [128, 6, 128], 2)
                eR, b_eR = T("eR", [64, 6, 128])
                qg, b_qg = T("qg", [64, 6, 128])
                tmp, b_tmp = T("tmp", [128, 3, 128], 6)
                E, b_E = T("E", [128, 3, 128], 6)
                W0, b_W0 = T("W0", [128, 3, 128], 6)
                W1, b_W1 = T("W1", [128, 3, 128], 6)
                qkT, b_qkT = T("qkT", [128, 128], 6)
                kg, b_kg = T("kg", [128, 64], 6)
                glI, b_glI = T("glI", [64, 64], 6)
                wTn, b_wTn = T("wTn", [64, 128], 6)
                Vn, b_Vn = T("Vn", [128, 64], 6)
                St, b_St = T("St", [64, 64], 6)
                osb, b_osb = T("osb", [128, 6, 64], 2)
                if dirn == 1:
                    gz, b_gz = T("gz", [128, 384], 2)
                    oft, b_oft = T("oft", [128, 6, 64], 2)
                    sqt, b_sqt = T("sqt", [128, 6, 64])
                    sz, b_sz = T("sz", [128, 6, 64])
                    rr, b_rr = T("rr", [128, 8])
                sc.dma(m1[:], dnc[4 + 3 * dirn], writes=[b_m1])
                sc.dma(m2[:], dnc[5 + 3 * dirn], writes=[b_m2])
                sc.dma(m3[:], dnc[6 + 3 * dirn], writes=[b_m3])
                sc.dma(gdn[:], dn_g.partition_broadcast(128), writes=[b_gdn])
                if dirn == 0 or not self.paired:
                    for h in range(6):
                        sc.op("pool", lambda e: e.memset(St[h][:], 0.0), writes=[b_St[h]])
                else:
                    sl, b_sl = T("sl", [128, 2])
                    s0, b_s0 = T("s0", [64, 6, 64])
                    s1, b_s1 = T("s1", [64, 6, 64])
                    sc.dma(sl[:], sel_d[:, :], writes=[b_sl])
                    sc.dma(s0[:], pr["gats"][0:384, :].rearrange("(h k) v -> k h v", h=6), writes=[b_s0])
                    sc.dma(s1[:], pr["gats"][384:768, :].rearrange("(h k) v -> k h v", h=6), writes=[b_s1])
                    sc.op("dve", lambda e: e.tensor_scalar_mul(out=s0[:], in0=s0[:], scalar1=sl[0:64, 0:1]),
                          reads=[b_s0, b_sl], writes=[b_s0])
                    for h in range(6):
                        sc.op("dve", lambda e: e.scalar_tensor_tensor(out=St[h][:], in0=s1[:, h, :], scalar=sl[0:64, 1:2],
                                                                      in1=s0[:, h, :], op0=ALU.mult, op1=ALU.add),
                              reads=[b_s0, b_s1, b_sl], writes=[b_St[h]])
                order = list(range(NT)) if dirn == 0 else list(range(NT - 1, -1, -1))
                j0 = dirn * 6

                def load(it):
                    n = order[it]
                    tb = it % 2
                    c0 = n * 128
                    sc.dma(kTt[tb][:], ds["ckT"][:, c0:c0 + 128].rearrange("(h d) t -> d h t", h=6), writes=[b_kTt[tb]])
                    sc.dma(qTt[tb][:], ds["cqT"][:, c0:c0 + 128].rearrange("(h d) t -> d h t", h=6), writes=[b_qTt[tb]])
                    sc.dma(kt[tb][:], ds["ck"][c0:c0 + 128, :], writes=[b_kt[tb]])
                    sc.dma(vt[tb][:], ds["cv"][c0:c0 + 128, :], writes=[b_vt[tb]])
                    sc.dma(Rg[tb][:], gcT[j0:j0 + 6, c0:c0 + 128].partition_broadcast(128), writes=[b_Rg[tb]])
                    sc.dma(Rn[tb][:], ngcT[j0:j0 + 6, c0:c0 + 128].partition_broadcast(128), writes=[b_Rn[tb]])
                    sc.dma(Ra[tb][:], acT[j0:j0 + 6, c0:c0 + 128].partition_broadcast(128), writes=[b_Ra[tb]])
                    if dirn == 1:
                        sc.dma(gz[tb][:], ms["cz"][c0:c0 + 128, 0:384], writes=[b_gz[tb]])
                        sc.dma(oft[tb][:].rearrange("p h d -> p (h d)"), ds["of"][c0:c0 + 128, :], writes=[b_oft[tb]])

                load(0)
                for it, n in enumerate(order):
                    tb = it % 2
                    c0 = n * 128
                    if it + 1 < NT:
                        load(it + 1)
                    sc.op("act", lambda e: e.activation(out=eR[:], in_=Rg[tb][0:64, :, :], func=AF.Exp),
                          reads=[b_Rg[tb]], writes=[b_eR])
                    sc.op("dve", lambda e: e.tensor_tensor(out=qg[:], in0=qTt[tb][:], in1=eR[:], op=ALU.mult),
                          reads=[b_qTt[tb], b_eR], writes=[b_qg])
                    for h in range(6):
                        bk = 2 + h
                        j = j0 + h
                        hs = slice(h * 64, (h + 1) * 64)
                        sc.op("pe", lambda e: e.matmul(ps[bk][:, 0:128], lhsT=kTt[tb][:, h, :], rhs=kTt[tb][:, h, :],
                                                       start=True, stop=True),
                              reads=[b_kTt[tb]], writes=[psb[bk]], signal=False)
                        sc.op("pe", lambda e: e.matmul(ps[bk][:, 128:256], lhsT=kTt[tb][:, h, :], rhs=qTt[tb][:, h, :],
                                                       start=True, stop=True),
                              reads=[b_kTt[tb], b_qTt[tb]], writes=[psb[bk]])
                        sc.op("dve", lambda e: e.scalar_tensor_tensor(out=tmp[h][:, 0, :], in0=Rn[tb][:, h, :],
                                                                      scalar=ac[:, n, j:j + 1], in1=m1[:],
                                                                      op0=ALU.add, op1=ALU.min),
                              reads=[b_Rn[tb], b_ac, b_m1], writes=[b_tmp[h]])
                        sc.op("dve", lambda e: e.scalar_tensor_tensor(out=tmp[h][:, 1, :], in0=Ra[tb][:, h, :],
                                                                      scalar=gc[:, n, j:j + 1], in1=m2[:],
                                                                      op0=ALU.subtract, op1=ALU.min),
                              reads=[b_Ra[tb], b_gc, b_m2], writes=[b_tmp[h]])
                        sc.op("dve", lambda e: e.scalar_tensor_tensor(out=tmp[h][:, 2, :], in0=Rg[tb][:, h, :],
                                                                      scalar=gc[:, n, j:j + 1], in1=m3[:],
                                                                      op0=ALU.subtract, op1=ALU.min),
                              reads=[b_Rg[tb], b_gc, b_m3], writes=[b_tmp[h]])
                        sc.op("act", lambda e: e.activation(out=E[h][:], in_=tmp[h][:], func=AF.Exp),
                              reads=[b_tmp[h]], writes=[b_E[h]])
                        sc.op("act", lambda e: e.activation(out=W0[h][:, 0, 0:64], in_=vt[tb][:, hs], func=AF.Copy,
                                                            scale=beta[:, n, j:j + 1]),
                              reads=[b_vt[tb], b_beta], writes=[b_W0[h]])
                        sc.op("act", lambda e: e.activation(out=W0[h][:, 0, 64:128], in_=kt[tb][:, hs], func=AF.Copy,
                                                            scale=ea[:, n, j:j + 1]),
                              reads=[b_kt[tb], b_ea], writes=[b_W0[h]])
                        sc.op("act", lambda e: e.activation(out=kg[h][:], in_=kt[tb][:, hs], func=AF.Copy,
                                                            scale=ekg[:, n, j:j + 1]),
                              reads=[b_kt[tb], b_ekg], writes=[b_kg[h]])
                        sc.op("act", lambda e: e.activation(out=glI[h][:], in_=ident[0:64, 0:64], func=AF.Copy,
                                                            scale=glv[0:64, n, j:j + 1]),
                              reads=[ident_b, b_glv], writes=[b_glI[h]])
                    for h in range(6):
                        bk = 2 + h
                        sc.op("dve", lambda e: e.scalar_tensor_tensor(out=W0[h][:, 1, :], in0=ps[bk][:, 0:128], scalar=-1.0,
                                                                      in1=E[h][:, 0, :], op0=ALU.mult, op1=ALU.mult),
                              reads=[psb[bk], b_E[h]], writes=[b_W0[h]])
                        sc.op("dve", lambda e: e.scalar_tensor_tensor(out=W0[h][:, 2, :], in0=ps[bk][:, 0:128], scalar=-1.0,
                                                                      in1=E[h][:, 1, :], op0=ALU.mult, op1=ALU.mult),
                              reads=[psb[bk], b_E[h]], writes=[b_W0[h]])
                        sc.op("dve", lambda e: e.tensor_tensor(out=qkT[h][:], in0=ps[bk][:, 128:256], in1=E[h][:, 2, :],
                                                               op=ALU.mult),
                              reads=[psb[bk], b_E[h]], writes=[b_qkT[h]])
                    WW = [(W0, b_W0), (W1, b_W1)]
                    DVE_H = (1, 3, 4, 5)
                    for i in range(NIT):
                        (Wc, b_Wc), (Wn, b_Wn) = WW[i % 2], WW[(i + 1) % 2]
                        last = (i == NIT - 1)
                        for h in range(6):
                            bk = 2 + h
                            cur = Wc[h]
                            flat = cur[:].rearrange("p a s -> p (a s)")
                            na = 128 if i >= NIT - 2 else 256
                            on_dve = h in DVE_H
                            sc.op("pe", lambda e: e.matmul(ps[bk][:, 0:na], lhsT=cur[:, 2, :], rhs=flat[:, 0:na],
                                                           start=True, stop=on_dve, skip_group_check=True),
                                  reads=[b_Wc[h]], writes=[psb[bk]], signal=(on_dve and last))
                            if not on_dve:
                                sc.op("pe", lambda e: e.matmul(ps[bk][:, 0:128], lhsT=ident[:], rhs=cur[:, 0, :],
                                                               start=False, stop=True, skip_group_check=True),
                                      reads=[b_Wc[h], ident_b], writes=[psb[bk]], signal=last)
                            if not last:
                                sc.op("pe", lambda e: e.matmul(ps[bk][:, 256:384], lhsT=cur[:, 1, :], rhs=cur[:, 2, :],
                                                               start=True, stop=True, skip_group_check=True),
                                      reads=[b_Wc[h]], writes=[psb[bk]])
                        for h in range(6):
                            bk = 2 + h
                            cur = Wc[h]
                            nflat = Wn[h][:].rearrange("p a s -> p (a s)")
                            if h in DVE_H:
                                sc.op("dve", lambda e: e.tensor_tensor(out=nflat[:, 0:128], in0=ps[bk][:, 0:128],
                                                                       in1=cur[:, 0, :], op=ALU.add),
                                      reads=[psb[bk], b_Wc[h]], writes=[b_Wn[h]])
                                if not last:
                                    lo = 128 if i < NIT - 2 else 256
                                    sc.op("dve", lambda e: e.tensor_copy(out=nflat[:, lo:384], in_=ps[bk][:, lo:384]),
                                          reads=[psb[bk]], writes=[b_Wn[h]])
                            else:
                                if last:
                                    sc.op("act", lambda e: e.copy(out=nflat[:, 0:128], in_=ps[bk][:, 0:128]),
                                          reads=[psb[bk]], writes=[b_Wn[h]])
                                elif i < NIT - 2:
                                    sc.op("act", lambda e: e.copy(out=nflat[:, 0:384], in_=ps[bk][:, 0:384]),
                                          reads=[psb[bk]], writes=[b_Wn[h]])
                                else:
                                    sc.op("act", lambda e: e.copy(out=nflat[:, 0:128], in_=ps[bk][:, 0:128]),
                                          reads=[psb[bk]], writes=[b_Wn[h]], )
                                    sc.op("act", lambda e: e.copy(out=nflat[:, 256:384], in_=ps[bk][:, 256:384]),
                                          reads=[psb[bk]], writes=[b_Wn[h]])
                        if hook is not None:
                            hook()
                    (Wf, b_Wf) = WW[NIT % 2]
                    ob = it % 2
                    for h in range(6):
                        bk = 2 + h
                        sc.op("pe", lambda e: e.transpose(ps[bk][0:64, 0:128], Wf[h][:, 0, 64:128], ident[:]),
                              reads=[b_Wf[h], ident_b], writes=[psb[bk]])
                    for h in range(6):
                        bk = 2 + h
                        if h % 2 == 0:
                            sc.op("act", lambda e: e.mul(out=wTn[h][:], in_=ps[bk][0:64, 0:128], mul=-1.0),
                                  reads=[psb[bk]], writes=[b_wTn[h]])
                        else:
                            sc.op("dve", lambda e: e.tensor_scalar_mul(out=wTn[h][:], in0=ps[bk][0:64, 0:128], scalar1=-1.0),
                                  reads=[psb[bk]], writes=[b_wTn[h]])
                    for h in range(6):
                        bk = 2 + h
                        sc.op("pe", lambda e: e.matmul(ps[bk][:, 128:192], lhsT=ident[:], rhs=Wf[h][:, 0, 0:64],
                                                       start=True, stop=False, skip_group_check=True),
                              reads=[b_Wf[h], ident_b], writes=[psb[bk]], signal=False)
                        sc.op("pe", lambda e: e.matmul(ps[bk][:, 128:192], lhsT=wTn[h][:], rhs=St[h][:],
                                                       start=False, stop=True, skip_group_check=True),
                              reads=[b_wTn[h], b_St[h]], writes=[psb[bk]])
                    for h in range(6):
                        bk = 2 + h
                        if h % 2 == 0:
                            sc.op("act", lambda e: e.copy(out=Vn[h][:], in_=ps[bk][:, 128:192]), reads=[psb[bk]], writes=[b_Vn[h]])
                        else:
                            sc.op("dve", lambda e: e.tensor_copy(out=Vn[h][:], in_=ps[bk][:, 128:192]), reads=[psb[bk]],
                                  writes=[b_Vn[h]])
                    for h in range(6):
                        bk = 2 + h
                        sc.op("pe", lambda e: e.matmul(ps[bk][:, 192:256], lhsT=qg[:, h, :], rhs=St[h][:],
                                                       start=True, stop=False, skip_group_check=True),
                              reads=[b_qg, b_St[h]], writes=[psb[bk]], signal=False)
                        sc.op("pe", lambda e: e.matmul(ps[bk][:, 192:256], lhsT=qkT[h][:], rhs=Vn[h][:],
                                                       start=False, stop=True, skip_group_check=True),
                              reads=[b_qkT[h], b_Vn[h]], writes=[psb[bk]], signal=False)
                        sc.op("pe", lambda e: e.matmul(ps[bk][0:64, 256:320], lhsT=kg[h][:], rhs=Vn[h][:],
                                                       start=True, stop=False, skip_group_check=True),
                              reads=[b_kg[h], b_Vn[h]], writes=[psb[bk]], signal=False)
                        sc.op("pe", lambda e: e.matmul(ps[bk][0:64, 256:320], lhsT=glI[h][:], rhs=St[h][:],
                                                       start=False, stop=True, skip_group_check=True),
                              reads=[b_glI[h], b_St[h]], writes=[psb[bk]])
                    for h in range(6):
                        bk = 2 + h
                        if h % 2 == 0:
                            sc.op("act", lambda e: e.copy(out=osb[ob][:, h, :], in_=ps[bk][:, 192:256]), reads=[psb[bk]],
                                  writes=[b_osb[ob]])
                            sc.op("act", lambda e: e.copy(out=St[h][:], in_=ps[bk][0:64, 256:320]), reads=[psb[bk]],
                                  writes=[b_St[h]])
                        else:
                            sc.op("dve", lambda e: e.tensor_copy(out=osb[ob][:, h, :], in_=ps[bk][:, 192:256]),
                                  reads=[psb[bk]], writes=[b_osb[ob]])
                            sc.op("dve", lambda e: e.tensor_copy(out=St[h][:], in_=ps[bk][0:64, 256:320]), reads=[psb[bk]],
                                  writes=[b_St[h]])
                    if dirn == 0:
                        sc.dma(ds["of"][c0:c0 + 128, :], osb[ob][:].rearrange("p h d -> p (h d)"), reads=[b_osb[ob]])
                    else:
                        sc.op("dve", lambda e: e.tensor_tensor(out=oft[tb][:], in0=oft[tb][:], in1=osb[ob][:], op=ALU.add),
                              reads=[b_oft[tb], b_osb[ob]], writes=[b_oft[tb]])
                        sc.op("act", lambda e: e.activation(out=sqt[:], in_=oft[tb][:], func=AF.Square),
                              reads=[b_oft[tb]], writes=[b_sqt])
                        sc.op("dve", lambda e: e.reduce_sum(out=rr[:, 0:6], in_=sqt[:], axis=AX.X), reads=[b_sqt],
                              writes=[b_rr])
                        sc.op("dve", lambda e: e.tensor_scalar(out=rr[:, 0:6], in0=rr[:, 0:6], scalar1=1.0 / 64,
                                                               scalar2=EPS, op0=ALU.mult, op1=ALU.add),
                              reads=[b_rr], writes=[b_rr])
                        sc.op("act", lambda e: e.sqrt(out=rr[:, 0:6], in_=rr[:, 0:6]), reads=[b_rr], writes=[b_rr])
                        sc.op("dve", lambda e: e.reciprocal(out=rr[:, 0:6], in_=rr[:, 0:6]), reads=[b_rr], writes=[b_rr])
                        sc.op("act", lambda e: e.activation(out=sz[:].rearrange("p h d -> p (h d)"), in_=gz[tb][:, 0:384],
                                                            func=AF.Silu), reads=[b_gz[tb]], writes=[b_sz])
                        sc.op("dve", lambda e: e.tensor_tensor(out=sz[:], in0=sz[:],
                                                                in1=gdn[:].unsqueeze(1).to_broadcast([128, 6, 64]),
                                                                op=ALU.mult), reads=[b_sz, b_gdn], writes=[b_sz])
                        sc.op("dve", lambda e: e.tensor_tensor(out=oft[tb][:], in0=oft[tb][:],
                                                                in1=rr[:, 0:6].unsqueeze(2).to_broadcast([128, 6, 64]),
                                                                op=ALU.mult), reads=[b_oft[tb], b_rr], writes=[b_oft[tb]])
                        sc.op("dve", lambda e: e.tensor_tensor(out=oft[tb][:], in0=oft[tb][:], in1=sz[:], op=ALU.mult),
                              reads=[b_oft[tb], b_sz], writes=[b_oft[tb]])
                        sc.dma(ms["omix"][c0:c0 + 128, 640:1024], oft[tb][:].rearrange("p h d -> p (h d)"),
                               reads=[b_oft[tb]])
                if dirn == 0 and self.paired:
                    for h in range(6):
                        sc.dma(pr["exps"][h * 64:(h + 1) * 64, :], St[h][:], reads=[b_St[h]])
                if dirn == 1 and hook is not None:
                    while hook():
                        pass
                sc.barrier()
            if dirn == 0 and self.paired:
                sc.collective(cstack, "AllGather", pr["exps"].opt(), pr["gats"].opt(), GROUPS)


K.dn2 = _dn2


def _conv_units(self, st, tag, ms, ds, conv_w, dnc, ident, ident_b):
    sc = self.sc
    S, NT = self.S, self.NT
    GT = 4 if NT % 4 == 0 else 1
    GW = GT * 128
    ps, psb = self.ps, self.psb
    blk1 = self.sb(st, tag + "blk1", [128, 128], F32)
    cw = self.sb(st, tag + "cw", [128, 9, 5], F32)
    xin = [self.sb(st, tag + "xin%d" % i, [128, GW + 4], F32) for i in range(2)]
    y = [self.sb(st, tag + "y%d" % i, [128, GW], F32) for i in range(2)]
    ee = [self.sb(st, tag + "ee%d" % i, [128, GW], F32) for i in range(2)]
    sq2 = [self.sb(st, tag + "sq%d" % i, [128, GW], F32) for i in range(2)]
    rs2 = [self.sb(st, tag + "rs%d" % i, [128, GW], F32) for i in range(2)]
    yn = [self.sb(st, tag + "yn%d" % i, [128, GW], F32) for i in range(2)]
    tk = [self.sb(st, tag + "tk%d" % i, [128, GW], F32) for i in range(2)]
    b_blk1, b_cw = Buf("blk1"), Buf("cw")
    b_xin = [Buf("xin0"), Buf("xin1")]
    b_y = [Buf("y0"), Buf("y1")]
    b_ee = [Buf("ee0"), Buf("ee1")]
    b_sq2 = [Buf("sq0"), Buf("sq1")]
    b_rs2 = [Buf("rs0"), Buf("rs1")]
    b_yn = [Buf("yn0"), Buf("yn1")]
    b_tk = [Buf("tk0"), Buf("tk1")]
    sc.dma(blk1[:], dnc[3], writes=[b_blk1])
    for ci in range(9):
        sc.dma(cw[:, ci, :], conv_w[:, ci * 128:(ci + 1) * 128].rearrange("j c -> c j"), writes=[b_cw],
               allow_slow_non_contiguous=True)
    units = []
    cnt = [0]

    def make(g0, ci):
        tok0 = g0 * 128
        b = cnt[0] % 2
        cnt[0] += 1
        sq, rs, b_sq, b_rs = sq2[b], rs2[b], b_sq2[b], b_rs2[b]

        def p1():
            sc.dma(xin[b][:], ms["cpre"][ci * 128:(ci + 1) * 128, tok0:tok0 + GW + 4], writes=[b_xin[b]])
            sc.op("dve", lambda e: e.tensor_scalar_mul(out=y[b][:], in0=xin[b][:, 0:GW], scalar1=cw[:, ci, 0:1]),
                  reads=[b_xin[b], b_cw], writes=[b_y[b]])
            for j in range(1, 5):
                sc.op("dve", lambda e: e.scalar_tensor_tensor(out=y[b][:], in0=xin[b][:, j:j + GW],
                                                              scalar=cw[:, ci, j:j + 1], in1=y[b][:],
                                                              op0=ALU.mult, op1=ALU.add),
                      reads=[b_xin[b], b_cw, b_y[b]], writes=[b_y[b]])
            sc.op("act", lambda e: e.activation(out=ee[b][:], in_=y[b][:], func=AF.Exp, scale=-1.0),
                  reads=[b_y[b]], writes=[b_ee[b]])
            sc.op("dve", lambda e: e.tensor_scalar_add(out=ee[b][:], in0=ee[b][:], scalar1=1.0),
                  reads=[b_ee[b]], writes=[b_ee[b]])
            sc.op("dve", lambda e: e.reciprocal(out=ee[b][:], in_=ee[b][:]), reads=[b_ee[b]], writes=[b_ee[b]])
            sc.op("dve", lambda e: e.tensor_tensor(out=y[b][:], in0=y[b][:], in1=ee[b][:], op=ALU.mult),
                  reads=[b_y[b], b_ee[b]], writes=[b_y[b]])
            if ci < 6:
                sc.op("dve", lambda e: e.tensor_tensor(out=sq[:], in0=y[b][:], in1=y[b][:], op=ALU.mult),
                      reads=[b_y[b]], writes=[b_sq])

        def p2():
            if ci < 6:
                sc.op("pe", lambda e: e.matmul(ps[6][:, :GW], lhsT=blk1[:], rhs=sq[:], start=True, stop=True),
                      reads=[b_blk1, b_sq], writes=[psb[6]])
                mul = 64.0 if ci < 3 else 1.0
                sc.op("dve", lambda e: e.tensor_scalar(out=rs[:], in0=ps[6][:, :GW], scalar1=EPS, scalar2=mul,
                                                       op0=ALU.add, op1=ALU.mult),
                      reads=[psb[6]], writes=[b_rs])
                sc.op("act", lambda e: e.activation(out=rs[:], in_=rs[:], func=AF.Ln), reads=[b_rs], writes=[b_rs])
                sc.op("act", lambda e: e.activation(out=rs[:], in_=rs[:], func=AF.Exp, scale=-0.5),
                      reads=[b_rs], writes=[b_rs])
                sc.op("dve", lambda e: e.tensor_tensor(out=yn[b][:], in0=y[b][:], in1=rs[:], op=ALU.mult),
                      reads=[b_y[b], b_rs], writes=[b_yn[b]])
                if ci < 3:
                    sc.dma(ds["cqT"][ci * 128:(ci + 1) * 128, tok0:tok0 + GW], yn[b][:], reads=[b_yn[b]])
                else:
                    sc.dma(ds["ckT"][(ci - 3) * 128:(ci - 2) * 128, tok0:tok0 + GW], yn[b][:], reads=[b_yn[b]])

        def p3():
            if ci < 6:
                src_t, src_b = yn[b], b_yn[b]
            else:
                src_t, src_b = y[b], b_y[b]
            if ci >= 3:
                for tt in range(GT):
                    sc.op("pe", lambda e: e.transpose(ps[7][:, tt * 128:(tt + 1) * 128],
                                                      src_t[:, tt * 128:(tt + 1) * 128], ident[:]),
                          reads=[src_b, ident_b], writes=[psb[7]], signal=(tt == GT - 1))
                sc.op("dve", lambda e: e.tensor_copy(out=tk[b][:], in_=ps[7][:, :GW]), reads=[psb[7]], writes=[b_tk[b]])
                dst = ds["ck"] if ci < 6 else ds["cv"]
                cc = (ci - 3) % 3
                sc.dma(dst[tok0:tok0 + GW, cc * 128:(cc + 1) * 128].rearrange("(t p) c -> p t c", p=128),
                       tk[b][:].rearrange("p (t c) -> p t c", c=128), reads=[b_tk[b]])
        return [p1, p2, p3]

    for g0 in range(0, NT, GT):
        for ci in range(9):
            units.extend(make(g0, ci))
    return units


K.conv_units = _conv_units


def _win_units(self, st, tag, ms, wbias, sink):
    sc = self.sc
    S, NT = self.S, self.NT
    NKA = NT + 1 if self.paired else NT
    ps, psb = self.ps, self.psb
    qT = self.sb(st, tag + "qT", [128, S], BF16)
    kT = self.sb(st, tag + "kT", [128, NKA * 128], BF16)
    vA = self.sb(st, tag + "vA", [128, NKA, 65], BF16)
    wb = self.sb(st, tag + "wb", [128, 384], F32)
    sT = [self.sb(st, tag + "sT%d" % i, [128, 384], F32) for i in range(2)]
    pT = [self.sb(st, tag + "pT%d" % i, [128, 384], BF16) for i in range(2)]
    ow = self.sb(st, tag + "ow", [128, NT, 64], F32)
    ou = self.sb(st, tag + "ou", [128, NT, 65], F32)
    dn_ = self.sb(st, tag + "dn", [128, NT], F32)
    es = self.sb(st, tag + "es", [128, 6], F32)
    b_qT, b_kT, b_vA, b_wb, b_ow, b_es, b_ou, b_dn = (Buf(n) for n in ("qT", "kT", "vA", "wb", "ow", "es", "ou", "dn"))
    b_sT = [Buf("sT0"), Buf("sT1")]
    b_pT = [Buf("pT0"), Buf("pT1")]
    units = []
    cnt = [0]

    def setup0():
        sc.op("dve", lambda e: e.memset(qT[:], 0.0), writes=[b_qT])
        sc.op("dve", lambda e: e.memset(kT[:], 0.0), writes=[b_kT])
        sc.dma(es[:], sink.partition_broadcast(128), writes=[b_es])
        sc.op("act", lambda e: e.activation(out=es[:], in_=es[:], func=AF.Exp), reads=[b_es], writes=[b_es])
    units.append(setup0)

    def mk_head(h):
        def f():
            kh = h // 3
            if h % 3 == 0:
                sc.dma(kT[0:64, :], ms["akT"][kh * 64:(kh + 1) * 64, :], writes=[b_kT])
                sc.dma(vA[:], ms["av"][:, kh, :].rearrange("(n p) d -> p n d", p=128), writes=[b_vA])
            sc.dma(qT[0:64, :], ms["aqT"][h * 64:(h + 1) * 64, :], writes=[b_qT])
            sc.dma(wb[:], wbias[h], writes=[b_wb])
        return f

    def mk_a(h, n):
        def f():
            js = [j for j in (0, 1, 2) if 0 <= n - 1 + j < NKA]
            c0 = js[0] * 128
            w = len(js) * 128
            sb_ = n % 2
            for jj, j in enumerate(js):
                kt = n - 1 + j
                sc.op("pe", lambda e: e.matmul(ps[0][:, jj * 128:(jj + 1) * 128], lhsT=kT[:, kt * 128:(kt + 1) * 128],
                                               rhs=qT[:, n * 128:(n + 1) * 128], start=True, stop=True),
                      reads=[b_kT, b_qT], writes=[psb[0]], signal=(jj == len(js) - 1))
            sc.op("dve", lambda e: e.tensor_tensor(out=sT[sb_][:, 0:w], in0=ps[0][:, 0:w], in1=wb[:, c0:c0 + w],
                                                   op=ALU.add),
                  reads=[psb[0], b_wb], writes=[b_sT[sb_]])
            sc.op("act", lambda e: e.activation(out=pT[sb_][:, 0:w], in_=sT[sb_][:, 0:w], func=AF.Exp),
                  reads=[b_sT[sb_]], writes=[b_pT[sb_]])
        return f

    def mk_b(h, n):
        def f():
            js = [j for j in (0, 1, 2) if 0 <= n - 1 + j < NKA]
            sb_ = n % 2
            for jj, j in enumerate(js):
                kt = n - 1 + j
                sc.op("pe", lambda e: e.matmul(ps[1][:, 0:65], lhsT=pT[sb_][:, jj * 128:(jj + 1) * 128],
                                               rhs=vA[:, kt, :], start=(jj == 0), stop=(jj == len(js) - 1)),
                      reads=[b_pT[sb_], b_vA], writes=[psb[1]], signal=(jj == len(js) - 1))
            sc.op("dve", lambda e: e.tensor_copy(out=ou[:, n, :], in_=ps[1][:, 0:65]), reads=[psb[1]], writes=[b_ou])
        return f

    def both(fb, fa):
        def f():
            fb()
            fa()
        return f

    def mk_fin(h):
        def f():
            sc.op("dve", lambda e: e.tensor_scalar(out=dn_[:], in0=ou[:, :, 64], scalar1=es[:, h:h + 1], scalar2=None,
                                                   op0=ALU.add), reads=[b_ou, b_es], writes=[b_dn])
            sc.op("dve", lambda e: e.reciprocal(out=dn_[:], in_=dn_[:]), reads=[b_dn], writes=[b_dn])
            sc.op("dve", lambda e: e.tensor_tensor(out=ow[:], in0=ou[:, :, 0:64],
                                                   in1=dn_[:].unsqueeze(2).to_broadcast([128, NT, 64]), op=ALU.mult),
                  reads=[b_ou, b_dn], writes=[b_ow])
            sc.dma(ms["omix"][:, h * 64:(h + 1) * 64].rearrange("(n p) d -> p n d", p=128), ow[:], reads=[b_ow])
        return f

    for h in range(6):
        units.append(mk_head(h))
        units.append(mk_a(h, 0))
        for n in range(1, NT):
            units.append(both(mk_b(h, n - 1), mk_a(h, n)))
        units.append(mk_b(h, NT - 1))
        units.append(mk_fin(h))
    return units


K.win_units = _win_units

import numpy as np, ml_dtypes
BF = ml_dtypes.bfloat16
def diff_consts(S, SK=None):
    SK = SK or S
    pos = np.arange(SK)
    H = (pos // 128) * 128.0
    L = (pos % 128) * 1.0
    daq = np.zeros((4, 4, SK), np.float32); dakp = np.zeros((4, 4, SK), np.float32)
    dbd = np.zeros((4, 128, 128), np.float32)
    for h in range(4):
        s = 2.0 ** (-8.0 * (h + 1) / 4)
        daq[h, 0] = -s * H; daq[h, 1] = -s * L; daq[h, 2] = 1; daq[h, 3] = 1
        dakp[h, 0] = 1; dakp[h, 1] = 1; dakp[h, 2] = s * H; dakp[h, 3] = s * L
        kk = np.arange(128)[:, None]; qq = np.arange(128)[None, :]
        dbd[h] = -s * np.abs(qq - kk)
    daq = daq[:, :, :S]
    c = dict(daq=np.ascontiguousarray(daq).astype(BF), dakp=dakp.astype(BF), dakm=(-dakp).astype(BF), dbd=dbd.astype(BF),
             identb=np.eye(128, dtype=np.float32).astype(BF))
    assert np.array_equal(c["daq"].astype(np.float32), daq) and np.array_equal(c["dakp"].astype(np.float32), dakp)
    assert np.array_equal(c["dbd"].astype(np.float32), dbd)
    return c

def win_consts():
    wb = np.zeros((6, 128, 384), np.float32)
    k = np.arange(128)[:, None]; q = np.arange(128)[None, :]
    for h in range(6):
        s = np.float32(2.0) ** np.float32(-8.0 * (h + 1) / 6)
        for j in range(3):
            rel = (j - 1) * 128 + k - q
            b = np.where(np.abs(rel) <= 128, -np.float32(s) * np.abs(rel).astype(np.float32), np.float32(-30000.0))
            wb[h, :, j * 128:(j + 1) * 128] = b
    return dict(wbias=wb)

def dn_consts():
    c = np.zeros((10, 128, 128), np.float32)
    p = np.arange(128)[:, None]; f = np.arange(128)[None, :]
    c[0] = 1.0
    c[1] = (p <= f)
    c[2] = (p >= f)
    c[3] = ((p // 64) == (f // 64))
    BIG = 30000.0
    c[4] = np.where(p > f, 0.0, -BIG)
    c[5] = np.where(f > p, 0.0, -BIG)
    c[6] = np.where(f >= p, 0.0, -BIG)
    c[7] = np.where(p < f, 0.0, -BIG)
    c[8] = np.where(f < p, 0.0, -BIG)
    c[9] = np.where(f <= p, 0.0, -BIG)
    return dict(dnc=c)


_CACHE = {}
N_CORES = 8


def _get_built(S_loc):
    if S_loc not in _CACHE:
        _CACHE[S_loc] = build_full(S_loc, 2, paired=True)
    return _CACHE[S_loc]


def kernel(**inputs):
    from concourse.bass_utils import run_bass_kernel_spmd
    x = np.asarray(inputs["x"])
    B, S, _ = x.shape
    assert 2 * B == N_CORES
    k = _get_built(S // 2)
    in_maps = pair_feeds(inputs, S)
    res = run_bass_kernel_spmd(k.nc, in_maps, core_ids=list(range(N_CORES)))
    return pair_gather(res.results, B, S)
```

```python
import numpy as np
import concourse.bass as bass
import concourse.mybir as mybir

F32 = mybir.dt.float32
BF16 = mybir.dt.bfloat16
AF = mybir.ActivationFunctionType
ALU = mybir.AluOpType
AX = mybir.AxisListType


class Buf:
    __slots__ = ("name", "w", "rs", "excl")

    def __init__(self, name, excl=False):
        self.name = name
        self.excl = excl
        self.w = None
        self.rs = []


class Sched:
    NDMA = 24

    def __init__(self, nc, stack):
        self.nc = nc
        self.eng = {"pe": nc.tensor, "act": nc.scalar, "dve": nc.vector, "pool": nc.gpsimd, "sp": nc.sync}
        self.sem = {}
        self.cnt = {}
        for k in self.eng:
            self.sem[k] = stack.enter_context(nc.semaphore("sem_" + k))
            self.cnt[k] = 0
        self.dsem = [stack.enter_context(nc.semaphore("dsem%d" % i)) for i in range(self.NDMA)]
        self.dgen = [0] * self.NDMA
        self.qslots = {"sp": list(range(0, 16)), "pool": list(range(16, 20)), "act": list(range(20, 24))}
        self.qnext = {"sp": 0, "pool": 0, "act": 0}
        self.waited = {k: {} for k in self.eng}
        self.pending_pe = False
        self.n_ins = 0
        self.n_wait = 0

    def _semobj(self, key):
        if isinstance(key, int):
            return self.dsem[key]
        return self.sem[key]

    def _need(self, e, evs):
        best = {}
        for ev in evs:
            if ev is None:
                continue
            k, v = ev
            if k == e and e in ("pe", "sp"):
                continue
            if best.get(k, 0) < v:
                best[k] = v
        w = self.waited[e]
        for k, v in best.items():
            if w.get(k, 0) >= v:
                continue
            self.eng[e].wait_ge(self._semobj(k), v)
            self.n_wait += 1
            w[k] = v

    def _deps(self, reads, writes, e=None):
        evs = []
        for b in reads:
            evs.append(b.w)
            if b.excl:
                evs.extend(r for r in b.rs if r[0] != e)
        for b in writes:
            evs.append(b.w)
            evs.extend(b.rs)
        return evs

    def _commit(self, ev, reads, writes):
        for b in reads:
            b.rs.append(ev)
            if len(b.rs) > 64:
                mx = {}
                for k, v in b.rs:
                    if mx.get(k, 0) < v:
                        mx[k] = v
                b.rs = list(mx.items())
        for b in writes:
            b.w = ev
            b.rs = []

    def op(self, e, fn, reads=(), writes=(), signal=True):
        if e != "pe":
            assert not self.pending_pe, "non-signaling PE op must be followed by signaling PE op"
        self._need(e, self._deps(reads, writes, e))
        ins = fn(self.eng[e])
        self.n_ins += 1
        if e == "pe":
            self.pending_pe = not signal
        if signal:
            self.cnt[e] += 1
            ins.then_inc(self.sem[e], 1)
            ev = (e, self.cnt[e])
        else:
            ev = (e, self.cnt[e] + 1)
        self._commit(ev, reads, writes)
        return ins

    def dma(self, out, in_, reads=(), writes=(), q="sp", **kw):
        assert not self.pending_pe
        sl = self.qslots[q]
        slot = sl[self.qnext[q] % len(sl)]
        self.qnext[q] += 1
        evs = self._deps(reads, writes)
        if self.dgen[slot] > 0:
            evs.append((slot, 16 * self.dgen[slot]))
        self._need(q, evs)
        self.dgen[slot] += 1
        ins = self.eng[q].dma_start(out=out, in_=in_, **kw)
        ins.then_inc(self.dsem[slot], 16)
        self.n_ins += 1
        ev = (slot, 16 * self.dgen[slot])
        self._commit(ev, reads, writes)
        return ins

    def barrier(self):
        evs = [(k, self.cnt[k]) for k in self.eng if self.cnt[k] > 0]
        evs += [(i, 16 * self.dgen[i]) for i in range(self.NDMA) if self.dgen[i] > 0]
        for e in self.eng:
            w = self.waited[e]
            for k, v in evs:
                if k == e and e in ("pe", "sp"):
                    continue
                if w.get(k, 0) >= v:
                    continue
                self.eng[e].wait_ge(self._semobj(k), v)
                w[k] = v

    def collective(self, stack, kind, in_ap, out_ap, groups):
        import concourse.mybir as mybir
        self.barrier()
        sem = stack.enter_context(self.nc.semaphore("ccsem%d" % self.n_ins))
        g = self.eng["pool"]
        g.collective_compute(kind, mybir.AluOpType.bypass, replica_groups=groups,
                             ins=[in_ap], outs=[out_ap]).then_inc(sem)
        g.wait_ge(sem, 1)
        self.n_ins += 1
        self.cnt["pool"] += 1
        g.engine_nop().then_inc(self.sem["pool"], 1)
        self.barrier()

    def finish(self, out_bufs):
        self.barrier()

import numpy as np
from contextlib import ExitStack
import concourse.bass as bass
import concourse.mybir as mybir

D = 1024
DFF = 2752
NFC = 22
EPS = 1e-6


class K:
    def __init__(self, S, depth=2, paired=False):
        self.S = S
        self.paired = paired
        self.SK = 2 * S if paired else S
        self.NTK = self.SK // 128
        self.NT = S // 128
        self.depth = depth
        self.nc = bass.Bass("TRN2", target_bir_lowering=False)
        self.stack = ExitStack()
        self.sc = Sched(self.nc, self.stack)
        self.ins = {}
        nc = self.nc
        self.ps = []
        self.psb = []
        for i in range(8):
            t = self.stack.enter_context(nc.psum_tensor("ps%d" % i, [128, 512], F32))
            self.ps.append(t)
            self.psb.append(Buf("ps%d" % i, excl=True))

    def inp(self, name, shape, dt=F32):
        t = self.nc.dram_tensor(name, list(shape), dt, kind="ExternalInput").ap()
        self.ins[name] = t
        return t

    def outp(self, name, shape, dt=F32):
        return self.nc.dram_tensor(name, list(shape), dt, kind="ExternalOutput").ap()

    def scratch(self, name, shape, dt=F32):
        return self.nc.dram_tensor(name, list(shape), dt, kind="Internal").ap()

    def sb(self, st, name, shape, dt=F32):
        return st.enter_context(self.nc.sbuf_tensor(name, list(shape), dt))

    def ffn_phase(self, tag, w_in, w_out, g, src, src_bufs, dst, dst_bufs, ident, ident_b):
        sc = self.sc
        S, NT = self.S, self.NT
        GT = 4 if NT % 4 == 0 else 1
        GW = GT * 128
        NG = NT // GT
        with ExitStack() as st:
            w1 = self.sb(st, tag + "w1", [128, 8, 2 * DFF], BF16)
            w2 = self.sb(st, tag + "w2", [128, NFC, D], BF16)
            gB = self.sb(st, tag + "gB", [128, D], F32)
            xt = [self.sb(st, tag + "xt%d" % i, [128, D], F32) for i in range(2)]
            xr = [self.sb(st, tag + "xr%d" % i, [128, D], F32) for i in range(2)]
            xn = self.sb(st, tag + "xn", [128, D], F32)
            xnT = [self.sb(st, tag + "xnT%d" % i, [128, 8, GW], BF16) for i in range(2)]
            hT = self.sb(st, tag + "hT", [128, NFC, GW], BF16)
            sg = [self.sb(st, tag + "sg%d" % i, [128, GW], F32) for i in range(2)]
            stat = self.sb(st, tag + "stat", [128, 8], F32)
            b_w1 = [Buf("w1_%d" % k) for k in range(8)]
            b_w2 = [Buf("w2_%d" % c) for c in range(NFC)]
            b_gB = Buf("gB")
            b_xt = [Buf("xt0"), Buf("xt1")]
            b_xr = [Buf("xr0"), Buf("xr1")]
            b_xn, b_stat = Buf("xn"), Buf("stat")
            b_xnT = [Buf("xnT0"), Buf("xnT1")]
            b_hT = [Buf("hT%d" % c) for c in range(NFC)]
            b_sg = [Buf("sg0"), Buf("sg1")]
            ps, psb = self.ps, self.psb
            for kc in range(8):
                for hf in range(2):
                    sc.dma(w1[:, kc, hf * DFF:(hf + 1) * DFF], w_in[kc * 128:(kc + 1) * 128, hf * DFF:(hf + 1) * DFF],
                           writes=[b_w1[kc]], q="pool")
            for c in range(NFC):
                cw = min(128, DFF - c * 128)
                sc.dma(w2[:cw, c, :], w_out[c * 128:c * 128 + cw, :], writes=[b_w2[c]], q="pool")
            sc.dma(gB[:], g.partition_broadcast(128), writes=[b_gB])
            tcount = [0]

            def ln_tile(gi, tt):
                t = gi * GT + tt
                xb = tcount[0] % 2
                tcount[0] += 1
                xq = xnT[gi % 2]
                bq = b_xnT[gi % 2]
                sc.dma(xt[xb][:], src[t * 128:(t + 1) * 128, :], reads=[src_bufs[t]], writes=[b_xt[xb]])
                sc.op("act", lambda e: e.activation(out=xn[:], in_=xt[xb][:], func=AF.Square, accum_out=stat[:, 0:1]),
                      reads=[b_xt[xb]], writes=[b_xn, b_stat])
                sc.op("dve", lambda e: e.tensor_scalar(out=stat[:, 1:2], in0=stat[:, 0:1], scalar1=1.0 / D,
                                                       scalar2=EPS, op0=ALU.mult, op1=ALU.add),
                      reads=[b_stat], writes=[b_stat])
                sc.op("act", lambda e: e.sqrt(out=stat[:, 3:4], in_=stat[:, 1:2]), reads=[b_stat], writes=[b_stat])
                sc.op("dve", lambda e: e.reciprocal(out=stat[:, 2:3], in_=stat[:, 3:4]), reads=[b_stat], writes=[b_stat])
                sc.op("dve", lambda e: e.scalar_tensor_tensor(out=xn[:], in0=xt[xb][:], scalar=stat[:, 2:3],
                                                              in1=gB[:], op0=ALU.mult, op1=ALU.mult),
                      reads=[b_xt[xb], b_stat, b_gB], writes=[b_xn])
                for kc in range(8):
                    bank = 6 + kc // 4
                    sc.op("pe", lambda e: e.transpose(ps[bank][:, (kc % 4) * 128:(kc % 4 + 1) * 128],
                                                      xn[:, kc * 128:(kc + 1) * 128], ident[:]),
                          reads=[b_xn, ident_b], writes=[psb[bank]], signal=(kc % 4 == 3))
                sc.op("act", lambda e: e.copy(out=xq[:, 0:4, tt * 128:(tt + 1) * 128],
                                              in_=ps[6][:, :].rearrange("p (k t) -> p k t", k=4)),
                      reads=[psb[6]], writes=[bq])
                sc.op("dve", lambda e: e.tensor_copy(out=xq[:, 4:8, tt * 128:(tt + 1) * 128],
                                                     in_=ps[7][:, :].rearrange("p (k t) -> p k t", k=4)),
                      reads=[psb[7]], writes=[bq])

            for tt in range(GT):
                ln_tile(0, tt)
            for gi in range(NG):
                g0 = gi * GT
                xq = xnT[gi % 2]
                bq = b_xnT[gi % 2]
                for c in range(NFC):
                    cw = min(128, DFF - c * 128)
                    pg, pu = 2 + 2 * (c % 2), 3 + 2 * (c % 2)
                    for kc in range(8):
                        sc.op("pe", lambda e: e.matmul(ps[pg][:cw, :GW], lhsT=w1[:, kc, c * 128:c * 128 + cw],
                                                       rhs=xq[:, kc, :], start=(kc == 0), stop=(kc == 7)),
                              reads=[b_w1[kc], bq], writes=[psb[pg]], signal=(kc == 7))
                    for kc in range(8):
                        sc.op("pe", lambda e: e.matmul(ps[pu][:cw, :GW],
                                                       lhsT=w1[:, kc, DFF + c * 128:DFF + c * 128 + cw],
                                                       rhs=xq[:, kc, :], start=(kc == 0), stop=(kc == 7)),
                              reads=[b_w1[kc], bq], writes=[psb[pu]], signal=(kc == 7))
                    sc.op("act", lambda e: e.activation(out=sg[c % 2][:cw, :], in_=ps[pg][:cw, :GW], func=AF.Silu),
                          reads=[psb[pg]], writes=[b_sg[c % 2]])
                    sc.op("dve", lambda e: e.tensor_tensor(out=hT[:cw, c, :], in0=ps[pu][:cw, :GW],
                                                           in1=sg[c % 2][:cw, :], op=ALU.mult),
                          reads=[psb[pu], b_sg[c % 2]], writes=[b_hT[c]])
                for tt in range(GT):
                    t = g0 + tt
                    rb = t % 2
                    sc.dma(xr[rb][:], src[t * 128:(t + 1) * 128, :], reads=[src_bufs[t]], writes=[b_xr[rb]])
                    for hf in range(2):
                        for c in range(NFC):
                            cw = min(128, DFF - c * 128)
                            sc.op("pe", lambda e: e.matmul(ps[hf][:, :], lhsT=hT[:cw, c, tt * 128:(tt + 1) * 128],
                                                           rhs=w2[:cw, c, hf * 512:(hf + 1) * 512],
                                                           start=(c == 0), stop=(c == NFC - 1)),
                                  reads=[b_hT[c], b_w2[c]], writes=[psb[hf]], signal=(c == NFC - 1))
                    if gi + 1 < NG:
                        ln_tile(gi + 1, tt)
                    for hf in range(2):
                        sc.op("dve", lambda e: e.scalar_tensor_tensor(
                            out=xr[rb][:, hf * 512:(hf + 1) * 512], in0=ps[hf][:, :], scalar=0.5,
                            in1=xr[rb][:, hf * 512:(hf + 1) * 512], op0=ALU.mult, op1=ALU.add),
                              reads=[psb[hf], b_xr[rb]], writes=[b_xr[rb]])
                    sc.dma(dst[t * 128:(t + 1) * 128, :], xr[rb][:], reads=[b_xr[rb]], writes=[dst_bufs[t]], q="pool")
            sc.barrier()


HD = 64
MIX_IN = 2968
C_AQ, C_AK, C_AV = 0, 384, 512
C_BQ, C_BK, C_BV = 640, 896, 1152
C_CQKV, C_CZ, C_CB, C_CA = 1408, 2560, 2944, 2956
FM_CHUNKS = ([(C_AQ + 128 * i, "aq", i) for i in range(3)] + [(C_AK, "ak", 0)] +
             [(C_BQ + 128 * i, "bq", i) for i in range(2)] + [(C_BK + 128 * i, "bk", i) for i in range(2)] +
             [(C_CQKV + 128 * i, "c", i) for i in range(9)])


def _mix_scratch(self):
    S = self.S
    SK = self.SK
    SA = S + 128 if self.paired else S
    d = {}
    d["aqT"] = self.scratch("aqT", [384, S], BF16)
    d["akT"] = self.scratch("akT", [128, SA], BF16)
    d["av"] = self.scratch("av", [SA, 2, 65], BF16)
    d["bqT"] = self.scratch("bqT", [256, S], BF16)
    d["bkT"] = self.scratch("bkT", [256, SK], BF16)
    d["bv"] = self.scratch("bv", [SK, 4, 65], BF16)
    d["cpre"] = self.scratch("cpre", [1152, S + 4], F32)
    d["cz"] = self.scratch("cz", [S, 408], F32)
    d["omix"] = self.scratch("omix", [S, D], F32)
    return d


K.mix_scratch = _mix_scratch


def _inproj_phase(self, tag, w_mi, g, src, src_bufs, ms, ident, ident_b):
    sc = self.sc
    S, NT = self.S, self.NT
    GT = 4 if NT % 4 == 0 else 1
    GW = GT * 128
    ps, psb = self.ps, self.psb
    with ExitStack() as st:
        wm = self.sb(st, tag + "wm", [128, 8, MIX_IN], BF16)
        gB = self.sb(st, tag + "gB", [128, D], F32)
        xt = [self.sb(st, tag + "xt%d" % i, [128, D], F32) for i in range(2)]
        xn = self.sb(st, tag + "xn", [128, D], F32)
        junk = self.sb(st, tag + "junk", [128, D], F32)
        xnT = self.sb(st, tag + "xnT", [128, 8, GW], BF16)
        stat = self.sb(st, tag + "stat", [128, 8], F32)
        ob16 = [self.sb(st, tag + "ob16_%d" % i, [128, GW], BF16) for i in range(3)]
        of32 = [self.sb(st, tag + "of32_%d" % i, [128, GW], F32) for i in range(3)]
        tv = [self.sb(st, tag + "tv%d" % i, [128, 6, 65], BF16) for i in range(2)]
        zt = self.sb(st, tag + "zt", [128, 2], F32)
        tz = [self.sb(st, tag + "tz%d" % i, [128, 408], F32) for i in range(2)]
        b_wm = [Buf("wm%d" % k) for k in range(8)]
        b_gB = Buf("gB")
        b_xt = [Buf("xt0"), Buf("xt1")]
        b_xn, b_junk, b_xnT, b_stat = Buf("xn"), Buf("junk"), Buf("xnT"), Buf("stat")
        b_ob16 = [Buf("ob16_%d" % i) for i in range(3)]
        b_of32 = [Buf("of32_%d" % i) for i in range(3)]
        b_tv = [Buf("tv0"), Buf("tv1")]
        b_tz = [Buf("tz0"), Buf("tz1")]
        for kc in range(8):
            sc.dma(wm[:, kc, :], w_mi[kc * 128:(kc + 1) * 128, :], writes=[b_wm[kc]], q="pool")
        sc.dma(gB[:], g.partition_broadcast(128), writes=[b_gB])
        b_zt = Buf("zt")
        sc.op("pool", lambda e: e.memset(zt[:], 0.0), writes=[b_zt])
        for i in range(9):
            sc.dma(ms["cpre"][i * 128:(i + 1) * 128, 0:2], zt[:, :], reads=[b_zt])
            if not self.paired:
                sc.dma(ms["cpre"][i * 128:(i + 1) * 128, S + 2:S + 4], zt[:, :], reads=[b_zt])
        for i in range(2):
            sc.op("pool", lambda e: e.memset(tv[i][:], 1.0), writes=[b_tv[i]])
        ti = 0
        n16 = n32 = 0
        for g0 in range(0, NT, GT):
            for tt in range(GT):
                t = g0 + tt
                xb = ti % 2
                ti += 1
                sc.dma(xt[xb][:], src[t * 128:(t + 1) * 128, :], reads=[src_bufs[t]], writes=[b_xt[xb]])
                sc.op("act", lambda e: e.activation(out=junk[:], in_=xt[xb][:], func=AF.Square,
                                                    accum_out=stat[:, 0:1]),
                      reads=[b_xt[xb]], writes=[b_junk, b_stat])
                sc.op("dve", lambda e: e.tensor_scalar(out=stat[:, 1:2], in0=stat[:, 0:1], scalar1=1.0 / D,
                                                       scalar2=EPS, op0=ALU.mult, op1=ALU.add),
                      reads=[b_stat], writes=[b_stat])
                sc.op("act", lambda e: e.sqrt(out=stat[:, 3:4], in_=stat[:, 1:2]), reads=[b_stat], writes=[b_stat])
                sc.op("dve", lambda e: e.reciprocal(out=stat[:, 2:3], in_=stat[:, 3:4]),
                      reads=[b_stat], writes=[b_stat])
                sc.op("dve", lambda e: e.scalar_tensor_tensor(out=xn[:], in0=xt[xb][:], scalar=stat[:, 2:3],
                                                              in1=gB[:], op0=ALU.mult, op1=ALU.mult),
                      reads=[b_xt[xb], b_stat, b_gB], writes=[b_xn])
                for kc in range(8):
                    bank = kc // 4
                    sc.op("pe", lambda e: e.transpose(ps[bank][:, (kc % 4) * 128:(kc % 4 + 1) * 128],
                                                      xn[:, kc * 128:(kc + 1) * 128], ident[:]),
                          reads=[b_xn, ident_b], writes=[psb[bank]], signal=(kc % 4 == 3))
                sc.op("act", lambda e: e.copy(out=xnT[:, 0:4, tt * 128:(tt + 1) * 128],
                                              in_=ps[0][:, :].rearrange("p (k t) -> p k t", k=4)),
                      reads=[psb[0]], writes=[b_xnT])
                sc.op("dve", lambda e: e.tensor_copy(out=xnT[:, 4:8, tt * 128:(tt + 1) * 128],
                                                     in_=ps[1][:, :].rearrange("p (k t) -> p k t", k=4)),
                      reads=[psb[1]], writes=[b_xnT])
            tok0 = g0 * 128
            for ci, (c0, kind, idx) in enumerate(FM_CHUNKS):
                pb = 2 + ci % 3
                for kc in range(8):
                    sc.op("pe", lambda e: e.matmul(ps[pb][:, :GW], lhsT=wm[:, kc, c0:c0 + 128], rhs=xnT[:, kc, :],
                                                   start=(kc == 0), stop=(kc == 7)),
                          reads=[b_wm[kc], b_xnT], writes=[psb[pb]], signal=(kc == 7))
                if kind == "c":
                    o = n32 % 3
                    n32 += 1
                    sc.op("dve" if ci % 2 else "act",
                          (lambda e: e.tensor_copy(out=of32[o][:, :], in_=ps[pb][:, :GW])) if ci % 2 else
                          (lambda e: e.copy(out=of32[o][:, :], in_=ps[pb][:, :GW])),
                          reads=[psb[pb]], writes=[b_of32[o]])
                    sc.dma(ms["cpre"][idx * 128:(idx + 1) * 128, 2 + tok0:2 + tok0 + GW], of32[o][:, :],
                           reads=[b_of32[o]])
                else:
                    o = n16 % 3
                    n16 += 1
                    scale = {"aq": HD ** -0.5, "bq": 32 ** -0.5, "ak": 1.0, "bk": 1.0}[kind]
                    sc.op("act", lambda e: e.mul(out=ob16[o][:, :], in_=ps[pb][:, :GW], mul=scale),
                          reads=[psb[pb]], writes=[b_ob16[o]])
                    dst = {"aq": ms["aqT"], "ak": ms["akT"], "bq": ms["bqT"], "bk": ms["bkT"]}[kind]
                    sc.dma(dst[idx * 128:(idx + 1) * 128, tok0:tok0 + GW], ob16[o][:, :], reads=[b_ob16[o]])
            for tt in range(GT):
                t = g0 + tt
                r0 = t * 128
                o = t % 2
                for (pb, c0, cw) in ((5, C_AV, 128), (6, C_BV, 256), (7, C_CZ, 408)):
                    for kc in range(8):
                        sc.op("pe", lambda e: e.matmul(ps[pb][:, :cw], lhsT=xnT[:, kc, tt * 128:(tt + 1) * 128],
                                                       rhs=wm[:, kc, c0:c0 + cw], start=(kc == 0), stop=(kc == 7)),
                              reads=[b_wm[kc], b_xnT], writes=[psb[pb]], signal=(kc == 7))
                sc.op("act", lambda e: e.copy(out=tv[o][:, 0:2, 0:64],
                                              in_=ps[5][:, 0:128].rearrange("p (h d) -> p h d", h=2)),
                      reads=[psb[5]], writes=[b_tv[o]])
                sc.op("dve", lambda e: e.tensor_copy(out=tv[o][:, 2:6, 0:64],
                                                     in_=ps[6][:, 0:256].rearrange("p (h d) -> p h d", h=4)),
                      reads=[psb[6]], writes=[b_tv[o]])
                sc.op("act", lambda e: e.copy(out=tz[o][:, :], in_=ps[7][:, 0:408]),
                      reads=[psb[7]], writes=[b_tz[o]])
                sc.dma(ms["av"][r0:r0 + 128, :, :], tv[o][:, 0:2, :], reads=[b_tv[o]])
                sc.dma(ms["bv"][r0:r0 + 128, :, :], tv[o][:, 2:6, :], reads=[b_tv[o]])
                sc.dma(ms["cz"][r0:r0 + 128, :], tz[o][:, :], reads=[b_tz[o]])
        sc.barrier()


K.inproj_phase = _inproj_phase


def _diffattn_phase(self, tag, ms, cst, dlam, dg, lam_init, identf, b_identf, hook=None):
    sc = self.sc
    S, NT = self.S, self.NT
    SK, NTK = self.SK, self.NTK
    GT = 4 if NT % 4 == 0 else 1
    GW = GT * 128
    NG = NT // GT
    ps, psb = self.ps, self.psb
    with ExitStack() as st:
        kTa = self.sb(st, tag + "kTa", [128, SK], BF16)
        kTb = self.sb(st, tag + "kTb", [128, SK], BF16)
        qTa = self.sb(st, tag + "qTa", [128, S], BF16)
        vA = self.sb(st, tag + "vA", [128, NTK, 65], BF16)
        bd = self.sb(st, tag + "bd", [128, 128], BF16)
        idb = self.sb(st, tag + "idb", [128, 128], BF16)
        oT = self.sb(st, tag + "oT", [65, GW], F32)
        b_oT = Buf("oT")
        pT = [self.sb(st, tag + "pT%d" % i, [128, GW], BF16) for i in range(3)]
        om = [self.sb(st, tag + "om%d" % i, [128, NT, 64], F32) for i in range(2)]
        dd = self.sb(st, tag + "dd", [128, NT, 64], F32)
        sq = self.sb(st, tag + "sq", [128, NT, 64], F32)
        ssq = self.sb(st, tag + "ssq", [128, NT], F32)
        rec = self.sb(st, tag + "rec", [128, 8], F32)
        lmb = self.sb(st, tag + "lmb", [128, 128], F32)
        lw = self.sb(st, tag + "lw", [128, 64], F32)
        ls = self.sb(st, tag + "ls", [128, 8], F32)
        gd = self.sb(st, tag + "gd", [128, 64], F32)
        b_kTa, b_kTb, b_qTa, b_vA, b_bd, b_idb = (Buf(n) for n in ("kTa", "kTb", "qTa", "vA", "bd", "idb"))
        b_pT = [Buf("pT%d" % i) for i in range(3)]
        b_om = [Buf("om0"), Buf("om1")]
        b_dd, b_sq, b_ssq, b_rec, b_lmb, b_lw, b_ls, b_gd = (Buf(n) for n in
                                                             ("dd", "sq", "ssq", "rec", "lmb", "lw", "ls", "gd"))
        sc.dma(idb[:], cst["identb"][:, :], writes=[b_idb])
        sc.op("dve", lambda e: e.memset(kTa[:], 0.0), writes=[b_kTa])
        sc.op("dve", lambda e: e.memset(kTb[:], 0.0), writes=[b_kTb])
        sc.op("dve", lambda e: e.memset(qTa[:], 0.0), writes=[b_qTa])
        sc.dma(lmb[:], dlam.rearrange("a b -> (a b)").partition_broadcast(128), writes=[b_lmb])
        sc.dma(gd[:], dg.partition_broadcast(128), writes=[b_gd])
        sc.op("dve", lambda e: e.tensor_tensor(out=lw[:, 0:32], in0=lmb[:, 0:32], in1=lmb[:, 32:64], op=ALU.mult),
              reads=[b_lmb], writes=[b_lw])
        sc.op("dve", lambda e: e.tensor_tensor(out=lw[:, 32:64], in0=lmb[:, 64:96], in1=lmb[:, 96:128], op=ALU.mult),
              reads=[b_lmb], writes=[b_lw])
        sc.op("dve", lambda e: e.reduce_sum(out=ls[:, 0:2], in_=lw[:, :].rearrange("p (a b) -> p a b", a=2),
                                            axis=AX.X), reads=[b_lw], writes=[b_ls])
        sc.op("act", lambda e: e.activation(out=ls[:, 2:4], in_=ls[:, 0:2], func=AF.Exp), reads=[b_ls], writes=[b_ls])
        sc.op("dve", lambda e: e.tensor_tensor(out=ls[:, 4:5], in0=ls[:, 3:4], in1=ls[:, 2:3], op=ALU.subtract),
              reads=[b_ls], writes=[b_ls])
        sc.op("dve", lambda e: e.tensor_scalar_add(out=ls[:, 5:6], in0=ls[:, 4:5], scalar1=-lam_init),
              reads=[b_ls], writes=[b_ls])
        sc.op("dve", lambda e: e.tensor_scalar_mul(out=gd[:], in0=gd[:], scalar1=1.0 - lam_init),
              reads=[b_gd], writes=[b_gd])
        blk = 0
        accn = 0
        pending = []
        for h in range(4):
            sc.dma(vA[:], ms["bv"][:, h, :].rearrange("(n p) d -> p n d", p=128), reads=[], writes=[b_vA])
            sc.dma(bd[:], cst["dbd"][h], writes=[b_bd])
            for m in range(2):
                r0 = (h * 2 + m) * 32
                sc.dma(kTa[0:32, :], ms["bkT"][r0:r0 + 32, :], writes=[b_kTa])
                sc.dma(kTa[32:36, :], cst["dakp"][h], writes=[b_kTa])
                sc.dma(kTb[0:32, :], ms["bkT"][r0:r0 + 32, :], writes=[b_kTb])
                sc.dma(kTb[32:36, :], cst["dakm"][h], writes=[b_kTb])
                sc.dma(qTa[0:32, :], ms["bqT"][r0:r0 + 32, :], writes=[b_qTa])
                sc.dma(qTa[32:36, :], cst["daq"][h], writes=[b_qTa])
                for qg in range(NG):
                    q0 = qg * GW
                    accb = 2 + accn % 2
                    accn += 1

                    def qk(kt, bank):
                        k0 = kt * 128
                        if kt < qg * GT:
                            sc.op("pe", lambda e: e.matmul(ps[bank][:, :GW], lhsT=kTa[:, k0:k0 + 128],
                                                           rhs=qTa[:, q0:q0 + GW], start=True, stop=True),
                                  reads=[b_kTa, b_qTa], writes=[psb[bank]])
                        elif kt >= (qg + 1) * GT:
                            sc.op("pe", lambda e: e.matmul(ps[bank][:, :GW], lhsT=kTb[:, k0:k0 + 128],
                                                           rhs=qTa[:, q0:q0 + GW], start=True, stop=True),
                                  reads=[b_kTb, b_qTa], writes=[psb[bank]])
                        else:
                            for i in range(GT):
                                qt = qg * GT + i
                                cs = slice(i * 128, (i + 1) * 128)
                                qs = slice(q0 + i * 128, q0 + (i + 1) * 128)
                                last = (i == GT - 1)
                                if kt < qt:
                                    sc.op("pe", lambda e: e.matmul(ps[bank][:, cs], lhsT=kTa[:, k0:k0 + 128],
                                                                   rhs=qTa[:, qs], start=True, stop=True),
                                          reads=[b_kTa, b_qTa], writes=[psb[bank]], signal=last)
                                elif kt > qt:
                                    sc.op("pe", lambda e: e.matmul(ps[bank][:, cs], lhsT=kTb[:, k0:k0 + 128],
                                                                   rhs=qTa[:, qs], start=True, stop=True),
                                          reads=[b_kTb, b_qTa], writes=[psb[bank]], signal=last)
                                else:
                                    sc.op("pe", lambda e: e.matmul(ps[bank][:, cs], lhsT=kTa[0:32, k0:k0 + 128],
                                                                   rhs=qTa[0:32, qs], start=True, stop=False),
                                          reads=[b_kTa, b_qTa], writes=[psb[bank]], signal=False)
                                    sc.op("pe", lambda e: e.matmul(ps[bank][:, cs], lhsT=idb[:, :], rhs=bd[:, :],
                                                                   start=False, stop=True),
                                          reads=[b_idb, b_bd], writes=[psb[bank]], signal=last)

                    def expv(kt, bank, pi):
                        sc.op("act", lambda e: e.activation(out=pT[pi][:, :], in_=ps[bank][:, :GW], func=AF.Exp),
                              reads=[psb[bank]], writes=[b_pT[pi]])
                        sc.op("pe", lambda e: e.matmul(ps[accb][0:65, :GW], lhsT=vA[:, kt, :], rhs=pT[pi][:, :],
                                                       start=(kt == 0), stop=(kt == NTK - 1)),
                              reads=[b_pT[pi], b_vA], writes=[psb[accb]])

                    for step in range(NTK + 1):
                        if step == 3 and pending:
                            pending.pop(0)()
                        if hook is not None and step in (8, 24, 40, 56):
                            hook()
                        if step < NTK:
                            qk(step, (blk + step) % 2)
                        if step >= 1:
                            expv(step - 1, (blk + step - 1) % 2, (blk + step - 1) % 3)
                    blk += NTK
                    def make_fin(accb=accb, qg=qg, m=m, accn=accn):
                        def fin():
                            sc.op("act", lambda e: e.copy(out=oT[:, :GW], in_=ps[accb][0:65, :GW]), reads=[psb[accb]], writes=[b_oT])
                            tb = 4 + (accn % 2)
                            for i in range(GT):
                                sc.op("pe", lambda e: e.transpose(ps[tb][:, i * 65:(i + 1) * 65], oT[:, i * 128:(i + 1) * 128],
                                                                  identf[0:65, 0:65]),
                                      reads=[b_oT, b_identf], writes=[psb[tb]], signal=(i == GT - 1))
                            tv = ps[tb][:, 0:GT * 65].rearrange("p (i d) -> p i d", d=65)
                            sc.op("dve", lambda e: e.reciprocal(out=rec[:, 0:GT], in_=tv[:, :, 64]),
                                  reads=[psb[tb]], writes=[b_rec])
                            sc.op("dve", lambda e: e.tensor_tensor(out=om[m][:, qg * GT:(qg + 1) * GT, :], in0=tv[:, :, 0:64],
                                                                   in1=rec[:, 0:GT].unsqueeze(2).to_broadcast([128, GT, 64]),
                                                                   op=ALU.mult),
                                  reads=[psb[tb], b_rec], writes=[b_om[m]])

                        return fin
                    pending.append(make_fin())
            while pending:
                pending.pop(0)()
            sc.op("dve", lambda e: e.scalar_tensor_tensor(out=dd[:], in0=om[1][:], scalar=ls[:, 5:6], in1=om[0][:],
                                                          op0=ALU.mult, op1=ALU.add),
                  reads=[b_om[0], b_om[1], b_ls], writes=[b_dd])
            sc.op("pool", lambda e: e.tensor_tensor(out=sq[:], in0=dd[:], in1=dd[:], op=ALU.mult),
                  reads=[b_dd], writes=[b_sq])
            sc.op("dve", lambda e: e.reduce_sum(out=ssq[:], in_=sq[:], axis=AX.X), reads=[b_sq], writes=[b_ssq])
            sc.op("dve", lambda e: e.tensor_scalar(out=ssq[:], in0=ssq[:], scalar1=1.0 / 64, scalar2=EPS,
                                                   op0=ALU.mult, op1=ALU.add), reads=[b_ssq], writes=[b_ssq])
            sc.op("act", lambda e: e.sqrt(out=ssq[:], in_=ssq[:]), reads=[b_ssq], writes=[b_ssq])
            sc.op("dve", lambda e: e.reciprocal(out=ssq[:], in_=ssq[:]), reads=[b_ssq], writes=[b_ssq])
            sc.op("dve", lambda e: e.tensor_tensor(out=dd[:], in0=dd[:],
                                                   in1=ssq[:].unsqueeze(2).to_broadcast([128, NT, 64]), op=ALU.mult),
                  reads=[b_dd, b_ssq], writes=[b_dd])
            sc.op("dve", lambda e: e.tensor_tensor(out=dd[:], in0=dd[:],
                                                   in1=gd[:].unsqueeze(1).to_broadcast([128, NT, 64]), op=ALU.mult),
                  reads=[b_dd, b_gd], writes=[b_dd])
            sc.dma(ms["omix"][:, 384 + h * 64:384 + (h + 1) * 64].rearrange("(n p) d -> p n d", p=128), dd[:],
                   reads=[b_dd])
        if hook is not None:
            while hook():
                pass
        sc.barrier()


K.diffattn_phase = _diffattn_phase


def _winattn_phase(self, tag, ms, wbias, sink):
    sc = self.sc
    S, NT = self.S, self.NT
    NKA = NT + 1 if self.paired else NT
    ps, psb = self.ps, self.psb
    with ExitStack() as st:
        qT = self.sb(st, tag + "qT", [128, S], BF16)
        kT = self.sb(st, tag + "kT", [128, NKA * 128], BF16)
        vA = self.sb(st, tag + "vA", [128, NKA, 65], BF16)
        wb = self.sb(st, tag + "wb", [128, 384], F32)
        sT = [self.sb(st, tag + "sT%d" % i, [128, 384], F32) for i in range(2)]
        pT = [self.sb(st, tag + "pT%d" % i, [128, 384], BF16) for i in range(2)]
        ow = self.sb(st, tag + "ow", [128, NT, 64], F32)
        ou = self.sb(st, tag + "ou", [128, NT, 65], F32)
        dn_ = self.sb(st, tag + "dn", [128, NT], F32)
        b_ou, b_dn = Buf("ou"), Buf("dn")
        es = self.sb(st, tag + "es", [128, 6], F32)
        rec = self.sb(st, tag + "rec", [128, 4], F32)
        b_qT, b_kT, b_vA, b_wb, b_ow, b_es, b_rec = (Buf(n) for n in ("qT", "kT", "vA", "wb", "ow", "es", "rec"))
        b_sT = [Buf("sT0"), Buf("sT1")]
        b_pT = [Buf("pT0"), Buf("pT1")]
        sc.op("dve", lambda e: e.memset(qT[:], 0.0), writes=[b_qT])
        sc.op("dve", lambda e: e.memset(kT[:], 0.0), writes=[b_kT])
        sc.dma(es[:], sink.partition_broadcast(128), writes=[b_es])
        sc.op("act", lambda e: e.activation(out=es[:], in_=es[:], func=AF.Exp), reads=[b_es], writes=[b_es])
        cnt = 0
        for h in range(6):
            kh = h // 3
            if h % 3 == 0:
                sc.dma(kT[0:64, :], ms["akT"][kh * 64:(kh + 1) * 64, :], writes=[b_kT])
                sc.dma(vA[:], ms["av"][:, kh, :].rearrange("(n p) d -> p n d", p=128), writes=[b_vA])
            sc.dma(qT[0:64, :], ms["aqT"][h * 64:(h + 1) * 64, :], writes=[b_qT])
            sc.dma(wb[:], wbias[h], writes=[b_wb])
            for n in range(NT):
                js = [j for j in (0, 1, 2) if 0 <= n - 1 + j < NKA]
                c0 = js[0] * 128
                w = len(js) * 128
                sbk = cnt % 2
                acc = 2 + cnt % 6
                cnt += 1
                for jj, j in enumerate(js):
                    kt = n - 1 + j
                    sc.op("pe", lambda e: e.matmul(ps[sbk][:, jj * 128:(jj + 1) * 128], lhsT=kT[:, kt * 128:(kt + 1) * 128],
                                                   rhs=qT[:, n * 128:(n + 1) * 128], start=True, stop=True),
                          reads=[b_kT, b_qT], writes=[psb[sbk]], signal=(jj == len(js) - 1))
                sc.op("dve", lambda e: e.tensor_tensor(out=sT[sbk][:, 0:w], in0=ps[sbk][:, 0:w], in1=wb[:, c0:c0 + w],
                                                       op=ALU.add),
                      reads=[psb[sbk], b_wb], writes=[b_sT[sbk]])
                sc.op("act", lambda e: e.activation(out=pT[sbk][:, 0:w], in_=sT[sbk][:, 0:w], func=AF.Exp),
                      reads=[b_sT[sbk]], writes=[b_pT[sbk]])
                for jj, j in enumerate(js):
                    kt = n - 1 + j
                    sc.op("pe", lambda e: e.matmul(ps[acc][:, 0:65], lhsT=pT[sbk][:, jj * 128:(jj + 1) * 128],
                                                   rhs=vA[:, kt, :], start=(jj == 0), stop=(jj == len(js) - 1)),
                          reads=[b_pT[sbk], b_vA], writes=[psb[acc]], signal=(jj == len(js) - 1))
                if n % 2 == 0:
                    sc.op("dve", lambda e: e.tensor_copy(out=ou[:, n, :], in_=ps[acc][:, 0:65]), reads=[psb[acc]], writes=[b_ou])
                else:
                    sc.op("act", lambda e: e.copy(out=ou[:, n, :], in_=ps[acc][:, 0:65]), reads=[psb[acc]], writes=[b_ou])
            sc.op("dve", lambda e: e.tensor_scalar(out=dn_[:], in0=ou[:, :, 64], scalar1=es[:, h:h + 1], scalar2=None,
                                                   op0=ALU.add), reads=[b_ou, b_es], writes=[b_dn])
            sc.op("dve", lambda e: e.reciprocal(out=dn_[:], in_=dn_[:]), reads=[b_dn], writes=[b_dn])
            sc.op("dve", lambda e: e.tensor_tensor(out=ow[:], in0=ou[:, :, 0:64],
                                                   in1=dn_[:].unsqueeze(2).to_broadcast([128, NT, 64]), op=ALU.mult),
                  reads=[b_ou, b_dn], writes=[b_ow])
            sc.dma(ms["omix"][:, h * 64:(h + 1) * 64].rearrange("(n p) d -> p n d", p=128), ow[:], reads=[b_ow])
        sc.barrier()


K.winattn_phase = _winattn_phase


def _dn_scratch(self):
    S = self.S
    d = {}
    d["cqT"] = self.scratch("cqT", [384, S], F32)
    d["ckT"] = self.scratch("ckT", [384, S], F32)
    d["ck"] = self.scratch("ck", [S, 384], F32)
    d["cv"] = self.scratch("cv", [S, 384], F32)
    d["of"] = self.scratch("of", [S, 384], F32)
    return d


K.dn_scratch = _dn_scratch


def _conv_phase(self, tag, ms, ds, conv_w, dnc, ident, ident_b):
    sc = self.sc
    S, NT = self.S, self.NT
    GT = 4 if NT % 4 == 0 else 1
    GW = GT * 128
    ps, psb = self.ps, self.psb
    with ExitStack() as st:
        blk1 = self.sb(st, tag + "blk1", [128, 128], F32)
        cw = self.sb(st, tag + "cw", [128, 9, 5], F32)
        xin = [self.sb(st, tag + "xin%d" % i, [128, GW + 4], F32) for i in range(2)]
        y = [self.sb(st, tag + "y%d" % i, [128, GW], F32) for i in range(2)]
        sq2 = [self.sb(st, tag + "sq%d" % i, [128, GW], F32) for i in range(2)]
        rs2 = [self.sb(st, tag + "rs%d" % i, [128, GW], F32) for i in range(2)]
        yn = [self.sb(st, tag + "yn%d" % i, [128, GW], F32) for i in range(2)]
        tk = [self.sb(st, tag + "tk%d" % i, [128, GW], F32) for i in range(2)]
        b_blk1, b_cw = Buf("blk1"), Buf("cw")
        b_sq2 = [Buf("sq0"), Buf("sq1")]
        b_rs2 = [Buf("rs0"), Buf("rs1")]
        b_xin = [Buf("xin0"), Buf("xin1")]
        b_y = [Buf("y0"), Buf("y1")]
        b_yn = [Buf("yn0"), Buf("yn1")]
        b_tk = [Buf("tk0"), Buf("tk1")]
        sc.dma(blk1[:], dnc[3], writes=[b_blk1])
        for ci in range(9):
            sc.dma(cw[:, ci, :], conv_w[:, ci * 128:(ci + 1) * 128].rearrange("j c -> c j"), writes=[b_cw],
                   allow_slow_non_contiguous=True)
        it = 0
        for g0 in range(0, NT, GT):
            tok0 = g0 * 128
            for ci in range(9):
                b = it % 2
                it += 1
                sq, rs, b_sq, b_rs = sq2[b], rs2[b], b_sq2[b], b_rs2[b]
                pA, pB = 2 * b, 2 * b + 1
                sc.dma(xin[b][:], ms["cpre"][ci * 128:(ci + 1) * 128, tok0:tok0 + GW + 4], writes=[b_xin[b]])
                eng = "dve"
                sc.op(eng, lambda e: e.tensor_scalar_mul(out=y[b][:], in0=xin[b][:, 0:GW], scalar1=cw[:, ci, 0:1]),
                      reads=[b_xin[b], b_cw], writes=[b_y[b]])
                for j in range(1, 5):
                    sc.op(eng, lambda e: e.scalar_tensor_tensor(out=y[b][:], in0=xin[b][:, j:j + GW],
                                                                scalar=cw[:, ci, j:j + 1], in1=y[b][:],
                                                                op0=ALU.mult, op1=ALU.add),
                          reads=[b_xin[b], b_cw, b_y[b]], writes=[b_y[b]])
                sc.op("act", lambda e: e.activation(out=y[b][:], in_=y[b][:], func=AF.Silu),
                      reads=[b_y[b]], writes=[b_y[b]])
                if ci < 6:
                    sc.op("act", lambda e: e.activation(out=sq[:], in_=y[b][:], func=AF.Square),
                          reads=[b_y[b]], writes=[b_sq])
                    sc.op("pe", lambda e: e.matmul(ps[pA][:, :GW], lhsT=blk1[:], rhs=sq[:], start=True, stop=True),
                          reads=[b_blk1, b_sq], writes=[psb[pA]])
                    mul = 64.0 if ci < 3 else 1.0
                    sc.op("dve", lambda e: e.tensor_scalar(out=rs[:], in0=ps[pA][:, :GW], scalar1=EPS, scalar2=mul,
                                                           op0=ALU.add, op1=ALU.mult),
                          reads=[psb[pA]], writes=[b_rs])
                    sc.op("act", lambda e: e.sqrt(out=rs[:], in_=rs[:]), reads=[b_rs], writes=[b_rs])
                    sc.op("dve", lambda e: e.reciprocal(out=rs[:], in_=rs[:]), reads=[b_rs], writes=[b_rs])
                    sc.op("dve", lambda e: e.tensor_tensor(out=yn[b][:], in0=y[b][:], in1=rs[:], op=ALU.mult),
                          reads=[b_y[b], b_rs], writes=[b_yn[b]])
                    src_t, src_b = yn[b], b_yn[b]
                    if ci < 3:
                        sc.dma(ds["cqT"][ci * 128:(ci + 1) * 128, tok0:tok0 + GW], yn[b][:], reads=[b_yn[b]])
                    else:
                        sc.dma(ds["ckT"][(ci - 3) * 128:(ci - 2) * 128, tok0:tok0 + GW], yn[b][:], reads=[b_yn[b]])
                else:
                    src_t, src_b = y[b], b_y[b]
                if ci >= 3:
                    for tt in range(GT):
                        sc.op("pe", lambda e: e.transpose(ps[pB][:, tt * 128:(tt + 1) * 128],
                                                          src_t[:, tt * 128:(tt + 1) * 128], ident[:]),
                              reads=[src_b, ident_b], writes=[psb[pB]], signal=(tt == GT - 1))
                    sc.op("act", lambda e: e.copy(out=tk[b][:], in_=ps[pB][:, :GW]), reads=[psb[pB]], writes=[b_tk[b]])
                    dst = ds["ck"] if ci < 6 else ds["cv"]
                    cc = (ci - 3) % 3
                    sc.dma(dst[tok0:tok0 + GW, cc * 128:(cc + 1) * 128].rearrange("(t p) c -> p t c", p=128),
                           tk[b][:].rearrange("p (t c) -> p t c", c=128), reads=[b_tk[b]])
        sc.barrier()


K.conv_phase = _conv_phase


def _dn_pass(self, tag, dirn, ms, ds, dnc, a_log, dt_bias, dn_g, ident, ident_b):
    sc = self.sc
    S, NT = self.S, self.NT
    ps, psb = self.ps, self.psb
    import os
    NIT = int(os.environ.get('DN_NIT', '7'))
    with ExitStack() as st:
        def T(name, shape, n=1, dt=F32):
            ts = [self.sb(st, "%s%s%d" % (tag, name, i), shape, dt) for i in range(n)]
            bs = [Buf("%s%d" % (name, i)) for i in range(n)]
            return (ts, bs) if n > 1 else (ts[0], bs[0])
        ones, b_ones = T("ones", [128, 128])
        tri, b_tri = T("tri", [128, 128])
        m1, b_m1 = T("m1", [128, 128])
        m2, b_m2 = T("m2", [128, 128])
        m3, b_m3 = T("m3", [128, 128])
        dtb, b_dtb = T("dtb", [128, 6])
        nega, b_nega = T("nega", [128, 6])
        gdn, b_gdn = T("gdn", [128, 64])
        kTt, b_kTt = T("kTt", [64, 6, 128], 2)
        qTt, b_qTt = T("qTt", [64, 6, 128], 2)
        kt, b_kt = T("kt", [128, 384], 2)
        vt, b_vt = T("vt", [128, 384], 2)
        gz, b_gz = T("gz", [128, 408], 2)
        gs, b_gs = T("gs", [128, 16, 6])
        D1, b_D1 = T("D1", [128, 6, 128])
        D2, b_D2 = T("D2", [128, 6, 128])
        Rs, b_Rs = T("Rs", [128, 6, 128])
        R2s, b_R2s = T("R2s", [128, 6, 128])
        tmp, b_tmp = T("tmp", [128, 3, 128], 2)
        E, b_E = T("E", [128, 3, 128], 2)
        X, b_X = T("X", [128, 128], 4)
        Y, b_Y = T("Y", [128, 128], 4)
        Rr, b_Rr = T("Rr", [128, 128], 4)
        qkT, b_qkT = T("qkT", [128, 128], 2)
        kg, b_kg = T("kg", [128, 64], 2)
        wT, b_wT = T("wT", [64, 128], 2)
        Vn, b_Vn = T("Vn", [128, 64], 2)
        t2, b_t2 = T("t2", [128, 64], 2)
        St, b_St = T("St", [64, 6, 64])
        osb, b_osb = T("osb", [128, 6, 64], 2)
        if dirn == 1:
            oft, b_oft = T("oft", [128, 6, 64], 2)
            sqt, b_sqt = T("sqt", [128, 6, 64])
            sz, b_sz = T("sz", [128, 6, 64])
            rr, b_rr = T("rr", [128, 8])
        sc.dma(ones[:], dnc[0], writes=[b_ones])
        sc.dma(tri[:], dnc[1 + dirn], writes=[b_tri])
        sc.dma(m1[:], dnc[4 + 3 * dirn], writes=[b_m1])
        sc.dma(m2[:], dnc[5 + 3 * dirn], writes=[b_m2])
        sc.dma(m3[:], dnc[6 + 3 * dirn], writes=[b_m3])
        sc.dma(dtb[:], dt_bias[dirn].partition_broadcast(128), writes=[b_dtb])
        sc.dma(nega[:], a_log[dirn].partition_broadcast(128), writes=[b_nega])
        sc.dma(gdn[:], dn_g.partition_broadcast(128), writes=[b_gdn])
        sc.op("act", lambda e: e.activation(out=nega[:], in_=nega[:], func=AF.Exp), reads=[b_nega], writes=[b_nega])
        sc.op("dve", lambda e: e.tensor_scalar_mul(out=nega[:], in0=nega[:], scalar1=-1.0),
              reads=[b_nega], writes=[b_nega])
        sc.op("pool", lambda e: e.memset(St[:], 0.0), writes=[b_St])
        order = list(range(NT)) if dirn == 0 else list(range(NT - 1, -1, -1))
        G_ = lambda i: gs[:, i, :]
        hcnt = 0
        for it, n in enumerate(order):
            tb = it % 2
            c0 = n * 128
            sc.dma(kTt[tb][:], ds["ckT"][:, c0:c0 + 128].rearrange("(h d) t -> d h t", h=6), writes=[b_kTt[tb]])
            sc.dma(qTt[tb][:], ds["cqT"][:, c0:c0 + 128].rearrange("(h d) t -> d h t", h=6), writes=[b_qTt[tb]])
            sc.dma(kt[tb][:], ds["ck"][c0:c0 + 128, :], writes=[b_kt[tb]])
            sc.dma(vt[tb][:], ds["cv"][c0:c0 + 128, :], writes=[b_vt[tb]])
            sc.dma(gz[tb][:], ms["cz"][c0:c0 + 128, :], writes=[b_gz[tb]])
            bcol = gz[tb][:, 384 + dirn * 6:384 + dirn * 6 + 6]
            acol = gz[tb][:, 396 + dirn * 6:396 + dirn * 6 + 6]
            RG = [b_gs]
            sc.op("act", lambda e: e.activation(out=G_(0), in_=bcol, func=AF.Exp, scale=-1.0),
                  reads=[b_gz[tb]], writes=RG)
            sc.op("dve", lambda e: e.tensor_scalar_add(out=G_(0), in0=G_(0), scalar1=1.0), reads=RG, writes=RG)
            sc.op("act", lambda e: e.activation(out=G_(1), in_=G_(0), func=AF.Ln), reads=RG, writes=RG)
            sc.op("dve", lambda e: e.tensor_tensor(out=G_(2), in0=acol, in1=dtb[:], op=ALU.add),
                  reads=[b_gz[tb], b_dtb], writes=RG)
            sc.op("act", lambda e: e.activation(out=G_(3), in_=G_(2), func=AF.Exp), reads=RG, writes=RG)
            sc.op("dve", lambda e: e.tensor_scalar_add(out=G_(3), in0=G_(3), scalar1=1.0), reads=RG, writes=RG)
            sc.op("act", lambda e: e.activation(out=G_(4), in_=G_(3), func=AF.Ln), reads=RG, writes=RG)
            sc.op("dve", lambda e: e.tensor_tensor(out=G_(5), in0=G_(4), in1=nega[:], op=ALU.mult),
                  reads=RG + [b_nega], writes=RG)
            sc.op("pe", lambda e: e.matmul(ps[0][:, 0:6], lhsT=tri[:], rhs=G_(5), start=True, stop=True),
                  reads=[b_tri] + RG, writes=[psb[0]], signal=False)
            sc.op("pe", lambda e: e.matmul(ps[0][:, 8:14], lhsT=ones[:], rhs=G_(5), start=True, stop=True),
                  reads=[b_ones] + RG, writes=[psb[0]])
            sc.op("dve", lambda e: e.tensor_copy(out=G_(6), in_=ps[0][:, 0:6]), reads=[psb[0]], writes=RG)
            sc.op("dve", lambda e: e.tensor_copy(out=G_(14), in_=ps[0][:, 8:14]), reads=[psb[0]], writes=RG)
            sc.op("dve", lambda e: e.tensor_tensor(out=G_(7), in0=G_(6), in1=G_(1), op=ALU.subtract),
                  reads=RG, writes=RG)
            sc.op("act", lambda e: e.activation(out=G_(8), in_=G_(7), func=AF.Exp), reads=RG, writes=RG)
            sc.op("act", lambda e: e.activation(out=G_(9), in_=G_(1), func=AF.Exp, scale=-1.0),
                  reads=RG, writes=RG)
            sc.op("act", lambda e: e.activation(out=G_(10), in_=G_(6), func=AF.Exp), reads=RG, writes=RG)
            sc.op("dve", lambda e: e.tensor_tensor(out=G_(11), in0=G_(14), in1=G_(6), op=ALU.subtract),
                  reads=RG, writes=RG)
            sc.op("act", lambda e: e.activation(out=G_(12), in_=G_(11), func=AF.Exp), reads=RG, writes=RG)
            sc.op("act", lambda e: e.activation(out=G_(13), in_=G_(14), func=AF.Exp), reads=RG, writes=RG)
            idb3 = ident[:].unsqueeze(1).to_broadcast([128, 6, 128])
            sc.op("dve", lambda e: e.tensor_tensor(out=D1[:], in0=idb3,
                                                   in1=G_(6).unsqueeze(2).to_broadcast([128, 6, 128]), op=ALU.mult),
                  reads=RG + [ident_b], writes=[b_D1])
            sc.op("pool", lambda e: e.tensor_tensor(out=D2[:], in0=idb3,
                                                    in1=G_(7).unsqueeze(2).to_broadcast([128, 6, 128]), op=ALU.mult),
                  reads=RG + [ident_b], writes=[b_D2])
            for (Dm, b_Dm, Rm, b_Rm) in ((D1, b_D1, Rs, b_Rs), (D2, b_D2, R2s, b_R2s)):
                Dm2 = Dm[:].rearrange("p j s -> p (j s)")
                Rm2 = Rm[:].rearrange("p j s -> p (j s)")
                sc.op("pe", lambda e: e.matmul(ps[1][:, 0:512], lhsT=ones[:], rhs=Dm2[:, 0:512], start=True, stop=True),
                      reads=[b_ones, b_Dm], writes=[psb[1]])
                sc.op("pe", lambda e: e.matmul(ps[2][:, 0:256], lhsT=ones[:], rhs=Dm2[:, 512:768], start=True,
                                               stop=True),
                      reads=[b_ones, b_Dm], writes=[psb[2]])
                sc.op("act", lambda e: e.copy(out=Rm2[:, 0:512], in_=ps[1][:, 0:512]), reads=[psb[1]], writes=[b_Rm])
                sc.op("act", lambda e: e.copy(out=Rm2[:, 512:768], in_=ps[2][:, 0:256]), reads=[psb[2]],
                      writes=[b_Rm])
            ob = it % 2
            import os
            STOP = os.environ.get("DN_STOP", "")
            for hh in range(6):
                if STOP == "gates":
                    break
                hb = hcnt % 2
                hcnt += 1
                sc.op("pe", lambda e: e.matmul(ps[3][:, 0:128], lhsT=kTt[tb][:, hh, :], rhs=kTt[tb][:, hh, :],
                                               start=True, stop=True),
                      reads=[b_kTt[tb]], writes=[psb[3]], signal=False)
                sc.op("pe", lambda e: e.matmul(ps[3][:, 128:256], lhsT=kTt[tb][:, hh, :], rhs=qTt[tb][:, hh, :],
                                               start=True, stop=True),
                      reads=[b_kTt[tb], b_qTt[tb]], writes=[psb[3]])
                sc.op("dve", lambda e: e.scalar_tensor_tensor(out=tmp[hb][:, 0, :], in0=Rs[:, hh, :],
                                                              scalar=gs[:, 7, hh:hh + 1], in1=m1[:],
                                                              op0=ALU.subtract, op1=ALU.max),
                      reads=[b_Rs, b_gs, b_m1], writes=[b_tmp[hb]])
                sc.op("dve", lambda e: e.scalar_tensor_tensor(out=tmp[hb][:, 1, :], in0=R2s[:, hh, :],
                                                              scalar=gs[:, 6, hh:hh + 1], in1=m2[:],
                                                              op0=ALU.subtract, op1=ALU.min),
                      reads=[b_R2s, b_gs, b_m2], writes=[b_tmp[hb]])
                sc.op("dve", lambda e: e.scalar_tensor_tensor(out=tmp[hb][:, 2, :], in0=Rs[:, hh, :],
                                                              scalar=gs[:, 6, hh:hh + 1], in1=m3[:],
                                                              op0=ALU.subtract, op1=ALU.min),
                      reads=[b_Rs, b_gs, b_m3], writes=[b_tmp[hb]])
                sc.op("act", lambda e: e.activation(out=E[hb][:, 0, :], in_=tmp[hb][:, 0, :], func=AF.Exp, scale=-1.0),
                      reads=[b_tmp[hb]], writes=[b_E[hb]])
                sc.op("act", lambda e: e.activation(out=E[hb][:, 1:3, :], in_=tmp[hb][:, 1:3, :], func=AF.Exp),
                      reads=[b_tmp[hb]], writes=[b_E[hb]])
                xi = 2 * hb
                sc.op("dve", lambda e: e.tensor_tensor(out=X[xi][:], in0=ps[3][:, 0:128], in1=E[hb][:, 0, :],
                                                       op=ALU.mult), reads=[psb[3], b_E[hb]], writes=[b_X[xi]])
                sc.op("dve", lambda e: e.tensor_tensor(out=Y[xi][:], in0=ps[3][:, 0:128], in1=E[hb][:, 1, :],
                                                       op=ALU.mult), reads=[psb[3], b_E[hb]], writes=[b_Y[xi]])
                sc.op("dve", lambda e: e.tensor_tensor(out=qkT[hb][:], in0=ps[3][:, 128:256], in1=E[hb][:, 2, :],
                                                       op=ALU.mult), reads=[psb[3], b_E[hb]], writes=[b_qkT[hb]])
                hs = slice(hh * 64, (hh + 1) * 64)
                sc.op("pool", lambda e: e.tensor_scalar_mul(out=Rr[xi][:, 0:64], in0=vt[tb][:, hs],
                                                            scalar1=gs[:, 9, hh:hh + 1]),
                      reads=[b_vt[tb], b_gs], writes=[b_Rr[xi]])
                sc.op("pool", lambda e: e.tensor_scalar_mul(out=Rr[xi][:, 64:128], in0=kt[tb][:, hs],
                                                            scalar1=gs[:, 8, hh:hh + 1]),
                      reads=[b_kt[tb], b_gs], writes=[b_Rr[xi]])
                sc.op("pool", lambda e: e.tensor_scalar_mul(out=kg[hb][:], in0=kt[tb][:, hs],
                                                            scalar1=gs[:, 12, hh:hh + 1]),
                      reads=[b_kt[tb], b_gs], writes=[b_kg[hb]])
                if STOP == "prep":
                    continue
                pn = 4 + hb
                pq = 6 + hb
                cur = xi
                for i in range(NIT):
                    nxt = 2 * hb + (1 - (cur - 2 * hb))
                    sc.op("pe", lambda e: e.matmul(ps[pn][:, 0:128], lhsT=Y[cur][:], rhs=Rr[cur][:], start=True,
                                                   stop=True),
                          reads=[b_Y[cur], b_Rr[cur]], writes=[psb[pn]], signal=(i == NIT - 1))
                    if i < NIT - 1:
                        sc.op("pe", lambda e: e.matmul(ps[pq][:, 128:256], lhsT=X[cur][:], rhs=Y[cur][:], start=True,
                                                       stop=True),
                              reads=[b_X[cur], b_Y[cur]], writes=[psb[pq]], signal=(i == NIT - 2))
                    if i < NIT - 2:
                        sc.op("pe", lambda e: e.matmul(ps[pq][:, 256:384], lhsT=Y[cur][:], rhs=X[cur][:], start=True,
                                                       stop=True),
                              reads=[b_X[cur], b_Y[cur]], writes=[psb[pq]], signal=True)
                    if i == 0:
                        sc.op("dve", lambda e: e.scalar_tensor_tensor(out=Rr[nxt][:], in0=ps[pn][:, 0:128], scalar=-1.0,
                                                                      in1=Rr[cur][:], op0=ALU.mult, op1=ALU.add),
                              reads=[b_Rr[cur], psb[pn]], writes=[b_Rr[nxt]])
                    else:
                        sc.op("dve", lambda e: e.tensor_tensor(out=Rr[nxt][:], in0=ps[pn][:, 0:128], in1=Rr[cur][:],
                                                               op=ALU.add),
                              reads=[b_Rr[cur], psb[pn]], writes=[b_Rr[nxt]])
                    if i < NIT - 1:
                        sc.op("act", lambda e: e.copy(out=Y[nxt][:], in_=ps[pq][:, 128:256]),
                              reads=[psb[pq]], writes=[b_Y[nxt]])
                    if i < NIT - 2:
                        sc.op("act", lambda e: e.copy(out=X[nxt][:], in_=ps[pq][:, 256:384]),
                              reads=[psb[pq]], writes=[b_X[nxt]])
                    cur = nxt
                Rf, b_Rf = Rr[cur], b_Rr[cur]
                if STOP == "neumann":
                    continue
                sc.op("pe", lambda e: e.transpose(ps[1][0:64, 0:128], Rf[:, 64:128], ident[:]),
                      reads=[b_Rf, ident_b], writes=[psb[1]])
                sc.op("act", lambda e: e.copy(out=wT[hb][:], in_=ps[1][0:64, 0:128]), reads=[psb[1]], writes=[b_wT[hb]])
                if STOP == "transp":
                    continue
                sc.op("pe", lambda e: e.matmul(ps[0][:, 0:64], lhsT=wT[hb][:], rhs=St[:, hh, :], start=True, stop=True),
                      reads=[b_wT[hb], b_St], writes=[psb[0]])
                sc.op("dve", lambda e: e.scalar_tensor_tensor(out=Vn[hb][:], in0=ps[0][:, 0:64], scalar=-1.0,
                                                              in1=Rf[:, 0:64], op0=ALU.mult, op1=ALU.add),
                      reads=[b_Rf, psb[0]], writes=[b_Vn[hb]])
                sc.op("pe", lambda e: e.matmul(ps[1][:, 128:192], lhsT=qTt[tb][:, hh, :], rhs=St[:, hh, :], start=True,
                                               stop=True),
                      reads=[b_qTt[tb], b_St], writes=[psb[1]], signal=False)
                sc.op("pe", lambda e: e.matmul(ps[0][:, 64:128], lhsT=qkT[hb][:], rhs=Vn[hb][:], start=True, stop=True),
                      reads=[b_qkT[hb], b_Vn[hb]], writes=[psb[0]], signal=False)
                sc.op("pe", lambda e: e.matmul(ps[2][0:64, 0:64], lhsT=kg[hb][:], rhs=Vn[hb][:], start=True, stop=True),
                      reads=[b_kg[hb], b_Vn[hb]], writes=[psb[2]])
                sc.op("act", lambda e: e.activation(out=t2[hb][:], in_=ps[1][:, 128:192], func=AF.Copy,
                                                    scale=gs[:, 10, hh:hh + 1]),
                      reads=[psb[1], b_gs], writes=[b_t2[hb]])
                sc.op("dve", lambda e: e.tensor_tensor(out=osb[ob][:, hh, :], in0=ps[0][:, 64:128], in1=t2[hb][:],
                                                       op=ALU.add),
                      reads=[b_t2[hb], psb[0]], writes=[b_osb[ob]])
                sc.op("pool", lambda e: e.tensor_scalar_mul(out=St[:, hh, :], in0=St[:, hh, :],
                                                            scalar1=gs[0:64, 13, hh:hh + 1]),
                      reads=[b_St, b_gs], writes=[b_St])
                sc.op("dve", lambda e: e.tensor_tensor(out=St[:, hh, :], in0=ps[2][0:64, 0:64], in1=St[:, hh, :],
                                                       op=ALU.add),
                      reads=[b_St, psb[2]], writes=[b_St])
            if dirn == 0:
                sc.dma(ds["of"][c0:c0 + 128, :], osb[ob][:].rearrange("p h d -> p (h d)"), reads=[b_osb[ob]])
            else:
                sc.dma(oft[ob][:].rearrange("p h d -> p (h d)"), ds["of"][c0:c0 + 128, :], writes=[b_oft[ob]])
                sc.op("dve", lambda e: e.tensor_tensor(out=oft[ob][:], in0=oft[ob][:], in1=osb[ob][:], op=ALU.add),
                      reads=[b_oft[ob], b_osb[ob]], writes=[b_oft[ob]])
                sc.op("pool", lambda e: e.tensor_tensor(out=sqt[:], in0=oft[ob][:], in1=oft[ob][:], op=ALU.mult),
                      reads=[b_oft[ob]], writes=[b_sqt])
                sc.op("dve", lambda e: e.reduce_sum(out=rr[:, 0:6], in_=sqt[:], axis=AX.X), reads=[b_sqt], writes=[b_rr])
                sc.op("dve", lambda e: e.tensor_scalar(out=rr[:, 0:6], in0=rr[:, 0:6], scalar1=1.0 / 64, scalar2=EPS,
                                                       op0=ALU.mult, op1=ALU.add), reads=[b_rr], writes=[b_rr])
                sc.op("act", lambda e: e.sqrt(out=rr[:, 0:6], in_=rr[:, 0:6]), reads=[b_rr], writes=[b_rr])
                sc.op("dve", lambda e: e.reciprocal(out=rr[:, 0:6], in_=rr[:, 0:6]), reads=[b_rr], writes=[b_rr])
                sc.op("act", lambda e: e.activation(out=sz[:].rearrange("p h d -> p (h d)"), in_=gz[tb][:, 0:384],
                                                    func=AF.Silu), reads=[b_gz[tb]], writes=[b_sz])
                sc.op("dve", lambda e: e.tensor_tensor(out=oft[ob][:], in0=oft[ob][:],
                                                       in1=rr[:, 0:6].unsqueeze(2).to_broadcast([128, 6, 64]),
                                                       op=ALU.mult), reads=[b_oft[ob], b_rr], writes=[b_oft[ob]])
                sc.op("pool", lambda e: e.tensor_tensor(out=sz[:], in0=sz[:],
                                                        in1=gdn[:].unsqueeze(1).to_broadcast([128, 6, 64]),
                                                        op=ALU.mult), reads=[b_sz, b_gdn], writes=[b_sz])
                sc.op("dve", lambda e: e.tensor_tensor(out=oft[ob][:], in0=oft[ob][:], in1=sz[:], op=ALU.mult),
                      reads=[b_oft[ob], b_sz], writes=[b_oft[ob]])
                sc.dma(ms["omix"][c0:c0 + 128, 640:1024], oft[ob][:].rearrange("p h d -> p (h d)"),
                       reads=[b_oft[ob]])
        sc.barrier()


K.dn_pass = _dn_pass


def _outproj_phase(self, tag, w_o, ms, xres, x_bufs, ident, ident_b):
    sc = self.sc
    S, NT = self.S, self.NT
    ps, psb = self.ps, self.psb
    with ExitStack() as st:
        wo = self.sb(st, tag + "wo", [128, 8, D], BF16)
        ot = [self.sb(st, tag + "ot%d" % i, [128, D], F32) for i in range(2)]
        xr = [self.sb(st, tag + "xr%d" % i, [128, D], F32) for i in range(2)]
        oT = [self.sb(st, tag + "oT%d" % i, [128, 8, 128], BF16) for i in range(2)]
        b_wo = [Buf("wo%d" % k) for k in range(8)]
        b_ot = [Buf("ot0"), Buf("ot1")]
        b_xr = [Buf("xr0"), Buf("xr1")]
        b_oT = [Buf("oT0"), Buf("oT1")]
        for kc in range(8):
            sc.dma(wo[:, kc, :], w_o[kc * 128:(kc + 1) * 128, :], writes=[b_wo[kc]], q="pool")
        for t in range(NT):
            b = t % 2
            r0 = t * 128
            sc.dma(ot[b][:], ms["omix"][r0:r0 + 128, :], writes=[b_ot[b]])
            sc.dma(xr[b][:], xres[r0:r0 + 128, :], reads=[x_bufs[t]], writes=[b_xr[b]])
            for kc in range(8):
                bank = kc // 4
                sc.op("pe", lambda e: e.transpose(ps[bank][:, (kc % 4) * 128:(kc % 4 + 1) * 128],
                                                  ot[b][:, kc * 128:(kc + 1) * 128], ident[:]),
                      reads=[b_ot[b], ident_b], writes=[psb[bank]], signal=(kc % 4 == 3))
            sc.op("act", lambda e: e.copy(out=oT[b][:, 0:4, :], in_=ps[0][:, :].rearrange("p (k t) -> p k t", k=4)),
                  reads=[psb[0]], writes=[b_oT[b]])
            sc.op("dve", lambda e: e.tensor_copy(out=oT[b][:, 4:8, :],
                                                 in_=ps[1][:, :].rearrange("p (k t) -> p k t", k=4)),
                  reads=[psb[1]], writes=[b_oT[b]])
            for hf in range(2):
                pb = 2 + 2 * b + hf
                for kc in range(8):
                    sc.op("pe", lambda e: e.matmul(ps[pb][:, :], lhsT=oT[b][:, kc, :], rhs=wo[:, kc, hf * 512:(hf + 1) * 512],
                                                   start=(kc == 0), stop=(kc == 7)),
                          reads=[b_oT[b], b_wo[kc]], writes=[psb[pb]], signal=(kc == 7))
            for hf in range(2):
                pb = 2 + 2 * b + hf
                sc.op("dve", lambda e: e.tensor_tensor(out=xr[b][:, hf * 512:(hf + 1) * 512], in0=ps[pb][:, :],
                                                       in1=xr[b][:, hf * 512:(hf + 1) * 512], op=ALU.add),
                      reads=[psb[pb], b_xr[b]], writes=[b_xr[b]])
            sc.dma(xres[r0:r0 + 128, :], xr[b][:], reads=[b_xr[b]], writes=[x_bufs[t]], q="pool")
        sc.barrier()


K.outproj_phase = _outproj_phase


def _final_phase(self, tag, g, xres, x_bufs, out):
    sc = self.sc
    S, NT = self.S, self.NT
    with ExitStack() as st:
        gB = self.sb(st, tag + "gB", [128, D], F32)
        xt = [self.sb(st, tag + "xt%d" % i, [128, D], F32) for i in range(2)]
        junk = self.sb(st, tag + "junk", [128, D], F32)
        stat = [self.sb(st, tag + "stat%d" % i, [128, 4], F32) for i in range(2)]
        b_gB, b_junk = Buf("gB"), Buf("junk")
        b_xt = [Buf("xt0"), Buf("xt1")]
        b_stat = [Buf("stat0"), Buf("stat1")]
        sc.dma(gB[:], g.partition_broadcast(128), writes=[b_gB])
        for t in range(NT):
            b = t % 2
            r0 = t * 128
            sc.dma(xt[b][:], xres[r0:r0 + 128, :], reads=[x_bufs[t]], writes=[b_xt[b]])
            sc.op("act", lambda e: e.activation(out=junk[:], in_=xt[b][:], func=AF.Square, accum_out=stat[b][:, 0:1]),
                  reads=[b_xt[b]], writes=[b_junk, b_stat[b]])
            sc.op("dve", lambda e: e.tensor_scalar(out=stat[b][:, 1:2], in0=stat[b][:, 0:1], scalar1=1.0 / D,
                                                   scalar2=EPS, op0=ALU.mult, op1=ALU.add),
                  reads=[b_stat[b]], writes=[b_stat[b]])
            sc.op("act", lambda e: e.sqrt(out=stat[b][:, 3:4], in_=stat[b][:, 1:2]), reads=[b_stat[b]],
                  writes=[b_stat[b]])
            sc.op("dve", lambda e: e.reciprocal(out=stat[b][:, 2:3], in_=stat[b][:, 3:4]), reads=[b_stat[b]],
                  writes=[b_stat[b]])
            sc.op("dve", lambda e: e.scalar_tensor_tensor(out=xt[b][:], in0=xt[b][:], scalar=stat[b][:, 2:3],
                                                          in1=gB[:], op0=ALU.mult, op1=ALU.mult),
                  reads=[b_xt[b], b_stat[b], b_gB], writes=[b_xt[b]])
            sc.dma(out[r0:r0 + 128, :], xt[b][:], reads=[b_xt[b]])
        sc.barrier()


K.final_phase = _final_phase

import math
WNAMES = [("ln_ffn1", [2, D]), ("ffn1_w_in", [2, D, 2 * DFF]), ("ffn1_w_out", [2, DFF, D]), ("ln_mix", [2, D]),
          ("w_mix_in", [2, D, MIX_IN]), ("conv_w", [2, 5, 1152]), ("sink_logits", [2, 6]),
          ("diff_lambda", [2, 4, 32]), ("diff_norm_g", [2, 64]), ("dn_A_log", [2, 2, 6]), ("dn_dt_bias", [2, 2, 6]),
          ("dn_norm_g", [2, 64]), ("w_mix_out", [2, D, D]), ("ln_ffn2", [2, D]), ("ffn2_w_in", [2, D, 2 * DFF]),
          ("ffn2_w_out", [2, DFF, D]), ("ln_final", [D])]


def build_full(S, depth=2, paired=False):
    k = K(S, depth, paired)
    SK = k.SK
    x = k.inp("x", [S, D])
    W = {n: k.inp(n, shp) for n, shp in WNAMES}
    cst = {n: k.inp(n, shp, BF16) for n, shp in (("daq", [4, 4, S]), ("dakp", [4, 4, SK]), ("dakm", [4, 4, SK]),
                                                  ("dbd", [4, 128, 128]), ("identb", [128, 128]))}
    wbias = k.inp("wbias", [6, 128, 384])
    dnc = k.inp("dnc", [10, 128, 128])
    idn = k.inp("ident", [128, 128])
    if paired:
        antiid = k.inp("antiid", [128, 128])
        sel_d = k.inp("sel", [128, 2])
        pr = k.pair_scratch()
    else:
        pr = sel_d = None
    out = k.outp("out", [S, D])
    xres = k.scratch("xres", [S, D])
    ms = k.mix_scratch()
    ds = k.dn_scratch()
    sc = k.sc
    with ExitStack() as st:
        ident = k.sb(st, "ident_sb", [128, 128], F32)
        ib = Buf("ident")
        sc.dma(ident[:], idn[:, :], writes=[ib])
        xb = [Buf("x%d" % i) for i in range(k.NT)]
        xin = [Buf("xin%d" % i) for i in range(k.NT)]
        for l in range(depth):
            lam_init = 0.8 - 0.6 * math.exp(-0.3 * l)
            k.ffn_phase("f1_%d" % l, W["ffn1_w_in"][l], W["ffn1_w_out"][l], W["ln_ffn1"][l],
                        x if l == 0 else xres, xin if l == 0 else xb, xres, xb, ident, ib)
            k.inproj_phase("ip%d" % l, W["w_mix_in"][l], W["ln_mix"][l], xres, xb, ms, ident, ib)
            if paired:
                k.export_phase("ex%d" % l, W["w_mix_in"][l], W["ln_mix"][l], xres, xb, pr, ident, ib, antiid)
                k.exchange_phase("xc%d" % l, pr, ms, sel_d, st)
            with ExitStack() as cst_:
                units = k.conv_units(cst_, "cu%d" % l, ms, ds, W["conv_w"][l], dnc, ident, ib)

                def hook(units=units):
                    if units:
                        units.pop(0)()
                    return len(units) > 0
                k.diffattn_phase("da%d" % l, ms, cst, W["diff_lambda"][l], W["diff_norm_g"][l], lam_init, ident, ib, hook)
            with ExitStack() as wst_:
                wunits = k.win_units(wst_, "wu%d" % l, ms, wbias, W["sink_logits"][l])

                def whook(units=wunits):
                    if units:
                        units.pop(0)()
                    return len(units) > 0
                k.dn2("d2_%d" % l, ms, ds, dnc, W["dn_A_log"][l], W["dn_dt_bias"][l], W["dn_norm_g"][l], ident, ib,
                      pr, sel_d, st, whook)
            k.outproj_phase("op%d" % l, W["w_mix_out"][l], ms, xres, xb, ident, ib)
            k.ffn_phase("f2_%d" % l, W["ffn2_w_in"][l], W["ffn2_w_out"][l], W["ln_ffn2"][l],
                        xres, xb, xres, xb, ident, ib)
        k.final_phase("fin", W["ln_final"], xres, xb, out)
        sc.finish([])
    k.stack.close()
    return k


def pair_feeds(inputs, S_full):
    x = np.ascontiguousarray(np.asarray(inputs["x"], dtype=np.float32))
    B = x.shape[0]
    S = S_full // 2
    base = {n: np.ascontiguousarray(np.asarray(inputs[n], dtype=np.float32)) for n, _ in WNAMES}
    odd = dict(base)
    odd["conv_w"] = np.ascontiguousarray(base["conv_w"][:, ::-1, :])
    wmi = base["w_mix_in"].copy()
    wmi[:, :, C_CB:C_CB + 6] = base["w_mix_in"][:, :, C_CB + 6:C_CB + 12]
    wmi[:, :, C_CB + 6:C_CB + 12] = base["w_mix_in"][:, :, C_CB:C_CB + 6]
    wmi[:, :, C_CA:C_CA + 6] = base["w_mix_in"][:, :, C_CA + 6:C_CA + 12]
    wmi[:, :, C_CA + 6:C_CA + 12] = base["w_mix_in"][:, :, C_CA:C_CA + 6]
    odd["w_mix_in"] = wmi
    odd["dn_A_log"] = np.ascontiguousarray(base["dn_A_log"][:, ::-1, :])
    odd["dn_dt_bias"] = np.ascontiguousarray(base["dn_dt_bias"][:, ::-1, :])
    common = {}
    common.update(diff_consts(S, 2 * S))
    common.update(win_consts())
    common.update(dn_consts())
    common["ident"] = np.eye(128, dtype=np.float32)
    common["antiid"] = np.ascontiguousarray(np.eye(128, dtype=np.float32)[::-1])
    maps = []
    for c in range(2 * B):
        b, r = c // 2, c % 2
        m = dict(base if r == 0 else odd)
        m.update(common)
        if r == 0:
            m["x"] = np.ascontiguousarray(x[b, 0:S])
        else:
            m["x"] = np.ascontiguousarray(x[b, S:2 * S][::-1])
        sel = np.zeros((128, 2), np.float32)
        sel[:, 1 - r] = 1.0
        m["sel"] = sel
        maps.append(m)
    return maps


def pair_gather(results, B, S_full):
    S = S_full // 2
    out = np.empty((B, S_full, D), np.float32)
    for b in range(B):
        out[b, 0:S] = results[2 * b]["out"]
        out[b, S:] = results[2 * b + 1]["out"][::-1]
    return out


GROUPS = [[0, 1], [2, 3], [4, 5], [6, 7]]


def _cdiv(a, b):
    return (a + b - 1) // b


def _pair_chunks(S):
    misc = _cdiv(128 * 128, S) + _cdiv(128 * 130, S)
    return [("bk0", 128), ("bk1", 128), ("bv0", 65), ("bv1", 65), ("bv2", 65), ("bv3", 65), ("misc", misc)]


def _pair_scratch(self):
    S = self.S
    d = {"exp": {}, "gat": {}, "rows": {}}
    for name, rows in _pair_chunks(S):
        d["exp"][name] = self.scratch("exp_" + name, [rows, S], BF16)
        d["gat"][name] = self.scratch("gat_" + name, [2 * rows, S], BF16)
        d["rows"][name] = rows
    d["expf"] = self.scratch("expf", [36, 64], F32)
    d["gatf"] = self.scratch("gatf", [72, 64], F32)
    d["exps"] = self.scratch("exps", [384, 64], F32)
    d["gats"] = self.scratch("gats", [768, 64], F32)
    return d


K.pair_scratch = _pair_scratch


def _pviews(pr, S, slot=None):
    def buf(name):
        if slot is None:
            return pr["exp"][name]
        r = pr["rows"][name]
        return pr["gat"][name][slot * r:(slot + 1) * r, :]
    v = {}
    v["bkT"] = [buf("bk0"), buf("bk1")]
    v["bv"] = [buf("bv%d" % q).rearrange("r c -> (r c)").rearrange("(t d) -> t d", d=260) for q in range(4)]
    mflat = buf("misc").rearrange("r c -> (r c)")
    o2 = _cdiv(128 * 128, S) * S
    v["akT"] = mflat[0:128 * 128].rearrange("(r c) -> r c", c=128)
    v["av"] = mflat[o2:o2 + 128 * 130].rearrange("(t d) -> t d", d=130)
    return v


def _export_phase(self, tag, w_mi, g, src, src_bufs, pr, ident, ident_b, antiid):
    sc = self.sc
    S, NT = self.S, self.NT
    ps, psb = self.ps, self.psb
    ev_ = _pviews(pr, S)
    Q4 = S // 4
    e_akT, e_av = ev_["akT"], ev_["av"]
    e_halo = pr["expf"].rearrange("r c -> (r c)").rearrange("(a b) -> a b", b=2)
    with ExitStack() as st:
        wm = self.sb(st, tag + "wm", [128, 8, MIX_IN], BF16)
        gB = self.sb(st, tag + "gB", [128, D], F32)
        J = self.sb(st, tag + "J", [128, 128], F32)
        xt = [self.sb(st, tag + "xt%d" % i, [128, D], F32) for i in range(2)]
        xn = self.sb(st, tag + "xn", [128, D], F32)
        junk = self.sb(st, tag + "junk", [128, D], F32)
        xr = [self.sb(st, tag + "xr%d" % i, [128, 8, 128], BF16) for i in range(2)]
        stat = self.sb(st, tag + "stat", [128, 8], F32)
        ok_ = [self.sb(st, tag + "ok%d" % i, [128, 2, 128], BF16) for i in range(2)]
        ov = [self.sb(st, tag + "ov%d" % i, [128, 4, 65], BF16) for i in range(2)]
        oak = self.sb(st, tag + "oak", [128, 128], BF16)
        oav = self.sb(st, tag + "oav", [128, 2, 65], BF16)
        oh = self.sb(st, tag + "oh", [128, 9, 2], F32)
        b_wm = [Buf("wm%d" % k) for k in range(8)]
        b_gB, b_J, b_xn, b_junk, b_stat, b_oak, b_oav, b_oh = (Buf(n) for n in
                                                               ("gB", "J", "xn", "junk", "stat", "oak", "oav", "oh"))
        b_xt = [Buf("xt0"), Buf("xt1")]
        b_xr = [Buf("xr0"), Buf("xr1")]
        b_ok = [Buf("ok0"), Buf("ok1")]
        b_ov = [Buf("ov0"), Buf("ov1")]
        for kc in range(8):
            sc.dma(wm[:, kc, :], w_mi[kc * 128:(kc + 1) * 128, :], writes=[b_wm[kc]], q="pool")
        sc.dma(gB[:], g.partition_broadcast(128), writes=[b_gB])
        sc.dma(J[:], antiid[:, :], writes=[b_J])
        for i in range(2):
            sc.op("pool", lambda e: e.memset(ov[i][:], 1.0), writes=[b_ov[i]])
        sc.op("pool", lambda e: e.memset(oav[:], 1.0), writes=[b_oav])
        for it, t in enumerate(range(NT - 1, -1, -1)):
            e = NT - 1 - t
            b = it % 2
            sc.dma(xt[b][:], src[t * 128:(t + 1) * 128, :], reads=[src_bufs[t]], writes=[b_xt[b]])
            sc.op("act", lambda e_: e_.activation(out=junk[:], in_=xt[b][:], func=AF.Square, accum_out=stat[:, 0:1]),
                  reads=[b_xt[b]], writes=[b_junk, b_stat])
            sc.op("dve", lambda e_: e_.tensor_scalar(out=stat[:, 1:2], in0=stat[:, 0:1], scalar1=1.0 / D,
                                                     scalar2=EPS, op0=ALU.mult, op1=ALU.add),
                  reads=[b_stat], writes=[b_stat])
            sc.op("act", lambda e_: e_.sqrt(out=stat[:, 3:4], in_=stat[:, 1:2]), reads=[b_stat], writes=[b_stat])
            sc.op("dve", lambda e_: e_.reciprocal(out=stat[:, 2:3], in_=stat[:, 3:4]), reads=[b_stat], writes=[b_stat])
            sc.op("dve", lambda e_: e_.scalar_tensor_tensor(out=xn[:], in0=xt[b][:], scalar=stat[:, 2:3],
                                                            in1=gB[:], op0=ALU.mult, op1=ALU.mult),
                  reads=[b_xt[b], b_stat, b_gB], writes=[b_xn])
            for kc in range(8):
                bank = kc // 4
                sc.op("pe", lambda e_: e_.matmul(ps[bank][:, (kc % 4) * 128:(kc % 4 + 1) * 128],
                                                 lhsT=xn[:, kc * 128:(kc + 1) * 128], rhs=J[:], start=True, stop=True),
                      reads=[b_xn, b_J], writes=[psb[bank]], signal=(kc % 4 == 3))
            sc.op("act", lambda e_: e_.copy(out=xr[b][:, 0:4, :], in_=ps[0][:, :].rearrange("p (k t) -> p k t", k=4)),
                  reads=[psb[0]], writes=[b_xr[b]])
            sc.op("dve", lambda e_: e_.tensor_copy(out=xr[b][:, 4:8, :],
                                                   in_=ps[1][:, :].rearrange("p (k t) -> p k t", k=4)),
                  reads=[psb[1]], writes=[b_xr[b]])
            for ci in range(2):
                for kc in range(8):
                    sc.op("pe", lambda e_: e_.matmul(ps[2][:, ci * 128:(ci + 1) * 128],
                                                     lhsT=wm[:, kc, C_BK + ci * 128:C_BK + (ci + 1) * 128],
                                                     rhs=xr[b][:, kc, :], start=(kc == 0), stop=(kc == 7)),
                          reads=[b_wm[kc], b_xr[b]], writes=[psb[2]], signal=(kc == 7 and ci == 1))
            for kc in range(8):
                sc.op("pe", lambda e_: e_.matmul(ps[3][:, 0:256], lhsT=xr[b][:, kc, :], rhs=wm[:, kc, C_BV:C_BV + 256],
                                                 start=(kc == 0), stop=(kc == 7)),
                      reads=[b_wm[kc], b_xr[b]], writes=[psb[3]], signal=(kc == 7))
            sc.op("act", lambda e_: e_.copy(out=ok_[b][:].rearrange("p c t -> p (c t)"), in_=ps[2][:, 0:256]),
                  reads=[psb[2]], writes=[b_ok[b]])
            sc.op("dve", lambda e_: e_.tensor_copy(out=ov[b][:, :, 0:64],
                                                   in_=ps[3][:, 0:256].rearrange("p (h d) -> p h d", h=4)),
                  reads=[psb[3]], writes=[b_ov[b]])
            for ci in range(2):
                sc.dma(ev_["bkT"][ci][:, e * 128:(e + 1) * 128], ok_[b][:, ci, :], reads=[b_ok[b]])
            q4 = (e * 128) // Q4
            r4 = e * 128 - q4 * Q4
            sc.dma(ev_["bv"][q4][r4:r4 + 128, :], ov[b][:].rearrange("p h d -> p (h d)"), reads=[b_ov[b]])
            if e == 0:
                for kc in range(8):
                    sc.op("pe", lambda e_: e_.matmul(ps[4][:, 0:128], lhsT=wm[:, kc, C_AK:C_AK + 128], rhs=xr[b][:, kc, :],
                                                     start=(kc == 0), stop=(kc == 7)),
                          reads=[b_wm[kc], b_xr[b]], writes=[psb[4]], signal=(kc == 7))
                for kc in range(8):
                    sc.op("pe", lambda e_: e_.matmul(ps[5][:, 0:128], lhsT=xr[b][:, kc, :], rhs=wm[:, kc, C_AV:C_AV + 128],
                                                     start=(kc == 0), stop=(kc == 7)),
                          reads=[b_wm[kc], b_xr[b]], writes=[psb[5]], signal=(kc == 7))
                for ci in range(9):
                    for kc in range(8):
                        sc.op("pe", lambda e_: e_.matmul(ps[6][:, ci * 2:ci * 2 + 2],
                                                         lhsT=wm[:, kc, C_CQKV + ci * 128:C_CQKV + (ci + 1) * 128],
                                                         rhs=xr[b][:, kc, 0:2], start=(kc == 0), stop=(kc == 7)),
                              reads=[b_wm[kc], b_xr[b]], writes=[psb[6]], signal=(kc == 7 and ci == 8))
                sc.op("act", lambda e_: e_.copy(out=oak[:], in_=ps[4][:, 0:128]), reads=[psb[4]], writes=[b_oak])
                sc.op("dve", lambda e_: e_.tensor_copy(out=oav[:, :, 0:64],
                                                       in_=ps[5][:, 0:128].rearrange("p (h d) -> p h d", h=2)),
                      reads=[psb[5]], writes=[b_oav])
                sc.op("act", lambda e_: e_.copy(out=oh[:].rearrange("p c t -> p (c t)"), in_=ps[6][:, 0:18]),
                      reads=[psb[6]], writes=[b_oh])
                sc.dma(e_akT[:, :], oak[:], reads=[b_oak])
                sc.dma(e_av[:, :], oav[:].rearrange("p h d -> p (h d)"), reads=[b_oav])
                sc.dma(e_halo.rearrange("(c p) t -> p c t", p=128), oh[:], reads=[b_oh])
        sc.barrier()


K.export_phase = _export_phase


def _exchange_phase(self, tag, pr, ms, sel_d, cstack):
    sc = self.sc
    S, NT = self.S, self.NT
    for name, _r in _pair_chunks(S):
        sc.collective(cstack, "AllGather", pr["exp"][name].opt(), pr["gat"][name].opt(), GROUPS)
    Q4 = S // 4
    sc.collective(cstack, "AllGather", pr["expf"].opt(), pr["gatf"].opt(), GROUPS)
    with ExitStack() as st:
        sel = self.sb(st, tag + "sel", [128, 2], F32)
        b_sel = Buf("sel")
        sc.dma(sel[:], sel_d[:, :], writes=[b_sel])
        CW = min(S, 2048)
        a0 = [self.sb(st, tag + "a0_%d" % i, [128, CW], BF16) for i in range(2)]
        a1 = [self.sb(st, tag + "a1_%d" % i, [128, CW], BF16) for i in range(2)]
        b_a0 = [Buf("a0_0"), Buf("a0_1")]
        b_a1 = [Buf("a1_0"), Buf("a1_1")]
        f0 = self.sb(st, tag + "f0", [128, 9, 2], F32)
        f1 = self.sb(st, tag + "f1", [128, 9, 2], F32)
        b_f0, b_f1 = Buf("f0"), Buf("f1")
        cnt = [0]

        def select(dst_ap, src0, src1, np_, w, view=None):
            i = cnt[0] % 2
            cnt[0] += 1
            t0 = a0[i][0:np_, 0:w]
            t1 = a1[i][0:np_, 0:w]
            if view is not None:
                t0v, t1v = view(t0), view(t1)
            else:
                t0v, t1v = t0, t1
            sc.dma(t0v, src0, writes=[b_a0[i]])
            sc.dma(t1v, src1, writes=[b_a1[i]])
            sc.op("dve", lambda e: e.tensor_scalar_mul(out=t0, in0=t0, scalar1=sel[0:np_, 0:1]),
                  reads=[b_a0[i], b_sel], writes=[b_a0[i]])
            sc.op("dve", lambda e: e.scalar_tensor_tensor(out=t1, in0=t1, scalar=sel[0:np_, 1:2], in1=t0,
                                                          op0=ALU.mult, op1=ALU.add),
                  reads=[b_a0[i], b_a1[i], b_sel], writes=[b_a1[i]])
            sc.dma(dst_ap, t1v, reads=[b_a1[i]])

        gv = [_pviews(pr, S, 0), _pviews(pr, S, 1)]
        for ci in range(2):
            for c0 in range(0, S, CW):
                v = lambda slot: gv[slot]["bkT"][ci][:, c0:c0 + CW]
                select(ms["bkT"][ci * 128:(ci + 1) * 128, S + c0:S + c0 + CW], v(0), v(1), 128, CW)
        TPB = max(1, min(CW // 260, Q4 // 128))
        for q4 in range(4):
            for t0_ in range(0, Q4 // 128, TPB):
                tn = min(TPB, Q4 // 128 - t0_)
                v = lambda slot: gv[slot]["bv"][q4][t0_ * 128:(t0_ + tn) * 128, :].rearrange("(n p) d -> p n d", p=128)
                r0 = S + q4 * Q4 + t0_ * 128
                dst = ms["bv"][r0:r0 + tn * 128, :, :].rearrange("(n p) h d -> p n (h d)", p=128)
                select(dst, v(0), v(1), 128, tn * 260, view=lambda t: t.rearrange("p (n d) -> p n d", d=260))
        v = lambda slot: gv[slot]["akT"]
        select(ms["akT"][:, S:S + 128], v(0), v(1), 128, 128)
        v = lambda slot: gv[slot]["av"]
        select(ms["av"][S:S + 128, :, :].rearrange("p h d -> p (h d)"), v(0), v(1), 128, 130)
        hv = lambda slot: pr["gatf"][slot * 36:(slot + 1) * 36, :].rearrange("r c -> (r c)") \
            .rearrange("(c p t) -> p c t", p=128, t=2)
        sc.dma(f0[:], hv(0), writes=[b_f0])
        sc.dma(f1[:], hv(1), writes=[b_f1])
        sc.op("dve", lambda e: e.tensor_scalar_mul(out=f0[:], in0=f0[:], scalar1=sel[:, 0:1]),
              reads=[b_f0, b_sel], writes=[b_f0])
        sc.op("dve", lambda e: e.scalar_tensor_tensor(out=f1[:], in0=f1[:], scalar=sel[:, 1:2], in1=f0[:],
                                                      op0=ALU.mult, op1=ALU.add),
              reads=[b_f0, b_f1, b_sel], writes=[b_f1])
        sc.dma(ms["cpre"][:, S + 2:S + 4].rearrange("(c p) t -> p c t", p=128), f1[:], reads=[b_f1])
        sc.barrier()


K.exchange_phase = _exchange_phase


def _dn2(self, tag, ms, ds, dnc, a_log, dt_bias, dn_g, ident, ident_b, pr=None, sel_d=None, cstack=None, hook=None):
    sc = self.sc
    S, NT = self.S, self.NT
    ps, psb = self.ps, self.psb
    NIT = 7
    gcT = self.scratch(tag + "gcT", [12, S], F32)
    ngcT = self.scratch(tag + "ngcT", [12, S], F32)
    acT = self.scratch(tag + "acT", [12, S], F32)
    with ExitStack() as st0:
        def T0(name, shape, dt=F32):
            return self.sb(st0, tag + name, shape, dt), Buf(name)
        gc, b_gc = T0("gc", [128, NT, 12])
        ac, b_ac = T0("ac", [128, NT, 12])
        beta, b_beta = T0("beta", [128, NT, 12])
        ea, b_ea = T0("ea", [128, NT, 12])
        ekg, b_ekg = T0("ekg", [128, NT, 12])
        glv, b_glv = T0("glv", [128, NT, 12])
        ones, b_ones = T0("ones", [128, 128])
        sc.dma(ones[:], dnc[0], writes=[b_ones])
        with ExitStack() as st:
            def T1(name, shape, dt=F32):
                return self.sb(st, tag + "g_" + name, shape, dt), Buf(name)
            ba, b_ba = T1("ba", [128, NT, 24])
            w1, b_w1 = T1("w1", [128, NT, 12])
            w2, b_w2 = T1("w2", [128, NT, 12])
            sp, b_sp = T1("sp", [128, NT, 12])
            gg, b_gg = T1("gg", [128, NT, 12])
            tt, b_tt = T1("tt", [128, NT, 12])
            triF, b_triF = T1("triF", [128, 128])
            triB, b_triB = T1("triB", [128, 128])
            dtb, b_dtb = T1("dtb", [128, 12])
            nega, b_nega = T1("nega", [128, 12])
            ev = [T1("ev%d" % i, [12, 3, 512]) for i in range(2)]
            sc.dma(ba[:], ms["cz"][:, 384:408].rearrange("(n p) c -> p n c", p=128), writes=[b_ba])
            sc.dma(triF[:], dnc[1], writes=[b_triF])
            sc.dma(triB[:], dnc[2], writes=[b_triB])
            sc.dma(dtb[:], dt_bias.rearrange("a b -> (a b)").partition_broadcast(128), writes=[b_dtb])
            sc.dma(nega[:], a_log.rearrange("a b -> (a b)").partition_broadcast(128), writes=[b_nega])
            sc.op("act", lambda e: e.activation(out=nega[:], in_=nega[:], func=AF.Exp), reads=[b_nega], writes=[b_nega])
            sc.op("dve", lambda e: e.tensor_scalar_mul(out=nega[:], in0=nega[:], scalar1=-1.0),
                  reads=[b_nega], writes=[b_nega])
            bc = lambda t: t[:].unsqueeze(1).to_broadcast([128, NT, 12])
            sc.op("act", lambda e: e.activation(out=w1[:], in_=ba[:, :, 0:12], func=AF.Exp, scale=-1.0),
                  reads=[b_ba], writes=[b_w1])
            sc.op("dve", lambda e: e.tensor_scalar_add(out=w1[:], in0=w1[:], scalar1=1.0), reads=[b_w1], writes=[b_w1])
            sc.op("act", lambda e: e.activation(out=sp[:], in_=w1[:], func=AF.Ln), reads=[b_w1], writes=[b_sp])
            sc.op("dve", lambda e: e.tensor_tensor(out=w2[:], in0=ba[:, :, 12:24], in1=bc(dtb), op=ALU.add),
                  reads=[b_ba, b_dtb], writes=[b_w2])
            sc.op("act", lambda e: e.activation(out=w2[:], in_=w2[:], func=AF.Exp), reads=[b_w2], writes=[b_w2])
            sc.op("dve", lambda e: e.tensor_scalar_add(out=w2[:], in0=w2[:], scalar1=1.0), reads=[b_w2], writes=[b_w2])
            sc.op("act", lambda e: e.activation(out=w2[:], in_=w2[:], func=AF.Ln), reads=[b_w2], writes=[b_w2])
            sc.op("dve", lambda e: e.tensor_tensor(out=gg[:], in0=w2[:], in1=bc(nega), op=ALU.mult),
                  reads=[b_w2, b_nega], writes=[b_gg])
            NC6 = NT * 6
            for c0 in range(0, NT, 64):
                c1 = min(NT, c0 + 64)
                w = (c1 - c0) * 6
                sc.op("pe", lambda e: e.matmul(ps[0][:, 0:w], lhsT=triF[:], rhs=gg[:, c0:c1, 0:6], start=True, stop=True),
                      reads=[b_triF, b_gg], writes=[psb[0]], signal=False)
                sc.op("pe", lambda e: e.matmul(ps[1][:, 0:w], lhsT=triB[:], rhs=gg[:, c0:c1, 6:12], start=True, stop=True),
                      reads=[b_triB, b_gg], writes=[psb[1]], signal=False)
                sc.op("pe", lambda e: e.matmul(ps[2][:, 0:w], lhsT=ones[:], rhs=gg[:, c0:c1, 0:6], start=True, stop=True),
                      reads=[b_ones, b_gg], writes=[psb[2]], signal=False)
                sc.op("pe", lambda e: e.matmul(ps[3][:, 0:w], lhsT=ones[:], rhs=gg[:, c0:c1, 6:12], start=True, stop=True),
                      reads=[b_ones, b_gg], writes=[psb[3]])
                v6 = lambda b: ps[b][:, 0:w].rearrange("p (n j) -> p n j", j=6)
                sc.op("dve", lambda e: e.tensor_copy(out=gc[:, c0:c1, 0:6], in_=v6(0)), reads=[psb[0]], writes=[b_gc])
                sc.op("dve", lambda e: e.tensor_copy(out=gc[:, c0:c1, 6:12], in_=v6(1)), reads=[psb[1]], writes=[b_gc])
                sc.op("dve", lambda e: e.tensor_copy(out=tt[:, c0:c1, 0:6], in_=v6(2)), reads=[psb[2]], writes=[b_tt])
                sc.op("dve", lambda e: e.tensor_copy(out=tt[:, c0:c1, 6:12], in_=v6(3)), reads=[psb[3]], writes=[b_tt])
            sc.op("dve", lambda e: e.tensor_tensor(out=ac[:], in0=gc[:], in1=sp[:], op=ALU.subtract),
                  reads=[b_gc, b_sp], writes=[b_ac])
            sc.op("act", lambda e: e.activation(out=ea[:], in_=ac[:], func=AF.Exp), reads=[b_ac], writes=[b_ea])
            sc.op("act", lambda e: e.activation(out=beta[:], in_=sp[:], func=AF.Exp, scale=-1.0),
                  reads=[b_sp], writes=[b_beta])
            sc.op("dve", lambda e: e.tensor_tensor(out=w1[:], in0=tt[:], in1=gc[:], op=ALU.subtract),
                  reads=[b_tt, b_gc, b_w1], writes=[b_w1])
            sc.op("act", lambda e: e.activation(out=ekg[:], in_=w1[:], func=AF.Exp), reads=[b_w1], writes=[b_ekg])
            sc.op("act", lambda e: e.activation(out=glv[:], in_=tt[:], func=AF.Exp), reads=[b_tt], writes=[b_glv])
            for q0 in range(0, NT, 4):
                qn = min(4, NT - q0)
                (evt, b_evt) = ev[(q0 // 4) % 2]
                for i in range(qn):
                    n = q0 + i
                    sc.op("pe", lambda e: e.transpose(ps[4][0:12, i * 128:(i + 1) * 128], gc[:, n, :], ident[:]),
                          reads=[b_gc, ident_b], writes=[psb[4]], signal=False)
                    sc.op("pe", lambda e: e.transpose(ps[5][0:12, i * 128:(i + 1) * 128], ac[:, n, :], ident[:]),
                          reads=[b_ac, ident_b], writes=[psb[5]], signal=(i == qn - 1))
                w = qn * 128
                sc.op("dve", lambda e: e.tensor_copy(out=evt[:, 0, 0:w], in_=ps[4][0:12, 0:w]), reads=[psb[4]], writes=[b_evt])
                sc.op("dve", lambda e: e.tensor_scalar_mul(out=evt[:, 1, 0:w], in0=ps[4][0:12, 0:w], scalar1=-1.0),
                      reads=[psb[4]], writes=[b_evt])
                sc.op("act", lambda e: e.copy(out=evt[:, 2, 0:w], in_=ps[5][0:12, 0:w]), reads=[psb[5]], writes=[b_evt])
                sc.dma(gcT[:, q0 * 128:q0 * 128 + w], evt[:, 0, 0:w], reads=[b_evt])
                sc.dma(ngcT[:, q0 * 128:q0 * 128 + w], evt[:, 1, 0:w], reads=[b_evt])
                sc.dma(acT[:, q0 * 128:q0 * 128 + w], evt[:, 2, 0:w], reads=[b_evt])
            sc.barrier()
        for dirn in (0, 1):
            with ExitStack() as st:
                def T(name, shape, n=1, dt=F32):
                    ts = [self.sb(st, "%s%d%s%d" % (tag, dirn, name, i), shape, dt) for i in range(n)]
                    bs = [Buf("%s%d" % (name, i)) for i in range(n)]
                    return (ts, bs) if n > 1 else (ts[0], bs[0])
                m1, b_m1 = T("m1", [128, 128])
                m2, b_m2 = T("m2", [128, 128])
                m3, b_m3 = T("m3", [128, 128])
                gdn, b_gdn = T("gdn", [128, 64])
                kTt, b_kTt = T("kTt", [64, 6, 128], 2)
                qTt, b_qTt = T("qTt", [64, 6, 128], 2)
                kt, b_kt = T("kt", [128, 384], 2)
                vt, b_vt = T("vt", [128, 384], 2)
                Rg, b_Rg = T("Rg", [128, 6, 128], 2)
                Rn, b_Rn = T("Rn", [128, 6, 128], 2)
                Ra, b_Ra = T("Ra", [128, 6, 128], 2)
                eR, b_eR = T("eR", [64, 6, 128])
                qg, b_qg = T("qg", [64, 6, 128])
                tmp, b_tmp = T("tmp", [128, 3, 128], 6)
                E, b_E = T("E", [128, 3, 128], 6)
                W0, b_W0 = T("W0", [128, 3, 128], 6)
                W1, b_W1 = T("W1", [128, 3, 128], 6)
                qkT, b_qkT = T("qkT", [128, 128], 6)
                kg, b_kg = T("kg", [128, 64], 6)
                glI, b_glI = T("glI", [64, 64], 6)
                wTn, b_wTn = T("wTn", [64, 128], 6)
                Vn, b_Vn = T("Vn", [128, 64], 6)
                St, b_St = T("St", [64, 64], 6)
                osb, b_osb = T("osb", [128, 6, 64], 2)
                if dirn == 1:
                    gz, b_gz = T("gz", [128, 384], 2)
                    oft, b_oft = T("oft", [128, 6, 64], 2)
                    sqt, b_sqt = T("sqt", [128, 6, 64])
                    sz, b_sz = T("sz", [128, 6, 64])
                    rr, b_rr = T("rr", [128, 8])
                sc.dma(m1[:], dnc[4 + 3 * dirn], writes=[b_m1])
                sc.dma(m2[:], dnc[5 + 3 * dirn], writes=[b_m2])
                sc.dma(m3[:], dnc[6 + 3 * dirn], writes=[b_m3])
                sc.dma(gdn[:], dn_g.partition_broadcast(128), writes=[b_gdn])
                if dirn == 0 or not self.paired:
                    for h in range(6):
                        sc.op("pool", lambda e: e.memset(St[h][:], 0.0), writes=[b_St[h]])
                else:
                    sl, b_sl = T("sl", [128, 2])
                    s0, b_s0 = T("s0", [64, 6, 64])
                    s1, b_s1 = T("s1", [64, 6, 64])
                    sc.dma(sl[:], sel_d[:, :], writes=[b_sl])
                    sc.dma(s0[:], pr["gats"][0:384, :].rearrange("(h k) v -> k h v", h=6), writes=[b_s0])
                    sc.dma(s1[:], pr["gats"][384:768, :].rearrange("(h k) v -> k h v", h=6), writes=[b_s1])
                    sc.op("dve", lambda e: e.tensor_scalar_mul(out=s0[:], in0=s0[:], scalar1=sl[0:64, 0:1]),
                          reads=[b_s0, b_sl], writes=[b_s0])
                    for h in range(6):
                        sc.op("dve", lambda e: e.scalar_tensor_tensor(out=St[h][:], in0=s1[:, h, :], scalar=sl[0:64, 1:2],
                                                                      in1=s0[:, h, :], op0=ALU.mult, op1=ALU.add),
                              reads=[b_s0, b_s1, b_sl], writes=[b_St[h]])
                order = list(range(NT)) if dirn == 0 else list(range(NT - 1, -1, -1))
                j0 = dirn * 6

                def load(it):
                    n = order[it]
                    tb = it % 2
                    c0 = n * 128
                    sc.dma(kTt[tb][:], ds["ckT"][:, c0:c0 + 128].rearrange("(h d) t -> d h t", h=6), writes=[b_kTt[tb]])
                    sc.dma(qTt[tb][:], ds["cqT"][:, c0:c0 + 128].rearrange("(h d) t -> d h t", h=6), writes=[b_qTt[tb]])
                    sc.dma(kt[tb][:], ds["ck"][c0:c0 + 128, :], writes=[b_kt[tb]])
                    sc.dma(vt[tb][:], ds["cv"][c0:c0 + 128, :], writes=[b_vt[tb]])
                    sc.dma(Rg[tb][:], gcT[j0:j0 + 6, c0:c0 + 128].partition_broadcast(128), writes=[b_Rg[tb]])
                    sc.dma(Rn[tb][:], ngcT[j0:j0 + 6, c0:c0 + 128].partition_broadcast(128), writes=[b_Rn[tb]])
                    sc.dma(Ra[tb][:], acT[j0:j0 + 6, c0:c0 + 128].partition_broadcast(128), writes=[b_Ra[tb]])
                    if dirn == 1:
                        sc.dma(gz[tb][:], ms["cz"][c0:c0 + 128, 0:384], writes=[b_gz[tb]])
                        sc.dma(oft[tb][:].rearrange("p h d -> p (h d)"), ds["of"][c0:c0 + 128, :], writes=[b_oft[tb]])

                load(0)
                for it, n in enumerate(order):
                    tb = it % 2
                    c0 = n * 128
                    if it + 1 < NT:
                        load(it + 1)
                    sc.op("act", lambda e: e.activation(out=eR[:], in_=Rg[tb][0:64, :, :], func=AF.Exp),
                          reads=[b_Rg[tb]], writes=[b_eR])
                    sc.op("dve", lambda e: e.tensor_tensor(out=qg[:], in0=qTt[tb][:], in1=eR[:], op=ALU.mult),
                          reads=[b_qTt[tb], b_eR], writes=[b_qg])
                    for h in range(6):
                        bk = 2 + h
                        j = j0 + h
                        hs = slice(h * 64, (h + 1) * 64)
                        sc.op("pe", lambda e: e.matmul(ps[bk][:, 0:128], lhsT=kTt[tb][:, h, :], rhs=kTt[tb][:, h, :],
                                                       start=True, stop=True),
                              reads=[b_kTt[tb]], writes=[psb[bk]], signal=False)
                        sc.op("pe", lambda e: e.matmul(ps[bk][:, 128:256], lhsT=kTt[tb][:, h, :], rhs=qTt[tb][:, h, :],
                                                       start=True, stop=True),
                              reads=[b_kTt[tb], b_qTt[tb]], writes=[psb[bk]])
                        sc.op("dve", lambda e: e.scalar_tensor_tensor(out=tmp[h][:, 0, :], in0=Rn[tb][:, h, :],
                                                                      scalar=ac[:, n, j:j + 1], in1=m1[:],
                                                                      op0=ALU.add, op1=ALU.min),
                              reads=[b_Rn[tb], b_ac, b_m1], writes=[b_tmp[h]])
                        sc.op("dve", lambda e: e.scalar_tensor_tensor(out=tmp[h][:, 1, :], in0=Ra[tb][:, h, :],
                                                                      scalar=gc[:, n, j:j + 1], in1=m2[:],
                                                                      op0=ALU.subtract, op1=ALU.min),
                              reads=[b_Ra[tb], b_gc, b_m2], writes=[b_tmp[h]])
                        sc.op("dve", lambda e: e.scalar_tensor_tensor(out=tmp[h][:, 2, :], in0=Rg[tb][:, h, :],
                                                                      scalar=gc[:, n, j:j + 1], in1=m3[:],
                                                                      op0=ALU.subtract, op1=ALU.min),
                              reads=[b_Rg[tb], b_gc, b_m3], writes=[b_tmp[h]])
                        sc.op("act", lambda e: e.activation(out=E[h][:], in_=tmp[h][:], func=AF.Exp),
                              reads=[b_tmp[h]], writes=[b_E[h]])
                        sc.op("act", lambda e: e.activation(out=W0[h][:, 0, 0:64], in_=vt[tb][:, hs], func=AF.Copy,
                                                            scale=beta[:, n, j:j + 1]),
                              reads=[b_vt[tb], b_beta], writes=[b_W0[h]])
                        sc.op("act", lambda e: e.activation(out=W0[h][:, 0, 64:128], in_=kt[tb][:, hs], func=AF.Copy,
                                                            scale=ea[:, n, j:j + 1]),
                              reads=[b_kt[tb], b_ea], writes=[b_W0[h]])
                        sc.op("act", lambda e: e.activation(out=kg[h][:], in_=kt[tb][:, hs], func=AF.Copy,
                                                            scale=ekg[:, n, j:j + 1]),
                              reads=[b_kt[tb], b_ekg], writes=[b_kg[h]])
                        sc.op("act", lambda e: e.activation(out=glI[h][:], in_=ident[0:64, 0:64], func=AF.Copy,
                                                            scale=glv[0:64, n, j:j + 1]),
                              reads=[ident_b, b_glv], writes=[b_glI[h]])
                    for h in range(6):
                        bk = 2 + h
                        sc.op("dve", lambda e: e.scalar_tensor_tensor(out=W0[h][:, 1, :], in0=ps[bk][:, 0:128], scalar=-1.0,
                                                                      in1=E[h][:, 0, :], op0=ALU.mult, op1=ALU.mult),
                              reads=[psb[bk], b_E[h]], writes=[b_W0[h]])
                        sc.op("dve", lambda e: e.scalar_tensor_tensor(out=W0[h][:, 2, :], in0=ps[bk][:, 0:128], scalar=-1.0,
                                                                      in1=E[h][:, 1, :], op0=ALU.mult, op1=ALU.mult),
                              reads=[psb[bk], b_E[h]], writes=[b_W0[h]])
                        sc.op("dve", lambda e: e.tensor_tensor(out=qkT[h][:], in0=ps[bk][:, 128:256], in1=E[h][:, 2, :],
                                                               op=ALU.mult),
                              reads=[psb[bk], b_E[h]], writes=[b_qkT[h]])
                    WW = [(W0, b_W0), (W1, b_W1)]
                    DVE_H = (1, 3, 4, 5)
                    for i in range(NIT):
                        (Wc, b_Wc), (Wn, b_Wn) = WW[i % 2], WW[(i + 1) % 2]
                        last = (i == NIT - 1)
                        for h in range(6):
                            bk = 2 + h
                            cur = Wc[h]
                            flat = cur[:].rearrange("p a s -> p (a s)")
                            na = 128 if i >= NIT - 2 else 256
                            on_dve = h in DVE_H
                            sc.op("pe", lambda e: e.matmul(ps[bk][:, 0:na], lhsT=cur[:, 2, :], rhs=flat[:, 0:na],
                                                           start=True, stop=on_dve, skip_group_check=True),
                                  reads=[b_Wc[h]], writes=[psb[bk]], signal=(on_dve and last))
                            if not on_dve:
                                sc.op("pe", lambda e: e.matmul(ps[bk][:, 0:128], lhsT=ident[:], rhs=cur[:, 0, :],
                                                               start=False, stop=True, skip_group_check=True),
                                      reads=[b_Wc[h], ident_b], writes=[psb[bk]], signal=last)
                            if not last:
                                sc.op("pe", lambda e: e.matmul(ps[bk][:, 256:384], lhsT=cur[:, 1, :], rhs=cur[:, 2, :],
                                                               start=True, stop=True, skip_group_check=True),
                                      reads=[b_Wc[h]], writes=[psb[bk]])
                        for h in range(6):
                            bk = 2 + h
                            cur = Wc[h]
                            nflat = Wn[h][:].rearrange("p a s -> p (a s)")
                            if h in DVE_H:
                                sc.op("dve", lambda e: e.tensor_tensor(out=nflat[:, 0:128], in0=ps[bk][:, 0:128],
                                                                       in1=cur[:, 0, :], op=ALU.add),
                                      reads=[psb[bk], b_Wc[h]], writes=[b_Wn[h]])
                                if not last:
                                    lo = 128 if i < NIT - 2 else 256
                                    sc.op("dve", lambda e: e.tensor_copy(out=nflat[:, lo:384], in_=ps[bk][:, lo:384]),
                                          reads=[psb[bk]], writes=[b_Wn[h]])
                            else:
                                if last:
                                    sc.op("act", lambda e: e.copy(out=nflat[:, 0:128], in_=ps[bk][:, 0:128]),
                                          reads=[psb[bk]], writes=[b_Wn[h]])
                                elif i < NIT - 2:
                                    sc.op("act", lambda e: e.copy(out=nflat[:, 0:384], in_=ps[bk][:, 0:384]),
                                          reads=[psb[bk]], writes=[b_Wn[h]])
                                else:
                                    sc.op("act", lambda e: e.copy(out=nflat[:, 0:128], in_=ps[bk][:, 0:128]),
                                          reads=[psb[bk]], writes=[b_Wn[h]], )
                                    sc.op("act", lambda e: e.copy(out=nflat[:, 256:384], in_=ps[bk][:, 256:384]),
                                          reads=[psb[bk]], writes=[b_Wn[h]])
                        if hook is not None:
                            hook()
                    (Wf, b_Wf) = WW[NIT % 2]
                    ob = it % 2
                    for h in range(6):
                        bk = 2 + h
                        sc.op("pe", lambda e: e.transpose(ps[bk][0:64, 0:128], Wf[h][:, 0, 64:128], ident[:]),
                              reads=[b_Wf[h], ident_b], writes=[psb[bk]])
                    for h in range(6):
                        bk = 2 + h
                        if h % 2 == 0:
                            sc.op("act", lambda e: e.mul(out=wTn[h][:], in_=ps[bk][0:64, 0:128], mul=-1.0),
                                  reads=[psb[bk]], writes=[b_wTn[h]])
                        else:
                            sc.op("dve", lambda e: e.tensor_scalar_mul(out=wTn[h][:], in0=ps[bk][0:64, 0:128], scalar1=-1.0),
                                  reads=[psb[bk]], writes=[b_wTn[h]])
                    for h in range(6):
                        bk = 2 + h
                        sc.op("pe", lambda e: e.matmul(ps[bk][:, 128:192], lhsT=ident[:], rhs=Wf[h][:, 0, 0:64],
                                                       start=True, stop=False, skip_group_check=True),
                              reads=[b_Wf[h], ident_b], writes=[psb[bk]], signal=False)
                        sc.op("pe", lambda e: e.matmul(ps[bk][:, 128:192], lhsT=wTn[h][:], rhs=St[h][:],
                                                       start=False, stop=True, skip_group_check=True),
                              reads=[b_wTn[h], b_St[h]], writes=[psb[bk]])
                    for h in range(6):
                        bk = 2 + h
                        if h % 2 == 0:
                            sc.op("act", lambda e: e.copy(out=Vn[h][:], in_=ps[bk][:, 128:192]), reads=[psb[bk]], writes=[b_Vn[h]])
                        else:
                            sc.op("dve", lambda e: e.tensor_copy(out=Vn[h][:], in_=ps[bk][:, 128:192]), reads=[psb[bk]],
                                  writes=[b_Vn[h]])
                    for h in range(6):
                        bk = 2 + h
                        sc.op("pe", lambda e: e.matmul(ps[bk][:, 192:256], lhsT=qg[:, h, :], rhs=St[h][:],
                                                       start=True, stop=False, skip_group_check=True),
                              reads=[b_qg, b_St[h]], writes=[psb[bk]], signal=False)
                        sc.op("pe", lambda e: e.matmul(ps[bk][:, 192:256], lhsT=qkT[h][:], rhs=Vn[h][:],
                                                       start=False, stop=True, skip_group_check=True),
                              reads=[b_qkT[h], b_Vn[h]], writes=[psb[bk]], signal=False)
                        sc.op("pe", lambda e: e.matmul(ps[bk][0:64, 256:320], lhsT=kg[h][:], rhs=Vn[h][:],
                                                       start=True, stop=False, skip_group_check=True),
                              reads=[b_kg[h], b_Vn[h]], writes=[psb[bk]], signal=False)
                        sc.op("pe", lambda e: e.matmul(ps[bk][0:64, 256:320], lhsT=glI[h][:], rhs=St[h][:],
                                                       start=False, stop=True, skip_group_check=True),
                              reads=[b_glI[h], b_St[h]], writes=[psb[bk]])
                    for h in range(6):
                        bk = 2 + h
                        if h % 2 == 0:
                            sc.op("act", lambda e: e.copy(out=osb[ob][:, h, :], in_=ps[bk][:, 192:256]), reads=[psb[bk]],
                                  writes=[b_osb[ob]])
                            sc.op("act", lambda e: e.copy(out=St[h][:], in_=ps[bk][0:64, 256:320]), reads=[psb[bk]],
                                  writes=[b_St[h]])
                        else:
                            sc.op("dve", lambda e: e.tensor_copy(out=osb[ob][:, h, :], in_=ps[bk][:, 192:256]),
                                  reads=[psb[bk]], writes=[b_osb[ob]])
                            sc.op("dve", lambda e: e.tensor_copy(out=St[h][:], in_=ps[bk][0:64, 256:320]), reads=[psb[bk]],
                                  writes=[b_St[h]])
                    if dirn == 0:
                        sc.dma(ds["of"][c0:c0 + 128, :], osb[ob][:].rearrange("p h d -> p (h d)"), reads=[b_osb[ob]])
                    else:
                        sc.op("dve", lambda e: e.tensor_tensor(out=oft[tb][:], in0=oft[tb][:], in1=osb[ob][:], op=ALU.add),
                              reads=[b_oft[tb], b_osb[ob]], writes=[b_oft[tb]])
                        sc.op("act", lambda e: e.activation(out=sqt[:], in_=oft[tb][:], func=AF.Square),
                              reads=[b_oft[tb]], writes=[b_sqt])
                        sc.op("dve", lambda e: e.reduce_sum(out=rr[:, 0:6], in_=sqt[:], axis=AX.X), reads=[b_sqt],
                              writes=[b_rr])
                        sc.op("dve", lambda e: e.tensor_scalar(out=rr[:, 0:6], in0=rr[:, 0:6], scalar1=1.0 / 64,
                                                               scalar2=EPS, op0=ALU.mult, op1=ALU.add),
                              reads=[b_rr], writes=[b_rr])
                        sc.op("act", lambda e: e.sqrt(out=rr[:, 0:6], in_=rr[:, 0:6]), reads=[b_rr], writes=[b_rr])
                        sc.op("dve", lambda e: e.reciprocal(out=rr[:, 0:6], in_=rr[:, 0:6]), reads=[b_rr], writes=[b_rr])
                        sc.op("act", lambda e: e.activation(out=sz[:].rearrange("p h d -> p (h d)"), in_=gz[tb][:, 0:384],
                                                            func=AF.Silu), reads=[b_gz[tb]], writes=[b_sz])
                        sc.op("dve", lambda e: e.tensor_tensor(out=sz[:], in0=sz[:],
                                                                in1=gdn[:].unsqueeze(1).to_broadcast([128, 6, 64]),
                                                                op=ALU.mult), reads=[b_sz, b_gdn], writes=[b_sz])
                        sc.op("dve", lambda e: e.tensor_tensor(out=oft[tb][:], in0=oft[tb][:],
                                                                in1=rr[:, 0:6].unsqueeze(2).to_broadcast([128, 6, 64]),
                                                                op=ALU.mult), reads=[b_oft[tb], b_rr], writes=[b_oft[tb]])
                        sc.op("dve", lambda e: e.tensor_tensor(out=oft[tb][:], in0=oft[tb][:], in1=sz[:], op=ALU.mult),
                              reads=[b_oft[tb], b_sz], writes=[b_oft[tb]])
                        sc.dma(ms["omix"][c0:c0 + 128, 640:1024], oft[tb][:].rearrange("p h d -> p (h d)"),
                               reads=[b_oft[tb]])
                if dirn == 0 and self.paired:
                    for h in range(6):
                        sc.dma(pr["exps"][h * 64:(h + 1) * 64, :], St[h][:], reads=[b_St[h]])
                if dirn == 1 and hook is not None:
                    while hook():
                        pass
                sc.barrier()
            if dirn == 0 and self.paired:
                sc.collective(cstack, "AllGather", pr["exps"].opt(), pr["gats"].opt(), GROUPS)


K.dn2 = _dn2


def _conv_units(self, st, tag, ms, ds, conv_w, dnc, ident, ident_b):
    sc = self.sc
    S, NT = self.S, self.NT
    GT = 4 if NT % 4 == 0 else 1
    GW = GT * 128
    ps, psb = self.ps, self.psb
    blk1 = self.sb(st, tag + "blk1", [128, 128], F32)
    cw = self.sb(st, tag + "cw", [128, 9, 5], F32)
    xin = [self.sb(st, tag + "xin%d" % i, [128, GW + 4], F32) for i in range(2)]
    y = [self.sb(st, tag + "y%d" % i, [128, GW], F32) for i in range(2)]
    ee = [self.sb(st, tag + "ee%d" % i, [128, GW], F32) for i in range(2)]
    sq2 = [self.sb(st, tag + "sq%d" % i, [128, GW], F32) for i in range(2)]
    rs2 = [self.sb(st, tag + "rs%d" % i, [128, GW], F32) for i in range(2)]
    yn = [self.sb(st, tag + "yn%d" % i, [128, GW], F32) for i in range(2)]
    tk = [self.sb(st, tag + "tk%d" % i, [128, GW], F32) for i in range(2)]
    b_blk1, b_cw = Buf("blk1"), Buf("cw")
    b_xin = [Buf("xin0"), Buf("xin1")]
    b_y = [Buf("y0"), Buf("y1")]
    b_ee = [Buf("ee0"), Buf("ee1")]
    b_sq2 = [Buf("sq0"), Buf("sq1")]
    b_rs2 = [Buf("rs0"), Buf("rs1")]
    b_yn = [Buf("yn0"), Buf("yn1")]
    b_tk = [Buf("tk0"), Buf("tk1")]
    sc.dma(blk1[:], dnc[3], writes=[b_blk1])
    for ci in range(9):
        sc.dma(cw[:, ci, :], conv_w[:, ci * 128:(ci + 1) * 128].rearrange("j c -> c j"), writes=[b_cw],
               allow_slow_non_contiguous=True)
    units = []
    cnt = [0]

    def make(g0, ci):
        tok0 = g0 * 128
        b = cnt[0] % 2
        cnt[0] += 1
        sq, rs, b_sq, b_rs = sq2[b], rs2[b], b_sq2[b], b_rs2[b]

        def p1():
            sc.dma(xin[b][:], ms["cpre"][ci * 128:(ci + 1) * 128, tok0:tok0 + GW + 4], writes=[b_xin[b]])
            sc.op("dve", lambda e: e.tensor_scalar_mul(out=y[b][:], in0=xin[b][:, 0:GW], scalar1=cw[:, ci, 0:1]),
                  reads=[b_xin[b], b_cw], writes=[b_y[b]])
            for j in range(1, 5):
                sc.op("dve", lambda e: e.scalar_tensor_tensor(out=y[b][:], in0=xin[b][:, j:j + GW],
                                                              scalar=cw[:, ci, j:j + 1], in1=y[b][:],
                                                              op0=ALU.mult, op1=ALU.add),
                      reads=[b_xin[b], b_cw, b_y[b]], writes=[b_y[b]])
            sc.op("act", lambda e: e.activation(out=ee[b][:], in_=y[b][:], func=AF.Exp, scale=-1.0),
                  reads=[b_y[b]], writes=[b_ee[b]])
            sc.op("dve", lambda e: e.tensor_scalar_add(out=ee[b][:], in0=ee[b][:], scalar1=1.0),
                  reads=[b_ee[b]], writes=[b_ee[b]])
            sc.op("dve", lambda e: e.reciprocal(out=ee[b][:], in_=ee[b][:]), reads=[b_ee[b]], writes=[b_ee[b]])
            sc.op("dve", lambda e: e.tensor_tensor(out=y[b][:], in0=y[b][:], in1=ee[b][:], op=ALU.mult),
                  reads=[b_y[b], b_ee[b]], writes=[b_y[b]])
            if ci < 6:
                sc.op("dve", lambda e: e.tensor_tensor(out=sq[:], in0=y[b][:], in1=y[b][:], op=ALU.mult),
                      reads=[b_y[b]], writes=[b_sq])

        def p2():
            if ci < 6:
                sc.op("pe", lambda e: e.matmul(ps[6][:, :GW], lhsT=blk1[:], rhs=sq[:], start=True, stop=True),
                      reads=[b_blk1, b_sq], writes=[psb[6]])
                mul = 64.0 if ci < 3 else 1.0
                sc.op("dve", lambda e: e.tensor_scalar(out=rs[:], in0=ps[6][:, :GW], scalar1=EPS, scalar2=mul,
                                                       op0=ALU.add, op1=ALU.mult),
                      reads=[psb[6]], writes=[b_rs])
                sc.op("act", lambda e: e.activation(out=rs[:], in_=rs[:], func=AF.Ln), reads=[b_rs], writes=[b_rs])
                sc.op("act", lambda e: e.activation(out=rs[:], in_=rs[:], func=AF.Exp, scale=-0.5),
                      reads=[b_rs], writes=[b_rs])
                sc.op("dve", lambda e: e.tensor_tensor(out=yn[b][:], in0=y[b][:], in1=rs[:], op=ALU.mult),
                      reads=[b_y[b], b_rs], writes=[b_yn[b]])
                if ci < 3:
                    sc.dma(ds["cqT"][ci * 128:(ci + 1) * 128, tok0:tok0 + GW], yn[b][:], reads=[b_yn[b]])
                else:
                    sc.dma(ds["ckT"][(ci - 3) * 128:(ci - 2) * 128, tok0:tok0 + GW], yn[b][:], reads=[b_yn[b]])

        def p3():
            if ci < 6:
                src_t, src_b = yn[b], b_yn[b]
            else:
                src_t, src_b = y[b], b_y[b]
            if ci >= 3:
                for tt in range(GT):
                    sc.op("pe", lambda e: e.transpose(ps[7][:, tt * 128:(tt + 1) * 128],
                                                      src_t[:, tt * 128:(tt + 1) * 128], ident[:]),
                          reads=[src_b, ident_b], writes=[psb[7]], signal=(tt == GT - 1))
                sc.op("dve", lambda e: e.tensor_copy(out=tk[b][:], in_=ps[7][:, :GW]), reads=[psb[7]], writes=[b_tk[b]])
                dst = ds["ck"] if ci < 6 else ds["cv"]
                cc = (ci - 3) % 3
                sc.dma(dst[tok0:tok0 + GW, cc * 128:(cc + 1) * 128].rearrange("(t p) c -> p t c", p=128),
                       tk[b][:].rearrange("p (t c) -> p t c", c=128), reads=[b_tk[b]])
        return [p1, p2, p3]

    for g0 in range(0, NT, GT):
        for ci in range(9):
            units.extend(make(g0, ci))
    return units


K.conv_units = _conv_units


def _win_units(self, st, tag, ms, wbias, sink):
    sc = self.sc
    S, NT = self.S, self.NT
    NKA = NT + 1 if self.paired else NT
    ps, psb = self.ps, self.psb
    qT = self.sb(st, tag + "qT", [128, S], BF16)
    kT = self.sb(st, tag + "kT", [128, NKA * 128], BF16)
    vA = self.sb(st, tag + "vA", [128, NKA, 65], BF16)
    wb = self.sb(st, tag + "wb", [128, 384], F32)
    sT = [self.sb(st, tag + "sT%d" % i, [128, 384], F32) for i in range(2)]
    pT = [self.sb(st, tag + "pT%d" % i, [128, 384], BF16) for i in range(2)]
    ow = self.sb(st, tag + "ow", [128, NT, 64], F32)
    ou = self.sb(st, tag + "ou", [128, NT, 65], F32)
    dn_ = self.sb(st, tag + "dn", [128, NT], F32)
    es = self.sb(st, tag + "es", [128, 6], F32)
    b_qT, b_kT, b_vA, b_wb, b_ow, b_es, b_ou, b_dn = (Buf(n) for n in ("qT", "kT", "vA", "wb", "ow", "es", "ou", "dn"))
    b_sT = [Buf("sT0"), Buf("sT1")]
    b_pT = [Buf("pT0"), Buf("pT1")]
    units = []
    cnt = [0]

    def setup0():
        sc.op("dve", lambda e: e.memset(qT[:], 0.0), writes=[b_qT])
        sc.op("dve", lambda e: e.memset(kT[:], 0.0), writes=[b_kT])
        sc.dma(es[:], sink.partition_broadcast(128), writes=[b_es])
        sc.op("act", lambda e: e.activation(out=es[:], in_=es[:], func=AF.Exp), reads=[b_es], writes=[b_es])
    units.append(setup0)

    def mk_head(h):
        def f():
            kh = h // 3
            if h % 3 == 0:
                sc.dma(kT[0:64, :], ms["akT"][kh * 64:(kh + 1) * 64, :], writes=[b_kT])
                sc.dma(vA[:], ms["av"][:, kh, :].rearrange("(n p) d -> p n d", p=128), writes=[b_vA])
            sc.dma(qT[0:64, :], ms["aqT"][h * 64:(h + 1) * 64, :], writes=[b_qT])
            sc.dma(wb[:], wbias[h], writes=[b_wb])
        return f

    def mk_a(h, n):
        def f():
            js = [j for j in (0, 1, 2) if 0 <= n - 1 + j < NKA]
            c0 = js[0] * 128
            w = len(js) * 128
            sb_ = n % 2
            for jj, j in enumerate(js):
                kt = n - 1 + j
                sc.op("pe", lambda e: e.matmul(ps[0][:, jj * 128:(jj + 1) * 128], lhsT=kT[:, kt * 128:(kt + 1) * 128],
                                               rhs=qT[:, n * 128:(n + 1) * 128], start=True, stop=True),
                      reads=[b_kT, b_qT], writes=[psb[0]], signal=(jj == len(js) - 1))
            sc.op("dve", lambda e: e.tensor_tensor(out=sT[sb_][:, 0:w], in0=ps[0][:, 0:w], in1=wb[:, c0:c0 + w],
                                                   op=ALU.add),
                  reads=[psb[0], b_wb], writes=[b_sT[sb_]])
            sc.op("act", lambda e: e.activation(out=pT[sb_][:, 0:w], in_=sT[sb_][:, 0:w], func=AF.Exp),
                  reads=[b_sT[sb_]], writes=[b_pT[sb_]])
        return f

    def mk_b(h, n):
        def f():
            js = [j for j in (0, 1, 2) if 0 <= n - 1 + j < NKA]
            sb_ = n % 2
            for jj, j in enumerate(js):
                kt = n - 1 + j
                sc.op("pe", lambda e: e.matmul(ps[1][:, 0:65], lhsT=pT[sb_][:, jj * 128:(jj + 1) * 128],
                                               rhs=vA[:, kt, :], start=(jj == 0), stop=(jj == len(js) - 1)),
                      reads=[b_pT[sb_], b_vA], writes=[psb[1]], signal=(jj == len(js) - 1))
            sc.op("dve", lambda e: e.tensor_copy(out=ou[:, n, :], in_=ps[1][:, 0:65]), reads=[psb[1]], writes=[b_ou])
        return f

    def both(fb, fa):
        def f():
            fb()
            fa()
        return f

    def mk_fin(h):
        def f():
            sc.op("dve", lambda e: e.tensor_scalar(out=dn_[:], in0=ou[:, :, 64], scalar1=es[:, h:h + 1], scalar2=None,
                                                   op0=ALU.add), reads=[b_ou, b_es], writes=[b_dn])
            sc.op("dve", lambda e: e.reciprocal(out=dn_[:], in_=dn_[:]), reads=[b_dn], writes=[b_dn])
            sc.op("dve", lambda e: e.tensor_tensor(out=ow[:], in0=ou[:, :, 0:64],
                                                   in1=dn_[:].unsqueeze(2).to_broadcast([128, NT, 64]), op=ALU.mult),
                  reads=[b_ou, b_dn], writes=[b_ow])
            sc.dma(ms["omix"][:, h * 64:(h + 1) * 64].rearrange("(n p) d -> p n d", p=128), ow[:], reads=[b_ow])
        return f

    for h in range(6):
        units.append(mk_head(h))
        units.append(mk_a(h, 0))
        for n in range(1, NT):
            units.append(both(mk_b(h, n - 1), mk_a(h, n)))
        units.append(mk_b(h, NT - 1))
        units.append(mk_fin(h))
    return units


K.win_units = _win_units

import numpy as np, ml_dtypes
BF = ml_dtypes.bfloat16
def diff_consts(S, SK=None):
    SK = SK or S
    pos = np.arange(SK)
    H = (pos // 128) * 128.0
    L = (pos % 128) * 1.0
    daq = np.zeros((4, 4, SK), np.float32); dakp = np.zeros((4, 4, SK), np.float32)
    dbd = np.zeros((4, 128, 128), np.float32)
    for h in range(4):
        s = 2.0 ** (-8.0 * (h + 1) / 4)
        daq[h, 0] = -s * H; daq[h, 1] = -s * L; daq[h, 2] = 1; daq[h, 3] = 1
        dakp[h, 0] = 1; dakp[h, 1] = 1; dakp[h, 2] = s * H; dakp[h, 3] = s * L
        kk = np.arange(128)[:, None]; qq = np.arange(128)[None, :]
        dbd[h] = -s * np.abs(qq - kk)
    daq = daq[:, :, :S]
    c = dict(daq=np.ascontiguousarray(daq).astype(BF), dakp=dakp.astype(BF), dakm=(-dakp).astype(BF), dbd=dbd.astype(BF),
             identb=np.eye(128, dtype=np.float32).astype(BF))
    assert np.array_equal(c["daq"].astype(np.float32), daq) and np.array_equal(c["dakp"].astype(np.float32), dakp)
    assert np.array_equal(c["dbd"].astype(np.float32), dbd)
    return c

def win_consts():
    wb = np.zeros((6, 128, 384), np.float32)
    k = np.arange(128)[:, None]; q = np.arange(128)[None, :]
    for h in range(6):
        s = np.float32(2.0) ** np.float32(-8.0 * (h + 1) / 6)
        for j in range(3):
            rel = (j - 1) * 128 + k - q
            b = np.where(np.abs(rel) <= 128, -np.float32(s) * np.abs(rel).astype(np.float32), np.float32(-30000.0))
            wb[h, :, j * 128:(j + 1) * 128] = b
    return dict(wbias=wb)

def dn_consts():
    c = np.zeros((10, 128, 128), np.float32)
    p = np.arange(128)[:, None]; f = np.arange(128)[None, :]
    c[0] = 1.0
    c[1] = (p <= f)
    c[2] = (p >= f)
    c[3] = ((p // 64) == (f // 64))
    BIG = 30000.0
    c[4] = np.where(p > f, 0.0, -BIG)
    c[5] = np.where(f > p, 0.0, -BIG)
    c[6] = np.where(f >= p, 0.0, -BIG)
    c[7] = np.where(p < f, 0.0, -BIG)
    c[8] = np.where(f < p, 0.0, -BIG)
    c[9] = np.where(f <= p, 0.0, -BIG)
    return dict(dnc=c)


_CACHE = {}
N_CORES = 8


def _get_built(S_loc):
    if S_loc not in _CACHE:
        _CACHE[S_loc] = build_full(S_loc, 2, paired=True)
    return _CACHE[S_loc]


def kernel(**inputs):
    from concourse.bass_utils import run_bass_kernel_spmd
    x = np.asarray(inputs["x"])
    B, S, _ = x.shape
    assert 2 * B == N_CORES
    k = _get_built(S // 2)
    in_maps = pair_feeds(inputs, S)
    res = run_bass_kernel_spmd(k.nc, in_maps, core_ids=list(range(N_CORES)))
    return pair_gather(res.results, B, S)
```

```python
import numpy as np
import concourse.bass as bass
import concourse.mybir as mybir

F32 = mybir.dt.float32
BF16 = mybir.dt.bfloat16
AF = mybir.ActivationFunctionType
ALU = mybir.AluOpType
AX = mybir.AxisListType


class Buf:
    __slots__ = ("name", "w", "rs", "excl")

    def __init__(self, name, excl=False):
        self.name = name
        self.excl = excl
        self.w = None
        self.rs = []


class Sched:
    NDMA = 24

    def __init__(self, nc, stack):
        self.nc = nc
        self.eng = {"pe": nc.tensor, "act": nc.scalar, "dve": nc.vector, "pool": nc.gpsimd, "sp": nc.sync}
        self.sem = {}
        self.cnt = {}
        for k in self.eng:
            self.sem[k] = stack.enter_context(nc.semaphore("sem_" + k))
            self.cnt[k] = 0
        self.dsem = [stack.enter_context(nc.semaphore("dsem%d" % i)) for i in range(self.NDMA)]
        self.dgen = [0] * self.NDMA
        self.qslots = {"sp": list(range(0, 16)), "pool": list(range(16, 20)), "act": list(range(20, 24))}
        self.qnext = {"sp": 0, "pool": 0, "act": 0}
        self.waited = {k: {} for k in self.eng}
        self.pending_pe = False
        self.n_ins = 0
        self.n_wait = 0

    def _semobj(self, key):
        if isinstance(key, int):
            return self.dsem[key]
        return self.sem[key]

    def _need(self, e, evs):
        best = {}
        for ev in evs:
            if ev is None:
                continue
            k, v = ev
            if k == e and e in ("pe", "sp"):
                continue
            if best.get(k, 0) < v:
                best[k] = v
        w = self.waited[e]
        for k, v in best.items():
            if w.get(k, 0) >= v:
                continue
            self.eng[e].wait_ge(self._semobj(k), v)
            self.n_wait += 1
            w[k] = v

    def _deps(self, reads, writes, e=None):
        evs = []
        for b in reads:
            evs.append(b.w)
            if b.excl:
                evs.extend(r for r in b.rs if r[0] != e)
        for b in writes:
            evs.append(b.w)
            evs.extend(b.rs)
        return evs

    def _commit(self, ev, reads, writes):
        for b in reads:
            b.rs.append(ev)
            if len(b.rs) > 64:
                mx = {}
                for k, v in b.rs:
                    if mx.get(k, 0) < v:
                        mx[k] = v
                b.rs = list(mx.items())
        for b in writes:
            b.w = ev
            b.rs = []

    def op(self, e, fn, reads=(), writes=(), signal=True):
        if e != "pe":
            assert not self.pending_pe, "non-signaling PE op must be followed by signaling PE op"
        self._need(e, self._deps(reads, writes, e))
        ins = fn(self.eng[e])
        self.n_ins += 1
        if e == "pe":
            self.pending_pe = not signal
        if signal:
            self.cnt[e] += 1
            ins.then_inc(self.sem[e], 1)
            ev = (e, self.cnt[e])
        else:
            ev = (e, self.cnt[e] + 1)
        self._commit(ev, reads, writes)
        return ins

    def dma(self, out, in_, reads=(), writes=(), q="sp", **kw):
        assert not self.pending_pe
        sl = self.qslots[q]
        slot = sl[self.qnext[q] % len(sl)]
        self.qnext[q] += 1
        evs = self._deps(reads, writes)
        if self.dgen[slot] > 0:
            evs.append((slot, 16 * self.dgen[slot]))
        self._need(q, evs)
        self.dgen[slot] += 1
        ins = self.eng[q].dma_start(out=out, in_=in_, **kw)
        ins.then_inc(self.dsem[slot], 16)
        self.n_ins += 1
        ev = (slot, 16 * self.dgen[slot])
        self._commit(ev, reads, writes)
        return ins

    def barrier(self):
        evs = [(k, self.cnt[k]) for k in self.eng if self.cnt[k] > 0]
        evs += [(i, 16 * self.dgen[i]) for i in range(self.NDMA) if self.dgen[i] > 0]
        for e in self.eng:
            w = self.waited[e]
            for k, v in evs:
                if k == e and e in ("pe", "sp"):
                    continue
                if w.get(k, 0) >= v:
                    continue
                self.eng[e].wait_ge(self._semobj(k), v)
                w[k] = v

    def collective(self, stack, kind, in_ap, out_ap, groups):
        import concourse.mybir as mybir
        self.barrier()
        sem = stack.enter_context(self.nc.semaphore("ccsem%d" % self.n_ins))
        g = self.eng["pool"]
        g.collective_compute(kind, mybir.AluOpType.bypass, replica_groups=groups,
                             ins=[in_ap], outs=[out_ap]).then_inc(sem)
        g.wait_ge(sem, 1)
        self.n_ins += 1
        self.cnt["pool"] += 1
        g.engine_nop().then_inc(self.sem["pool"], 1)
        self.barrier()

    def finish(self, out_bufs):
        self.barrier()

import numpy as np
from contextlib import ExitStack
import concourse.bass as bass
import concourse.mybir as mybir

D = 1024
DFF = 2752
NFC = 22
EPS = 1e-6


class K:
    def __init__(self, S, depth=2, paired=False):
        self.S = S
        self.paired = paired
        self.SK = 2 * S if paired else S
        self.NTK = self.SK // 128
        self.NT = S // 128
        self.depth = depth
        self.nc = bass.Bass("TRN2", target_bir_lowering=False)
        self.stack = ExitStack()
        self.sc = Sched(self.nc, self.stack)
        self.ins = {}
        nc = self.nc
        self.ps = []
        self.psb = []
        for i in range(8):
            t = self.stack.enter_context(nc.psum_tensor("ps%d" % i, [128, 512], F32))
            self.ps.append(t)
            self.psb.append(Buf("ps%d" % i, excl=True))

    def inp(self, name, shape, dt=F32):
        t = self.nc.dram_tensor(name, list(shape), dt, kind="ExternalInput").ap()
        self.ins[name] = t
        return t

    def outp(self, name, shape, dt=F32):
        return self.nc.dram_tensor(name, list(shape), dt, kind="ExternalOutput").ap()

    def scratch(self, name, shape, dt=F32):
        return self.nc.dram_tensor(name, list(shape), dt, kind="Internal").ap()

    def sb(self, st, name, shape, dt=F32):
        return st.enter_context(self.nc.sbuf_tensor(name, list(shape), dt))

    def ffn_phase(self, tag, w_in, w_out, g, src, src_bufs, dst, dst_bufs, ident, ident_b):
        sc = self.sc
        S, NT = self.S, self.NT
        GT = 4 if NT % 4 == 0 else 1
        GW = GT * 128
        NG = NT // GT
        with ExitStack() as st:
            w1 = self.sb(st, tag + "w1", [128, 8, 2 * DFF], BF16)
            w2 = self.sb(st, tag + "w2", [128, NFC, D], BF16)
            gB = self.sb(st, tag + "gB", [128, D], F32)
            xt = [self.sb(st, tag + "xt%d" % i, [128, D], F32) for i in range(2)]
            xr = [self.sb(st, tag + "xr%d" % i, [128, D], F32) for i in range(2)]
            xn = self.sb(st, tag + "xn", [128, D], F32)
            xnT = [self.sb(st, tag + "xnT%d" % i, [128, 8, GW], BF16) for i in range(2)]
            hT = self.sb(st, tag + "hT", [128, NFC, GW], BF16)
            sg = [self.sb(st, tag + "sg%d" % i, [128, GW], F32) for i in range(2)]
            stat = self.sb(st, tag + "stat", [128, 8], F32)
            b_w1 = [Buf("w1_%d" % k) for k in range(8)]
            b_w2 = [Buf("w2_%d" % c) for c in range(NFC)]
            b_gB = Buf("gB")
            b_xt = [Buf("xt0"), Buf("xt1")]
            b_xr = [Buf("xr0"), Buf("xr1")]
            b_xn, b_stat = Buf("xn"), Buf("stat")
            b_xnT = [Buf("xnT0"), Buf("xnT1")]
            b_hT = [Buf("hT%d" % c) for c in range(NFC)]
            b_sg = [Buf("sg0"), Buf("sg1")]
            ps, psb = self.ps, self.psb
            for kc in range(8):
                for hf in range(2):
                    sc.dma(w1[:, kc, hf * DFF:(hf + 1) * DFF], w_in[kc * 128:(kc + 1) * 128, hf * DFF:(hf + 1) * DFF],
                           writes=[b_w1[kc]], q="pool")
            for c in range(NFC):
                cw = min(128, DFF - c * 128)
                sc.dma(w2[:cw, c, :], w_out[c * 128:c * 128 + cw, :], writes=[b_w2[c]], q="pool")
            sc.dma(gB[:], g.partition_broadcast(128), writes=[b_gB])
            tcount = [0]

            def ln_tile(gi, tt):
                t = gi * GT + tt
                xb = tcount[0] % 2
                tcount[0] += 1
                xq = xnT[gi % 2]
                bq = b_xnT[gi % 2]
                sc.dma(xt[xb][:], src[t * 128:(t + 1) * 128, :], reads=[src_bufs[t]], writes=[b_xt[xb]])
                sc.op("act", lambda e: e.activation(out=xn[:], in_=xt[xb][:], func=AF.Square, accum_out=stat[:, 0:1]),
                      reads=[b_xt[xb]], writes=[b_xn, b_stat])
                sc.op("dve", lambda e: e.tensor_scalar(out=stat[:, 1:2], in0=stat[:, 0:1], scalar1=1.0 / D,
                                                       scalar2=EPS, op0=ALU.mult, op1=ALU.add),
                      reads=[b_stat], writes=[b_stat])
                sc.op("act", lambda e: e.sqrt(out=stat[:, 3:4], in_=stat[:, 1:2]), reads=[b_stat], writes=[b_stat])
                sc.op("dve", lambda e: e.reciprocal(out=stat[:, 2:3], in_=stat[:, 3:4]), reads=[b_stat], writes=[b_stat])
                sc.op("dve", lambda e: e.scalar_tensor_tensor(out=xn[:], in0=xt[xb][:], scalar=stat[:, 2:3],
                                                              in1=gB[:], op0=ALU.mult, op1=ALU.mult),
                      reads=[b_xt[xb], b_stat, b_gB], writes=[b_xn])
                for kc in range(8):
                    bank = 6 + kc // 4
                    sc.op("pe", lambda e: e.transpose(ps[bank][:, (kc % 4) * 128:(kc % 4 + 1) * 128],
                                                      xn[:, kc * 128:(kc + 1) * 128], ident[:]),
                          reads=[b_xn, ident_b], writes=[psb[bank]], signal=(kc % 4 == 3))
                sc.op("act", lambda e: e.copy(out=xq[:, 0:4, tt * 128:(tt + 1) * 128],
                                              in_=ps[6][:, :].rearrange("p (k t) -> p k t", k=4)),
                      reads=[psb[6]], writes=[bq])
                sc.op("dve", lambda e: e.tensor_copy(out=xq[:, 4:8, tt * 128:(tt + 1) * 128],
                                                     in_=ps[7][:, :].rearrange("p (k t) -> p k t", k=4)),
                      reads=[psb[7]], writes=[bq])

            for tt in range(GT):
                ln_tile(0, tt)
            for gi in range(NG):
                g0 = gi * GT
                xq = xnT[gi % 2]
                bq = b_xnT[gi % 2]
                for c in range(NFC):
                    cw = min(128, DFF - c * 128)
                    pg, pu = 2 + 2 * (c % 2), 3 + 2 * (c % 2)
                    for kc in range(8):
                        sc.op("pe", lambda e: e.matmul(ps[pg][:cw, :GW], lhsT=w1[:, kc, c * 128:c * 128 + cw],
                                                       rhs=xq[:, kc, :], start=(kc == 0), stop=(kc == 7)),
                              reads=[b_w1[kc], bq], writes=[psb[pg]], signal=(kc == 7))
                    for kc in range(8):
                        sc.op("pe", lambda e: e.matmul(ps[pu][:cw, :GW],
                                                       lhsT=w1[:, kc, DFF + c * 128:DFF + c * 128 + cw],
                                                       rhs=xq[:, kc, :], start=(kc == 0), stop=(kc == 7)),
                              reads=[b_w1[kc], bq], writes=[psb[pu]], signal=(kc == 7))
                    sc.op("act", lambda e: e.activation(out=sg[c % 2][:cw, :], in_=ps[pg][:cw, :GW], func=AF.Silu),
                          reads=[psb[pg]], writes=[b_sg[c % 2]])
                    sc.op("dve", lambda e: e.tensor_tensor(out=hT[:cw, c, :], in0=ps[pu][:cw, :GW],
                                                           in1=sg[c % 2][:cw, :], op=ALU.mult),
                          reads=[psb[pu], b_sg[c % 2]], writes=[b_hT[c]])
                for tt in range(GT):
                    t = g0 + tt
                    rb = t % 2
                    sc.dma(xr[rb][:], src[t * 128:(t + 1) * 128, :], reads=[src_bufs[t]], writes=[b_xr[rb]])
                    for hf in range(2):
                        for c in range(NFC):
                            cw = min(128, DFF - c * 128)
                            sc.op("pe", lambda e: e.matmul(ps[hf][:, :], lhsT=hT[:cw, c, tt * 128:(tt + 1) * 128],
                                                           rhs=w2[:cw, c, hf * 512:(hf + 1) * 512],
                                                           start=(c == 0), stop=(c == NFC - 1)),
                                  reads=[b_hT[c], b_w2[c]], writes=[psb[hf]], signal=(c == NFC - 1))
                    if gi + 1 < NG:
                        ln_tile(gi + 1, tt)
                    for hf in range(2):
                        sc.op("dve", lambda e: e.scalar_tensor_tensor(
                            out=xr[rb][:, hf * 512:(hf + 1) * 512], in0=ps[hf][:, :], scalar=0.5,
                            in1=xr[rb][:, hf * 512:(hf + 1) * 512], op0=ALU.mult, op1=ALU.add),
                              reads=[psb[hf], b_xr[rb]], writes=[b_xr[rb]])
                    sc.dma(dst[t * 128:(t + 1) * 128, :], xr[rb][:], reads=[b_xr[rb]], writes=[dst_bufs[t]], q="pool")
            sc.barrier()


HD = 64
MIX_IN = 2968
C_AQ, C_AK, C_AV = 0, 384, 512
C_BQ, C_BK, C_BV = 640, 896, 1152
C_CQKV, C_CZ, C_CB, C_CA = 1408, 2560, 2944, 2956
FM_CHUNKS = ([(C_AQ + 128 * i, "aq", i) for i in range(3)] + [(C_AK, "ak", 0)] +
             [(C_BQ + 128 * i, "bq", i) for i in range(2)] + [(C_BK + 128 * i, "bk", i) for i in range(2)] +
             [(C_CQKV + 128 * i, "c", i) for i in range(9)])


def _mix_scratch(self):
    S = self.S
    SK = self.SK
    SA = S + 128 if self.paired else S
    d = {}
    d["aqT"] = self.scratch("aqT", [384, S], BF16)
    d["akT"] = self.scratch("akT", [128, SA], BF16)
    d["av"] = self.scratch("av", [SA, 2, 65], BF16)
    d["bqT"] = self.scratch("bqT", [256, S], BF16)
    d["bkT"] = self.scratch("bkT", [256, SK], BF16)
    d["bv"] = self.scratch("bv", [SK, 4, 65], BF16)
    d["cpre"] = self.scratch("cpre", [1152, S + 4], F32)
    d["cz"] = self.scratch("cz", [S, 408], F32)
    d["omix"] = self.scratch("omix", [S, D], F32)
    return d


K.mix_scratch = _mix_scratch


def _inproj_phase(self, tag, w_mi, g, src, src_bufs, ms, ident, ident_b):
    sc = self.sc
    S, NT = self.S, self.NT
    GT = 4 if NT % 4 == 0 else 1
    GW = GT * 128
    ps, psb = self.ps, self.psb
    with ExitStack() as st:
        wm = self.sb(st, tag + "wm", [128, 8, MIX_IN], BF16)
        gB = self.sb(st, tag + "gB", [128, D], F32)
        xt = [self.sb(st, tag + "xt%d" % i, [128, D], F32) for i in range(2)]
        xn = self.sb(st, tag + "xn", [128, D], F32)
        junk = self.sb(st, tag + "junk", [128, D], F32)
        xnT = self.sb(st, tag + "xnT", [128, 8, GW], BF16)
        stat = self.sb(st, tag + "stat", [128, 8], F32)
        ob16 = [self.sb(st, tag + "ob16_%d" % i, [128, GW], BF16) for i in range(3)]
        of32 = [self.sb(st, tag + "of32_%d" % i, [128, GW], F32) for i in range(3)]
        tv = [self.sb(st, tag + "tv%d" % i, [128, 6, 65], BF16) for i in range(2)]
        zt = self.sb(st, tag + "zt", [128, 2], F32)
        tz = [self.sb(st, tag + "tz%d" % i, [128, 408], F32) for i in range(2)]
        b_wm = [Buf("wm%d" % k) for k in range(8)]
        b_gB = Buf("gB")
        b_xt = [Buf("xt0"), Buf("xt1")]
        b_xn, b_junk, b_xnT, b_stat = Buf("xn"), Buf("junk"), Buf("xnT"), Buf("stat")
        b_ob16 = [Buf("ob16_%d" % i) for i in range(3)]
        b_of32 = [Buf("of32_%d" % i) for i in range(3)]
        b_tv = [Buf("tv0"), Buf("tv1")]
        b_tz = [Buf("tz0"), Buf("tz1")]
        for kc in range(8):
            sc.dma(wm[:, kc, :], w_mi[kc * 128:(kc + 1) * 128, :], writes=[b_wm[kc]], q="pool")
        sc.dma(gB[:], g.partition_broadcast(128), writes=[b_gB])
        b_zt = Buf("zt")
        sc.op("pool", lambda e: e.memset(zt[:], 0.0), writes=[b_zt])
        for i in range(9):
            sc.dma(ms["cpre"][i * 128:(i + 1) * 128, 0:2], zt[:, :], reads=[b_zt])
            if not self.paired:
                sc.dma(ms["cpre"][i * 128:(i + 1) * 128, S + 2:S + 4], zt[:, :], reads=[b_zt])
        for i in range(2):
            sc.op("pool", lambda e: e.memset(tv[i][:], 1.0), writes=[b_tv[i]])
        ti = 0
        n16 = n32 = 0
        for g0 in range(0, NT, GT):
            for tt in range(GT):
                t = g0 + tt
                xb = ti % 2
                ti += 1
                sc.dma(xt[xb][:], src[t * 128:(t + 1) * 128, :], reads=[src_bufs[t]], writes=[b_xt[xb]])
                sc.op("act", lambda e: e.activation(out=junk[:], in_=xt[xb][:], func=AF.Square,
                                                    accum_out=stat[:, 0:1]),
                      reads=[b_xt[xb]], writes=[b_junk, b_stat])
                sc.op("dve", lambda e: e.tensor_scalar(out=stat[:, 1:2], in0=stat[:, 0:1], scalar1=1.0 / D,
                                                       scalar2=EPS, op0=ALU.mult, op1=ALU.add),
                      reads=[b_stat], writes=[b_stat])
                sc.op("act", lambda e: e.sqrt(out=stat[:, 3:4], in_=stat[:, 1:2]), reads=[b_stat], writes=[b_stat])
                sc.op("dve", lambda e: e.reciprocal(out=stat[:, 2:3], in_=stat[:, 3:4]),
                      reads=[b_stat], writes=[b_stat])
                sc.op("dve", lambda e: e.scalar_tensor_tensor(out=xn[:], in0=xt[xb][:], scalar=stat[:, 2:3],
                                                              in1=gB[:], op0=ALU.mult, op1=ALU.mult),
                      reads=[b_xt[xb], b_stat, b_gB], writes=[b_xn])
                for kc in range(8):
                    bank = kc // 4
                    sc.op("pe", lambda e: e.transpose(ps[bank][:, (kc % 4) * 128:(kc % 4 + 1) * 128],
                                                      xn[:, kc * 128:(kc + 1) * 128], ident[:]),
                          reads=[b_xn, ident_b], writes=[psb[bank]], signal=(kc % 4 == 3))
                sc.op("act", lambda e: e.copy(out=xnT[:, 0:4, tt * 128:(tt + 1) * 128],
                                              in_=ps[0][:, :].rearrange("p (k t) -> p k t", k=4)),
                      reads=[psb[0]], writes=[b_xnT])
                sc.op("dve", lambda e: e.tensor_copy(out=xnT[:, 4:8, tt * 128:(tt + 1) * 128],
                                                     in_=ps[1][:, :].rearrange("p (k t) -> p k t", k=4)),
                      reads=[psb[1]], writes=[b_xnT])
            tok0 = g0 * 128
            for ci, (c0, kind, idx) in enumerate(FM_CHUNKS):
                pb = 2 + ci % 3
                for kc in range(8):
                    sc.op("pe", lambda e: e.matmul(ps[pb][:, :GW], lhsT=wm[:, kc, c0:c0 + 128], rhs=xnT[:, kc, :],
                                                   start=(kc == 0), stop=(kc == 7)),
                          reads=[b_wm[kc], b_xnT], writes=[psb[pb]], signal=(kc == 7))
                if kind == "c":
                    o = n32 % 3
                    n32 += 1
                    sc.op("dve" if ci % 2 else "act",
                          (lambda e: e.tensor_copy(out=of32[o][:, :], in_=ps[pb][:, :GW])) if ci % 2 else
                          (lambda e: e.copy(out=of32[o][:, :], in_=ps[pb][:, :GW])),
                          reads=[psb[pb]], writes=[b_of32[o]])
                    sc.dma(ms["cpre"][idx * 128:(idx + 1) * 128, 2 + tok0:2 + tok0 + GW], of32[o][:, :],
                           reads=[b_of32[o]])
                else:
                    o = n16 % 3
                    n16 += 1
                    scale = {"aq": HD ** -0.5, "bq": 32 ** -0.5, "ak": 1.0, "bk": 1.0}[kind]
                    sc.op("act", lambda e: e.mul(out=ob16[o][:, :], in_=ps[pb][:, :GW], mul=scale),
                          reads=[psb[pb]], writes=[b_ob16[o]])
                    dst = {"aq": ms["aqT"], "ak": ms["akT"], "bq": ms["bqT"], "bk": ms["bkT"]}[kind]
                    sc.dma(dst[idx * 128:(idx + 1) * 128, tok0:tok0 + GW], ob16[o][:, :], reads=[b_ob16[o]])
            for tt in range(GT):
                t = g0 + tt
                r0 = t * 128
                o = t % 2
                for (pb, c0, cw) in ((5, C_AV, 128), (6, C_BV, 256), (7, C_CZ, 408)):
                    for kc in range(8):
                        sc.op("pe", lambda e: e.matmul(ps[pb][:, :cw], lhsT=xnT[:, kc, tt * 128:(tt + 1) * 128],
                                                       rhs=wm[:, kc, c0:c0 + cw], start=(kc == 0), stop=(kc == 7)),
                              reads=[b_wm[kc], b_xnT], writes=[psb[pb]], signal=(kc == 7))
                sc.op("act", lambda e: e.copy(out=tv[o][:, 0:2, 0:64],
                                              in_=ps[5][:, 0:128].rearrange("p (h d) -> p h d", h=2)),
                      reads=[psb[5]], writes=[b_tv[o]])
                sc.op("dve", lambda e: e.tensor_copy(out=tv[o][:, 2:6, 0:64],
                                                     in_=ps[6][:, 0:256].rearrange("p (h d) -> p h d", h=4)),
                      reads=[psb[6]], writes=[b_tv[o]])
                sc.op("act", lambda e: e.copy(out=tz[o][:, :], in_=ps[7][:, 0:408]),
                      reads=[psb[7]], writes=[b_tz[o]])
                sc.dma(ms["av"][r0:r0 + 128, :, :], tv[o][:, 0:2, :], reads=[b_tv[o]])
                sc.dma(ms["bv"][r0:r0 + 128, :, :], tv[o][:, 2:6, :], reads=[b_tv[o]])
                sc.dma(ms["cz"][r0:r0 + 128, :], tz[o][:, :], reads=[b_tz[o]])
        sc.barrier()


K.inproj_phase = _inproj_phase


def _diffattn_phase(self, tag, ms, cst, dlam, dg, lam_init, identf, b_identf, hook=None):
    sc = self.sc
    S, NT = self.S, self.NT
    SK, NTK = self.SK, self.NTK
    GT = 4 if NT % 4 == 0 else 1
    GW = GT * 128
    NG = NT // GT
    ps, psb = self.ps, self.psb
    with ExitStack() as st:
        kTa = self.sb(st, tag + "kTa", [128, SK], BF16)
        kTb = self.sb(st, tag + "kTb", [128, SK], BF16)
        qTa = self.sb(st, tag + "qTa", [128, S], BF16)
        vA = self.sb(st, tag + "vA", [128, NTK, 65], BF16)
        bd = self.sb(st, tag + "bd", [128, 128], BF16)
        idb = self.sb(st, tag + "idb", [128, 128], BF16)
        oT = self.sb(st, tag + "oT", [65, GW], F32)
        b_oT = Buf("oT")
        pT = [self.sb(st, tag + "pT%d" % i, [128, GW], BF16) for i in range(3)]
        om = [self.sb(st, tag + "om%d" % i, [128, NT, 64], F32) for i in range(2)]
        dd = self.sb(st, tag + "dd", [128, NT, 64], F32)
        sq = self.sb(st, tag + "sq", [128, NT, 64], F32)
        ssq = self.sb(st, tag + "ssq", [128, NT], F32)
        rec = self.sb(st, tag + "rec", [128, 8], F32)
        lmb = self.sb(st, tag + "lmb", [128, 128], F32)
        lw = self.sb(st, tag + "lw", [128, 64], F32)
        ls = self.sb(st, tag + "ls", [128, 8], F32)
        gd = self.sb(st, tag + "gd", [128, 64], F32)
        b_kTa, b_kTb, b_qTa, b_vA, b_bd, b_idb = (Buf(n) for n in ("kTa", "kTb", "qTa", "vA", "bd", "idb"))
        b_pT = [Buf("pT%d" % i) for i in range(3)]
        b_om = [Buf("om0"), Buf("om1")]
        b_dd, b_sq, b_ssq, b_rec, b_lmb, b_lw, b_ls, b_gd = (Buf(n) for n in
                                                             ("dd", "sq", "ssq", "rec", "lmb", "lw", "ls", "gd"))
        sc.dma(idb[:], cst["identb"][:, :], writes=[b_idb])
        sc.op("dve", lambda e: e.memset(kTa[:], 0.0), writes=[b_kTa])
        sc.op("dve", lambda e: e.memset(kTb[:], 0.0), writes=[b_kTb])
        sc.op("dve", lambda e: e.memset(qTa[:], 0.0), writes=[b_qTa])
        sc.dma(lmb[:], dlam.rearrange("a b -> (a b)").partition_broadcast(128), writes=[b_lmb])
        sc.dma(gd[:], dg.partition_broadcast(128), writes=[b_gd])
        sc.op("dve", lambda e: e.tensor_tensor(out=lw[:, 0:32], in0=lmb[:, 0:32], in1=lmb[:, 32:64], op=ALU.mult),
              reads=[b_lmb], writes=[b_lw])
        sc.op("dve", lambda e: e.tensor_tensor(out=lw[:, 32:64], in0=lmb[:, 64:96], in1=lmb[:, 96:128], op=ALU.mult),
              reads=[b_lmb], writes=[b_lw])
        sc.op("dve", lambda e: e.reduce_sum(out=ls[:, 0:2], in_=lw[:, :].rearrange("p (a b) -> p a b", a=2),
                                            axis=AX.X), reads=[b_lw], writes=[b_ls])
        sc.op("act", lambda e: e.activation(out=ls[:, 2:4], in_=ls[:, 0:2], func=AF.Exp), reads=[b_ls], writes=[b_ls])
        sc.op("dve", lambda e: e.tensor_tensor(out=ls[:, 4:5], in0=ls[:, 3:4], in1=ls[:, 2:3], op=ALU.subtract),
              reads=[b_ls], writes=[b_ls])
        sc.op("dve", lambda e: e.tensor_scalar_add(out=ls[:, 5:6], in0=ls[:, 4:5], scalar1=-lam_init),
              reads=[b_ls], writes=[b_ls])
        sc.op("dve", lambda e: e.tensor_scalar_mul(out=gd[:], in0=gd[:], scalar1=1.0 - lam_init),
              reads=[b_gd], writes=[b_gd])
        blk = 0
        accn = 0
        pending = []
        for h in range(4):
            sc.dma(vA[:], ms["bv"][:, h, :].rearrange("(n p) d -> p n d", p=128), reads=[], writes=[b_vA])
            sc.dma(bd[:], cst["dbd"][h], writes=[b_bd])
            for m in range(2):
                r0 = (h * 2 + m) * 32
                sc.dma(kTa[0:32, :], ms["bkT"][r0:r0 + 32, :], writes=[b_kTa])
                sc.dma(kTa[32:36, :], cst["dakp"][h], writes=[b_kTa])
                sc.dma(kTb[0:32, :], ms["bkT"][r0:r0 + 32, :], writes=[b_kTb])
                sc.dma(kTb[32:36, :], cst["dakm"][h], writes=[b_kTb])
                sc.dma(qTa[0:32, :], ms["bqT"][r0:r0 + 32, :], writes=[b_qTa])
                sc.dma(qTa[32:36, :], cst["daq"][h], writes=[b_qTa])
                for qg in range(NG):
                    q0 = qg * GW
                    accb = 2 + accn % 2
                    accn += 1

                    def qk(kt, bank):
                        k0 = kt * 128
                        if kt < qg * GT:
                            sc.op("pe", lambda e: e.matmul(ps[bank][:, :GW], lhsT=kTa[:, k0:k0 + 128],
                                                           rhs=qTa[:, q0:q0 + GW], start=True, stop=True),
                                  reads=[b_kTa, b_qTa], writes=[psb[bank]])
                        elif kt >= (qg + 1) * GT:
                            sc.op("pe", lambda e: e.matmul(ps[bank][:, :GW], lhsT=kTb[:, k0:k0 + 128],
                                                           rhs=qTa[:, q0:q0 + GW], start=True, stop=True),
                                  reads=[b_kTb, b_qTa], writes=[psb[bank]])
                        else:
                            for i in range(GT):
                                qt = qg * GT + i
                                cs = slice(i * 128, (i + 1) * 128)
                                qs = slice(q0 + i * 128, q0 + (i + 1) * 128)
                                last = (i == GT - 1)
                                if kt < qt:
                                    sc.op("pe", lambda e: e.matmul(ps[bank][:, cs], lhsT=kTa[:, k0:k0 + 128],
                                                                   rhs=qTa[:, qs], start=True, stop=True),
                                          reads=[b_kTa, b_qTa], writes=[psb[bank]], signal=last)
                                elif kt > qt:
                                    sc.op("pe", lambda e: e.matmul(ps[bank][:, cs], lhsT=kTb[:, k0:k0 + 128],
                                                                   rhs=qTa[:, qs], start=True, stop=True),
                                          reads=[b_kTb, b_qTa], writes=[psb[bank]], signal=last)
                                else:
                                    sc.op("pe", lambda e: e.matmul(ps[bank][:, cs], lhsT=kTa[0:32, k0:k0 + 128],
                                                                   rhs=qTa[0:32, qs], start=True, stop=False),
                                          reads=[b_kTa, b_qTa], writes=[psb[bank]], signal=False)
                                    sc.op("pe", lambda e: e.matmul(ps[bank][:, cs], lhsT=idb[:, :], rhs=bd[:, :],
                                                                   start=False, stop=True),
                                          reads=[b_idb, b_bd], writes=[psb[bank]], signal=last)

                    def expv(kt, bank, pi):
                        sc.op("act", lambda e: e.activation(out=pT[pi][:, :], in_=ps[bank][:, :GW], func=AF.Exp),
                              reads=[psb[bank]], writes=[b_pT[pi]])
                        sc.op("pe", lambda e: e.matmul(ps[accb][0:65, :GW], lhsT=vA[:, kt, :], rhs=pT[pi][:, :],
                                                       start=(kt == 0), stop=(kt == NTK - 1)),
                              reads=[b_pT[pi], b_vA], writes=[psb[accb]])

                    for step in range(NTK + 1):
                        if step == 3 and pending:
                            pending.pop(0)()
                        if hook is not None and step in (8, 24, 40, 56):
                            hook()
                        if step < NTK:
                            qk(step, (blk + step) % 2)
                        if step >= 1:
                            expv(step - 1, (blk + step - 1) % 2, (blk + step - 1) % 3)
                    blk += NTK
                    def make_fin(accb=accb, qg=qg, m=m, accn=accn):
                        def fin():
                            sc.op("act", lambda e: e.copy(out=oT[:, :GW], in_=ps[accb][0:65, :GW]), reads=[psb[accb]], writes=[b_oT])
                            tb = 4 + (accn % 2)
                            for i in range(GT):
                                sc.op("pe", lambda e: e.transpose(ps[tb][:, i * 65:(i + 1) * 65], oT[:, i * 128:(i + 1) * 128],
                                                                  identf[0:65, 0:65]),
                                      reads=[b_oT, b_identf], writes=[psb[tb]], signal=(i == GT - 1))
                            tv = ps[tb][:, 0:GT * 65].rearrange("p (i d) -> p i d", d=65)
                            sc.op("dve", lambda e: e.reciprocal(out=rec[:, 0:GT], in_=tv[:, :, 64]),
                                  reads=[psb[tb]], writes=[b_rec])
                            sc.op("dve", lambda e: e.tensor_tensor(out=om[m][:, qg * GT:(qg + 1) * GT, :], in0=tv[:, :, 0:64],
                                                                   in1=rec[:, 0:GT].unsqueeze(2).to_broadcast([128, GT, 64]),
                                                                   op=ALU.mult),
                                  reads=[psb[tb], b_rec], writes=[b_om[m]])

                        return fin
                    pending.append(make_fin())
            while pending:
                pending.pop(0)()
            sc.op("dve", lambda e: e.scalar_tensor_tensor(out=dd[:], in0=om[1][:], scalar=ls[:, 5:6], in1=om[0][:],
                                                          op0=ALU.mult, op1=ALU.add),
                  reads=[b_om[0], b_om[1], b_ls], writes=[b_dd])
            sc.op("pool", lambda e: e.tensor_tensor(out=sq[:], in0=dd[:], in1=dd[:], op=ALU.mult),
                  reads=[b_dd], writes=[b_sq])
            sc.op("dve", lambda e: e.reduce_sum(out=ssq[:], in_=sq[:], axis=AX.X), reads=[b_sq], writes=[b_ssq])
            sc.op("dve", lambda e: e.tensor_scalar(out=ssq[:], in0=ssq[:], scalar1=1.0 / 64, scalar2=EPS,
                                                   op0=ALU.mult, op1=ALU.add), reads=[b_ssq], writes=[b_ssq])
            sc.op("act", lambda e: e.sqrt(out=ssq[:], in_=ssq[:]), reads=[b_ssq], writes=[b_ssq])
            sc.op("dve", lambda e: e.reciprocal(out=ssq[:], in_=ssq[:]), reads=[b_ssq], writes=[b_ssq])
            sc.op("dve", lambda e: e.tensor_tensor(out=dd[:], in0=dd[:],
                                                   in1=ssq[:].unsqueeze(2).to_broadcast([128, NT, 64]), op=ALU.mult),
                  reads=[b_dd, b_ssq], writes=[b_dd])
            sc.op("dve", lambda e: e.tensor_tensor(out=dd[:], in0=dd[:],
                                                   in1=gd[:].unsqueeze(1).to_broadcast([128, NT, 64]), op=ALU.mult),
                  reads=[b_dd, b_gd], writes=[b_dd])
            sc.dma(ms["omix"][:, 384 + h * 64:384 + (h + 1) * 64].rearrange("(n p) d -> p n d", p=128), dd[:],
                   reads=[b_dd])
        if hook is not None:
            while hook():
                pass
        sc.barrier()


K.diffattn_phase = _diffattn_phase


def _winattn_phase(self, tag, ms, wbias, sink):
    sc = self.sc
    S, NT = self.S, self.NT
    NKA = NT + 1 if self.paired else NT
    ps, psb = self.ps, self.psb
    with ExitStack() as st:
        qT = self.sb(st, tag + "qT", [128, S], BF16)
        kT = self.sb(st, tag + "kT", [128, NKA * 128], BF16)
        vA = self.sb(st, tag + "vA", [128, NKA, 65], BF16)
        wb = self.sb(st, tag + "wb", [128, 384], F32)
        sT = [self.sb(st, tag + "sT%d" % i, [128, 384], F32) for i in range(2)]
        pT = [self.sb(st, tag + "pT%d" % i, [128, 384], BF16) for i in range(2)]
        ow = self.sb(st, tag + "ow", [128, NT, 64], F32)
        ou = self.sb(st, tag + "ou", [128, NT, 65], F32)
        dn_ = self.sb(st, tag + "dn", [128, NT], F32)
        b_ou, b_dn = Buf("ou"), Buf("dn")
        es = self.sb(st, tag + "es", [128, 6], F32)
        rec = self.sb(st, tag + "rec", [128, 4], F32)
        b_qT, b_kT, b_vA, b_wb, b_ow, b_es, b_rec = (Buf(n) for n in ("qT", "kT", "vA", "wb", "ow", "es", "rec"))
        b_sT = [Buf("sT0"), Buf("sT1")]
        b_pT = [Buf("pT0"), Buf("pT1")]
        sc.op("dve", lambda e: e.memset(qT[:], 0.0), writes=[b_qT])
        sc.op("dve", lambda e: e.memset(kT[:], 0.0), writes=[b_kT])
        sc.dma(es[:], sink.partition_broadcast(128), writes=[b_es])
        sc.op("act", lambda e: e.activation(out=es[:], in_=es[:], func=AF.Exp), reads=[b_es], writes=[b_es])
        cnt = 0
        for h in range(6):
            kh = h // 3
            if h % 3 == 0:
                sc.dma(kT[0:64, :], ms["akT"][kh * 64:(kh + 1) * 64, :], writes=[b_kT])
                sc.dma(vA[:], ms["av"][:, kh, :].rearrange("(n p) d -> p n d", p=128), writes=[b_vA])
            sc.dma(qT[0:64, :], ms["aqT"][h * 64:(h + 1) * 64, :], writes=[b_qT])
            sc.dma(wb[:], wbias[h], writes=[b_wb])
            for n in range(NT):
                js = [j for j in (0, 1, 2) if 0 <= n - 1 + j < NKA]
                c0 = js[0] * 128
                w = len(js) * 128
                sbk = cnt % 2
                acc = 2 + cnt % 6
                cnt += 1
                for jj, j in enumerate(js):
                    kt = n - 1 + j
                    sc.op("pe", lambda e: e.matmul(ps[sbk][:, jj * 128:(jj + 1) * 128], lhsT=kT[:, kt * 128:(kt + 1) * 128],
                                                   rhs=qT[:, n * 128:(n + 1) * 128], start=True, stop=True),
                          reads=[b_kT, b_qT], writes=[psb[sbk]], signal=(jj == len(js) - 1))
                sc.op("dve", lambda e: e.tensor_tensor(out=sT[sbk][:, 0:w], in0=ps[sbk][:, 0:w], in1=wb[:, c0:c0 + w],
                                                       op=ALU.add),
                      reads=[psb[sbk], b_wb], writes=[b_sT[sbk]])
                sc.op("act", lambda e: e.activation(out=pT[sbk][:, 0:w], in_=sT[sbk][:, 0:w], func=AF.Exp),
                      reads=[b_sT[sbk]], writes=[b_pT[sbk]])
                for jj, j in enumerate(js):
                    kt = n - 1 + j
                    sc.op("pe", lambda e: e.matmul(ps[acc][:, 0:65], lhsT=pT[sbk][:, jj * 128:(jj + 1) * 128],
                                                   rhs=vA[:, kt, :], start=(jj == 0), stop=(jj == len(js) - 1)),
                          reads=[b_pT[sbk], b_vA], writes=[psb[acc]], signal=(jj == len(js) - 1))
                if n % 2 == 0:
                    sc.op("dve", lambda e: e.tensor_copy(out=ou[:, n, :], in_=ps[acc][:, 0:65]), reads=[psb[acc]], writes=[b_ou])
                else:
                    sc.op("act", lambda e: e.copy(out=ou[:, n, :], in_=ps[acc][:, 0:65]), reads=[psb[acc]], writes=[b_ou])
            sc.op("dve", lambda e: e.tensor_scalar(out=dn_[:], in0=ou[:, :, 64], scalar1=es[:, h:h + 1], scalar2=None,
                                                   op0=ALU.add), reads=[b_ou, b_es], writes=[b_dn])
            sc.op("dve", lambda e: e.reciprocal(out=dn_[:], in_=dn_[:]), reads=[b_dn], writes=[b_dn])
            sc.op("dve", lambda e: e.tensor_tensor(out=ow[:], in0=ou[:, :, 0:64],
                                                   in1=dn_[:].unsqueeze(2).to_broadcast([128, NT, 64]), op=ALU.mult),
                  reads=[b_ou, b_dn], writes=[b_ow])
            sc.dma(ms["omix"][:, h * 64:(h + 1) * 64].rearrange("(n p) d -> p n d", p=128), ow[:], reads=[b_ow])
        sc.barrier()


K.winattn_phase = _winattn_phase


def _dn_scratch(self):
    S = self.S
    d = {}
    d["cqT"] = self.scratch("cqT", [384, S], F32)
    d["ckT"] = self.scratch("ckT", [384, S], F32)
    d["ck"] = self.scratch("ck", [S, 384], F32)
    d["cv"] = self.scratch("cv", [S, 384], F32)
    d["of"] = self.scratch("of", [S, 384], F32)
    return d


K.dn_scratch = _dn_scratch


def _conv_phase(self, tag, ms, ds, conv_w, dnc, ident, ident_b):
    sc = self.sc
    S, NT = self.S, self.NT
    GT = 4 if NT % 4 == 0 else 1
    GW = GT * 128
    ps, psb = self.ps, self.psb
    with ExitStack() as st:
        blk1 = self.sb(st, tag + "blk1", [128, 128], F32)
        cw = self.sb(st, tag + "cw", [128, 9, 5], F32)
        xin = [self.sb(st, tag + "xin%d" % i, [128, GW + 4], F32) for i in range(2)]
        y = [self.sb(st, tag + "y%d" % i, [128, GW], F32) for i in range(2)]
        sq2 = [self.sb(st, tag + "sq%d" % i, [128, GW], F32) for i in range(2)]
        rs2 = [self.sb(st, tag + "rs%d" % i, [128, GW], F32) for i in range(2)]
        yn = [self.sb(st, tag + "yn%d" % i, [128, GW], F32) for i in range(2)]
        tk = [self.sb(st, tag + "tk%d" % i, [128, GW], F32) for i in range(2)]
        b_blk1, b_cw = Buf("blk1"), Buf("cw")
        b_sq2 = [Buf("sq0"), Buf("sq1")]
        b_rs2 = [Buf("rs0"), Buf("rs1")]
        b_xin = [Buf("xin0"), Buf("xin1")]
        b_y = [Buf("y0"), Buf("y1")]
        b_yn = [Buf("yn0"), Buf("yn1")]
        b_tk = [Buf("tk0"), Buf("tk1")]
        sc.dma(blk1[:], dnc[3], writes=[b_blk1])
        for ci in range(9):
            sc.dma(cw[:, ci, :], conv_w[:, ci * 128:(ci + 1) * 128].rearrange("j c -> c j"), writes=[b_cw],
                   allow_slow_non_contiguous=True)
        it = 0
        for g0 in range(0, NT, GT):
            tok0 = g0 * 128
            for ci in range(9):
                b = it % 2
                it += 1
                sq, rs, b_sq, b_rs = sq2[b], rs2[b], b_sq2[b], b_rs2[b]
                pA, pB = 2 * b, 2 * b + 1
                sc.dma(xin[b][:], ms["cpre"][ci * 128:(ci + 1) * 128, tok0:tok0 + GW + 4], writes=[b_xin[b]])
                eng = "dve"
                sc.op(eng, lambda e: e.tensor_scalar_mul(out=y[b][:], in0=xin[b][:, 0:GW], scalar1=cw[:, ci, 0:1]),
                      reads=[b_xin[b], b_cw], writes=[b_y[b]])
                for j in range(1, 5):
                    sc.op(eng, lambda e: e.scalar_tensor_tensor(out=y[b][:], in0=xin[b][:, j:j + GW],
                                                                scalar=cw[:, ci, j:j + 1], in1=y[b][:],
                                                                op0=ALU.mult, op1=ALU.add),
                          reads=[b_xin[b], b_cw, b_y[b]], writes=[b_y[b]])
                sc.op("act", lambda e: e.activation(out=y[b][:], in_=y[b][:], func=AF.Silu),
                      reads=[b_y[b]], writes=[b_y[b]])
                if ci < 6:
                    sc.op("act", lambda e: e.activation(out=sq[:], in_=y[b][:], func=AF.Square),
                          reads=[b_y[b]], writes=[b_sq])
                    sc.op("pe", lambda e: e.matmul(ps[pA][:, :GW], lhsT=blk1[:], rhs=sq[:], start=True, stop=True),
                          reads=[b_blk1, b_sq], writes=[psb[pA]])
                    mul = 64.0 if ci < 3 else 1.0
                    sc.op("dve", lambda e: e.tensor_scalar(out=rs[:], in0=ps[pA][:, :GW], scalar1=EPS, scalar2=mul,
                                                           op0=ALU.add, op1=ALU.mult),
                          reads=[psb[pA]], writes=[b_rs])
                    sc.op("act", lambda e: e.sqrt(out=rs[:], in_=rs[:]), reads=[b_rs], writes=[b_rs])
                    sc.op("dve", lambda e: e.reciprocal(out=rs[:], in_=rs[:]), reads=[b_rs], writes=[b_rs])
                    sc.op("dve", lambda e: e.tensor_tensor(out=yn[b][:], in0=y[b][:], in1=rs[:], op=ALU.mult),
                          reads=[b_y[b], b_rs], writes=[b_yn[b]])
                    src_t, src_b = yn[b], b_yn[b]
                    if ci < 3:
                        sc.dma(ds["cqT"][ci * 128:(ci + 1) * 128, tok0:tok0 + GW], yn[b][:], reads=[b_yn[b]])
                    else:
                        sc.dma(ds["ckT"][(ci - 3) * 128:(ci - 2) * 128, tok0:tok0 + GW], yn[b][:], reads=[b_yn[b]])
                else:
                    src_t, src_b = y[b], b_y[b]
                if ci >= 3:
                    for tt in range(GT):
                        sc.op("pe", lambda e: e.transpose(ps[pB][:, tt * 128:(tt + 1) * 128],
                                                          src_t[:, tt * 128:(tt + 1) * 128], ident[:]),
                              reads=[src_b, ident_b], writes=[psb[pB]], signal=(tt == GT - 1))
                    sc.op("act", lambda e: e.copy(out=tk[b][:], in_=ps[pB][:, :GW]), reads=[psb[pB]], writes=[b_tk[b]])
                    dst = ds["ck"] if ci < 6 else ds["cv"]
                    cc = (ci - 3) % 3
                    sc.dma(dst[tok0:tok0 + GW, cc * 128:(cc + 1) * 128].rearrange("(t p) c -> p t c", p=128),
                           tk[b][:].rearrange("p (t c) -> p t c", c=128), reads=[b_tk[b]])
        sc.barrier()


K.conv_phase = _conv_phase


def _dn_pass(self, tag, dirn, ms, ds, dnc, a_log, dt_bias, dn_g, ident, ident_b):
    sc = self.sc
    S, NT = self.S, self.NT
    ps, psb = self.ps, self.psb
    import os
    NIT = int(os.environ.get('DN_NIT', '7'))
    with ExitStack() as st:
        def T(name, shape, n=1, dt=F32):
            ts = [self.sb(st, "%s%s%d" % (tag, name, i), shape, dt) for i in range(n)]
            bs = [Buf("%s%d" % (name, i)) for i in range(n)]
            return (ts, bs) if n > 1 else (ts[0], bs[0])
        ones, b_ones = T("ones", [128, 128])
        tri, b_tri = T("tri", [128, 128])
        m1, b_m1 = T("m1", [128, 128])
        m2, b_m2 = T("m2", [128, 128])
        m3, b_m3 = T("m3", [128, 128])
        dtb, b_dtb = T("dtb", [128, 6])
        nega, b_nega = T("nega", [128, 6])
        gdn, b_gdn = T("gdn", [128, 64])
        kTt, b_kTt = T("kTt", [64, 6, 128], 2)
        qTt, b_qTt = T("qTt", [64, 6, 128], 2)
        kt, b_kt = T("kt", [128, 384], 2)
        vt, b_vt = T("vt", [128, 384], 2)
        gz, b_gz = T("gz", [128, 408], 2)
        gs, b_gs = T("gs", [128, 16, 6])
        D1, b_D1 = T("D1", [128, 6, 128])
        D2, b_D2 = T("D2", [128, 6, 128])
        Rs, b_Rs = T("Rs", [128, 6, 128])
        R2s, b_R2s = T("R2s", [128, 6, 128])
        tmp, b_tmp = T("tmp", [128, 3, 128], 2)
        E, b_E = T("E", [128, 3, 128], 2)
        X, b_X = T("X", [128, 128], 4)
        Y, b_Y = T("Y", [128, 128], 4)
        Rr, b_Rr = T("Rr", [128, 128], 4)
        qkT, b_qkT = T("qkT", [128, 128], 2)
        kg, b_kg = T("kg", [128, 64], 2)
        wT, b_wT = T("wT", [64, 128], 2)
        Vn, b_Vn = T("Vn", [128, 64], 2)
        t2, b_t2 = T("t2", [128, 64], 2)
        St, b_St = T("St", [64, 6, 64])
        osb, b_osb = T("osb", [128, 6, 64], 2)
        if dirn == 1:
            oft, b_oft = T("oft", [128, 6, 64], 2)
            sqt, b_sqt = T("sqt", [128, 6, 64])
            sz, b_sz = T("sz", [128, 6, 64])
            rr, b_rr = T("rr", [128, 8])
        sc.dma(ones[:], dnc[0], writes=[b_ones])
        sc.dma(tri[:], dnc[1 + dirn], writes=[b_tri])
        sc.dma(m1[:], dnc[4 + 3 * dirn], writes=[b_m1])
        sc.dma(m2[:], dnc[5 + 3 * dirn], writes=[b_m2])
        sc.dma(m3[:], dnc[6 + 3 * dirn], writes=[b_m3])
        sc.dma(dtb[:], dt_bias[dirn].partition_broadcast(128), writes=[b_dtb])
        sc.dma(nega[:], a_log[dirn].partition_broadcast(128), writes=[b_nega])
        sc.dma(gdn[:], dn_g.partition_broadcast(128), writes=[b_gdn])
        sc.op("act", lambda e: e.activation(out=nega[:], in_=nega[:], func=AF.Exp), reads=[b_nega], writes=[b_nega])
        sc.op("dve", lambda e: e.tensor_scalar_mul(out=nega[:], in0=nega[:], scalar1=-1.0),
              reads=[b_nega], writes=[b_nega])
        sc.op("pool", lambda e: e.memset(St[:], 0.0), writes=[b_St])
        order = list(range(NT)) if dirn == 0 else list(range(NT - 1, -1, -1))
        G_ = lambda i: gs[:, i, :]
        hcnt = 0
        for it, n in enumerate(order):
            tb = it % 2
            c0 = n * 128
            sc.dma(kTt[tb][:], ds["ckT"][:, c0:c0 + 128].rearrange("(h d) t -> d h t", h=6), writes=[b_kTt[tb]])
            sc.dma(qTt[tb][:], ds["cqT"][:, c0:c0 + 128].rearrange("(h d) t -> d h t", h=6), writes=[b_qTt[tb]])
            sc.dma(kt[tb][:], ds["ck"][c0:c0 + 128, :], writes=[b_kt[tb]])
            sc.dma(vt[tb][:], ds["cv"][c0:c0 + 128, :], writes=[b_vt[tb]])
            sc.dma(gz[tb][:], ms["cz"][c0:c0 + 128, :], writes=[b_gz[tb]])
            bcol = gz[tb][:, 384 + dirn * 6:384 + dirn * 6 + 6]
            acol = gz[tb][:, 396 + dirn * 6:396 + dirn * 6 + 6]
            RG = [b_gs]
            sc.op("act", lambda e: e.activation(out=G_(0), in_=bcol, func=AF.Exp, scale=-1.0),
                  reads=[b_gz[tb]], writes=RG)
            sc.op("dve", lambda e: e.tensor_scalar_add(out=G_(0), in0=G_(0), scalar1=1.0), reads=RG, writes=RG)
            sc.op("act", lambda e: e.activation(out=G_(1), in_=G_(0), func=AF.Ln), reads=RG, writes=RG)
            sc.op("dve", lambda e: e.tensor_tensor(out=G_(2), in0=acol, in1=dtb[:], op=ALU.add),
                  reads=[b_gz[tb], b_dtb], writes=RG)
            sc.op("act", lambda e: e.activation(out=G_(3), in_=G_(2), func=AF.Exp), reads=RG, writes=RG)
            sc.op("dve", lambda e: e.tensor_scalar_add(out=G_(3), in0=G_(3), scalar1=1.0), reads=RG, writes=RG)
            sc.op("act", lambda e: e.activation(out=G_(4), in_=G_(3), func=AF.Ln), reads=RG, writes=RG)
            sc.op("dve", lambda e: e.tensor_tensor(out=G_(5), in0=G_(4), in1=nega[:], op=ALU.mult),
                  reads=RG + [b_nega], writes=RG)
            sc.op("pe", lambda e: e.matmul(ps[0][:, 0:6], lhsT=tri[:], rhs=G_(5), start=True, stop=True),
                  reads=[b_tri] + RG, writes=[psb[0]], signal=False)
            sc.op("pe", lambda e: e.matmul(ps[0][:, 8:14], lhsT=ones[:], rhs=G_(5), start=True, stop=True),
                  reads=[b_ones] + RG, writes=[psb[0]])
            sc.op("dve", lambda e: e.tensor_copy(out=G_(6), in_=ps[0][:, 0:6]), reads=[psb[0]], writes=RG)
            sc.op("dve", lambda e: e.tensor_copy(out=G_(14), in_=ps[0][:, 8:14]), reads=[psb[0]], writes=RG)
            sc.op("dve", lambda e: e.tensor_tensor(out=G_(7), in0=G_(6), in1=G_(1), op=ALU.subtract),
                  reads=RG, writes=RG)
            sc.op("act", lambda e: e.activation(out=G_(8), in_=G_(7), func=AF.Exp), reads=RG, writes=RG)
            sc.op("act", lambda e: e.activation(out=G_(9), in_=G_(1), func=AF.Exp, scale=-1.0),
                  reads=RG, writes=RG)
            sc.op("act", lambda e: e.activation(out=G_(10), in_=G_(6), func=AF.Exp), reads=RG, writes=RG)
            sc.op("dve", lambda e: e.tensor_tensor(out=G_(11), in0=G_(14), in1=G_(6), op=ALU.subtract),
                  reads=RG, writes=RG)
            sc.op("act", lambda e: e.activation(out=G_(12), in_=G_(11), func=AF.Exp), reads=RG, writes=RG)
            sc.op("act", lambda e: e.activation(out=G_(13), in_=G_(14), func=AF.Exp), reads=RG, writes=RG)
            idb3 = ident[:].unsqueeze(1).to_broadcast([128, 6, 128])
            sc.op("dve", lambda e: e.tensor_tensor(out=D1[:], in0=idb3,
                                                   in1=G_(6).unsqueeze(2).to_broadcast([128, 6, 128]), op=ALU.mult),
                  reads=RG + [ident_b], writes=[b_D1])
            sc.op("pool", lambda e: e.tensor_tensor(out=D2[:], in0=idb3,
                                                    in1=G_(7).unsqueeze(2).to_broadcast([128, 6, 128]), op=ALU.mult),
                  reads=RG + [ident_b], writes=[b_D2])
            for (Dm, b_Dm, Rm, b_Rm) in ((D1, b_D1, Rs, b_Rs), (D2, b_D2, R2s, b_R2s)):
                Dm2 = Dm[:].rearrange("p j s -> p (j s)")
                Rm2 = Rm[:].rearrange("p j s -> p (j s)")
                sc.op("pe", lambda e: e.matmul(ps[1][:, 0:512], lhsT=ones[:], rhs=Dm2[:, 0:512], start=True, stop=True),
                      reads=[b_ones, b_Dm], writes=[psb[1]])
                sc.op("pe", lambda e: e.matmul(ps[2][:, 0:256], lhsT=ones[:], rhs=Dm2[:, 512:768], start=True,
                                               stop=True),
                      reads=[b_ones, b_Dm], writes=[psb[2]])
                sc.op("act", lambda e: e.copy(out=Rm2[:, 0:512], in_=ps[1][:, 0:512]), reads=[psb[1]], writes=[b_Rm])
                sc.op("act", lambda e: e.copy(out=Rm2[:, 512:768], in_=ps[2][:, 0:256]), reads=[psb[2]],
                      writes=[b_Rm])
            ob = it % 2
            import os
            STOP = os.environ.get("DN_STOP", "")
            for hh in range(6):
                if STOP == "gates":
                    break
                hb = hcnt % 2
                hcnt += 1
                sc.op("pe", lambda e: e.matmul(ps[3][:, 0:128], lhsT=kTt[tb][:, hh, :], rhs=kTt[tb][:, hh, :],
                                               start=True, stop=True),
                      reads=[b_kTt[tb]], writes=[psb[3]], signal=False)
                sc.op("pe", lambda e: e.matmul(ps[3][:, 128:256], lhsT=kTt[tb][:, hh, :], rhs=qTt[tb][:, hh, :],
                                               start=True, stop=True),
                      reads=[b_kTt[tb], b_qTt[tb]], writes=[psb[3]])
                sc.op("dve", lambda e: e.scalar_tensor_tensor(out=tmp[hb][:, 0, :], in0=Rs[:, hh, :],
                                                              scalar=gs[:, 7, hh:hh + 1], in1=m1[:],
                                                              op0=ALU.subtract, op1=ALU.max),
                      reads=[b_Rs, b_gs, b_m1], writes=[b_tmp[hb]])
                sc.op("dve", lambda e: e.scalar_tensor_tensor(out=tmp[hb][:, 1, :], in0=R2s[:, hh, :],
                                                              scalar=gs[:, 6, hh:hh + 1], in1=m2[:],
                                                              op0=ALU.subtract, op1=ALU.min),
                      reads=[b_R2s, b_gs, b_m2], writes=[b_tmp[hb]])
                sc.op("dve", lambda e: e.scalar_tensor_tensor(out=tmp[hb][:, 2, :], in0=Rs[:, hh, :],
                                                              scalar=gs[:, 6, hh:hh + 1], in1=m3[:],
                                                              op0=ALU.subtract, op1=ALU.min),
                      reads=[b_Rs, b_gs, b_m3], writes=[b_tmp[hb]])
                sc.op("act", lambda e: e.activation(out=E[hb][:, 0, :], in_=tmp[hb][:, 0, :], func=AF.Exp, scale=-1.0),
                      reads=[b_tmp[hb]], writes=[b_E[hb]])
                sc.op("act", lambda e: e.activation(out=E[hb][:, 1:3, :], in_=tmp[hb][:, 1:3, :], func=AF.Exp),
                      reads=[b_tmp[hb]], writes=[b_E[hb]])
                xi = 2 * hb
                sc.op("dve", lambda e: e.tensor_tensor(out=X[xi][:], in0=ps[3][:, 0:128], in1=E[hb][:, 0, :],
                                                       op=ALU.mult), reads=[psb[3], b_E[hb]], writes=[b_X[xi]])
                sc.op("dve", lambda e: e.tensor_tensor(out=Y[xi][:], in0=ps[3][:, 0:128], in1=E[hb][:, 1, :],
                                                       op=ALU.mult), reads=[psb[3], b_E[hb]], writes=[b_Y[xi]])
                sc.op("dve", lambda e: e.tensor_tensor(out=qkT[hb][:], in0=ps[3][:, 128:256], in1=E[hb][:, 2, :],
                                                       op=ALU.mult), reads=[psb[3], b_E[hb]], writes=[b_qkT[hb]])
                hs = slice(hh * 64, (hh + 1) * 64)
                sc.op("pool", lambda e: e.tensor_scalar_mul(out=Rr[xi][:, 0:64], in0=vt[tb][:, hs],
                                                            scalar1=gs[:, 9, hh:hh + 1]),
                      reads=[b_vt[tb], b_gs], writes=[b_Rr[xi]])
                sc.op("pool", lambda e: e.tensor_scalar_mul(out=Rr[xi][:, 64:128], in0=kt[tb][:, hs],
                                                            scalar1=gs[:, 8, hh:hh + 1]),
                      reads=[b_kt[tb], b_gs], writes=[b_Rr[xi]])
                sc.op("pool", lambda e: e.tensor_scalar_mul(out=kg[hb][:], in0=kt[tb][:, hs],
                                                            scalar1=gs[:, 12, hh:hh + 1]),
                      reads=[b_kt[tb], b_gs], writes=[b_kg[hb]])
                if STOP == "prep":
                    continue
                pn = 4 + hb
                pq = 6 + hb
                cur = xi
                for i in range(NIT):
                    nxt = 2 * hb + (1 - (cur - 2 * hb))
                    sc.op("pe", lambda e: e.matmul(ps[pn][:, 0:128], lhsT=Y[cur][:], rhs=Rr[cur][:], start=True,
                                                   stop=True),
                          reads=[b_Y[cur], b_Rr[cur]], writes=[psb[pn]], signal=(i == NIT - 1))
                    if i < NIT - 1:
                        sc.op("pe", lambda e: e.matmul(ps[pq][:, 128:256], lhsT=X[cur][:], rhs=Y[cur][:], start=True,
                                                       stop=True),
                              reads=[b_X[cur], b_Y[cur]], writes=[psb[pq]], signal=(i == NIT - 2))
                    if i < NIT - 2:
                        sc.op("pe", lambda e: e.matmul(ps[pq][:, 256:384], lhsT=Y[cur][:], rhs=X[cur][:], start=True,
                                                       stop=True),
                              reads=[b_X[cur], b_Y[cur]], writes=[psb[pq]], signal=True)
                    if i == 0:
                        sc.op("dve", lambda e: e.scalar_tensor_tensor(out=Rr[nxt][:], in0=ps[pn][:, 0:128], scalar=-1.0,
                                                                      in1=Rr[cur][:], op0=ALU.mult, op1=ALU.add),
                              reads=[b_Rr[cur], psb[pn]], writes=[b_Rr[nxt]])
                    else:
                        sc.op("dve", lambda e: e.tensor_tensor(out=Rr[nxt][:], in0=ps[pn][:, 0:128], in1=Rr[cur][:],
                                                               op=ALU.add),
                              reads=[b_Rr[cur], psb[pn]], writes=[b_Rr[nxt]])
                    if i < NIT - 1:
                        sc.op("act", lambda e: e.copy(out=Y[nxt][:], in_=ps[pq][:, 128:256]),
                              reads=[psb[pq]], writes=[b_Y[nxt]])
                    if i < NIT - 2:
                        sc.op("act", lambda e: e.copy(out=X[nxt][:], in_=ps[pq][:, 256:384]),
                              reads=[psb[pq]], writes=[b_X[nxt]])
                    cur = nxt
                Rf, b_Rf = Rr[cur], b_Rr[cur]
                if STOP == "neumann":
                    continue
                sc.op("pe", lambda e: e.transpose(ps[1][0:64, 0:128], Rf[:, 64:128], ident[:]),
                      reads=[b_Rf, ident_b], writes=[psb[1]])
                sc.op("act", lambda e: e.copy(out=wT[hb][:], in_=ps[1][0:64, 0:128]), reads=[psb[1]], writes=[b_wT[hb]])
                if STOP == "transp":
                    continue
                sc.op("pe", lambda e: e.matmul(ps[0][:, 0:64], lhsT=wT[hb][:], rhs=St[:, hh, :], start=True, stop=True),
                      reads=[b_wT[hb], b_St], writes=[psb[0]])
                sc.op("dve", lambda e: e.scalar_tensor_tensor(out=Vn[hb][:], in0=ps[0][:, 0:64], scalar=-1.0,
                                                              in1=Rf[:, 0:64], op0=ALU.mult, op1=ALU.add),
                      reads=[b_Rf, psb[0]], writes=[b_Vn[hb]])
                sc.op("pe", lambda e: e.matmul(ps[1][:, 128:192], lhsT=qTt[tb][:, hh, :], rhs=St[:, hh, :], start=True,
                                               stop=True),
                      reads=[b_qTt[tb], b_St], writes=[psb[1]], signal=False)
                sc.op("pe", lambda e: e.matmul(ps[0][:, 64:128], lhsT=qkT[hb][:], rhs=Vn[hb][:], start=True, stop=True),
                      reads=[b_qkT[hb], b_Vn[hb]], writes=[psb[0]], signal=False)
                sc.op("pe", lambda e: e.matmul(ps[2][0:64, 0:64], lhsT=kg[hb][:], rhs=Vn[hb][:], start=True, stop=True),
                      reads=[b_kg[hb], b_Vn[hb]], writes=[psb[2]])
                sc.op("act", lambda e: e.activation(out=t2[hb][:], in_=ps[1][:, 128:192], func=AF.Copy,
                                                    scale=gs[:, 10, hh:hh + 1]),
                      reads=[psb[1], b_gs], writes=[b_t2[hb]])
                sc.op("dve", lambda e: e.tensor_tensor(out=osb[ob][:, hh, :], in0=ps[0][:, 64:128], in1=t2[hb][:],
                                                       op=ALU.add),
                      reads=[b_t2[hb], psb[0]], writes=[b_osb[ob]])
                sc.op("pool", lambda e: e.tensor_scalar_mul(out=St[:, hh, :], in0=St[:, hh, :],
                                                            scalar1=gs[0:64, 13, hh:hh + 1]),
                      reads=[b_St, b_gs], writes=[b_St])
                sc.op("dve", lambda e: e.tensor_tensor(out=St[:, hh, :], in0=ps[2][0:64, 0:64], in1=St[:, hh, :],
                                                       op=ALU.add),
                      reads=[b_St, psb[2]], writes=[b_St])
            if dirn == 0:
                sc.dma(ds["of"][c0:c0 + 128, :], osb[ob][:].rearrange("p h d -> p (h d)"), reads=[b_osb[ob]])
            else:
                sc.dma(oft[ob][:].rearrange("p h d -> p (h d)"), ds["of"][c0:c0 + 128, :], writes=[b_oft[ob]])
                sc.op("dve", lambda e: e.tensor_tensor(out=oft[ob][:], in0=oft[ob][:], in1=osb[ob][:], op=ALU.add),
                      reads=[b_oft[ob], b_osb[ob]], writes=[b_oft[ob]])
                sc.op("pool", lambda e: e.tensor_tensor(out=sqt[:], in0=oft[ob][:], in1=oft[ob][:], op=ALU.mult),
                      reads=[b_oft[ob]], writes=[b_sqt])
                sc.op("dve", lambda e: e.reduce_sum(out=rr[:, 0:6], in_=sqt[:], axis=AX.X), reads=[b_sqt], writes=[b_rr])
                sc.op("dve", lambda e: e.tensor_scalar(out=rr[:, 0:6], in0=rr[:, 0:6], scalar1=1.0 / 64, scalar2=EPS,
                                                       op0=ALU.mult, op1=ALU.add), reads=[b_rr], writes=[b_rr])
                sc.op("act", lambda e: e.sqrt(out=rr[:, 0:6], in_=rr[:, 0:6]), reads=[b_rr], writes=[b_rr])
                sc.op("dve", lambda e: e.reciprocal(out=rr[:, 0:6], in_=rr[:, 0:6]), reads=[b_rr], writes=[b_rr])
                sc.op("act", lambda e: e.activation(out=sz[:].rearrange("p h d -> p (h d)"), in_=gz[tb][:, 0:384],
                                                    func=AF.Silu), reads=[b_gz[tb]], writes=[b_sz])
                sc.op("dve", lambda e: e.tensor_tensor(out=oft[ob][:], in0=oft[ob][:],
                                                       in1=rr[:, 0:6].unsqueeze(2).to_broadcast([128, 6, 64]),
                                                       op=ALU.mult), reads=[b_oft[ob], b_rr], writes=[b_oft[ob]])
                sc.op("pool", lambda e: e.tensor_tensor(out=sz[:], in0=sz[:],
                                                        in1=gdn[:].unsqueeze(1).to_broadcast([128, 6, 64]),
                                                        op=ALU.mult), reads=[b_sz, b_gdn], writes=[b_sz])
                sc.op("dve", lambda e: e.tensor_tensor(out=oft[ob][:], in0=oft[ob][:], in1=sz[:], op=ALU.mult),
                      reads=[b_oft[ob], b_sz], writes=[b_oft[ob]])
                sc.dma(ms["omix"][c0:c0 + 128, 640:1024], oft[ob][:].rearrange("p h d -> p (h d)"),
                       reads=[b_oft[ob]])
        sc.barrier()


K.dn_pass = _dn_pass


def _outproj_phase(self, tag, w_o, ms, xres, x_bufs, ident, ident_b):
    sc = self.sc
    S, NT = self.S, self.NT
    ps, psb = self.ps, self.psb
    with ExitStack() as st:
        wo = self.sb(st, tag + "wo", [128, 8, D], BF16)
        ot = [self.sb(st, tag + "ot%d" % i, [128, D], F32) for i in range(2)]
        xr = [self.sb(st, tag + "xr%d" % i, [128, D], F32) for i in range(2)]
        oT = [self.sb(st, tag + "oT%d" % i, [128, 8, 128], BF16) for i in range(2)]
        b_wo = [Buf("wo%d" % k) for k in range(8)]
        b_ot = [Buf("ot0"), Buf("ot1")]
        b_xr = [Buf("xr0"), Buf("xr1")]
        b_oT = [Buf("oT0"), Buf("oT1")]
        for kc in range(8):
            sc.dma(wo[:, kc, :], w_o[kc * 128:(kc + 1) * 128, :], writes=[b_wo[kc]], q="pool")
        for t in range(NT):
            b = t % 2
            r0 = t * 128
            sc.dma(ot[b][:], ms["omix"][r0:r0 + 128, :], writes=[b_ot[b]])
            sc.dma(xr[b][:], xres[r0:r0 + 128, :], reads=[x_bufs[t]], writes=[b_xr[b]])
            for kc in range(8):
                bank = kc // 4
                sc.op("pe", lambda e: e.transpose(ps[bank][:, (kc % 4) * 128:(kc % 4 + 1) * 128],
                                                  ot[b][:, kc * 128:(kc + 1) * 128], ident[:]),
                      reads=[b_ot[b], ident_b], writes=[psb[bank]], signal=(kc % 4 == 3))
            sc.op("act", lambda e: e.copy(out=oT[b][:, 0:4, :], in_=ps[0][:, :].rearrange("p (k t) -> p k t", k=4)),
                  reads=[psb[0]], writes=[b_oT[b]])
            sc.op("dve", lambda e: e.tensor_copy(out=oT[b][:, 4:8, :],
                                                 in_=ps[1][:, :].rearrange("p (k t) -> p k t", k=4)),
                  reads=[psb[1]], writes=[b_oT[b]])
            for hf in range(2):
                pb = 2 + 2 * b + hf
                for kc in range(8):
                    sc.op("pe", lambda e: e.matmul(ps[pb][:, :], lhsT=oT[b][:, kc, :], rhs=wo[:, kc, hf * 512:(hf + 1) * 512],
                                                   start=(kc == 0), stop=(kc == 7)),
                          reads=[b_oT[b], b_wo[kc]], writes=[psb[pb]], signal=(kc == 7))
            for hf in range(2):
                pb = 2 + 2 * b + hf
                sc.op("dve", lambda e: e.tensor_tensor(out=xr[b][:, hf * 512:(hf + 1) * 512], in0=ps[pb][:, :],
                                                       in1=xr[b][:, hf * 512:(hf + 1) * 512], op=ALU.add),
                      reads=[psb[pb], b_xr[b]], writes=[b_xr[b]])
            sc.dma(xres[r0:r0 + 128, :], xr[b][:], reads=[b_xr[b]], writes=[x_bufs[t]], q="pool")
        sc.barrier()


K.outproj_phase = _outproj_phase


def _final_phase(self, tag, g, xres, x_bufs, out):
    sc = self.sc
    S, NT = self.S, self.NT
    with ExitStack() as st:
        gB = self.sb(st, tag + "gB", [128, D], F32)
        xt = [self.sb(st, tag + "xt%d" % i, [128, D], F32) for i in range(2)]
        junk = self.sb(st, tag + "junk", [128, D], F32)
        stat = [self.sb(st, tag + "stat%d" % i, [128, 4], F32) for i in range(2)]
        b_gB, b_junk = Buf("gB"), Buf("junk")
        b_xt = [Buf("xt0"), Buf("xt1")]
        b_stat = [Buf("stat0"), Buf("stat1")]
        sc.dma(gB[:], g.partition_broadcast(128), writes=[b_gB])
        for t in range(NT):
            b = t % 2
            r0 = t * 128
            sc.dma(xt[b][:], xres[r0:r0 + 128, :], reads=[x_bufs[t]], writes=[b_xt[b]])
            sc.op("act", lambda e: e.activation(out=junk[:], in_=xt[b][:], func=AF.Square, accum_out=stat[b][:, 0:1]),
                  reads=[b_xt[b]], writes=[b_junk, b_stat[b]])
            sc.op("dve", lambda e: e.tensor_scalar(out=stat[b][:, 1:2], in0=stat[b][:, 0:1], scalar1=1.0 / D,
                                                   scalar2=EPS, op0=ALU.mult, op1=ALU.add),
                  reads=[b_stat[b]], writes=[b_stat[b]])
            sc.op("act", lambda e: e.sqrt(out=stat[b][:, 3:4], in_=stat[b][:, 1:2]), reads=[b_stat[b]],
                  writes=[b_stat[b]])
            sc.op("dve", lambda e: e.reciprocal(out=stat[b][:, 2:3], in_=stat[b][:, 3:4]), reads=[b_stat[b]],
                  writes=[b_stat[b]])
            sc.op("dve", lambda e: e.scalar_tensor_tensor(out=xt[b][:], in0=xt[b][:], scalar=stat[b][:, 2:3],
                                                          in1=gB[:], op0=ALU.mult, op1=ALU.mult),
                  reads=[b_xt[b], b_stat[b], b_gB], writes=[b_xt[b]])
            sc.dma(out[r0:r0 + 128, :], xt[b][:], reads=[b_xt[b]])
        sc.barrier()


K.final_phase = _final_phase

import math
WNAMES = [("ln_ffn1", [2, D]), ("ffn1_w_in", [2, D, 2 * DFF]), ("ffn1_w_out", [2, DFF, D]), ("ln_mix", [2, D]),
          ("w_mix_in", [2, D, MIX_IN]), ("conv_w", [2, 5, 1152]), ("sink_logits", [2, 6]),
          ("diff_lambda", [2, 4, 32]), ("diff_norm_g", [2, 64]), ("dn_A_log", [2, 2, 6]), ("dn_dt_bias", [2, 2, 6]),
          ("dn_norm_g", [2, 64]), ("w_mix_out", [2, D, D]), ("ln_ffn2", [2, D]), ("ffn2_w_in", [2, D, 2 * DFF]),
          ("ffn2_w_out", [2, DFF, D]), ("ln_final", [D])]


def build_full(S, depth=2, paired=False):
    k = K(S, depth, paired)
    SK = k.SK
    x = k.inp("x", [S, D])
    W = {n: k.inp(n, shp) for n, shp in WNAMES}
    cst = {n: k.inp(n, shp, BF16) for n, shp in (("daq", [4, 4, S]), ("dakp", [4, 4, SK]), ("dakm", [4, 4, SK]),
                                                  ("dbd", [4, 128, 128]), ("identb", [128, 128]))}
    wbias = k.inp("wbias", [6, 128, 384])
    dnc = k.inp("dnc", [10, 128, 128])
    idn = k.inp("ident", [128, 128])
    if paired:
        antiid = k.inp("antiid", [128, 128])
        sel_d = k.inp("sel", [128, 2])
        pr = k.pair_scratch()
    else:
        pr = sel_d = None
    out = k.outp("out", [S, D])
    xres = k.scratch("xres", [S, D])
    ms = k.mix_scratch()
    ds = k.dn_scratch()
    sc = k.sc
    with ExitStack() as st:
        ident = k.sb(st, "ident_sb", [128, 128], F32)
        ib = Buf("ident")
        sc.dma(ident[:], idn[:, :], writes=[ib])
        xb = [Buf("x%d" % i) for i in range(k.NT)]
        xin = [Buf("xin%d" % i) for i in range(k.NT)]
        for l in range(depth):
            lam_init = 0.8 - 0.6 * math.exp(-0.3 * l)
            k.ffn_phase("f1_%d" % l, W["ffn1_w_in"][l], W["ffn1_w_out"][l], W["ln_ffn1"][l],
                        x if l == 0 else xres, xin if l == 0 else xb, xres, xb, ident, ib)
            k.inproj_phase("ip%d" % l, W["w_mix_in"][l], W["ln_mix"][l], xres, xb, ms, ident, ib)
            if paired:
                k.export_phase("ex%d" % l, W["w_mix_in"][l], W["ln_mix"][l], xres, xb, pr, ident, ib, antiid)
                k.exchange_phase("xc%d" % l, pr, ms, sel_d, st)
            with ExitStack() as cst_:
                units = k.conv_units(cst_, "cu%d" % l, ms, ds, W["conv_w"][l], dnc, ident, ib)

                def hook(units=units):
                    if units:
                        units.pop(0)()
                    return len(units) > 0
                k.diffattn_phase("da%d" % l, ms, cst, W["diff_lambda"][l], W["diff_norm_g"][l], lam_init, ident, ib, hook)
            with ExitStack() as wst_:
                wunits = k.win_units(wst_, "wu%d" % l, ms, wbias, W["sink_logits"][l])

                def whook(units=wunits):
                    if units:
                        units.pop(0)()
                    return len(units) > 0
                k.dn2("d2_%d" % l, ms, ds, dnc, W["dn_A_log"][l], W["dn_dt_bias"][l], W["dn_norm_g"][l], ident, ib,
                      pr, sel_d, st, whook)
            k.outproj_phase("op%d" % l, W["w_mix_out"][l], ms, xres, xb, ident, ib)
            k.ffn_phase("f2_%d" % l, W["ffn2_w_in"][l], W["ffn2_w_out"][l], W["ln_ffn2"][l],
                        xres, xb, xres, xb, ident, ib)
        k.final_phase("fin", W["ln_final"], xres, xb, out)
        sc.finish([])
    k.stack.close()
    return k


def pair_feeds(inputs, S_full):
    x = np.ascontiguousarray(np.asarray(inputs["x"], dtype=np.float32))
    B = x.shape[0]
    S = S_full // 2
    base = {n: np.ascontiguousarray(np.asarray(inputs[n], dtype=np.float32)) for n, _ in WNAMES}
    odd = dict(base)
    odd["conv_w"] = np.ascontiguousarray(base["conv_w"][:, ::-1, :])
    wmi = base["w_mix_in"].copy()
    wmi[:, :, C_CB:C_CB + 6] = base["w_mix_in"][:, :, C_CB + 6:C_CB + 12]
    wmi[:, :, C_CB + 6:C_CB + 12] = base["w_mix_in"][:, :, C_CB:C_CB + 6]
    wmi[:, :, C_CA:C_CA + 6] = base["w_mix_in"][:, :, C_CA + 6:C_CA + 12]
    wmi[:, :, C_CA + 6:C_CA + 12] = base["w_mix_in"][:, :, C_CA:C_CA + 6]
    odd["w_mix_in"] = wmi
    odd["dn_A_log"] = np.ascontiguousarray(base["dn_A_log"][:, ::-1, :])
    odd["dn_dt_bias"] = np.ascontiguousarray(base["dn_dt_bias"][:, ::-1, :])
    common = {}
    common.update(diff_consts(S, 2 * S))
    common.update(win_consts())
    common.update(dn_consts())
    common["ident"] = np.eye(128, dtype=np.float32)
    common["antiid"] = np.ascontiguousarray(np.eye(128, dtype=np.float32)[::-1])
    maps = []
    for c in range(2 * B):
        b, r = c // 2, c % 2
        m = dict(base if r == 0 else odd)
        m.update(common)
        if r == 0:
            m["x"] = np.ascontiguousarray(x[b, 0:S])
        else:
            m["x"] = np.ascontiguousarray(x[b, S:2 * S][::-1])
        sel = np.zeros((128, 2), np.float32)
        sel[:, 1 - r] = 1.0
        m["sel"] = sel
        maps.append(m)
    return maps


def pair_gather(results, B, S_full):
    S = S_full // 2
    out = np.empty((B, S_full, D), np.float32)
    for b in range(B):
        out[b, 0:S] = results[2 * b]["out"]
        out[b, S:] = results[2 * b + 1]["out"][::-1]
    return out


GROUPS = [[0, 1], [2, 3], [4, 5], [6, 7]]


def _cdiv(a, b):
    return (a + b - 1) // b


def _pair_chunks(S):
    misc = _cdiv(128 * 128, S) + _cdiv(128 * 130, S)
    return [("bk0", 128), ("bk1", 128), ("bv0", 65), ("bv1", 65), ("bv2", 65), ("bv3", 65), ("misc", misc)]


def _pair_scratch(self):
    S = self.S
    d = {"exp": {}, "gat": {}, "rows": {}}
    for name, rows in _pair_chunks(S):
        d["exp"][name] = self.scratch("exp_" + name, [rows, S], BF16)
        d["gat"][name] = self.scratch("gat_" + name, [2 * rows, S], BF16)
        d["rows"][name] = rows
    d["expf"] = self.scratch("expf", [36, 64], F32)
    d["gatf"] = self.scratch("gatf", [72, 64], F32)
    d["exps"] = self.scratch("exps", [384, 64], F32)
    d["gats"] = self.scratch("gats", [768, 64], F32)
    return d


K.pair_scratch = _pair_scratch


def _pviews(pr, S, slot=None):
    def buf(name):
        if slot is None:
            return pr["exp"][name]
        r = pr["rows"][name]
        return pr["gat"][name][slot * r:(slot + 1) * r, :]
    v = {}
    v["bkT"] = [buf("bk0"), buf("bk1")]
    v["bv"] = [buf("bv%d" % q).rearrange("r c -> (r c)").rearrange("(t d) -> t d", d=260) for q in range(4)]
    mflat = buf("misc").rearrange("r c -> (r c)")
    o2 = _cdiv(128 * 128, S) * S
    v["akT"] = mflat[0:128 * 128].rearrange("(r c) -> r c", c=128)
    v["av"] = mflat[o2:o2 + 128 * 130].rearrange("(t d) -> t d", d=130)
    return v


def _export_phase(self, tag, w_mi, g, src, src_bufs, pr, ident, ident_b, antiid):
    sc = self.sc
    S, NT = self.S, self.NT
    ps, psb = self.ps, self.psb
    ev_ = _pviews(pr, S)
    Q4 = S // 4
    e_akT, e_av = ev_["akT"], ev_["av"]
    e_halo = pr["expf"].rearrange("r c -> (r c)").rearrange("(a b) -> a b", b=2)
    with ExitStack() as st:
        wm = self.sb(st, tag + "wm", [128, 8, MIX_IN], BF16)
        gB = self.sb(st, tag + "gB", [128, D], F32)
        J = self.sb(st, tag + "J", [128, 128], F32)
        xt = [self.sb(st, tag + "xt%d" % i, [128, D], F32) for i in range(2)]
        xn = self.sb(st, tag + "xn", [128, D], F32)
        junk = self.sb(st, tag + "junk", [128, D], F32)
        xr = [self.sb(st, tag + "xr%d" % i, [128, 8, 128], BF16) for i in range(2)]
        stat = self.sb(st, tag + "stat", [128, 8], F32)
        ok_ = [self.sb(st, tag + "ok%d" % i, [128, 2, 128], BF16) for i in range(2)]
        ov = [self.sb(st, tag + "ov%d" % i, [128, 4, 65], BF16) for i in range(2)]
        oak = self.sb(st, tag + "oak", [128, 128], BF16)
        oav = self.sb(st, tag + "oav", [128, 2, 65], BF16)
        oh = self.sb(st, tag + "oh", [128, 9, 2], F32)
        b_wm = [Buf("wm%d" % k) for k in range(8)]
        b_gB, b_J, b_xn, b_junk, b_stat, b_oak, b_oav, b_oh = (Buf(n) for n in
                                                               ("gB", "J", "xn", "junk", "stat", "oak", "oav", "oh"))
        b_xt = [Buf("xt0"), Buf("xt1")]
        b_xr = [Buf("xr0"), Buf("xr1")]
        b_ok = [Buf("ok0"), Buf("ok1")]
        b_ov = [Buf("ov0"), Buf("ov1")]
        for kc in range(8):
            sc.dma(wm[:, kc, :], w_mi[kc * 128:(kc + 1) * 128, :], writes=[b_wm[kc]], q="pool")
        sc.dma(gB[:], g.partition_broadcast(128), writes=[b_gB])
        sc.dma(J[:], antiid[:, :], writes=[b_J])
        for i in range(2):
            sc.op("pool", lambda e: e.memset(ov[i][:], 1.0), writes=[b_ov[i]])
        sc.op("pool", lambda e: e.memset(oav[:], 1.0), writes=[b_oav])
        mrows = pr["rows"]["misc"]
        zt = self.sb(st, tag + "zt", [mrows, S], BF16)
        b_zt, b_misc = Buf("zt"), Buf("misc")
        sc.op("dve", lambda e_: e_.memset(zt[:], 0.0), writes=[b_zt])
        sc.dma(pr["exp"]["misc"][:, :], zt[:], reads=[b_zt], writes=[b_misc])
        for it, t in enumerate(range(NT - 1, -1, -1)):
            e = NT - 1 - t
            b = it % 2
            sc.dma(xt[b][:], src[t * 128:(t + 1) * 128, :], reads=[src_bufs[t]], writes=[b_xt[b]])
            sc.op("act", lambda e_: e_.activation(out=junk[:], in_=xt[b][:], func=AF.Square, accum_out=stat[:, 0:1]),
                  reads=[b_xt[b]], writes=[b_junk, b_stat])
            sc.op("dve", lambda e_: e_.tensor_scalar(out=stat[:, 1:2], in0=stat[:, 0:1], scalar1=1.0 / D,
                                                     scalar2=EPS, op0=ALU.mult, op1=ALU.add),
                  reads=[b_stat], writes=[b_stat])
            sc.op("act", lambda e_: e_.sqrt(out=stat[:, 3:4], in_=stat[:, 1:2]), reads=[b_stat], writes=[b_stat])
            sc.op("dve", lambda e_: e_.reciprocal(out=stat[:, 2:3], in_=stat[:, 3:4]), reads=[b_stat], writes=[b_stat])
            sc.op("dve", lambda e_: e_.scalar_tensor_tensor(out=xn[:], in0=xt[b][:], scalar=stat[:, 2:3],
                                                            in1=gB[:], op0=ALU.mult, op1=ALU.mult),
                  reads=[b_xt[b], b_stat, b_gB], writes=[b_xn])
            for kc in range(8):
                bank = kc // 4
                sc.op("pe", lambda e_: e_.matmul(ps[bank][:, (kc % 4) * 128:(kc % 4 + 1) * 128],
                                                 lhsT=xn[:, kc * 128:(kc + 1) * 128], rhs=J[:], start=True, stop=True),
                      reads=[b_xn, b_J], writes=[psb[bank]], signal=(kc % 4 == 3))
            sc.op("act", lambda e_: e_.copy(out=xr[b][:, 0:4, :], in_=ps[0][:, :].rearrange("p (k t) -> p k t", k=4)),
                  reads=[psb[0]], writes=[b_xr[b]])
            sc.op("dve", lambda e_: e_.tensor_copy(out=xr[b][:, 4:8, :],
                                                   in_=ps[1][:, :].rearrange("p (k t) -> p k t", k=4)),
                  reads=[psb[1]], writes=[b_xr[b]])
            for ci in range(2):
                for kc in range(8):
                    sc.op("pe", lambda e_: e_.matmul(ps[2][:, ci * 128:(ci + 1) * 128],
                                                     lhsT=wm[:, kc, C_BK + ci * 128:C_BK + (ci + 1) * 128],
                                                     rhs=xr[b][:, kc, :], start=(kc == 0), stop=(kc == 7)),
                          reads=[b_wm[kc], b_xr[b]], writes=[psb[2]], signal=(kc == 7 and ci == 1))
            for kc in range(8):
                sc.op("pe", lambda e_: e_.matmul(ps[3][:, 0:256], lhsT=xr[b][:, kc, :], rhs=wm[:, kc, C_BV:C_BV + 256],
                                                 start=(kc == 0), stop=(kc == 7)),
                      reads=[b_wm[kc], b_xr[b]], writes=[psb[3]], signal=(kc == 7))
            sc.op("act", lambda e_: e_.copy(out=ok_[b][:].rearrange("p c t -> p (c t)"), in_=ps[2][:, 0:256]),
                  reads=[psb[2]], writes=[b_ok[b]])
            sc.op("dve", lambda e_: e_.tensor_copy(out=ov[b][:, :, 0:64],
                                                   in_=ps[3][:, 0:256].rearrange("p (h d) -> p h d", h=4)),
                  reads=[psb[3]], writes=[b_ov[b]])
            for ci in range(2):
                sc.dma(ev_["bkT"][ci][:, e * 128:(e + 1) * 128], ok_[b][:, ci, :], reads=[b_ok[b]])
            q4 = (e * 128) // Q4
            r4 = e * 128 - q4 * Q4
            sc.dma(ev_["bv"][q4][r4:r4 + 128, :], ov[b][:].rearrange("p h d -> p (h d)"), reads=[b_ov[b]])
            if e == 0:
                for kc in range(8):
                    sc.op("pe", lambda e_: e_.matmul(ps[4][:, 0:128], lhsT=wm[:, kc, C_AK:C_AK + 128], rhs=xr[b][:, kc, :],
                                                     start=(kc == 0), stop=(kc == 7)),
                          reads=[b_wm[kc], b_xr[b]], writes=[psb[4]], signal=(kc == 7))
                for kc in range(8):
                    sc.op("pe", lambda e_: e_.matmul(ps[5][:, 0:128], lhsT=xr[b][:, kc, :], rhs=wm[:, kc, C_AV:C_AV + 128],
                                                     start=(kc == 0), stop=(kc == 7)),
                          reads=[b_wm[kc], b_xr[b]], writes=[psb[5]], signal=(kc == 7))
                for ci in range(9):
                    for kc in range(8):
                        sc.op("pe", lambda e_: e_.matmul(ps[6][:, ci * 2:ci * 2 + 2],
                                                         lhsT=wm[:, kc, C_CQKV + ci * 128:C_CQKV + (ci + 1) * 128],
                                                         rhs=xr[b][:, kc, 0:2], start=(kc == 0), stop=(kc == 7)),
                              reads=[b_wm[kc], b_xr[b]], writes=[psb[6]], signal=(kc == 7 and ci == 8))
                sc.op("act", lambda e_: e_.copy(out=oak[:], in_=ps[4][:, 0:128]), reads=[psb[4]], writes=[b_oak])
                sc.op("dve", lambda e_: e_.tensor_copy(out=oav[:, :, 0:64],
                                                       in_=ps[5][:, 0:128].rearrange("p (h d) -> p h d", h=2)),
                      reads=[psb[5]], writes=[b_oav])
                sc.op("act", lambda e_: e_.copy(out=oh[:].rearrange("p c t -> p (c t)"), in_=ps[6][:, 0:18]),
                      reads=[psb[6]], writes=[b_oh])
                sc.dma(e_akT[:, :], oak[:], reads=[b_oak], writes=[b_misc])
                sc.dma(e_av[:, :], oav[:].rearrange("p h d -> p (h d)"), reads=[b_oav], writes=[b_misc])
                sc.dma(e_halo.rearrange("(c p) t -> p c t", p=128), oh[:], reads=[b_oh])
        sc.barrier()


K.export_phase = _export_phase


def _exchange_phase(self, tag, pr, ms, sel_d, cstack):
    sc = self.sc
    S, NT = self.S, self.NT
    for name, _r in _pair_chunks(S):
        sc.collective(cstack, "AllGather", pr["exp"][name].opt(), pr["gat"][name].opt(), GROUPS)
    Q4 = S // 4
    sc.collective(cstack, "AllGather", pr["expf"].opt(), pr["gatf"].opt(), GROUPS)
    with ExitStack() as st:
        sel = self.sb(st, tag + "sel", [128, 2], F32)
        b_sel = Buf("sel")
        sc.dma(sel[:], sel_d[:, :], writes=[b_sel])
        CW = min(S, 2048)
        a0 = [self.sb(st, tag + "a0_%d" % i, [128, CW], BF16) for i in range(2)]
        a1 = [self.sb(st, tag + "a1_%d" % i, [128, CW], BF16) for i in range(2)]
        b_a0 = [Buf("a0_0"), Buf("a0_1")]
        b_a1 = [Buf("a1_0"), Buf("a1_1")]
        f0 = self.sb(st, tag + "f0", [128, 9, 2], F32)
        f1 = self.sb(st, tag + "f1", [128, 9, 2], F32)
        b_f0, b_f1 = Buf("f0"), Buf("f1")
        cnt = [0]

        def select(dst_ap, src0, src1, np_, w, view=None):
            i = cnt[0] % 2
            cnt[0] += 1
            t0 = a0[i][0:np_, 0:w]
            t1 = a1[i][0:np_, 0:w]
            if view is not None:
                t0v, t1v = view(t0), view(t1)
            else:
                t0v, t1v = t0, t1
            sc.dma(t0v, src0, writes=[b_a0[i]])
            sc.dma(t1v, src1, writes=[b_a1[i]])
            sc.op("dve", lambda e: e.tensor_scalar_mul(out=t0, in0=t0, scalar1=sel[0:np_, 0:1]),
                  reads=[b_a0[i], b_sel], writes=[b_a0[i]])
            sc.op("dve", lambda e: e.scalar_tensor_tensor(out=t1, in0=t1, scalar=sel[0:np_, 1:2], in1=t0,
                                                          op0=ALU.mult, op1=ALU.add),
                  reads=[b_a0[i], b_a1[i], b_sel], writes=[b_a1[i]])
            sc.dma(dst_ap, t1v, reads=[b_a1[i]])

        gv = [_pviews(pr, S, 0), _pviews(pr, S, 1)]
        for ci in range(2):
            for c0 in range(0, S, CW):
                v = lambda slot: gv[slot]["bkT"][ci][:, c0:c0 + CW]
                select(ms["bkT"][ci * 128:(ci + 1) * 128, S + c0:S + c0 + CW], v(0), v(1), 128, CW)
        TPB = max(1, min(CW // 260, Q4 // 128))
        for q4 in range(4):
            for t0_ in range(0, Q4 // 128, TPB):
                tn = min(TPB, Q4 // 128 - t0_)
                v = lambda slot: gv[slot]["bv"][q4][t0_ * 128:(t0_ + tn) * 128, :].rearrange("(n p) d -> p n d", p=128)
                r0 = S + q4 * Q4 + t0_ * 128
                dst = ms["bv"][r0:r0 + tn * 128, :, :].rearrange("(n p) h d -> p n (h d)", p=128)
                select(dst, v(0), v(1), 128, tn * 260, view=lambda t: t.rearrange("p (n d) -> p n d", d=260))
        v = lambda slot: gv[slot]["akT"]
        select(ms["akT"][:, S:S + 128], v(0), v(1), 128, 128)
        v = lambda slot: gv[slot]["av"]
        select(ms["av"][S:S + 128, :, :].rearrange("p h d -> p (h d)"), v(0), v(1), 128, 130)
        hv = lambda slot: pr["gatf"][slot * 36:(slot + 1) * 36, :].rearrange("r c -> (r c)") \
            .rearrange("(c p t) -> p c t", p=128, t=2)
        sc.dma(f0[:], hv(0), writes=[b_f0])
        sc.dma(f1[:], hv(1), writes=[b_f1])
        sc.op("dve", lambda e: e.tensor_scalar_mul(out=f0[:], in0=f0[:], scalar1=sel[:, 0:1]),
              reads=[b_f0, b_sel], writes=[b_f0])
        sc.op("dve", lambda e: e.scalar_tensor_tensor(out=f1[:], in0=f1[:], scalar=sel[:, 1:2], in1=f0[:],
                                                      op0=ALU.mult, op1=ALU.add),
              reads=[b_f0, b_f1, b_sel], writes=[b_f1])
        sc.dma(ms["cpre"][:, S + 2:S + 4].rearrange("(c p) t -> p c t", p=128), f1[:], reads=[b_f1])
        sc.barrier()


K.exchange_phase = _exchange_phase


def _dn2(self, tag, ms, ds, dnc, a_log, dt_bias, dn_g, ident, ident_b, pr=None, sel_d=None, cstack=None, hook=None):
    sc = self.sc
    S, NT = self.S, self.NT
    ps, psb = self.ps, self.psb
    NIT = 7
    gcT = self.scratch(tag + "gcT", [12, S], F32)
    ngcT = self.scratch(tag + "ngcT", [12, S], F32)
    acT = self.scratch(tag + "acT", [12, S], F32)
    with ExitStack() as st0:
        def T0(name, shape, dt=F32):
            return self.sb(st0, tag + name, shape, dt), Buf(name)
        gc, b_gc = T0("gc", [128, NT, 12])
        ac, b_ac = T0("ac", [128, NT, 12])
        beta, b_beta = T0("beta", [128, NT, 12])
        ea, b_ea = T0("ea", [128, NT, 12])
        ekg, b_ekg = T0("ekg", [128, NT, 12])
        glv, b_glv = T0("glv", [128, NT, 12])
        ones, b_ones = T0("ones", [128, 128])
        sc.dma(ones[:], dnc[0], writes=[b_ones])
        with ExitStack() as st:
            def T1(name, shape, dt=F32):
                return self.sb(st, tag + "g_" + name, shape, dt), Buf(name)
            ba, b_ba = T1("ba", [128, NT, 24])
            w1, b_w1 = T1("w1", [128, NT, 12])
            w2, b_w2 = T1("w2", [128, NT, 12])
            sp, b_sp = T1("sp", [128, NT, 12])
            gg, b_gg = T1("gg", [128, NT, 12])
            tt, b_tt = T1("tt", [128, NT, 12])
            triF, b_triF = T1("triF", [128, 128])
            triB, b_triB = T1("triB", [128, 128])
            dtb, b_dtb = T1("dtb", [128, 12])
            nega, b_nega = T1("nega", [128, 12])
            ev = [T1("ev%d" % i, [12, 3, 512]) for i in range(2)]
            sc.dma(ba[:], ms["cz"][:, 384:408].rearrange("(n p) c -> p n c", p=128), writes=[b_ba])
            sc.dma(triF[:], dnc[1], writes=[b_triF])
            sc.dma(triB[:], dnc[2], writes=[b_triB])
            sc.dma(dtb[:], dt_bias.rearrange("a b -> (a b)").partition_broadcast(128), writes=[b_dtb])
            sc.dma(nega[:], a_log.rearrange("a b -> (a b)").partition_broadcast(128), writes=[b_nega])
            sc.op("act", lambda e: e.activation(out=nega[:], in_=nega[:], func=AF.Exp), reads=[b_nega], writes=[b_nega])
            sc.op("dve", lambda e: e.tensor_scalar_mul(out=nega[:], in0=nega[:], scalar1=-1.0),
                  reads=[b_nega], writes=[b_nega])
            bc = lambda t: t[:].unsqueeze(1).to_broadcast([128, NT, 12])
            sc.op("act", lambda e: e.activation(out=w1[:], in_=ba[:, :, 0:12], func=AF.Exp, scale=-1.0),
                  reads=[b_ba], writes=[b_w1])
            sc.op("dve", lambda e: e.tensor_scalar_add(out=w1[:], in0=w1[:], scalar1=1.0), reads=[b_w1], writes=[b_w1])
            sc.op("act", lambda e: e.activation(out=sp[:], in_=w1[:], func=AF.Ln), reads=[b_w1], writes=[b_sp])
            sc.op("dve", lambda e: e.tensor_tensor(out=w2[:], in0=ba[:, :, 12:24], in1=bc(dtb), op=ALU.add),
                  reads=[b_ba, b_dtb], writes=[b_w2])
            sc.op("act", lambda e: e.activation(out=w2[:], in_=w2[:], func=AF.Exp), reads=[b_w2], writes=[b_w2])
            sc.op("dve", lambda e: e.tensor_scalar_add(out=w2[:], in0=w2[:], scalar1=1.0), reads=[b_w2], writes=[b_w2])
            sc.op("act", lambda e: e.activation(out=w2[:], in_=w2[:], func=AF.Ln), reads=[b_w2], writes=[b_w2])
            sc.op("dve", lambda e: e.tensor_tensor(out=gg[:], in0=w2[:], in1=bc(nega), op=ALU.mult),
                  reads=[b_w2, b_nega], writes=[b_gg])
            NC6 = NT * 6
            for c0 in range(0, NT, 64):
                c1 = min(NT, c0 + 64)
                w = (c1 - c0) * 6
                sc.op("pe", lambda e: e.matmul(ps[0][:, 0:w], lhsT=triF[:], rhs=gg[:, c0:c1, 0:6], start=True, stop=True),
                      reads=[b_triF, b_gg], writes=[psb[0]], signal=False)
                sc.op("pe", lambda e: e.matmul(ps[1][:, 0:w], lhsT=triB[:], rhs=gg[:, c0:c1, 6:12], start=True, stop=True),
                      reads=[b_triB, b_gg], writes=[psb[1]], signal=False)
                sc.op("pe", lambda e: e.matmul(ps[2][:, 0:w], lhsT=ones[:], rhs=gg[:, c0:c1, 0:6], start=True, stop=True),
                      reads=[b_ones, b_gg], writes=[psb[2]], signal=False)
                sc.op("pe", lambda e: e.matmul(ps[3][:, 0:w], lhsT=ones[:], rhs=gg[:, c0:c1, 6:12], start=True, stop=True),
                      reads=[b_ones, b_gg], writes=[psb[3]])
                v6 = lambda b: ps[b][:, 0:w].rearrange("p (n j) -> p n j", j=6)
                sc.op("dve", lambda e: e.tensor_copy(out=gc[:, c0:c1, 0:6], in_=v6(0)), reads=[psb[0]], writes=[b_gc])
                sc.op("dve", lambda e: e.tensor_copy(out=gc[:, c0:c1, 6:12], in_=v6(1)), reads=[psb[1]], writes=[b_gc])
                sc.op("dve", lambda e: e.tensor_copy(out=tt[:, c0:c1, 0:6], in_=v6(2)), reads=[psb[2]], writes=[b_tt])
                sc.op("dve", lambda e: e.tensor_copy(out=tt[:, c0:c1, 6:12], in_=v6(3)), reads=[psb[3]], writes=[b_tt])
            sc.op("dve", lambda e: e.tensor_tensor(out=ac[:], in0=gc[:], in1=sp[:], op=ALU.subtract),
                  reads=[b_gc, b_sp], writes=[b_ac])
            sc.op("act", lambda e: e.activation(out=ea[:], in_=ac[:], func=AF.Exp), reads=[b_ac], writes=[b_ea])
            sc.op("act", lambda e: e.activation(out=beta[:], in_=sp[:], func=AF.Exp, scale=-1.0),
                  reads=[b_sp], writes=[b_beta])
            sc.op("dve", lambda e: e.tensor_tensor(out=w1[:], in0=tt[:], in1=gc[:], op=ALU.subtract),
                  reads=[b_tt, b_gc, b_w1], writes=[b_w1])
            sc.op("act", lambda e: e.activation(out=ekg[:], in_=w1[:], func=AF.Exp), reads=[b_w1], writes=[b_ekg])
            sc.op("act", lambda e: e.activation(out=glv[:], in_=tt[:], func=AF.Exp), reads=[b_tt], writes=[b_glv])
            for q0 in range(0, NT, 4):
                qn = min(4, NT - q0)
                (evt, b_evt) = ev[(q0 // 4) % 2]
                for i in range(qn):
                    n = q0 + i
                    sc.op("pe", lambda e: e.transpose(ps[4][0:12, i * 128:(i + 1) * 128], gc[:, n, :], ident[:]),
                          reads=[b_gc, ident_b], writes=[psb[4]], signal=False)
                    sc.op("pe", lambda e: e.transpose(ps[5][0:12, i * 128:(i + 1) * 128], ac[:, n, :], ident[:]),
                          reads=[b_ac, ident_b], writes=[psb[5]], signal=(i == qn - 1))
                w = qn * 128
                sc.op("dve", lambda e: e.tensor_copy(out=evt[:, 0, 0:w], in_=ps[4][0:12, 0:w]), reads=[psb[4]], writes=[b_evt])
                sc.op("dve", lambda e: e.tensor_scalar_mul(out=evt[:, 1, 0:w], in0=ps[4][0:12, 0:w], scalar1=-1.0),
                      reads=[psb[4]], writes=[b_evt])
                sc.op("act", lambda e: e.copy(out=evt[:, 2, 0:w], in_=ps[5][0:12, 0:w]), reads=[psb[5]], writes=[b_evt])
                sc.dma(gcT[:, q0 * 128:q0 * 128 + w], evt[:, 0, 0:w], reads=[b_evt])
                sc.dma(ngcT[:, q0 * 128:q0 * 128 + w], evt[:, 1, 0:w], reads=[b_evt])
                sc.dma(acT[:, q0 * 128:q0 * 128 + w], evt[:, 2, 0:w], reads=[b_evt])
            sc.barrier()
        for dirn in (0, 1):
            with ExitStack() as st:
                def T(name, shape, n=1, dt=F32):
                    ts = [self.sb(st, "%s%d%s%d" % (tag, dirn, name, i), shape, dt) for i in range(n)]
                    bs = [Buf("%s%d" % (name, i)) for i in range(n)]
                    return (ts, bs) if n > 1 else (ts[0], bs[0])
                m1, b_m1 = T("m1", [128, 128])
                m2, b_m2 = T("m2", [128, 128])
                m3, b_m3 = T("m3", [128, 128])
                gdn, b_gdn = T("gdn", [128, 64])
                kTt, b_kTt = T("kTt", [64, 6, 128], 2)
                qTt, b_qTt = T("qTt", [64, 6, 128], 2)
                kt, b_kt = T("kt", [128, 384], 2)
                vt, b_vt = T("vt", [128, 384], 2)
                Rg, b_Rg = T("Rg", [128, 6, 128], 2)
                Rn, b_Rn = T("Rn", [128, 6, 128], 2)
                Ra, b_Ra = T("Ra", [128, 6, 128], 2)
                eR, b_eR = T("eR", [64, 6, 128])
                qg, b_qg = T("qg", [64, 6, 128])
                tmp, b_tmp = T("tmp", [128, 3, 128], 6)
                E, b_E = T("E", [128, 3, 128], 6)
                W0, b_W0 = T("W0", [128, 3, 128], 6)
                W1, b_W1 = T("W1", [128, 3, 128], 6)
                qkT, b_qkT = T("qkT", [128, 128], 6)
                kg, b_kg = T("kg", [128, 64], 6)
                glI, b_glI = T("glI", [64, 64], 6)
                wTn, b_wTn = T("wTn", [64, 128], 6)
                Vn, b_Vn = T("Vn", [128, 64], 6)
                St, b_St = T("St", [64, 64], 6)
                osb, b_osb = T("osb", [128, 6, 64], 2)
                if dirn == 1:
                    gz, b_gz = T("gz", [128, 384], 2)
                    oft, b_oft = T("oft", [128, 6, 64], 2)
                    sqt, b_sqt = T("sqt", [128, 6, 64])
                    sz, b_sz = T("sz", [128, 6, 64])
                    rr, b_rr = T("rr", [128, 8])
                sc.dma(m1[:], dnc[4 + 3 * dirn], writes=[b_m1])
                sc.dma(m2[:], dnc[5 + 3 * dirn], writes=[b_m2])
                sc.dma(m3[:], dnc[6 + 3 * dirn], writes=[b_m3])
                sc.dma(gdn[:], dn_g.partition_broadcast(128), writes=[b_gdn])
                if dirn == 0 or not self.paired:
                    for h in range(6):
                        sc.op("pool", lambda e: e.memset(St[h][:], 0.0), writes=[b_St[h]])
                else:
                    sl, b_sl = T("sl", [128, 2])
                    s0, b_s0 = T("s0", [64, 6, 64])
                    s1, b_s1 = T("s1", [64, 6, 64])
                    sc.dma(sl[:], sel_d[:, :], writes=[b_sl])
                    sc.dma(s0[:], pr["gats"][0:384, :].rearrange("(h k) v -> k h v", h=6), writes=[b_s0])
                    sc.dma(s1[:], pr["gats"][384:768, :].rearrange("(h k) v -> k h v", h=6), writes=[b_s1])
                    sc.op("dve", lambda e: e.tensor_scalar_mul(out=s0[:], in0=s0[:], scalar1=sl[0:64, 0:1]),
                          reads=[b_s0, b_sl], writes=[b_s0])
                    for h in range(6):
                        sc.op("dve", lambda e: e.scalar_tensor_tensor(out=St[h][:], in0=s1[:, h, :], scalar=sl[0:64, 1:2],
                                                                      in1=s0[:, h, :], op0=ALU.mult, op1=ALU.add),
                              reads=[b_s0, b_s1, b_sl], writes=[b_St[h]])
                order = list(range(NT)) if dirn == 0 else list(range(NT - 1, -1, -1))
                j0 = dirn * 6

                def load(it):
                    n = order[it]
                    tb = it % 2
                    c0 = n * 128
                    sc.dma(kTt[tb][:], ds["ckT"][:, c0:c0 + 128].rearrange("(h d) t -> d h t", h=6), writes=[b_kTt[tb]])
                    sc.dma(qTt[tb][:], ds["cqT"][:, c0:c0 + 128].rearrange("(h d) t -> d h t", h=6), writes=[b_qTt[tb]])
                    sc.dma(kt[tb][:], ds["ck"][c0:c0 + 128, :], writes=[b_kt[tb]])
                    sc.dma(vt[tb][:], ds["cv"][c0:c0 + 128, :], writes=[b_vt[tb]])
                    sc.dma(Rg[tb][:], gcT[j0:j0 + 6, c0:c0 + 128].partition_broadcast(128), writes=[b_Rg[tb]])
                    sc.dma(Rn[tb][:], ngcT[j0:j0 + 6, c0:c0 + 128].partition_broadcast(128), writes=[b_Rn[tb]])
                    sc.dma(Ra[tb][:], acT[j0:j0 + 6, c0:c0 + 128].partition_broadcast(128), writes=[b_Ra[tb]])
                    if dirn == 1:
                        sc.dma(gz[tb][:], ms["cz"][c0:c0 + 128, 0:384], writes=[b_gz[tb]])
                        sc.dma(oft[tb][:].rearrange("p h d -> p (h d)"), ds["of"][c0:c0 + 128, :], writes=[b_oft[tb]])

                load(0)
                for it, n in enumerate(order):
                    tb = it % 2
                    c0 = n * 128
                    if it + 1 < NT:
                        load(it + 1)
                    sc.op("act", lambda e: e.activation(out=eR[:], in_=Rg[tb][0:64, :, :], func=AF.Exp),
                          reads=[b_Rg[tb]], writes=[b_eR])
                    sc.op("dve", lambda e: e.tensor_tensor(out=qg[:], in0=qTt[tb][:], in1=eR[:], op=ALU.mult),
                          reads=[b_qTt[tb], b_eR], writes=[b_qg])
                    for h in range(6):
                        bk = 2 + h
                        j = j0 + h
                        hs = slice(h * 64, (h + 1) * 64)
                        sc.op("pe", lambda e: e.matmul(ps[bk][:, 0:128], lhsT=kTt[tb][:, h, :], rhs=kTt[tb][:, h, :],
                                                       start=True, stop=True),
                              reads=[b_kTt[tb]], writes=[psb[bk]], signal=False)
                        sc.op("pe", lambda e: e.matmul(ps[bk][:, 128:256], lhsT=kTt[tb][:, h, :], rhs=qTt[tb][:, h, :],
                                                       start=True, stop=True),
                              reads=[b_kTt[tb], b_qTt[tb]], writes=[psb[bk]])
                        sc.op("dve", lambda e: e.scalar_tensor_tensor(out=tmp[h][:, 0, :], in0=Rn[tb][:, h, :],
                                                                      scalar=ac[:, n, j:j + 1], in1=m1[:],
                                                                      op0=ALU.add, op1=ALU.min),
                              reads=[b_Rn[tb], b_ac, b_m1], writes=[b_tmp[h]])
                        sc.op("dve", lambda e: e.scalar_tensor_tensor(out=tmp[h][:, 1, :], in0=Ra[tb][:, h, :],
                                                                      scalar=gc[:, n, j:j + 1], in1=m2[:],
                                                                      op0=ALU.subtract, op1=ALU.min),
                              reads=[b_Ra[tb], b_gc, b_m2], writes=[b_tmp[h]])
                        sc.op("dve", lambda e: e.scalar_tensor_tensor(out=tmp[h][:, 2, :], in0=Rg[tb][:, h, :],
                                                                      scalar=gc[:, n, j:j + 1], in1=m3[:],
                                                                      op0=ALU.subtract, op1=ALU.min),
                              reads=[b_Rg[tb], b_gc, b_m3], writes=[b_tmp[h]])
                        sc.op("act", lambda e: e.activation(out=E[h][:], in_=tmp[h][:], func=AF.Exp),
                              reads=[b_tmp[h]], writes=[b_E[h]])
                        sc.op("act", lambda e: e.activation(out=W0[h][:, 0, 0:64], in_=vt[tb][:, hs], func=AF.Copy,
                                                            scale=beta[:, n, j:j + 1]),
                              reads=[b_vt[tb], b_beta], writes=[b_W0[h]])
                        sc.op("act", lambda e: e.activation(out=W0[h][:, 0, 64:128], in_=kt[tb][:, hs], func=AF.Copy,
                                                            scale=ea[:, n, j:j + 1]),
                              reads=[b_kt[tb], b_ea], writes=[b_W0[h]])
                        sc.op("act", lambda e: e.activation(out=kg[h][:], in_=kt[tb][:, hs], func=AF.Copy,
                                                            scale=ekg[:, n, j:j + 1]),
                              reads=[b_kt[tb], b_ekg], writes=[b_kg[h]])
                        sc.op("act", lambda e: e.activation(out=glI[h][:], in_=ident[0:64, 0:64], func=AF.Copy,
                                                            scale=glv[0:64, n, j:j + 1]),
                              reads=[ident_b, b_glv], writes=[b_glI[h]])
                    for h in range(6):
                        bk = 2 + h
                        sc.op("dve", lambda e: e.scalar_tensor_tensor(out=W0[h][:, 1, :], in0=ps[bk][:, 0:128], scalar=-1.0,
                                                                      in1=E[h][:, 0, :], op0=ALU.mult, op1=ALU.mult),
                              reads=[psb[bk], b_E[h]], writes=[b_W0[h]])
                        sc.op("dve", lambda e: e.scalar_tensor_tensor(out=W0[h][:, 2, :], in0=ps[bk][:, 0:128], scalar=-1.0,
                                                                      in1=E[h][:, 1, :], op0=ALU.mult, op1=ALU.mult),
                              reads=[psb[bk], b_E[h]], writes=[b_W0[h]])
                        sc.op("dve", lambda e: e.tensor_tensor(out=qkT[h][:], in0=ps[bk][:, 128:256], in1=E[h][:, 2, :],
                                                               op=ALU.mult),
                              reads=[psb[bk], b_E[h]], writes=[b_qkT[h]])
                    WW = [(W0, b_W0), (W1, b_W1)]
                    DVE_H = (1, 3, 4, 5)
                    for i in range(NIT):
                        (Wc, b_Wc), (Wn, b_Wn) = WW[i % 2], WW[(i + 1) % 2]
                        last = (i == NIT - 1)
                        for h in range(6):
                            bk = 2 + h
                            cur = Wc[h]
                            flat = cur[:].rearrange("p a s -> p (a s)")
                            na = 128 if i >= NIT - 2 else 256
                            on_dve = h in DVE_H
                            sc.op("pe", lambda e: e.matmul(ps[bk][:, 0:na], lhsT=cur[:, 2, :], rhs=flat[:, 0:na],
                                                           start=True, stop=on_dve, skip_group_check=True),
                                  reads=[b_Wc[h]], writes=[psb[bk]], signal=(on_dve and last))
                            if not on_dve:
                                sc.op("pe", lambda e: e.matmul(ps[bk][:, 0:128], lhsT=ident[:], rhs=cur[:, 0, :],
                                                               start=False, stop=True, skip_group_check=True),
                                      reads=[b_Wc[h], ident_b], writes=[psb[bk]], signal=last)
                            if not last:
                                sc.op("pe", lambda e: e.matmul(ps[bk][:, 256:384], lhsT=cur[:, 1, :], rhs=cur[:, 2, :],
                                                               start=True, stop=True, skip_group_check=True),
                                      reads=[b_Wc[h]], writes=[psb[bk]])
                        for h in range(6):
                            bk = 2 + h
                            cur = Wc[h]
                            nflat = Wn[h][:].rearrange("p a s -> p (a s)")
                            if h in DVE_H:
                                sc.op("dve", lambda e: e.tensor_tensor(out=nflat[:, 0:128], in0=ps[bk][:, 0:128],
                                                                       in1=cur[:, 0, :], op=ALU.add),
                                      reads=[psb[bk], b_Wc[h]], writes=[b_Wn[h]])
                                if not last:
                                    lo = 128 if i < NIT - 2 else 256
                                    sc.op("dve", lambda e: e.tensor_copy(out=nflat[:, lo:384], in_=ps[bk][:, lo:384]),
                                          reads=[psb[bk]], writes=[b_Wn[h]])
                            else:
                                if last:
                                    sc.op("act", lambda e: e.copy(out=nflat[:, 0:128], in_=ps[bk][:, 0:128]),
                                          reads=[psb[bk]], writes=[b_Wn[h]])
                                elif i < NIT - 2:
                                    sc.op("act", lambda e: e.copy(out=nflat[:, 0:384], in_=ps[bk][:, 0:384]),
                                          reads=[psb[bk]], writes=[b_Wn[h]])
                                else:
                                    sc.op("act", lambda e: e.copy(out=nflat[:, 0:128], in_=ps[bk][:, 0:128]),
                                          reads=[psb[bk]], writes=[b_Wn[h]], )
                                    sc.op("act", lambda e: e.copy(out=nflat[:, 256:384], in_=ps[bk][:, 256:384]),
                                          reads=[psb[bk]], writes=[b_Wn[h]])
                        if hook is not None:
                            hook()
                    (Wf, b_Wf) = WW[NIT % 2]
                    ob = it % 2
                    for h in range(6):
                        bk = 2 + h
                        sc.op("pe", lambda e: e.transpose(ps[bk][0:64, 0:128], Wf[h][:, 0, 64:128], ident[:]),
                              reads=[b_Wf[h], ident_b], writes=[psb[bk]])
                    for h in range(6):
                        bk = 2 + h
                        if h % 2 == 0:
                            sc.op("act", lambda e: e.mul(out=wTn[h][:], in_=ps[bk][0:64, 0:128], mul=-1.0),
                                  reads=[psb[bk]], writes=[b_wTn[h]])
                        else:
                            sc.op("dve", lambda e: e.tensor_scalar_mul(out=wTn[h][:], in0=ps[bk][0:64, 0:128], scalar1=-1.0),
                                  reads=[psb[bk]], writes=[b_wTn[h]])
                    for h in range(6):
                        bk = 2 + h
                        sc.op("pe", lambda e: e.matmul(ps[bk][:, 128:192], lhsT=ident[:], rhs=Wf[h][:, 0, 0:64],
                                                       start=True, stop=False, skip_group_check=True),
                              reads=[b_Wf[h], ident_b], writes=[psb[bk]], signal=False)
                        sc.op("pe", lambda e: e.matmul(ps[bk][:, 128:192], lhsT=wTn[h][:], rhs=St[h][:],
                                                       start=False, stop=True, skip_group_check=True),
                              reads=[b_wTn[h], b_St[h]], writes=[psb[bk]])
                    for h in range(6):
                        bk = 2 + h
                        if h % 2 == 0:
                            sc.op("act", lambda e: e.copy(out=Vn[h][:], in_=ps[bk][:, 128:192]), reads=[psb[bk]], writes=[b_Vn[h]])
                        else:
                            sc.op("dve", lambda e: e.tensor_copy(out=Vn[h][:], in_=ps[bk][:, 128:192]), reads=[psb[bk]],
                                  writes=[b_Vn[h]])
                    for h in range(6):
                        bk = 2 + h
                        sc.op("pe", lambda e: e.matmul(ps[bk][:, 192:256], lhsT=qg[:, h, :], rhs=St[h][:],
                                                       start=True, stop=False, skip_group_check=True),
                              reads=[b_qg, b_St[h]], writes=[psb[bk]], signal=False)
                        sc.op("pe", lambda e: e.matmul(ps[bk][:, 192:256], lhsT=qkT[h][:], rhs=Vn[h][:],
                                                       start=False, stop=True, skip_group_check=True),
                              reads=[b_qkT[h], b_Vn[h]], writes=[psb[bk]], signal=False)
                        sc.op("pe", lambda e: e.matmul(ps[bk][0:64, 256:320], lhsT=kg[h][:], rhs=Vn[h][:],
                                                       start=True, stop=False, skip_group_check=True),
                              reads=[b_kg[h], b_Vn[h]], writes=[psb[bk]], signal=False)
                        sc.op("pe", lambda e: e.matmul(ps[bk][0:64, 256:320], lhsT=glI[h][:], rhs=St[h][:],
                                                       start=False, stop=True, skip_group_check=True),
                              reads=[b_glI[h], b_St[h]], writes=[psb[bk]])
                    for h in range(6):
                        bk = 2 + h
                        if h % 2 == 0:
                            sc.op("act", lambda e: e.copy(out=osb[ob][:, h, :], in_=ps[bk][:, 192:256]), reads=[psb[bk]],
                                  writes=[b_osb[ob]])
                            sc.op("act", lambda e: e.copy(out=St[h][:], in_=ps[bk][0:64, 256:320]), reads=[psb[bk]],
                                  writes=[b_St[h]])
                        else:
                            sc.op("dve", lambda e: e.tensor_copy(out=osb[ob][:, h, :], in_=ps[bk][:, 192:256]),
                                  reads=[psb[bk]], writes=[b_osb[ob]])
                            sc.op("dve", lambda e: e.tensor_copy(out=St[h][:], in_=ps[bk][0:64, 256:320]), reads=[psb[bk]],
                                  writes=[b_St[h]])
                    if dirn == 0:
                        sc.dma(ds["of"][c0:c0 + 128, :], osb[ob][:].rearrange("p h d -> p (h d)"), reads=[b_osb[ob]])
                    else:
                        sc.op("dve", lambda e: e.tensor_tensor(out=oft[tb][:], in0=oft[tb][:], in1=osb[ob][:], op=ALU.add),
                              reads=[b_oft[tb], b_osb[ob]], writes=[b_oft[tb]])
                        sc.op("act", lambda e: e.activation(out=sqt[:], in_=oft[tb][:], func=AF.Square),
                              reads=[b_oft[tb]], writes=[b_sqt])
                        sc.op("dve", lambda e: e.reduce_sum(out=rr[:, 0:6], in_=sqt[:], axis=AX.X), reads=[b_sqt],
                              writes=[b_rr])
                        sc.op("dve", lambda e: e.tensor_scalar(out=rr[:, 0:6], in0=rr[:, 0:6], scalar1=1.0 / 64,
                                                               scalar2=EPS, op0=ALU.mult, op1=ALU.add),
                              reads=[b_rr], writes=[b_rr])
                        sc.op("act", lambda e: e.sqrt(out=rr[:, 0:6], in_=rr[:, 0:6]), reads=[b_rr], writes=[b_rr])
                        sc.op("dve", lambda e: e.reciprocal(out=rr[:, 0:6], in_=rr[:, 0:6]), reads=[b_rr], writes=[b_rr])
                        sc.op("act", lambda e: e.activation(out=sz[:].rearrange("p h d -> p (h d)"), in_=gz[tb][:, 0:384],
                                                            func=AF.Silu), reads=[b_gz[tb]], writes=[b_sz])
                        sc.op("dve", lambda e: e.tensor_tensor(out=sz[:], in0=sz[:],
                                                                in1=gdn[:].unsqueeze(1).to_broadcast([128, 6, 64]),
                                                                op=ALU.mult), reads=[b_sz, b_gdn], writes=[b_sz])
                        sc.op("dve", lambda e: e.tensor_tensor(out=oft[tb][:], in0=oft[tb][:],
                                                                in1=rr[:, 0:6].unsqueeze(2).to_broadcast([128, 6, 64]),
                                                                op=ALU.mult), reads=[b_oft[tb], b_rr], writes=[b_oft[tb]])
                        sc.op("dve", lambda e: e.tensor_tensor(out=oft[tb][:], in0=oft[tb][:], in1=sz[:], op=ALU.mult),
                              reads=[b_oft[tb], b_sz], writes=[b_oft[tb]])
                        sc.dma(ms["omix"][c0:c0 + 128, 640:1024], oft[tb][:].rearrange("p h d -> p (h d)"),
                               reads=[b_oft[tb]])
                if dirn == 0 and self.paired:
                    for h in range(6):
                        sc.dma(pr["exps"][h * 64:(h + 1) * 64, :], St[h][:], reads=[b_St[h]])
                if dirn == 1 and hook is not None:
                    while hook():
                        pass
                sc.barrier()
            if dirn == 0 and self.paired:
                sc.collective(cstack, "AllGather", pr["exps"].opt(), pr["gats"].opt(), GROUPS)


K.dn2 = _dn2


def _conv_units(self, st, tag, ms, ds, conv_w, dnc, ident, ident_b):
    sc = self.sc
    S, NT = self.S, self.NT
    GT = 4 if NT % 4 == 0 else 1
    GW = GT * 128
    ps, psb = self.ps, self.psb
    blk1 = self.sb(st, tag + "blk1", [128, 128], F32)
    cw = self.sb(st, tag + "cw", [128, 9, 5], F32)
    xin = [self.sb(st, tag + "xin%d" % i, [128, GW + 4], F32) for i in range(2)]
    y = [self.sb(st, tag + "y%d" % i, [128, GW], F32) for i in range(2)]
    ee = [self.sb(st, tag + "ee%d" % i, [128, GW], F32) for i in range(2)]
    sq2 = [self.sb(st, tag + "sq%d" % i, [128, GW], F32) for i in range(2)]
    rs2 = [self.sb(st, tag + "rs%d" % i, [128, GW], F32) for i in range(2)]
    yn = [self.sb(st, tag + "yn%d" % i, [128, GW], F32) for i in range(2)]
    tk = [self.sb(st, tag + "tk%d" % i, [128, GW], F32) for i in range(2)]
    b_blk1, b_cw = Buf("blk1"), Buf("cw")
    b_xin = [Buf("xin0"), Buf("xin1")]
    b_y = [Buf("y0"), Buf("y1")]
    b_ee = [Buf("ee0"), Buf("ee1")]
    b_sq2 = [Buf("sq0"), Buf("sq1")]
    b_rs2 = [Buf("rs0"), Buf("rs1")]
    b_yn = [Buf("yn0"), Buf("yn1")]
    b_tk = [Buf("tk0"), Buf("tk1")]
    sc.dma(blk1[:], dnc[3], writes=[b_blk1])
    for ci in range(9):
        sc.dma(cw[:, ci, :], conv_w[:, ci * 128:(ci + 1) * 128].rearrange("j c -> c j"), writes=[b_cw],
               allow_slow_non_contiguous=True)
    units = []
    cnt = [0]

    def make(g0, ci):
        tok0 = g0 * 128
        b = cnt[0] % 2
        cnt[0] += 1
        sq, rs, b_sq, b_rs = sq2[b], rs2[b], b_sq2[b], b_rs2[b]

        def p1():
            sc.dma(xin[b][:], ms["cpre"][ci * 128:(ci + 1) * 128, tok0:tok0 + GW + 4], writes=[b_xin[b]])
            sc.op("dve", lambda e: e.tensor_scalar_mul(out=y[b][:], in0=xin[b][:, 0:GW], scalar1=cw[:, ci, 0:1]),
                  reads=[b_xin[b], b_cw], writes=[b_y[b]])
            for j in range(1, 5):
                sc.op("dve", lambda e: e.scalar_tensor_tensor(out=y[b][:], in0=xin[b][:, j:j + GW],
                                                              scalar=cw[:, ci, j:j + 1], in1=y[b][:],
                                                              op0=ALU.mult, op1=ALU.add),
                      reads=[b_xin[b], b_cw, b_y[b]], writes=[b_y[b]])
            sc.op("act", lambda e: e.activation(out=ee[b][:], in_=y[b][:], func=AF.Exp, scale=-1.0),
                  reads=[b_y[b]], writes=[b_ee[b]])
            sc.op("dve", lambda e: e.tensor_scalar_add(out=ee[b][:], in0=ee[b][:], scalar1=1.0),
                  reads=[b_ee[b]], writes=[b_ee[b]])
            sc.op("dve", lambda e: e.reciprocal(out=ee[b][:], in_=ee[b][:]), reads=[b_ee[b]], writes=[b_ee[b]])
            sc.op("dve", lambda e: e.tensor_tensor(out=y[b][:], in0=y[b][:], in1=ee[b][:], op=ALU.mult),
                  reads=[b_y[b], b_ee[b]], writes=[b_y[b]])
            if ci < 6:
                sc.op("dve", lambda e: e.tensor_tensor(out=sq[:], in0=y[b][:], in1=y[b][:], op=ALU.mult),
                      reads=[b_y[b]], writes=[b_sq])

        def p2():
            if ci < 6:
                sc.op("pe", lambda e: e.matmul(ps[6][:, :GW], lhsT=blk1[:], rhs=sq[:], start=True, stop=True),
                      reads=[b_blk1, b_sq], writes=[psb[6]])
                mul = 64.0 if ci < 3 else 1.0
                sc.op("dve", lambda e: e.tensor_scalar(out=rs[:], in0=ps[6][:, :GW], scalar1=EPS, scalar2=mul,
                                                       op0=ALU.add, op1=ALU.mult),
                      reads=[psb[6]], writes=[b_rs])
                sc.op("act", lambda e: e.activation(out=rs[:], in_=rs[:], func=AF.Ln), reads=[b_rs], writes=[b_rs])
                sc.op("act", lambda e: e.activation(out=rs[:], in_=rs[:], func=AF.Exp, scale=-0.5),
                      reads=[b_rs], writes=[b_rs])
                sc.op("dve", lambda e: e.tensor_tensor(out=yn[b][:], in0=y[b][:], in1=rs[:], op=ALU.mult),
                      reads=[b_y[b], b_rs], writes=[b_yn[b]])
                if ci < 3:
                    sc.dma(ds["cqT"][ci * 128:(ci + 1) * 128, tok0:tok0 + GW], yn[b][:], reads=[b_yn[b]])
                else:
                    sc.dma(ds["ckT"][(ci - 3) * 128:(ci - 2) * 128, tok0:tok0 + GW], yn[b][:], reads=[b_yn[b]])

        def p3():
            if ci < 6:
                src_t, src_b = yn[b], b_yn[b]
            else:
                src_t, src_b = y[b], b_y[b]
            if ci >= 3:
                for tt in range(GT):
                    sc.op("pe", lambda e: e.transpose(ps[7][:, tt * 128:(tt + 1) * 128],
                                                      src_t[:, tt * 128:(tt + 1) * 128], ident[:]),
                          reads=[src_b, ident_b], writes=[psb[7]], signal=(tt == GT - 1))
                sc.op("dve", lambda e: e.tensor_copy(out=tk[b][:], in_=ps[7][:, :GW]), reads=[psb[7]], writes=[b_tk[b]])
                dst = ds["ck"] if ci < 6 else ds["cv"]
                cc = (ci - 3) % 3
                sc.dma(dst[tok0:tok0 + GW, cc * 128:(cc + 1) * 128].rearrange("(t p) c -> p t c", p=128),
                       tk[b][:].rearrange("p (t c) -> p t c", c=128), reads=[b_tk[b]])
        return [p1, p2, p3]

    for g0 in range(0, NT, GT):
        for ci in range(9):
            units.extend(make(g0, ci))
    return units


K.conv_units = _conv_units


def _win_units(self, st, tag, ms, wbias, sink):
    sc = self.sc
    S, NT = self.S, self.NT
    NKA = NT + 1 if self.paired else NT
    ps, psb = self.ps, self.psb
    qT = self.sb(st, tag + "qT", [128, S], BF16)
    kT = self.sb(st, tag + "kT", [128, NKA * 128], BF16)
    vA = self.sb(st, tag + "vA", [128, NKA, 65], BF16)
    wb = self.sb(st, tag + "wb", [128, 384], F32)
    sT = [self.sb(st, tag + "sT%d" % i, [128, 384], F32) for i in range(2)]
    pT = [self.sb(st, tag + "pT%d" % i, [128, 384], BF16) for i in range(2)]
    ow = self.sb(st, tag + "ow", [128, NT, 64], F32)
    ou = self.sb(st, tag + "ou", [128, NT, 65], F32)
    dn_ = self.sb(st, tag + "dn", [128, NT], F32)
    es = self.sb(st, tag + "es", [128, 6], F32)
    b_qT, b_kT, b_vA, b_wb, b_ow, b_es, b_ou, b_dn = (Buf(n) for n in ("qT", "kT", "vA", "wb", "ow", "es", "ou", "dn"))
    b_sT = [Buf("sT0"), Buf("sT1")]
    b_pT = [Buf("pT0"), Buf("pT1")]
    units = []
    cnt = [0]

    def setup0():
        sc.op("dve", lambda e: e.memset(qT[:], 0.0), writes=[b_qT])
        sc.op("dve", lambda e: e.memset(kT[:], 0.0), writes=[b_kT])
        sc.dma(es[:], sink.partition_broadcast(128), writes=[b_es])
        sc.op("act", lambda e: e.activation(out=es[:], in_=es[:], func=AF.Exp), reads=[b_es], writes=[b_es])
    units.append(setup0)

    def mk_head(h):
        def f():
            kh = h // 3
            if h % 3 == 0:
                sc.dma(kT[0:64, :], ms["akT"][kh * 64:(kh + 1) * 64, :], writes=[b_kT])
                sc.dma(vA[:], ms["av"][:, kh, :].rearrange("(n p) d -> p n d", p=128), writes=[b_vA])
            sc.dma(qT[0:64, :], ms["aqT"][h * 64:(h + 1) * 64, :], writes=[b_qT])
            sc.dma(wb[:], wbias[h], writes=[b_wb])
        return f

    def mk_a(h, n):
        def f():
            js = [j for j in (0, 1, 2) if 0 <= n - 1 + j < NKA]
            c0 = js[0] * 128
            w = len(js) * 128
            sb_ = n % 2
            for jj, j in enumerate(js):
                kt = n - 1 + j
                sc.op("pe", lambda e: e.matmul(ps[0][:, jj * 128:(jj + 1) * 128], lhsT=kT[:, kt * 128:(kt + 1) * 128],
                                               rhs=qT[:, n * 128:(n + 1) * 128], start=True, stop=True),
                      reads=[b_kT, b_qT], writes=[psb[0]], signal=(jj == len(js) - 1))
            sc.op("dve", lambda e: e.tensor_tensor(out=sT[sb_][:, 0:w], in0=ps[0][:, 0:w], in1=wb[:, c0:c0 + w],
                                                   op=ALU.add),
                  reads=[psb[0], b_wb], writes=[b_sT[sb_]])
            sc.op("act", lambda e: e.activation(out=pT[sb_][:, 0:w], in_=sT[sb_][:, 0:w], func=AF.Exp),
                  reads=[b_sT[sb_]], writes=[b_pT[sb_]])
        return f

    def mk_b(h, n):
        def f():
            js = [j for j in (0, 1, 2) if 0 <= n - 1 + j < NKA]
            sb_ = n % 2
            for jj, j in enumerate(js):
                kt = n - 1 + j
                sc.op("pe", lambda e: e.matmul(ps[1][:, 0:65], lhsT=pT[sb_][:, jj * 128:(jj + 1) * 128],
                                               rhs=vA[:, kt, :], start=(jj == 0), stop=(jj == len(js) - 1)),
                      reads=[b_pT[sb_], b_vA], writes=[psb[1]], signal=(jj == len(js) - 1))
            sc.op("dve", lambda e: e.tensor_copy(out=ou[:, n, :], in_=ps[1][:, 0:65]), reads=[psb[1]], writes=[b_ou])
        return f

    def both(fb, fa):
        def f():
            fb()
            fa()
        return f

    def mk_fin(h):
        def f():
            sc.op("dve", lambda e: e.tensor_scalar(out=dn_[:], in0=ou[:, :, 64], scalar1=es[:, h:h + 1], scalar2=None,
                                                   op0=ALU.add), reads=[b_ou, b_es], writes=[b_dn])
            sc.op("dve", lambda e: e.reciprocal(out=dn_[:], in_=dn_[:]), reads=[b_dn], writes=[b_dn])
            sc.op("dve", lambda e: e.tensor_tensor(out=ow[:], in0=ou[:, :, 0:64],
                                                   in1=dn_[:].unsqueeze(2).to_broadcast([128, NT, 64]), op=ALU.mult),
                  reads=[b_ou, b_dn], writes=[b_ow])
            sc.dma(ms["omix"][:, h * 64:(h + 1) * 64].rearrange("(n p) d -> p n d", p=128), ow[:], reads=[b_ow])
        return f

    for h in range(6):
        units.append(mk_head(h))
        units.append(mk_a(h, 0))
        for n in range(1, NT):
            units.append(both(mk_b(h, n - 1), mk_a(h, n)))
        units.append(mk_b(h, NT - 1))
        units.append(mk_fin(h))
    return units


K.win_units = _win_units

import numpy as np, ml_dtypes
BF = ml_dtypes.bfloat16
def diff_consts(S, SK=None):
    SK = SK or S
    pos = np.arange(SK)
    H = (pos // 128) * 128.0
    L = (pos % 128) * 1.0
    daq = np.zeros((4, 4, SK), np.float32); dakp = np.zeros((4, 4, SK), np.float32)
    dbd = np.zeros((4, 128, 128), np.float32)
    for h in range(4):
        s = 2.0 ** (-8.0 * (h + 1) / 4)
        daq[h, 0] = -s * H; daq[h, 1] = -s * L; daq[h, 2] = 1; daq[h, 3] = 1
        dakp[h, 0] = 1; dakp[h, 1] = 1; dakp[h, 2] = s * H; dakp[h, 3] = s * L
        kk = np.arange(128)[:, None]; qq = np.arange(128)[None, :]
        dbd[h] = -s * np.abs(qq - kk)
    daq = daq[:, :, :S]
    c = dict(daq=np.ascontiguousarray(daq).astype(BF), dakp=dakp.astype(BF), dakm=(-dakp).astype(BF), dbd=dbd.astype(BF),
             identb=np.eye(128, dtype=np.float32).astype(BF))
    assert np.array_equal(c["daq"].astype(np.float32), daq) and np.array_equal(c["dakp"].astype(np.float32), dakp)
    assert np.array_equal(c["dbd"].astype(np.float32), dbd)
    return c

def win_consts():
    wb = np.zeros((6, 128, 384), np.float32)
    k = np.arange(128)[:, None]; q = np.arange(128)[None, :]
    for h in range(6):
        s = np.float32(2.0) ** np.float32(-8.0 * (h + 1) / 6)
        for j in range(3):
            rel = (j - 1) * 128 + k - q
            b = np.where(np.abs(rel) <= 128, -np.float32(s) * np.abs(rel).astype(np.float32), np.float32(-30000.0))
            wb[h, :, j * 128:(j + 1) * 128] = b
    return dict(wbias=wb)

def dn_consts():
    c = np.zeros((10, 128, 128), np.float32)
    p = np.arange(128)[:, None]; f = np.arange(128)[None, :]
    c[0] = 1.0
    c[1] = (p <= f)
    c[2] = (p >= f)
    c[3] = ((p // 64) == (f // 64))
    BIG = 30000.0
    c[4] = np.where(p > f, 0.0, -BIG)
    c[5] = np.where(f > p, 0.0, -BIG)
    c[6] = np.where(f >= p, 0.0, -BIG)
    c[7] = np.where(p < f, 0.0, -BIG)
    c[8] = np.where(f < p, 0.0, -BIG)
    c[9] = np.where(f <= p, 0.0, -BIG)
    return dict(dnc=c)


_CACHE = {}
N_CORES = 8


def _get_built(S_loc):
    if S_loc not in _CACHE:
        _CACHE[S_loc] = build_full(S_loc, 2, paired=True)
    return _CACHE[S_loc]


def kernel(**inputs):
    from concourse.bass_utils import run_bass_kernel_spmd
    x = np.asarray(inputs["x"])
    B, S, _ = x.shape
    assert 2 * B == N_CORES
    k = _get_built(S // 2)
    in_maps = pair_feeds(inputs, S)
    res = run_bass_kernel_spmd(k.nc, in_maps, core_ids=list(range(N_CORES)))
    return pair_gather(res.results, B, S)
```
